# Optimizing a Trainium2 kernel written in Bass

```python
import math
import jax
import jax.numpy as jnp
from jax import lax
import numpy as np

D_MODEL = 2048
BATCH = 4
SEQ = 4096
DEPTH = 2

GRID_W = 64
CTX_LEN = 256
N_MOD = 9
D_FF = 5632
RET_WIDTH = D_MODEL // 2
RET_HEADS = 8
RET_HEAD_DIM = RET_WIDTH // RET_HEADS
HY_WIDTH = D_MODEL - RET_WIDTH
HY_ORDER = 2
HY_SHORT = 3
HY_EMB_DIM = 33
HY_HIDDEN = 64
HY_DECAY_SHORT_PCT = 0.3
HY_DECAY_LONG_PCT = 1.5
HY_DECAY_TARGET = 1e-2
CHUNK = 128
ROPE_BASE = 10000.0
POOL_WINDOWS = (2, 4, 8, 16)
POOL_GROUP = D_MODEL // len(POOL_WINDOWS)
PROJ_WIDTH = 4 * RET_WIDTH + (HY_ORDER + 1) * HY_WIDTH
EPS = 1e-6
F32 = jnp.float32

kernel_name = "hybrid_retnet_hyena_pool_macaron"


def rms_norm(x):
    xf = x.astype(F32)
    return (xf * lax.rsqrt(jnp.mean(xf * xf, axis=-1, keepdims=True) + EPS)).astype(x.dtype)


def modulate(x, mod, base):
    return rms_norm(x) * (1.0 + mod[:, base + 1]) + mod[:, base]


def swiglu(h, w1, w3, w2):
    return (jax.nn.silu(h @ w1) * (h @ w3)) @ w2


def ffn_half(x, mod, base, w1, w3, w2):
    h = modulate(x, mod, base)
    return x + 0.5 * mod[:, base + 2] * swiglu(h, w1, w3, w2)


def axial_rotary(n, dtype):
    pos = jnp.arange(n)
    row = (pos // GRID_W).astype(F32)
    col = (pos % GRID_W).astype(F32)
    n_freq = RET_HEAD_DIM // 4
    inv = ROPE_BASE ** (-jnp.arange(n_freq, dtype=F32) / n_freq)
    ang = jnp.concatenate([row[:, None] * inv, col[:, None] * inv], axis=-1)
    return jnp.cos(ang).astype(dtype), jnp.sin(ang).astype(dtype)


def apply_rotary(x, cos, sin):
    half = x.shape[-1] // 2
    x1, x2 = x[..., :half], x[..., half:]
    c, s = cos[:, None, :], sin[:, None, :]
    return jnp.concatenate([x1 * c - x2 * s, x2 * c + x1 * s], axis=-1)


def retention_chunkwise(q, k, v, log_gamma, s0):
    b, n, h, d = q.shape
    nc = n // CHUNK
    dt = q.dtype
    lg = log_gamma.astype(F32)
    pos = jnp.arange(CHUNK, dtype=F32)
    diff = pos[:, None] - pos[None, :]
    decay_in = jnp.where(diff >= 0, jnp.exp(jnp.maximum(diff, 0.0)[None] * lg[:, None, None]), 0.0).astype(dt)
    q_dec = jnp.exp((pos + 1.0)[None, :] * lg[:, None]).astype(dt)
    k_dec = jnp.exp((CHUNK - 1.0 - pos)[None, :] * lg[:, None]).astype(dt)
    chunk_dec = jnp.exp(CHUNK * lg).astype(dt)
    qc = q.reshape(b, nc, CHUNK, h, d)
    kc = k.reshape(b, nc, CHUNK, h, d)
    vc = v.reshape(b, nc, CHUNK, h, d)
    scores = jnp.einsum('bnihd,bnjhd->bnhij', qc, kc) * decay_in
    inner = jnp.einsum('bnhij,bnjhe->bnihe', scores, vc)
    upd = jnp.einsum('bnjhd,hj,bnjhe->nbhde', kc, k_dec, vc)

    def step(s, u):
        return chunk_dec[None, :, None, None] * s + u, s

    s_fin, s_prev = lax.scan(step, s0, upd)
    cross = jnp.einsum('bnihd,hi,nbhde->bnihe', qc, q_dec, s_prev)
    return (inner + cross).reshape(b, n, h, d), s_fin


def bi_retention(q, k, v, log_gamma, s0_f, s0_b):
    out_f, s_f = retention_chunkwise(q, k, v, log_gamma[0], s0_f)
    out_b, s_b = retention_chunkwise(q[:, ::-1], k[:, ::-1], v[:, ::-1], log_gamma[1], s0_b)
    return out_f + out_b[:, ::-1], s_f, s_b


def retention_final_states(k, v, log_gamma):
    n = k.shape[1]
    pos = jnp.arange(n, dtype=F32)
    lg = log_gamma.astype(F32)
    w_f = jnp.exp((n - 1.0 - pos)[None, :] * lg[0][:, None]).astype(k.dtype)
    w_b = jnp.exp(pos[None, :] * lg[1][:, None]).astype(k.dtype)
    s_f = jnp.einsum('blhd,hl,blhe->bhde', k, w_f, v)
    s_b = jnp.einsum('blhd,hl,blhe->bhde', k, w_b, v)
    return s_f, s_b


def short_conv(u, w, bias):
    n = u.shape[1]
    pad = HY_SHORT // 2
    up = jnp.pad(u, ((0, 0), (pad, HY_SHORT - 1 - pad), (0, 0)))
    out = bias + up[:, 0:n] * w[0]
    for j in range(1, HY_SHORT):
        out = out + up[:, j:j + n] * w[j]
    return out


def hyena_filters(n, fw1, fb1, fw2, fb2, fw3, fb3, freq, fw4):
    t = jnp.linspace(0.0, 1.0, n, dtype=F32)[:, None]
    bands = (HY_EMB_DIM - 1) // 2
    f = jnp.linspace(1e-4, bands - 1, bands, dtype=F32)[None, :]
    w = 2.0 * math.pi * jnp.arange(n, dtype=F32)[:, None] / n
    z = jnp.concatenate([t, jnp.cos(f * w), -jnp.sin(f * w)], axis=-1).astype(fw1.dtype)
    a = jnp.sin(freq[0] * (z @ fw1 + fb1))
    a = jnp.sin(freq[1] * (a @ fw2 + fb2))
    a = jnp.sin(freq[2] * (a @ fw3 + fb3))
    hf = (a @ fw4).astype(F32).reshape(n, HY_ORDER, 2, HY_WIDTH)
    max_decay = math.log(HY_DECAY_TARGET) / HY_DECAY_SHORT_PCT
    min_decay = math.log(HY_DECAY_TARGET) / HY_DECAY_LONG_PCT
    deltas = jnp.abs(jnp.linspace(min_decay, max_decay, HY_WIDTH, dtype=F32))
    window = jnp.exp(-t * deltas[None, :])
    return hf * window[:, None, None, :]


def long_conv_bidir(u, h_fwd, h_bwd, bias):
    n = u.shape[1]
    k_full = jnp.concatenate([h_fwd, jnp.zeros_like(h_fwd[:1]), h_bwd[:0:-1]], axis=0)
    uf = jnp.fft.rfft(u.astype(F32), n=2 * n, axis=1)
    kf = jnp.fft.rfft(k_full, n=2 * n, axis=0)
    y = jnp.fft.irfft(uf * kf[None], n=2 * n, axis=1)[:, :n]
    return (y + u.astype(F32) * bias.astype(F32)).astype(u.dtype)


def hyena(p, conv_w, conv_b, filt, bias):
    u = short_conv(p, conv_w, conv_b)
    v, x1, x2 = jnp.split(u, HY_ORDER + 1, axis=-1)
    z = x1 * long_conv_bidir(v, filt[:, 0, 0], filt[:, 0, 1], bias[0])
    return x2 * long_conv_bidir(z, filt[:, 1, 0], filt[:, 1, 1], bias[1])


def ab_mixer(h, w_in, w_out, log_decay, conv_w, conv_b, filt, hy_bias, rot, s0_f, s0_b):
    b, n, _ = h.shape
    p = h @ w_in
    q, k, v, g = jnp.split(p[..., :4 * RET_WIDTH], 4, axis=-1)
    heads = (b, n, RET_HEADS, RET_HEAD_DIM)
    q = q.reshape(heads)
    k = k.reshape(heads) * RET_HEAD_DIM ** -0.5
    v = v.reshape(heads)
    if rot is not None:
        q = apply_rotary(q, rot[0], rot[1])
        k = apply_rotary(k, rot[0], rot[1])
    r, s_f, s_b = bi_retention(q, k, v, log_decay, s0_f, s0_b)
    r = rms_norm(r).reshape(b, n, RET_WIDTH) * jax.nn.silu(g)
    y_h = hyena(p[..., 4 * RET_WIDTH:], conv_w, conv_b, filt, hy_bias)
    y = jnp.concatenate([r, y_h], axis=-1) @ w_out
    return y, s_f, s_b


def box_mean(x, axis, win):
    n = x.shape[axis]
    lo, hi = -(win // 2), win - 1 - win // 2
    xf = x.astype(F32)
    zero = jnp.zeros_like(lax.slice_in_dim(xf, 0, 1, axis=axis))
    cs = jnp.concatenate([zero, jnp.cumsum(xf, axis=axis)], axis=axis)
    idx = jnp.arange(n)
    top = jnp.minimum(idx + hi, n - 1) + 1
    bot = jnp.maximum(idx + lo, 0)
    total = jnp.take(cs, top, axis=axis) - jnp.take(cs, bot, axis=axis)
    shape = [1] * x.ndim
    shape[axis] = n
    cnt = (top - bot).astype(F32).reshape(shape)
    return (total / cnt).astype(x.dtype)


def pool_mixer(h, w, scale, on_grid):
    b, n, d = h.shape
    if on_grid:
        rows = n // GRID_W
        hh = h.reshape(b, rows, GRID_W, d)
    else:
        hh = h
    outs = []
    for gi, win in enumerate(POOL_WINDOWS):
        xg = hh[..., gi * POOL_GROUP:(gi + 1) * POOL_GROUP]
        if on_grid:
            m = box_mean(box_mean(xg, 2, win), 1, win)
        else:
            m = box_mean(xg, 1, win)
        outs.append((m - xg) @ w[gi])
    return jnp.concatenate(outs, axis=-1).reshape(b, n, d) * scale


def setup_inputs(seed: int = 0) -> dict:
    key = jax.random.key(seed)
    ks = jax.random.split(key, 29)
    n_even = (DEPTH + 1) // 2
    n_odd = DEPTH // 2

    def nrm(k, shape, std):
        return jax.random.normal(k, shape, F32) * std

    base_decay = jnp.log(1.0 - 2.0 ** (-5.0 - jnp.arange(RET_HEADS, dtype=F32)))
    return {
        'x': nrm(ks[0], (BATCH, SEQ, D_MODEL), 1.0),
        'c': nrm(ks[1], (BATCH, D_MODEL), 1.0),
        'ctx': nrm(ks[2], (BATCH, CTX_LEN, D_MODEL), 1.0),
        'c_ctx': nrm(ks[3], (D_MODEL,), 1.0),
        'w_mod': nrm(ks[4], (DEPTH, D_MODEL, N_MOD * D_MODEL), 0.5 * D_MODEL ** -0.5),
        'b_mod': nrm(ks[5], (DEPTH, N_MOD * D_MODEL), 0.02),
        'ffn1_w1': nrm(ks[6], (DEPTH, D_MODEL, D_FF), D_MODEL ** -0.5),
        'ffn1_w3': nrm(ks[7], (DEPTH, D_MODEL, D_FF), D_MODEL ** -0.5),
        'ffn1_w2': nrm(ks[8], (DEPTH, D_FF, D_MODEL), D_FF ** -0.5),
        'ffn2_w1': nrm(ks[9], (DEPTH, D_MODEL, D_FF), D_MODEL ** -0.5),
        'ffn2_w3': nrm(ks[10], (DEPTH, D_MODEL, D_FF), D_MODEL ** -0.5),
        'ffn2_w2': nrm(ks[11], (DEPTH, D_FF, D_MODEL), D_FF ** -0.5),
        'ab_w_in': nrm(ks[12], (n_even, D_MODEL, PROJ_WIDTH), D_MODEL ** -0.5),
        'ab_w_out': nrm(ks[13], (n_even, D_MODEL, D_MODEL), D_MODEL ** -0.5),
        'ret_log_decay': base_decay * (1.0 + 0.05 * jax.random.normal(ks[14], (n_even, 2, RET_HEADS), F32)),
        'hy_conv_w': nrm(ks[15], (n_even, HY_SHORT, (HY_ORDER + 1) * HY_WIDTH), HY_SHORT ** -0.5),
        'hy_conv_b': nrm(ks[16], (n_even, (HY_ORDER + 1) * HY_WIDTH), 0.02),
        'hy_f_w1': nrm(ks[17], (n_even, HY_EMB_DIM, HY_HIDDEN), HY_EMB_DIM ** -0.5),
        'hy_f_b1': nrm(ks[18], (n_even, HY_HIDDEN), 0.1),
        'hy_f_w2': nrm(ks[19], (n_even, HY_HIDDEN, HY_HIDDEN), HY_HIDDEN ** -0.5),
        'hy_f_b2': nrm(ks[20], (n_even, HY_HIDDEN), 0.1),
        'hy_f_w3': nrm(ks[21], (n_even, HY_HIDDEN, HY_HIDDEN), HY_HIDDEN ** -0.5),
        'hy_f_b3': nrm(ks[22], (n_even, HY_HIDDEN), 0.1),
        'hy_f_freq': 1.0 + nrm(ks[23], (n_even, 3, HY_HIDDEN), 0.1),
        'hy_f_w4': nrm(ks[24], (n_even, HY_HIDDEN, HY_ORDER * 2 * HY_WIDTH), 0.005),
        'hy_bias': nrm(ks[25], (n_even, HY_ORDER, HY_WIDTH), 0.5),
        'pool_w': nrm(ks[26], (n_odd, len(POOL_WINDOWS), POOL_GROUP, POOL_GROUP), POOL_GROUP ** -0.5),
        'pool_scale': 1.0 + nrm(ks[27], (n_odd, D_MODEL), 0.1),
        'final_gain': 1.0 + nrm(ks[28], (D_MODEL,), 0.1),
    }


def reference(x, c, ctx, c_ctx, w_mod, b_mod, ffn1_w1, ffn1_w3, ffn1_w2, ffn2_w1, ffn2_w3, ffn2_w2,
              ab_w_in, ab_w_out, ret_log_decay, hy_conv_w, hy_conv_b, hy_f_w1, hy_f_b1, hy_f_w2, hy_f_b2,
              hy_f_w3, hy_f_b3, hy_f_freq, hy_f_w4, hy_bias, pool_w, pool_scale, final_gain):
    b, n, d = x.shape
    n_ctx = ctx.shape[1]
    rot = axial_rotary(n, x.dtype)
    sc = jax.nn.silu(c)
    sc_ctx = jax.nn.silu(c_ctx)[None, :]
    for i in range(DEPTH):
        ctx_out = any(j % 2 == 0 for j in range(i + 1, DEPTH))
        ctx_in = (i % 2 == 0) or ctx_out
        mod = (sc @ w_mod[i] + b_mod[i]).reshape(b, N_MOD, 1, d)
        x = ffn_half(x, mod, 0, ffn1_w1[i], ffn1_w3[i], ffn1_w2[i])
        h = modulate(x, mod, 3)
        if ctx_in:
            mod_c = (sc_ctx @ w_mod[i] + b_mod[i]).reshape(1, N_MOD, 1, d)
            ctx = ffn_half(ctx, mod_c, 0, ffn1_w1[i], ffn1_w3[i], ffn1_w2[i])
            h_c = modulate(ctx, mod_c, 3)
        if i % 2 == 0:
            e = i // 2
            fparams = (hy_f_w1[e], hy_f_b1[e], hy_f_w2[e], hy_f_b2[e], hy_f_w3[e], hy_f_b3[e], hy_f_freq[e], hy_f_w4[e])
            if ctx_out:
                zero = jnp.zeros((ctx.shape[0], RET_HEADS, RET_HEAD_DIM, RET_HEAD_DIM), ctx.dtype)
                y_c, s_f, s_b = ab_mixer(h_c, ab_w_in[e], ab_w_out[e], ret_log_decay[e], hy_conv_w[e], hy_conv_b[e],
                                         hyena_filters(n_ctx, *fparams), hy_bias[e], None, zero, zero)
            else:
                heads_c = (ctx.shape[0], n_ctx, RET_HEADS, RET_HEAD_DIM)
                k_c = (h_c @ ab_w_in[e][:, RET_WIDTH:2 * RET_WIDTH]).reshape(heads_c) * RET_HEAD_DIM ** -0.5
                v_c = (h_c @ ab_w_in[e][:, 2 * RET_WIDTH:3 * RET_WIDTH]).reshape(heads_c)
                s_f, s_b = retention_final_states(k_c, v_c, ret_log_decay[e])
            y, _, _ = ab_mixer(h, ab_w_in[e], ab_w_out[e], ret_log_decay[e], hy_conv_w[e], hy_conv_b[e],
                               hyena_filters(n, *fparams), hy_bias[e], rot, s_f, s_b)
        else:
            o = i // 2
            y = pool_mixer(h, pool_w[o], pool_scale[o], True)
            if ctx_out:
                y_c = pool_mixer(h_c, pool_w[o], pool_scale[o], False)
        x = x + mod[:, 5] * y
        x = ffn_half(x, mod, 6, ffn2_w1[i], ffn2_w3[i], ffn2_w2[i])
        if ctx_out:
            ctx = ctx + mod_c[:, 5] * y_c
            ctx = ffn_half(ctx, mod_c, 6, ffn2_w1[i], ffn2_w3[i], ffn2_w2[i])
    return rms_norm(x) * final_gain
```

```python
import math
from contextlib import ExitStack
import numpy as np
import ml_dtypes
import concourse.bass as bass
import concourse.mybir as mybir
from concourse.bass_utils import run_bass_kernel_spmd

F32 = mybir.dt.float32
BF16 = mybir.dt.bfloat16
I32 = mybir.dt.int32
ALU = mybir.AluOpType
AF = mybir.ActivationFunctionType
NCORES = 8


class Cfg:
    def __init__(s, D=2048, DFF=5632, N=4096, NCTX=256, NH=8, GRID_W=64):
        s.B = 4
        s.D, s.DFF, s.N, s.NCTX, s.NH, s.GRID_W = D, DFF, N, NCTX, NH, GRID_W
        s.HD = 128
        s.RW = NH * 128
        s.HW = D - s.RW
        assert s.RW == D // 2
        s.PROJ = 4 * s.RW + 3 * s.HW
        s.T = N // 2
        s.DT = D // 128
        s.FT = DFF // 128
        s.PT = s.PROJ // 128
        s.HC = s.HW // 2
        s.HCT = s.HC // 128
        s.G = D // 4
        s.GT = s.G // 128
        s.NMODT = 9 * s.DT
        s.MODI = s.NMODT // 8
        assert s.NMODT % 8 == 0
        s.ROWS = s.T // GRID_W
        s.NCH = s.T // 128
        s.NF = N
        s.EPS = 1e-6


class CSem:
    def __init__(s, nc, name):
        s.h = nc.alloc_semaphore(name)
        s.v = 0
        s.name = name


class Eng:
    def __init__(s, ctx, e, name):
        s.ctx, s.e, s.name = ctx, e, name
        s.sem = CSem(ctx.nc, "p_" + name)
        s.seen = {}

    def wait(s, ev):
        if ev is None:
            return
        sem, v = ev
        if s.seen.get(sem, 0) >= v:
            return
        s.e.wait_ge(sem.h, v)
        s.seen[sem] = v

    def tag(s, inst):
        s.sem.v += 1
        inst.then_inc(s.sem.h, 1)
        return (s.sem, s.sem.v)


class Buf:
    def __init__(s, t=None, name=""):
        s.t = t
        s.name = name
        s.w = None
        s.r = {}
        s.dsem = None

    def __getitem__(s, idx):
        return s.t[idx]


class Ctx:
    def __init__(s, nc):
        s.nc = nc
        s.pe = Eng(s, nc.tensor, "pe")
        s.act = Eng(s, nc.scalar, "act")
        s.dve = Eng(s, nc.vector, "dve")
        s.pool = Eng(s, nc.gpsimd, "pool")
        s.sp = Eng(s, nc.sync, "sp")
        s.engs = [s.pe, s.act, s.dve, s.pool, s.sp]
        s.free_dsems = []
        s.used_dsems = []
        s.all_dsems = []
        s.bar = CSem(nc, "bar")
        s.ccsem = CSem(nc, "cc")
        s.gsem = CSem(nc, "gdma")
        s.bufs = []
        s.nsem = 0

    def op(s, eng, fn, reads=(), writes=(), tag=True):
        for b in reads:
            eng.wait(b.w)
        for b in writes:
            eng.wait(b.w)
            for sem, v in list(b.r.items()):
                eng.wait((sem, v))
        inst = fn()
        if tag:
            ev = eng.tag(inst)
            s.mark(ev, reads, writes)
        return inst

    def mark(s, ev, reads, writes):
        for b in reads:
            if b.r.get(ev[0], 0) < ev[1]:
                b.r[ev[0]] = ev[1]
        for b in writes:
            b.w = ev
            b.r = {}

    def prewait(s, eng, reads=(), writes=()):
        for b in reads:
            eng.wait(b.w)
        for b in writes:
            eng.wait(b.w)
            for sem, v in list(b.r.items()):
                eng.wait((sem, v))

    def _dsem(s, b):
        if b.dsem is None:
            if s.free_dsems:
                b.dsem = s.free_dsems.pop()
            else:
                b.dsem = CSem(s.nc, "d%d" % len(s.all_dsems))
                s.all_dsems.append(b.dsem)
            s.used_dsems.append(b.dsem)
            s.bufs.append(b)
        return b.dsem

    def load(s, q, sb, out_ap, in_ap, multi=False, **kw):
        sem = s._dsem(sb)
        if not (multi and sb.w is not None and sb.w[0] is sem):
            q.wait(sb.w)
        for se, v in list(sb.r.items()):
            q.wait((se, v))
        inst = q.e.dma_start(out=out_ap, in_=in_ap, **kw)
        sem.v += 16
        inst.then_inc(sem.h, 16)
        sb.w = (sem, sem.v)
        sb.r = {}
        return inst

    def store(s, q, sb, out_ap, in_ap, **kw):
        sem = s._dsem(sb)
        q.wait(sb.w)
        inst = q.e.dma_start(out=out_ap, in_=in_ap, **kw)
        sem.v += 16
        inst.then_inc(sem.h, 16)
        sb.r[sem] = sem.v
        return inst

    def gdma(s, out_ap, in_ap, **kw):
        inst = s.nc.gpsimd.dma_start(out=out_ap, in_=in_ap, **kw)
        s.gsem.v += 16
        inst.then_inc(s.gsem.h, 16)
        return (s.gsem, s.gsem.v)

    def allgather(s, in_ap, out_ap, groups):
        inst = s.nc.gpsimd.collective_compute("AllGather", ALU.bypass, replica_groups=groups,
                                              ins=[in_ap], outs=[out_ap])
        s.ccsem.v += 1
        inst.then_inc(s.ccsem.h, 1)
        return (s.ccsem, s.ccsem.v)

    def barrier(s):
        evs = []
        for e in s.engs:
            if e.sem.v > 0:
                evs.append((e.sem, e.sem.v))
        for d in s.used_dsems:
            evs.append((d, d.v))
        if s.gsem.v:
            evs.append((s.gsem, s.gsem.v))
        if s.ccsem.v:
            evs.append((s.ccsem, s.ccsem.v))
        for ev in evs:
            s.sp.wait(ev)
        s.bar.v += 1
        s.nc.sync.sem_inc(s.bar.h, 1)
        for e in s.engs:
            if e is not s.sp:
                e.wait((s.bar, s.bar.v))
                for ev in evs:
                    e.seen[ev[0]] = max(e.seen.get(ev[0], 0), ev[1])
        for b in s.bufs:
            b.dsem = None
        s.bufs = []
        s.free_dsems.extend(s.used_dsems)
        s.used_dsems = []


class Prog:
    def __init__(s, cfg, stop_after=None):
        s.cfg = cfg
        s.stop_after = stop_after
        s.nc = bass.Bass("TRN2", target_bir_lowering=False)
        s.cx = Ctx(s.nc)
        s.inputs = {}
        s.inp_t = {}
        s.uid = 0
        s.es = ExitStack()

    def inp(s, name, shape, dt=F32):
        if name in s.inputs:
            assert s.inputs[name][0] == tuple(shape)
            return s.inp_t[name]
        t = s.nc.dram_tensor(name, list(shape), dt, kind="ExternalInput")
        s.inputs[name] = (tuple(shape), dt)
        s.inp_t[name] = t
        return t

    def dram(s, name, shape, dt):
        return s.nc.dram_tensor(name, list(shape), dt)

    def sb(s, st, name, shape, dt):
        s.uid += 1
        t = st.enter_context(s.nc.sbuf_tensor("s%d_%s" % (s.uid, name), list(shape), dt))
        return Buf(t, name)

    def ps(s, st, name, shape, dt=F32):
        s.uid += 1
        t = st.enter_context(s.nc.psum_tensor("p%d_%s" % (s.uid, name), list(shape), dt))
        return Buf(t, name)

    def castgather(s, name, rows, cols):
        cx = s.cx
        sh = s.inp(name, [rows // 8, cols])
        tmp = s.dram(name + "_b", [rows // 8, cols], BF16)
        full = s.dram(name + "_f", [rows, cols], BF16)
        n = cols
        step = 2048
        if n <= step:
            ev = cx.gdma(tmp[:, :], sh[:, :])
        else:
            assert n % step == 0 or True
            ev = cx.gdma(tmp[:, :], sh[:, :], max_dma_last_dim=step * 4)
        cx.pool.wait(ev)
        cx.allgather(tmp[:, :], full[:, :], [list(range(NCORES))])
        return full

    def gather_bf(s, name, rows, cols):
        cx = s.cx
        sh = s.inp(name, [rows // 8, cols], BF16)
        tmp = s.dram(name + "_b", [rows // 8, cols], BF16)
        full = s.dram(name + "_f", [rows, cols], BF16)
        ev = cx.gdma(tmp[:, :], sh[:, :])
        cx.pool.wait(ev)
        cx.allgather(tmp[:, :], full[:, :], [list(range(NCORES))])
        return full


def _acts(P):
    return P.cx.act, P.cx.dve, P.cx.pe, P.cx.sp, P.cx.pool


class Phases(Prog):
    def wprep_begin(s):
        s.wsem = CSem(s.nc, "wcast")
        s.wcc = CSem(s.nc, "wcc")
        s.wpending = []
        s.wpieces = {}
        s.W = {}

    def wcast(s, key, name, rows, cols, dt_in=F32):
        rows_p = 128
        while rows_p * cols > 256 * 1024 or (rows // 8) % rows_p != 0:
            rows_p //= 2
        assert rows_p >= 1
        npc = (rows // 8) // rows_p
        sh = s.inp(name, [rows // 8, cols], dt_in)
        tmp = s.dram(name + "_b", [rows // 8, cols], BF16)
        full = s.dram(name + "_f", [rows, cols], BF16)
        s.wpieces[name] = rows_p
        for k in range(npc):
            s.wpending.append((key, sh, tmp, full, k, rows_p, cols, dt_in, k == npc - 1))

    def wlocal(s, key, name, rows, cols):
        sh = s.inp(name, [rows, cols], F32)
        full = s.dram(name + "_f", [rows, cols], BF16)
        inst = s.nc.gpsimd.dma_start(out=full[:, :], in_=sh[:, :])
        s.wsem.v += 16
        inst.then_inc(s.wsem.h, 16)
        s.cx.pool.wait((s.wsem, s.wsem.v))
        s.W[key] = (full, (s.wsem, s.wsem.v), [(s.wsem, s.wsem.v)], rows)

    def wflush(s, chunk=3, max_pieces=None):
        quads = [[0, 1, 2, 3], [4, 5, 6, 7]]
        pairs = [[0, 4], [1, 5], [2, 6], [3, 7]]
        pend = sorted(s.wpending, key=lambda x: x[4])
        s.wpending = []
        if max_pieces is not None and len(pend) > max_pieces:
            s.wpending = pend[max_pieces:]
            pend = pend[:max_pieces]
        pool = s.cx.pool
        for i0 in range(0, len(pend), chunk):
            grp = pend[i0:i0 + chunk]
            for key, sh, tmp, full, k, rp, cols, dt_in, last in grp:
                kw = {}
                if dt_in == F32 and cols > 2048:
                    kw["max_dma_last_dim"] = 2048 * 4
                inst = s.nc.gpsimd.dma_start(out=tmp[k * rp:(k + 1) * rp, :], in_=sh[k * rp:(k + 1) * rp, :], **kw)
                s.wsem.v += 16
                inst.then_inc(s.wsem.h, 16)
            pool.wait((s.wsem, s.wsem.v))
            mids = []
            for key, sh, tmp, full, k, rp, cols, dt_in, last in grp:
                mid = s.dram("wq%d" % s.uid, [4 * rp, cols], BF16)
                s.uid += 1
                inst = s.nc.gpsimd.collective_compute("AllGather", ALU.bypass, replica_groups=quads,
                                                      ins=[tmp[k * rp:(k + 1) * rp, :]], outs=[mid[:, :]])
                s.wcc.v += 1
                inst.then_inc(s.wcc.h, 1)
                mids.append(mid)
            pool.wait((s.wcc, s.wcc.v))
            for (key, sh, tmp, full, k, rp, cols, dt_in, last), mid in zip(grp, mids):
                inst = s.nc.gpsimd.collective_compute("AllGather", ALU.bypass, replica_groups=pairs,
                                                      ins=[mid[:, :]], outs=[full[k * 8 * rp:(k + 1) * 8 * rp, :]])
                s.wcc.v += 1
                inst.then_inc(s.wcc.h, 1)
                if key not in s.W:
                    s.W[key] = (full, None, [], 8 * rp)
                f_, _, evs_, rpp_ = s.W[key]
                assert len(evs_) == k
                evs_.append((s.wcc, s.wcc.v))
                s.W[key] = (f_, (s.wcc, s.wcc.v), evs_, rpp_)
            pool.wait((s.wcc, s.wcc.v))

    def consts(s):
        c = s.cfg
        st = s.es
        cx = s.cx
        s.ident = s.sb(st, "ident", [128, 128], F32)
        s.identb = s.sb(st, "identb", [128, 128], BF16)
        s.onesb = s.sb(st, "onesb", [128, 128], BF16)
        idt = s.inp("ident", [128, 128])
        cx.load(cx.sp, s.ident, s.ident[:, :], idt[:, :])
        cx.op(cx.dve, lambda: s.nc.vector.tensor_copy(out=s.identb[:, :], in_=s.ident[:, :]),
              reads=[s.ident], writes=[s.identb])
        cx.op(cx.dve, lambda: s.nc.vector.memset(s.onesb[:, :], 1.0), writes=[s.onesb])
        s.epsc = s.sb(st, "epsc", [128, 1], F32)
        cx.op(cx.dve, lambda: s.nc.vector.memset(s.epsc[:, :], c.EPS), writes=[s.epsc])
        pid = s.nc.sync.partition_id()
        s.rank_sp = pid % 2
        s.b_sp = pid // 2
        pidg = s.nc.gpsimd.partition_id()
        s.rank_g = pidg % 2

    def phase_mod(s):
        c = s.cfg
        nc, cx = s.nc, s.cx
        act, dve, pe, sp, pool = _acts(s)
        ccT_in = s.inp("ccT", [128, c.DT, 8])
        s.modT = [s.sb(s.es, "modT%d" % l, [128, 8, c.MODI], F32) for l in range(2)]
        s.modC = s.sb(s.es, "modC", [128, 8, c.MODI], F32)
        with ExitStack() as st:
            scT = s.sb(st, "scT", [128, c.DT, 8], F32)
            cx.load(sp, scT, scT[:, :, :], ccT_in[:, :, :])
            cx.op(act, lambda: nc.scalar.activation(out=scT[:, :, :], in_=scT[:, :, :], func=AF.Silu),
                  reads=[scT], writes=[scT])
            wts = [s.sb(st, "wmt%d" % i, [128, c.DT * 128], F32) for i in range(2)]
            pss = [s.ps(st, "modps%d" % i, [128, 8]) for i in range(2)]
            k = 0
            for l in range(2):
                wm = s.inp("wmod%d" % l, [c.MODI, 128, c.DT * 128])
                bm_in = s.inp("bmod%d" % l, [128, c.MODI])
                bm = s.sb(st, "bm%d" % l, [128, c.MODI], F32)
                cx.load(sp, bm, bm[:, :], bm_in[:, :])
                modS = s.sb(st, "modS%d" % l, [128, c.MODI, 8], F32)
                modR = s.sb(st, "modR%d" % l, [128, 8, c.MODI], F32)
                for i in range(c.MODI):
                    wt = wts[k % 2]
                    ps = pss[k % 2]
                    k += 1
                    cx.load(sp, wt, wt[:, :], wm[i, :, :])
                    cx.prewait(pe, reads=[wt, scT], writes=[ps])
                    for kt in range(c.DT):
                        inst = nc.tensor.matmul(ps[:, :], lhsT=wt[:, kt * 128:(kt + 1) * 128],
                                                rhs=scT[:, kt, :], start=(kt == 0), stop=(kt == c.DT - 1))
                    cx.mark(pe.tag(inst), [wt, scT], [ps])
                    cx.op(act, lambda: nc.scalar.activation(out=modS[:, i, :], in_=ps[:, :], func=AF.Identity,
                                                            bias=bm[:, i:i + 1], scale=1.0),
                          reads=[ps, bm], writes=[modS])
                cx.op(dve, lambda: nc.vector.tensor_copy(out=modR[:, :, :],
                                                         in_=modS[:, :, :].rearrange("p i r -> p r i")),
                      reads=[modS], writes=[modR])
                msh = s.dram("modsh%d" % l, [8 * 128, c.MODI], F32)
                mfull = s.dram("modfull%d" % l, [64 * 128, c.MODI], F32)
                cx.store(sp, modR, msh.ap().rearrange("(r p) i -> p r i", p=128), modR[:, :, :])
                pool.wait((modR.dsem, modR.r[modR.dsem]))
                mmid = s.dram("modmid%d" % l, [32 * 128, c.MODI], F32)
                ev = cx.allgather(msh[:, :], mmid[:, :], [[0, 1, 2, 3], [4, 5, 6, 7]])
                pool.wait(ev)
                ev = cx.allgather(mmid[:, :], mfull[:, :], [[0, 4], [1, 5], [2, 6], [3, 7]])
                pool.wait(ev)
                sp.wait(ev)
                mf3 = mfull.ap().rearrange("(j r p) i -> j r p i", r=8, p=128)
                mf4 = mfull.ap().rearrange("(j r p) i -> r p j i", r=8, p=128)
                cx.load(sp, s.modT[l], s.modT[l][:, :, :],
                        mf4[bass.ds(s.b_sp, 1), :, :, :].rearrange("o p j i -> p (o j) i"))
                if l == 0:
                    cx.load(sp, s.modC, s.modC[:, :, :], mf4[4, :, :, :])
            cx.barrier()
        for l in range(2):
            s.post_mod(s.modT[l])
        s.post_mod(s.modC)

    def post_mod(s, mt):
        c = s.cfg
        nc, cx = s.nc, s.cx
        flat = mt.t[:, :, :].rearrange("p j i -> p (j i)")
        for m in (1, 4, 7):
            cx.op(cx.dve, lambda: nc.vector.tensor_scalar(out=flat[:, m * c.DT:(m + 1) * c.DT],
                                                          in0=flat[:, m * c.DT:(m + 1) * c.DT],
                                                          scalar1=1.0, scalar2=None, op0=ALU.add),
                  reads=[mt], writes=[mt])
        for m in (2, 8):
            cx.op(cx.dve, lambda: nc.vector.tensor_scalar(out=flat[:, m * c.DT:(m + 1) * c.DT],
                                                          in0=flat[:, m * c.DT:(m + 1) * c.DT],
                                                          scalar1=0.5, scalar2=None, op0=ALU.mult),
                  reads=[mt], writes=[mt])

    def modcol(s, mt, m, dt):
        g = m * s.cfg.DT + dt
        return mt.t[:, :, :].rearrange("p j i -> p (j i)")[:, g:g + 1]

    def phase_xin(s, src, dstT, TOK):
        c = s.cfg
        nc, cx = s.nc, s.cx
        act, dve, pe, sp, pool = _acts(s)
        GS = min(4, TOK // 128)
        with ExitStack() as st:
            xin = [s.sb(st, "xin%d" % i, [128, c.D], F32) for i in range(2)]
            xo = [s.sb(st, "xo%d" % i, [128, c.DT, GS * 128], F32) for i in range(2)]
            pst = [s.ps(st, "xps%d" % i, [128, 4, 128]) for i in range(4)]
            k = 0
            for g in range(TOK // (GS * 128)):
                o = xo[g % 2]
                for j in range(GS):
                    tt = g * GS + j
                    xi = xin[tt % 2]
                    cx.load(sp, xi, xi[:, :], src[tt * 128:(tt + 1) * 128, :])
                    for q in range(c.DT // 4):
                        p = pst[k % 4]
                        cx.prewait(pe, reads=[xi, s.ident], writes=[p])
                        for u in range(4):
                            dt = q * 4 + u
                            inst = nc.tensor.transpose(out=p[:, u, :], in_=xi[:, dt * 128:(dt + 1) * 128],
                                                       identity=s.ident[:, :])
                        cx.mark(pe.tag(inst), [xi], [p])
                        dst = o[:, q * 4:(q + 1) * 4, j * 128:(j + 1) * 128]
                        if k % 2 == 0:
                            cx.op(act, lambda: nc.scalar.copy(out=dst, in_=p[:, :, :]), reads=[p], writes=[o])
                        else:
                            cx.op(dve, lambda: nc.vector.tensor_copy(out=dst, in_=p[:, :, :]), reads=[p], writes=[o])
                        k += 1
                cx.store(sp, o, dstT.ap().rearrange("k p t -> p k t")[:, :, g * GS * 128:(g + 1) * GS * 128],
                         o[:, :, :])
            cx.barrier()

    def phase_norm(s, srcT, dstT, TOK, mt, m_shift, out_dt, name, gain=None):
        c = s.cfg
        nc, cx = s.nc, s.cx
        act, dve, pe, sp, pool = _acts(s)
        BLK = min(512, TOK)
        with ExitStack() as st:
            xs = [[s.sb(st, "%sx%d_%d" % (name, S, d), [128, BLK], F32) for d in range(c.DT)] for S in range(2)]
            sq = [s.sb(st, "%ssq%d" % (name, i), [128, BLK], BF16) for i in range(3)]
            ssq = [s.ps(st, "%sssq%d" % (name, i), [128, BLK]) for i in range(2)]
            sd = [s.sb(st, "%ssd%d" % (name, i), [128, BLK], F32) for i in range(2)]
            rstd = [s.sb(st, "%srs%d" % (name, i), [128, BLK], F32) for i in range(2)]
            tmp = [s.sb(st, "%stmp%d" % (name, i), [128, BLK], F32) for i in range(3)]
            ho = [s.sb(st, "%sho%d" % (name, i), [128, BLK], out_dt) for i in range(3)]
            for tb in range(TOK // BLK):
                S = tb % 2
                sl = slice(tb * BLK, (tb + 1) * BLK)
                for dt in range(c.DT):
                    x = xs[S][dt]
                    cx.load(sp, x, x[:, :], srcT[dt, :, sl])
                    q = sq[dt % 3]
                    cx.op(act, lambda: nc.scalar.activation(out=q[:, :], in_=x[:, :], func=AF.Square),
                          reads=[x], writes=[q])
                    cx.op(pe, lambda: nc.tensor.matmul(ssq[S][:, :], lhsT=s.onesb[:, :], rhs=q[:, :],
                                                       start=(dt == 0), stop=(dt == c.DT - 1)),
                          reads=[q, s.onesb], writes=[ssq[S]])
                cx.op(act, lambda: nc.scalar.activation(out=sd[S][:, :], in_=ssq[S][:, :], func=AF.Sqrt,
                                                        bias=s.epsc[:, 0:1], scale=1.0 / c.D),
                      reads=[ssq[S], s.epsc], writes=[sd[S]])
                cx.op(dve, lambda: nc.vector.reciprocal(out=rstd[S][:, :], in_=sd[S][:, :]),
                      reads=[sd[S]], writes=[rstd[S]])
                for dt in range(c.DT):
                    x = xs[S][dt]
                    t = tmp[dt % 3]
                    h = ho[dt % 3]
                    cx.op(dve, lambda: nc.vector.tensor_tensor(out=t[:, :], in0=x[:, :], in1=rstd[S][:, :],
                                                               op=ALU.mult),
                          reads=[x, rstd[S]], writes=[t])
                    if gain is None:
                        sc_ap = s.modcol(mt, m_shift + 1, dt)
                        sh_ap = s.modcol(mt, m_shift, dt)
                        cx.op(act, lambda: nc.scalar.activation(out=h[:, :], in_=t[:, :], func=AF.Identity,
                                                                bias=sh_ap, scale=sc_ap),
                              reads=[t, mt], writes=[h])
                    else:
                        cx.op(act, lambda: nc.scalar.activation(out=h[:, :], in_=t[:, :], func=AF.Copy,
                                                                scale=gain[:, dt:dt + 1]),
                              reads=[t, gain], writes=[h])
                    cx.store(sp, h, dstT[dt, :, sl], h[:, :])
            cx.barrier()

    def linear(s, name, mvT, KT, TOK, tok0, Wkeys, mt_list, SB, epi, wrow0=0, kt0=0, pre=None):
        c = s.cfg
        nc, cx = s.nc, s.cx
        act, dve, pe, sp, pool = _acts(s)
        nW = len(Wkeys)
        NSET = 3 if nW == 2 else 4
        NSLOT = 4 if (nW == 1 and KT * 128 * 2 * 4 <= 48 * 1024) else 3
        with ExitStack() as st:
            NSB = TOK // SB
            mvb = [s.sb(st, "%smv%d" % (name, g), [128, KT, SB], BF16) for g in range(NSB)]
            def mvload(g):
                cx.load(sp, mvb[g], mvb[g][:, :, :],
                        mvT.ap().rearrange("k p t -> p k t")[:, kt0:kt0 + KT, tok0 + g * SB:tok0 + (g + 1) * SB])
            wts = [[s.sb(st, "%sw%d_%d" % (name, wi, sl), [128, KT * 128], BF16) for sl in range(NSLOT)]
                   for wi in range(nW)]
            pss = [[s.ps(st, "%sps%d_%d" % (name, wi, se), [128, SB]) for wi in range(nW)] for se in range(NSET)]
            Wf = []
            Wev = []
            for k in Wkeys:
                full, ev, evs, rpp = s.W[k]
                Wf.append(full)
                Wev.append((evs, rpp))
            epi_state = epi(st, None, None, None, None)
            cnt = 0

            def wload(mi_):
                mt_ = mt_list[mi_]
                for wi_ in range(nW):
                    w_ = wts[wi_][mi_ % NSLOT]
                    evs_, rpp_ = Wev[wi_]
                    sp.wait(evs_[min(len(evs_) - 1, (wrow0 + mt_ * 128 + 127) // rpp_)])
                    cx.load(sp, w_, w_[:, :], Wf[wi_][wrow0 + mt_ * 128: wrow0 + (mt_ + 1) * 128, :])
            wload(0)
            mvload(0)
            if pre is not None:
                pre(epi_state, 0, mt_list[0])
            for mi in range(1, min(NSLOT - 1, len(mt_list))):
                wload(mi)
                if mi < NSB:
                    mvload(mi)
            for g in range(min(NSLOT - 1, len(mt_list)), NSB):
                mvload(g)
            for g in range(1, NSB):
                pass
            for mi, mt in enumerate(mt_list):
                slot = mi % NSLOT
                if pre is not None and mi + 1 < len(mt_list):
                    pre(epi_state, mi + 1, mt_list[mi + 1])
                if mi + NSLOT - 1 < len(mt_list):
                    wload(mi + NSLOT - 1)
                for sbi in range(TOK // SB):
                    se = cnt % NSET
                    cnt += 1
                    for wi in range(nW):
                        w = wts[wi][slot]
                        p = pss[se][wi]
                        cx.prewait(pe, reads=[w, mvb[sbi]], writes=[p])
                        for kt in range(KT):
                            inst = nc.tensor.matmul(p[:, :], lhsT=w[:, kt * 128:(kt + 1) * 128],
                                                    rhs=mvb[sbi][:, kt, :],
                                                    start=(kt == 0), stop=(kt == KT - 1))
                        cx.mark(pe.tag(inst), [w], [p])
                    epi(st, epi_state, mi, mt, (sbi, pss[se]))
            cx.barrier()

    def ffn(s, name, l, which, xT, TOK, mt, m0):
        c = s.cfg
        nc, cx = s.nc, s.cx
        act, dve, pe, sp, pool = _acts(s)
        hT = s.scr_h(TOK)
        gT = s.scr_g(TOK)
        s.phase_norm(xT, hT, TOK, mt, m0, BF16, name + "n")
        SB = min(512, TOK)

        TB = min(2048, TOK)
        for tb in range(TOK // TB):
            def epi_up(st, state, mi, ft, extra):
                if state is None:
                    return dict(sg=[s.sb(st, name + "sg%d" % i, [128, SB], F32) for i in range(3)],
                                go=[s.sb(st, name + "go%d" % i, [128, TB], BF16) for i in range(3)], k=[0])
                sbi, pp = extra
                sg = state["sg"][state["k"][0] % 3]
                state["k"][0] += 1
                go = state["go"][mi % 3]
                cx.op(act, lambda: nc.scalar.activation(out=sg[:, :], in_=pp[0][:, :], func=AF.Silu),
                      reads=[pp[0]], writes=[sg])
                cx.op(dve, lambda: nc.vector.tensor_tensor(out=go[:, sbi * SB:(sbi + 1) * SB], in0=sg[:, :],
                                                           in1=pp[1][:, :], op=ALU.mult),
                      reads=[sg, pp[1]], writes=[go])
                if sbi == TB // SB - 1:
                    cx.store(sp, go, gT[ft, :, tb * TB:(tb + 1) * TB], go[:, :])
            s.linear(name + "u", hT, c.DT, TB, tb * TB, [("w1", l, which), ("w3", l, which)],
                     list(range(c.FT)), SB, epi_up)

        TBD = min(1024, TOK)
        gcol = m0 + 2
        for tb in range(TOK // TBD):
            NSBI = TBD // SB

            def epi_dn(st, state, mi, dt, extra):
                if state is None:
                    return dict(xs=[s.sb(st, name + "dx%d" % i, [128, SB], F32) for i in range(2 * NSBI)],
                                xo=[s.sb(st, name + "do%d" % i, [128, SB], F32) for i in range(3)], k=[0])
                sbi, pp = extra
                k = state["k"][0]
                state["k"][0] += 1
                xs = state["xs"][(mi % 2) * NSBI + sbi]
                xo = state["xo"][k % 3]
                sl = slice(tb * TBD + sbi * SB, tb * TBD + (sbi + 1) * SB)
                cx.op(dve, lambda: nc.vector.scalar_tensor_tensor(out=xo[:, :], in0=pp[0][:, :],
                                                                  scalar=s.modcol(mt, gcol, dt), in1=xs[:, :],
                                                                  op0=ALU.mult, op1=ALU.add),
                      reads=[pp[0], xs, mt], writes=[xo])
                cx.store(sp, xo, xT[dt, :, sl], xo[:, :])

            def pre_dn(state, mi, dt):
                for sbi in range(NSBI):
                    xs = state["xs"][(mi % 2) * NSBI + sbi]
                    sl = slice(tb * TBD + sbi * SB, tb * TBD + (sbi + 1) * SB)
                    cx.load(sp, xs, xs[:, :], xT[dt, :, sl])
            s.linear(name + "d", gT, c.FT, TBD, tb * TBD, [("w2", l, which)], list(range(c.DT)), SB, epi_dn,
                     pre=pre_dn)

    def scr_h(s, TOK):
        key = ("h", TOK)
        if key not in s.scr:
            s.scr[key] = s.dram("hT_%d" % TOK, [s.cfg.DT, 128, TOK], BF16)
        return s.scr[key]

    def scr_g(s, TOK):
        key = ("g", TOK)
        if key not in s.scr:
            s.scr[key] = s.dram("gT_%d" % TOK, [s.cfg.FT, 128, TOK], BF16)
        return s.scr[key]

    def phase_out(s, xT, mt_gain):
        c = s.cfg
        nc, cx = s.nc, s.cx
        act, dve, pe, sp, pool = _acts(s)
        oT = s.dram("oT", [c.DT, 128, c.T], F32)
        if mt_gain is None:
            oT = xT
        else:
            s.phase_norm(xT, oT, c.T, None, 0, F32, "fn", gain=mt_gain)
        with ExitStack() as st:
            xi = [s.sb(st, "oxi%d" % i, [128, c.DT, 128], F32) for i in range(2)]
            xo = [s.sb(st, "oxo%d" % i, [128, c.D], F32) for i in range(2)]
            pst = [s.ps(st, "ops%d" % i, [128, 4, 128]) for i in range(4)]
            k = 0
            for tt in range(c.T // 128):
                a = xi[tt % 2]
                o = xo[tt % 2]
                cx.load(sp, a, a[:, :, :], oT.ap().rearrange("k p t -> p k t")[:, :, tt * 128:(tt + 1) * 128])
                for q in range(c.DT // 4):
                    p = pst[k % 4]
                    cx.prewait(pe, reads=[a, s.ident], writes=[p])
                    for u in range(4):
                        dt = q * 4 + u
                        inst = nc.tensor.transpose(out=p[:, u, :], in_=a[:, dt, :], identity=s.ident[:, :])
                    cx.mark(pe.tag(inst), [a], [p])
                    dst = o[:, q * 512:(q + 1) * 512]
                    src = p[:, :, :].rearrange("p u j -> p (u j)")
                    if k % 2 == 0:
                        cx.op(act, lambda: nc.scalar.copy(out=dst, in_=src), reads=[p], writes=[o])
                    else:
                        cx.op(dve, lambda: nc.vector.tensor_copy(out=dst, in_=src), reads=[p], writes=[o])
                    k += 1
                cx.store(sp, o, s.out[tt * 128:(tt + 1) * 128, :], o[:, :])
            cx.barrier()


class Mixer0:
    def mixer0(s):
        c = s.cfg
        nc, cx = s.nc, s.cx
        act, dve, pe, sp, pool = _acts(s)
        NH, HWT = c.NH, c.HW // 128
        hT = s.scr_h(c.T)
        hcT = s.scr_h(c.NCTX)
        s.phase_norm(s.xT, hT, c.T, s.modT[0], 3, BF16, "m0n")
        s.phase_norm(s.cT, hcT, c.NCTX, s.modC, 3, BF16, "m0c")
        pT = s.dram("pT", [4 * NH, 128, c.T], F32)
        hyT = s.dram("hyT", [3 * HWT * 128, c.T], BF16)
        hyG = s.dram("hyG", [2 * 3 * HWT * 128, c.T], BF16)
        pcT = s.dram("pcT", [2 * NH, 128, c.NCTX], F32)
        s.pT, s.pcT = pT, pcT
        SB = min(512, c.T)
        kscale = float(c.HD) ** -0.5

        def epi_in(st, state, mi, mt, extra):
            if state is None:
                return dict(o=[s.sb(st, "ipo%d" % i, [128, c.T], F32) for i in range(2)],
                            ob=[s.sb(st, "ipb%d" % i, [128, c.T], BF16) for i in range(2)], k=[0])
            sbi, pp = extra
            hy = mt >= 4 * NH
            o = (state["ob"] if hy else state["o"])[mi % 2]
            dst = o[:, sbi * SB:(sbi + 1) * SB]
            sc = kscale if (NH <= mt < 2 * NH) else 1.0
            k = state["k"][0]
            state["k"][0] += 1
            if k % 2 == 0:
                cx.op(act, lambda: nc.scalar.mul(out=dst, in_=pp[0][:, :], mul=sc), reads=[pp[0]], writes=[o])
            else:
                cx.op(dve, lambda: nc.vector.tensor_scalar(out=dst, in0=pp[0][:, :], scalar1=sc, scalar2=None,
                                                           op0=ALU.mult), reads=[pp[0]], writes=[o])
            if sbi == c.T // SB - 1:
                if hy:
                    m = mt - 4 * NH
                    cx.store(sp, o, hyT[m * 128:(m + 1) * 128, :], o[:, :])
                else:
                    cx.store(sp, o, pT[mt, :, :], o[:, :])
        s.linear("ip", hT, c.DT, c.T, 0, ["win"], list(range(c.PT)), SB, epi_in)
        for m in range(3 * HWT):
            ev = cx.allgather(hyT[m * 128:(m + 1) * 128, :], hyG[m * 256:(m + 1) * 256, :],
                              [[0, 1], [2, 3], [4, 5], [6, 7]])
        s.hy_ev = ev
        s.hyG = hyG

        SBc = min(512, c.NCTX)

        def epi_ctx(st, state, mi, mt, extra):
            if state is None:
                return dict(o=[s.sb(st, "ico%d" % i, [128, c.NCTX], F32) for i in range(2)])
            sbi, pp = extra
            o = state["o"][mi % 2]
            sc = kscale if mt < 2 * NH else 1.0
            cx.op(act, lambda: nc.scalar.mul(out=o[:, sbi * SBc:(sbi + 1) * SBc], in_=pp[0][:, :], mul=sc),
                  reads=[pp[0]], writes=[o])
            if sbi == c.NCTX // SBc - 1:
                cx.store(sp, o, pcT[mt - NH, :, :], o[:, :])
        s.linear("ic", hcT, c.DT, c.NCTX, 0, ["win"], list(range(NH, 3 * NH)), SBc, epi_ctx)

        ypT = s.dram("ypT", [c.DT, 128, c.T], BF16)
        s.ypT = ypT
        if s.stop_after == "m0a":
            return
        s.retention()
        if s.stop_after == "m0b":
            return
        if s.hyena() == "stop":
            return
        if s.stop_after == "h7":
            return

        NSBO = c.T // SB

        def epi_out(st, state, mi, dt, extra):
            if state is None:
                return dict(xs=[s.sb(st, "opx%d" % i, [128, SB], F32) for i in range(2 * NSBO)],
                            xo=[s.sb(st, "opo%d" % i, [128, SB], F32) for i in range(3)], k=[0])
            sbi, pp = extra
            k = state["k"][0]
            state["k"][0] += 1
            xs = state["xs"][(mi % 2) * NSBO + sbi]
            xo = state["xo"][k % 3]
            sl = slice(sbi * SB, (sbi + 1) * SB)
            cx.op(dve, lambda: nc.vector.scalar_tensor_tensor(out=xo[:, :], in0=pp[0][:, :],
                                                              scalar=s.modcol(s.modT[0], 5, dt), in1=xs[:, :],
                                                              op0=ALU.mult, op1=ALU.add),
                  reads=[pp[0], xs, s.modT[0]], writes=[xo])
            cx.store(sp, xo, s.xT[dt, :, sl], xo[:, :])

        def pre_out(state, mi, dt):
            for sbi in range(NSBO):
                xs = state["xs"][(mi % 2) * NSBO + sbi]
                cx.load(sp, xs, xs[:, :], s.xT[dt, :, sbi * SB:(sbi + 1) * SB])
        s.linear("op", ypT, c.DT, c.T, 0, ["wout"], list(range(c.DT)), SB, epi_out, pre=pre_out)

    def rotary(s, st, name, src_tile, C, S, out_bf):
        c = s.cfg
        nc, cx = s.nc, s.cx
        act, dve, pe, sp, pool = _acts(s)
        x, xs, t1, t2 = s.rt["x"], s.rt["xs"], s.rt["t1"], s.rt["t2"]
        cx.load(sp, x, x[:, :], src_tile)
        cx.load(sp, xs, xs[0:64, :], src_tile[64:128, :])
        cx.load(sp, xs, xs[64:128, :], src_tile[0:64, :], multi=True)
        cx.op(dve, lambda: nc.vector.tensor_tensor(out=t1[:, :], in0=x[:, :], in1=C[:, :], op=ALU.mult),
              reads=[x, C], writes=[t1])
        cx.op(dve, lambda: nc.vector.tensor_tensor(out=t2[:, :], in0=xs[:, :], in1=S[:, :], op=ALU.mult),
              reads=[xs, S], writes=[t2])
        cx.op(dve, lambda: nc.vector.tensor_tensor(out=out_bf[:, :], in0=t1[:, :], in1=t2[:, :], op=ALU.add),
              reads=[t1, t2], writes=[out_bf])

    def tposes(s, src_bf, dst_bf, nchunks, pst_list, kcount):
        nc, cx = s.nc, s.cx
        act, dve, pe, sp, pool = _acts(s)
        per = 8
        for g in range(0, nchunks, per):
            n = min(per, nchunks - g)
            p = pst_list[kcount[0] % len(pst_list)]
            cx.prewait(pe, reads=[src_bf, s.identb], writes=[p])
            for u in range(n):
                cc = g + u
                inst = nc.tensor.transpose(out=p[:, u, :], in_=src_bf[:, cc * 128:(cc + 1) * 128],
                                           identity=s.identb[:, :])
            cx.mark(pe.tag(inst), [src_bf], [p])
            if kcount[0] % 2 == 0:
                cx.op(act, lambda: nc.scalar.copy(out=dst_bf[:, g:g + n, :], in_=p[:, 0:n, :]),
                      reads=[p], writes=[dst_bf])
            else:
                cx.op(dve, lambda: nc.vector.tensor_copy(out=dst_bf[:, g:g + n, :], in_=p[:, 0:n, :]),
                      reads=[p], writes=[dst_bf])
            kcount[0] += 1

    def retention(s):
        c = s.cfg
        nc, cx = s.nc, s.cx
        act, dve, pe, sp, pool = _acts(s)
        NH, NCH, T = c.NH, c.NCH, c.T
        NCC = c.NCTX // 128
        pT, pcT, ypT = s.pT, s.pcT, s.ypT
        rotC_in = s.inp("rotC", [128, T])
        rotS_in = s.inp("rotS", [128, T])
        lg_in = s.inp("lgrep", [128, 2 * NH])
        rc128_in = s.inp("rc128", [128, 4, 128])
        rcT_in = s.inp("rcT", [128, 2, T])
        rcp_in = s.inp("rcp", [128, 2 + 2 * NCC])
        rankv_in = s.inp("rankv", [128, 4])
        rscr = s.dram("rscr_k", [NH, 128, T], BF16)
        rscr_v = s.dram("rscr_v", [NH, 128, T], BF16)
        rscr_u = s.dram("rscr_u", [NH, 2, 128, T], F32)
        exs = s.dram("exs", [2 * 128, NH * 128], F32)
        exg = s.dram("exg", [2 * 2 * 128, NH * 128], F32)
        with ExitStack() as st:
            C = s.sb(st, "rotC", [128, T], F32)
            S = s.sb(st, "rotS", [128, T], F32)
            lg = s.sb(st, "lg", [128, 2 * NH], F32)
            nlg = s.sb(st, "nlg", [128, 2 * NH], F32)
            rc128 = s.sb(st, "rc128", [128, 4, 128], F32)
            rcT = s.sb(st, "rcT", [128, 2, T], F32)
            rcp = s.sb(st, "rcp", [128, 2 + 2 * NCC], F32)
            rankv = s.sb(st, "rankv", [128, 4], F32)
            for b_, i_ in ((C, rotC_in), (S, rotS_in), (lg, lg_in), (rcp, rcp_in), (rankv, rankv_in)):
                cx.load(sp, b_, b_[:, :], i_[:, :])
            cx.load(sp, rc128, rc128[:, :, :], rc128_in[:, :, :])
            cx.load(sp, rcT, rcT[:, :, :], rcT_in[:, :, :])
            cx.op(dve, lambda: nc.vector.tensor_scalar(out=nlg[:, :], in0=lg[:, :], scalar1=-1.0, scalar2=None,
                                                       op0=ALU.mult), reads=[lg], writes=[nlg])
            s.rt = dict(x=s.sb(st, "rx", [128, T], F32), xs=s.sb(st, "rxs", [128, T], F32),
                        t1=s.sb(st, "rt1", [128, T], F32), t2=s.sb(st, "rt2", [128, T], F32))
            kr = s.sb(st, "kr", [128, T], BF16)
            vb = s.sb(st, "vb", [128, T], BF16)
            vf = s.rt["x"]
            ktm = s.sb(st, "ktm", [128, NCH, 128], BF16)
            vtm = s.sb(st, "vtm", [128, NCH, 128], BF16)
            vdf = s.sb(st, "vdf", [128, NCH, 128], BF16)
            vdb = s.sb(st, "vdb", [128, NCH, 128], BF16)
            U = [s.sb(st, "U%d" % i, [128, NCH, 128], F32) for i in range(2)]
            dec = s.sb(st, "dec", [128, 8], F32)
            ex = s.sb(st, "ex", [128, NH, 2, 128], F32)
            ctxS = s.sb(st, "ctxS", [128, NH, 2, 128], F32)
            Sst = [s.sb(st, "Sst%d" % i, [128, 128], F32) for i in range(2)]
            pst = [s.ps(st, "rtp%d" % i, [128, 8, 128], BF16) for i in range(2)]
            pu = [s.ps(st, "rup%d" % i, [128, 4, 128]) for i in range(3)]
            kc = [0]
            kcb = s.sb(st, "kcb", [128, c.NCTX], BF16)
            vcb = s.sb(st, "vcb", [128, c.NCTX], BF16)
            kcf = s.sb(st, "kcf", [128, c.NCTX], F32)
            vcf = s.sb(st, "vcf", [128, c.NCTX], F32)
            kctm = s.sb(st, "kctm", [128, NCC, 128], BF16)
            vctm = s.sb(st, "vctm", [128, NCC, 128], BF16)
            vcd = s.sb(st, "vcd", [128, 2, NCC, 128], BF16)
            cw = s.sb(st, "cw", [128, 2 * NCC], F32)

            def expcol(dst_ap, in_ap, scale_ap, reads, wbuf):
                cx.op(act, lambda: nc.scalar.activation(out=dst_ap, in_=in_ap, func=AF.Exp, scale=scale_ap),
                      reads=reads, writes=[wbuf])

            for h in range(NH):
                lf = lg[:, h:h + 1]
                lb = lg[:, NH + h:NH + h + 1]
                expcol(dec[:, 0:1], rcp[:, 0:1], lf, [rcp, lg], dec)
                expcol(dec[:, 1:2], rcp[:, 1:2], lb, [rcp, lg], dec)
                cx.op(act, lambda: nc.scalar.activation(out=dec[:, 2:3], in_=lf, func=AF.Exp, scale=128.0),
                      reads=[lg], writes=[dec])
                cx.op(act, lambda: nc.scalar.activation(out=dec[:, 3:4], in_=lb, func=AF.Exp, scale=128.0),
                      reads=[lg], writes=[dec])
                s.rotary(st, "k", pT[NH + h, :, :], C, S, kr)
                cx.store(sp, kr, rscr[h, :, :], kr[:, :])
                cx.load(sp, vf, vf[:, :], pT[2 * NH + h, :, :])
                cx.op(act, lambda: nc.scalar.copy(out=vb[:, :], in_=vf[:, :]), reads=[vf], writes=[vb])
                s.tposes(kr, ktm, NCH, pst, kc)
                s.tposes(vb, vtm, NCH, pst, kc)
                cx.store(sp, vtm, rscr_v[h, :, :], vtm[:, :, :].rearrange("p c e -> p (c e)"))
                cx.op(dve, lambda: nc.vector.tensor_scalar(out=vdf[:, :, :], in0=vtm[:, :, :], scalar1=dec[:, 0:1],
                                                           scalar2=None, op0=ALU.mult),
                      reads=[vtm, dec], writes=[vdf])
                cx.op(dve, lambda: nc.vector.tensor_scalar(out=vdb[:, :, :], in0=vtm[:, :, :], scalar1=dec[:, 1:2],
                                                           scalar2=None, op0=ALU.mult),
                      reads=[vtm, dec], writes=[vdb])
                ku = 0
                for di, vd in enumerate((vdf, vdb)):
                    for g in range(0, NCH, 4):
                        n = min(4, NCH - g)
                        p = pu[ku % 3]
                        ku += 1
                        cx.prewait(pe, reads=[ktm, vd], writes=[p])
                        for u in range(n):
                            inst = nc.tensor.matmul(p[:, u, :], lhsT=ktm[:, g + u, :], rhs=vd[:, g + u, :],
                                                    start=True, stop=True)
                        cx.mark(pe.tag(inst), [ktm, vd], [p])
                        cx.op(act, lambda: nc.scalar.copy(out=U[di][:, g:g + n, :], in_=p[:, 0:n, :]),
                              reads=[p], writes=[U[di]])
                    cx.store(sp, U[di], rscr_u[h, di, :, :], U[di][:, :, :].rearrange("p c e -> p (c e)"))
                for di in range(2):
                    order = range(NCH) if di == 0 else range(NCH - 1, -1, -1)
                    first = True
                    for cc in order:
                        if first:
                            cx.op(dve, lambda: nc.vector.tensor_copy(out=ex[:, h, di, :], in_=U[di][:, cc, :]),
                                  reads=[U[di]], writes=[ex])
                            first = False
                        else:
                            cx.op(dve, lambda: nc.vector.scalar_tensor_tensor(
                                out=ex[:, h, di, :], in0=ex[:, h, di, :], scalar=dec[:, 2 + di:3 + di],
                                in1=U[di][:, cc, :], op0=ALU.mult, op1=ALU.add),
                                reads=[ex, U[di], dec], writes=[ex])
                for cc in range(NCC):
                    expcol(cw[:, cc:cc + 1], rcp[:, 2 + cc:3 + cc], lf, [rcp, lg], cw)
                    expcol(cw[:, NCC + cc:NCC + cc + 1], rcp[:, 2 + NCC + cc:3 + NCC + cc], lb, [rcp, lg], cw)
                cx.load(sp, kcf, kcf[:, :], pcT[h, :, :])
                cx.load(sp, vcf, vcf[:, :], pcT[NH + h, :, :])
                cx.op(act, lambda: nc.scalar.copy(out=kcb[:, :], in_=kcf[:, :]), reads=[kcf], writes=[kcb])
                cx.op(act, lambda: nc.scalar.copy(out=vcb[:, :], in_=vcf[:, :]), reads=[vcf], writes=[vcb])
                s.tposes(kcb, kctm, NCC, pst, kc)
                s.tposes(vcb, vctm, NCC, pst, kc)
                for di in range(2):
                    for cc in range(NCC):
                        cx.op(dve, lambda: nc.vector.tensor_scalar(
                            out=vcd[:, di, cc, :], in0=vctm[:, cc, :], scalar1=cw[:, di * NCC + cc:di * NCC + cc + 1],
                            scalar2=None, op0=ALU.mult), reads=[vctm, cw], writes=[vcd])
                p = pu[ku % 3]
                ku += 1
                cx.prewait(pe, reads=[kctm, vcd], writes=[p])
                for di in range(2):
                    for cc in range(NCC):
                        inst = nc.tensor.matmul(p[:, di, :], lhsT=kctm[:, cc, :], rhs=vcd[:, di, cc, :],
                                                start=(cc == 0), stop=(cc == NCC - 1))
                cx.mark(pe.tag(inst), [kctm, vcd], [p])
                cx.op(act, lambda: nc.scalar.copy(out=ctxS[:, h, :, :], in_=p[:, 0:2, :]), reads=[p], writes=[ctxS])

            for di in range(2):
                cx.store(sp, ex, exs.ap().rearrange("(d p) (h e) -> d p h e", d=2, h=NH)[di], ex[:, :, di, :])
            pool.wait((ex.dsem, ex.r[ex.dsem]))
            for di in range(2):
                ev = cx.allgather(exs[di * 128:(di + 1) * 128, :], exg[di * 256:(di + 1) * 256, :],
                                  [[0, 1], [2, 3], [4, 5], [6, 7]])
            s.wprep_C()
            exr = s.sb(st, "exr", [128, 2, NH, 128], F32)
            sp.wait(ev)
            egv = exg.ap().rearrange("(d j p) (h e) -> d j p h e", d=2, j=2, h=NH)
            cx.load(sp, exr, exr[:, 0, :, :], egv[0, 0])
            cx.load(sp, exr, exr[:, 1, :, :], egv[1, 1], multi=True)

            qr = s.sb(st, "qr", [128, T], BF16)
            qf = s.sb(st, "qf", [128, T], BF16)
            qb = s.sb(st, "qb", [128, T], BF16)
            Gf = s.sb(st, "Gf", [128, T], BF16)
            Gb = s.sb(st, "Gb", [128, T], BF16)
            DT_ = s.sb(st, "DTm", [128, 128], F32)
            Et = [s.sb(st, "Et%d" % i, [128, 128], F32) for i in range(2)]
            SD = s.sb(st, "SD", [128, NCH, 128], BF16)
            Sbf = [s.sb(st, "Sbf%d" % i, [128, NCH, 128], BF16) for i in range(2)]
            o = s.rt["t2"]
            osq = s.sb(st, "rosq", [128, T], BF16)
            gf = s.rt["x"]
            sg = s.sb(st, "rsg", [128, T], BF16)
            rs = s.rt["t1"]
            rout = s.sb(st, "rout", [128, T], BF16)
            alpha = s.sb(st, "alpha", [128, 2], F32)
            pss = [s.ps(st, "rsp%d" % i, [128, 4, 128]) for i in range(2)]
            NB = max(1, T // 512)
            BL = T // NB
            for h in range(NH):
                lf = lg[:, h:h + 1]
                lb = lg[:, NH + h:NH + h + 1]
                cx.op(act, lambda: nc.scalar.activation(out=dec[:, 2:3], in_=lf, func=AF.Exp, scale=128.0),
                      reads=[lg], writes=[dec])
                cx.op(act, lambda: nc.scalar.activation(out=dec[:, 3:4], in_=lb, func=AF.Exp, scale=128.0),
                      reads=[lg], writes=[dec])
                expcol(alpha[:, 0:1], lf, rankv[:, 0:1], [lg, rankv], alpha)
                expcol(alpha[:, 1:2], lb, rankv[:, 1:2], [lg, rankv], alpha)
                expcol(Et[0][:, :], rc128[:, 0, :], lf, [rc128, lg], Et[0])
                expcol(Et[1][:, :], rc128[:, 0, :], nlg[:, NH + h:NH + h + 1], [rc128, nlg], Et[1])
                cx.op(dve, lambda: nc.vector.tensor_tensor(out=Et[0][:, :], in0=Et[0][:, :], in1=rc128[:, 1, :],
                                                           op=ALU.mult), reads=[Et[0], rc128], writes=[Et[0]])
                cx.op(dve, lambda: nc.vector.tensor_tensor(out=Et[1][:, :], in0=Et[1][:, :], in1=rc128[:, 2, :],
                                                           op=ALU.mult), reads=[Et[1], rc128], writes=[Et[1]])
                cx.op(dve, lambda: nc.vector.tensor_tensor(out=DT_[:, :], in0=Et[0][:, :], in1=Et[1][:, :],
                                                           op=ALU.add), reads=[Et[0], Et[1]], writes=[DT_])
                cx.op(dve, lambda: nc.vector.tensor_tensor(out=DT_[:, :], in0=DT_[:, :], in1=rc128[:, 3, :],
                                                           op=ALU.add), reads=[DT_, rc128], writes=[DT_])
                expcol(Gf[:, :], rcT[:, 0, :], lf, [rcT, lg], Gf)
                expcol(Gb[:, :], rcT[:, 1, :], lb, [rcT, lg], Gb)
                s.rotary(st, "q", pT[h, :, :], C, S, qr)
                cx.op(dve, lambda: nc.vector.tensor_tensor(out=qf[:, :], in0=qr[:, :], in1=Gf[:, :], op=ALU.mult),
                      reads=[qr, Gf], writes=[qf])
                cx.op(dve, lambda: nc.vector.tensor_tensor(out=qb[:, :], in0=qr[:, :], in1=Gb[:, :], op=ALU.mult),
                      reads=[qr, Gb], writes=[qb])
                cx.load(sp, kr, kr[:, :], rscr[h, :, :])
                cx.load(sp, vtm, vtm[:, :, :].rearrange("p c e -> p (c e)"), rscr_v[h, :, :])
                for di in range(2):
                    cx.load(sp, U[di], U[di][:, :, :].rearrange("p c e -> p (c e)"), rscr_u[h, di, :, :])
                for di in range(2):
                    Sx = Sst[di]
                    cx.op(dve, lambda: nc.vector.tensor_scalar(out=Sx[:, :], in0=ctxS[:, h, di, :],
                                                               scalar1=alpha[:, di:di + 1], scalar2=None,
                                                               op0=ALU.mult), reads=[ctxS, alpha], writes=[Sx])
                    src = exr[:, di, h, :]
                    cx.op(dve, lambda: nc.vector.scalar_tensor_tensor(out=Sx[:, :], in0=src,
                                                                      scalar=rankv[:, 2 + di:3 + di], in1=Sx[:, :],
                                                                      op0=ALU.mult, op1=ALU.add),
                          reads=[exr, rankv, Sx], writes=[Sx])
                    order = range(NCH) if di == 0 else range(NCH - 1, -1, -1)
                    for cc in order:
                        cx.op(act, lambda: nc.scalar.copy(out=Sbf[di][:, cc, :], in_=Sx[:, :]),
                              reads=[Sx], writes=[Sbf[di]])
                        cx.op(dve, lambda: nc.vector.scalar_tensor_tensor(
                            out=Sx[:, :], in0=Sx[:, :], scalar=dec[:, 2 + di:3 + di], in1=U[di][:, cc, :],
                            op0=ALU.mult, op1=ALU.add), reads=[Sx, U[di], dec], writes=[Sx])
                k2 = 0
                for g in range(0, NCH, 4):
                    n = min(4, NCH - g)
                    p = pss[k2 % 2]
                    k2 += 1
                    cx.prewait(pe, reads=[kr, qr], writes=[p])
                    for u in range(n):
                        cc = g + u
                        inst = nc.tensor.matmul(p[:, u, :], lhsT=kr[:, cc * 128:(cc + 1) * 128],
                                                rhs=qr[:, cc * 128:(cc + 1) * 128], start=True, stop=True)
                    cx.mark(pe.tag(inst), [kr, qr], [p])
                    cx.op(dve, lambda: nc.vector.tensor_tensor(
                        out=SD[:, g:g + n, :], in0=p[:, 0:n, :],
                        in1=DT_[:, :].unsqueeze(1).broadcast_to([128, n, 128]), op=ALU.mult),
                        reads=[p, DT_], writes=[SD])
                for g in range(0, NCH, 4):
                    n = min(4, NCH - g)
                    p = pss[k2 % 2]
                    k2 += 1
                    cx.prewait(pe, reads=[vtm, SD, Sbf[0], Sbf[1], qf, qb], writes=[p])
                    for u in range(n):
                        cc = g + u
                        sl = slice(cc * 128, (cc + 1) * 128)
                        nc.tensor.matmul(p[:, u, :], lhsT=vtm[:, cc, :], rhs=SD[:, cc, :], start=True, stop=False)
                        nc.tensor.matmul(p[:, u, :], lhsT=Sbf[0][:, cc, :], rhs=qf[:, sl], start=False, stop=False)
                        inst = nc.tensor.matmul(p[:, u, :], lhsT=Sbf[1][:, cc, :], rhs=qb[:, sl], start=False,
                                                stop=True)
                    cx.mark(pe.tag(inst), [vtm, SD, Sbf[0], Sbf[1], qf, qb], [p])
                    dsl = slice(g * 128, (g + n) * 128)
                    cx.op(act, lambda: nc.scalar.copy(out=o[:, dsl], in_=p[:, 0:n, :].rearrange("p u i -> p (u i)")),
                          reads=[p], writes=[o])
                    cx.op(act, lambda: nc.scalar.activation(out=osq[:, dsl],
                                                            in_=p[:, 0:n, :].rearrange("p u i -> p (u i)"),
                                                            func=AF.Square), reads=[p], writes=[osq])
                cx.load(sp, gf, gf[:, :], pT[3 * NH + h, :, :])
                cx.op(act, lambda: nc.scalar.activation(out=sg[:, :], in_=gf[:, :], func=AF.Silu),
                      reads=[gf], writes=[sg])
                for nb in range(NB):
                    bsl = slice(nb * BL, (nb + 1) * BL)
                    p = pss[k2 % 2]
                    k2 += 1
                    pv = p[:, :, :].rearrange("p u i -> p (u i)")[:, 0:BL]
                    cx.op(pe, lambda: nc.tensor.matmul(pv, lhsT=s.onesb[:, :], rhs=osq[:, bsl], start=True, stop=True),
                          reads=[osq, s.onesb], writes=[p])
                    cx.op(act, lambda: nc.scalar.activation(out=rs[:, bsl], in_=pv, func=AF.Sqrt,
                                                            bias=s.epsc[:, 0:1], scale=1.0 / c.HD),
                          reads=[p, s.epsc], writes=[rs])
                cx.op(dve, lambda: nc.vector.reciprocal(out=rs[:, :], in_=rs[:, :]), reads=[rs], writes=[rs])
                cx.op(dve, lambda: nc.vector.tensor_tensor(out=o[:, :], in0=o[:, :], in1=rs[:, :], op=ALU.mult),
                      reads=[o, rs], writes=[o])
                cx.op(dve, lambda: nc.vector.tensor_tensor(out=rout[:, :], in0=o[:, :], in1=sg[:, :], op=ALU.mult),
                      reads=[o, sg], writes=[rout])
                cx.store(sp, rout, ypT[h, :, :], rout[:, :])
            cx.barrier()

    def hyena(s):
        c = s.cfg
        nc, cx = s.nc, s.cx
        act, dve, pe, sp, pool = _acts(s)
        N, T, HC, HCT, NH = c.N, c.T, c.HC, c.HCT, c.NH
        ST = N // 128
        HWT = c.HW // 128
        C4 = 4 * HC
        zT_in = s.inp("zT", [33, N])
        fw1_in = s.inp("fw1", [33, 64])
        fw2_in = s.inp("fw2", [64, 64])
        fw3_in = s.inp("fw3", [64, 64])
        fbf_in = s.inp("fbf", [64, 6])
        w4_in = s.inp("w4my", [64, C4])
        delta_in = s.inp("deltarow", [128, HC])
        negt_in = s.inp("negt", [128, ST])
        hcw_in = s.inp("hcw", [128, 3, HCT, 3])
        hcb_in = s.inp("hcb", [128, 3, HCT])
        bias_in = s.inp("biasrow", [128, 2, HC])
        hfilt = s.dram("hfilt", [ST, 128, C4], BF16)
        Hspec = s.dram("Hspec", [2, ST, 128, C4], F32)
        Kspec = s.dram("Kspec", [2, 2, ST, 128, HC], F32)
        u32 = [s.dram("u32_%d" % i, [ST, 128, HC], F32) for i in range(3)]
        vbf = s.dram("vbf", [ST, 128, HC], BF16)
        z32 = s.dram("z32", [ST, 128, HC], F32)
        zbf = s.dram("zbf", [ST, 128, HC], BF16)
        Ysp = s.dram("Ysp", [2 * ST, 128, HC], BF16)
        yh = s.dram("yh", [HCT * 2 * 128, T], BF16)
        yhG = s.dram("yhG", [HCT * 2 * 2 * 128, T], BF16)
        TWO_PI = 2.0 * math.pi
        MAGIC = 12582912.0
        PI_LO = 3.1415925

        with ExitStack() as st:
            zT = s.sb(st, "zT", [33, N], F32)
            fw1 = s.sb(st, "fw1", [33, 64], F32)
            fw2 = s.sb(st, "fw2", [64, 64], F32)
            fw3 = s.sb(st, "fw3", [64, 64], F32)
            fbf = s.sb(st, "fbf", [64, 6], F32)
            fb = s.sb(st, "fbm", [64, 3], F32)
            w4 = s.sb(st, "w4", [64, C4], F32)
            delta = s.sb(st, "delta", [128, HC], F32)
            negt = s.sb(st, "negt", [128, ST], F32)
            for b_, i_ in ((zT, zT_in), (fw1, fw1_in), (fw2, fw2_in), (fw3, fw3_in), (fbf, fbf_in), (w4, w4_in),
                           (delta, delta_in), (negt, negt_in)):
                cx.load(sp, b_, b_[:, :], i_[:, :])
            cx.op(dve, lambda: nc.vector.tensor_tensor(out=fb[:, :], in0=fbf[:, 0:3], in1=fbf[:, 3:6], op=ALU.mult),
                  reads=[fbf], writes=[fb])
            a3 = s.sb(st, "a3", [64, N], F32)
            BL = min(512, N)
            cur = [s.sb(st, "fa%d" % i, [64, BL], F32) for i in range(2)]
            v_ = s.sb(st, "fv", [64, BL], F32)
            t_ = s.sb(st, "ft", [64, BL], F32)
            n_ = s.sb(st, "fn", [64, BL], F32)
            psm = [s.ps(st, "fps%d" % i, [64, BL]) for i in range(2)]
            km = 0
            for blk in range(N // BL):
                bsl = slice(blk * BL, (blk + 1) * BL)
                for layer in range(3):
                    p = psm[km % 2]
                    km += 1
                    if layer == 0:
                        cx.op(pe, lambda: nc.tensor.matmul(p[:, :], lhsT=fw1[:, :], rhs=zT[:, bsl], start=True,
                                                           stop=True), reads=[fw1, zT], writes=[p])
                    else:
                        w = fw2 if layer == 1 else fw3
                        src = cur[(layer - 1) % 2]
                        cx.op(pe, lambda: nc.tensor.matmul(p[:, :], lhsT=w[:, :], rhs=src[:, :], start=True,
                                                           stop=True), reads=[w, src], writes=[p])
                    cx.op(act, lambda: nc.scalar.activation(out=v_[:, :], in_=p[:, :], func=AF.Identity,
                                                            bias=fb[:, layer:layer + 1],
                                                            scale=fbf[:, 3 + layer:4 + layer]),
                          reads=[p, fb, fbf], writes=[v_])
                    cx.op(dve, lambda: nc.vector.tensor_scalar(out=t_[:, :], in0=v_[:, :], scalar1=1.0 / TWO_PI,
                                                               scalar2=MAGIC, op0=ALU.mult, op1=ALU.add),
                          reads=[v_], writes=[t_])
                    cx.op(dve, lambda: nc.vector.tensor_scalar(out=n_[:, :], in0=t_[:, :], scalar1=-MAGIC,
                                                               scalar2=None, op0=ALU.add), reads=[t_], writes=[n_])
                    cx.op(dve, lambda: nc.vector.scalar_tensor_tensor(out=t_[:, :], in0=n_[:, :], scalar=-TWO_PI,
                                                                      in1=v_[:, :], op0=ALU.mult, op1=ALU.add),
                          reads=[n_, v_], writes=[t_])
                    cx.op(dve, lambda: nc.vector.tensor_scalar(out=t_[:, :], in0=t_[:, :], scalar1=-PI_LO,
                                                               scalar2=PI_LO, op0=ALU.max, op1=ALU.min),
                          reads=[t_], writes=[t_])
                    dst = a3[:, bsl] if layer == 2 else cur[layer % 2][:, :]
                    dbuf = a3 if layer == 2 else cur[layer % 2]
                    cx.op(act, lambda: nc.scalar.activation(out=dst, in_=t_[:, :], func=AF.Sin),
                          reads=[t_], writes=[dbuf])
            nbk = C4 // 512 if C4 >= 512 else 1
            BW = min(512, C4)
            psf = [[s.ps(st, "hps%d_%d" % (i, j), [128, BW]) for j in range(nbk)] for i in range(1)]
            win = [s.sb(st, "win%d" % i, [128, HC], F32) for i in range(2)]
            fo = [s.sb(st, "fo%d" % i, [128, C4], BF16) for i in range(2)]
            for pt in range(ST):
                pp = psf[0]
                for j in range(nbk):
                    cx.op(pe, lambda: nc.tensor.matmul(pp[j][:, :], lhsT=a3[:, pt * 128:(pt + 1) * 128],
                                                       rhs=w4[:, j * BW:(j + 1) * BW], start=True, stop=True),
                          reads=[a3, w4], writes=[pp[j]])
                wn = win[pt % 2]
                cx.op(act, lambda: nc.scalar.activation(out=wn[:, :], in_=delta[:, :], func=AF.Exp,
                                                        scale=negt[:, pt:pt + 1]),
                      reads=[delta, negt], writes=[wn])
                f = fo[pt % 2]
                for g in range(4):
                    j = (g * HC) // BW
                    off = (g * HC) % BW
                    cx.op(dve, lambda: nc.vector.tensor_tensor(out=f[:, g * HC:(g + 1) * HC],
                                                               in0=pp[j][:, off:off + HC], in1=wn[:, :],
                                                               op=ALU.mult), reads=[pp[j], wn], writes=[f])
                if pt == 0:
                    for g in (1, 3):
                        cx.op(dve, lambda: nc.vector.memset(f[0:1, g * HC:(g + 1) * HC], 0.0), writes=[f])
                cx.store(sp, f, hfilt[pt, :, :], f[:, :])
            cx.barrier()

        if s.stop_after == "h1":
            return "stop"
        TK = min(1024, C4)
        SBF = min(512, TK)
        for run in range(C4 // TK):
            def epi_spec(st, state, mi, ft, extra):
                if state is None:
                    return dict(o=[s.sb(st, "hso%d" % i, [128, SBF], F32) for i in range(4)], k=[0])
                sbi, pp = extra
                for ri in range(2):
                    k = state["k"][0]
                    state["k"][0] += 1
                    o = state["o"][k % 4]
                    if ri == 0:
                        cx.op(act, lambda: nc.scalar.copy(out=o[:, :], in_=pp[ri][:, :]), reads=[pp[ri]], writes=[o])
                    else:
                        cx.op(dve, lambda: nc.vector.tensor_copy(out=o[:, :], in_=pp[ri][:, :]), reads=[pp[ri]],
                              writes=[o])
                    c0 = run * TK + sbi * SBF
                    cx.store(sp, o, Hspec[ri, ft, :, c0:c0 + SBF], o[:, :])
            s.linear("hs%d" % run, hfilt, ST, TK, run * TK, ["dftc", "dftsf"], list(range(ST)), SBF, epi_spec)

        if s.stop_after == "h2":
            return "stop"
        with ExitStack() as st:
            hr = [s.sb(st, "hr%d" % i, [128, 2 * HC], F32) for i in range(2)]
            hi = [s.sb(st, "hi%d" % i, [128, 2 * HC], F32) for i in range(2)]
            tq = [s.sb(st, "tq%d" % i, [128, HC], F32) for i in range(2)]
            kr_ = [s.sb(st, "kkr%d" % i, [128, HC], F32) for i in range(2)]
            ki_ = [s.sb(st, "kki%d" % i, [128, HC], F32) for i in range(2)]
            sc = 1.0 / N
            k = 0
            for o_ in range(2):
                for ft in range(ST):
                    a, b_ = hr[k % 2], hi[k % 2]
                    t = tq[k % 2]
                    kr, ki = kr_[k % 2], ki_[k % 2]
                    k += 1
                    cs = slice(o_ * 2 * HC, (o_ + 1) * 2 * HC)
                    cx.load(sp, a, a[:, :], Hspec[0, ft, :, cs])
                    cx.load(sp, b_, b_[:, :], Hspec[1, ft, :, cs])
                    cx.op(dve, lambda: nc.vector.tensor_scalar(out=t[:, :], in0=a[:, HC:2 * HC], scalar1=sc,
                                                               scalar2=None, op0=ALU.mult), reads=[a], writes=[t])
                    cx.op(dve, lambda: nc.vector.scalar_tensor_tensor(out=kr[:, :], in0=a[:, 0:HC], scalar=sc,
                                                                      in1=t[:, :], op0=ALU.mult, op1=ALU.add),
                          reads=[a, t], writes=[kr])
                    cx.op(dve, lambda: nc.vector.tensor_scalar(out=t[:, :], in0=b_[:, HC:2 * HC], scalar1=-sc,
                                                               scalar2=None, op0=ALU.mult), reads=[b_], writes=[t])
                    cx.op(dve, lambda: nc.vector.scalar_tensor_tensor(out=ki[:, :], in0=b_[:, 0:HC], scalar=sc,
                                                                      in1=t[:, :], op0=ALU.mult, op1=ALU.add),
                          reads=[b_, t], writes=[ki])
                    if ft == 0:
                        cx.op(dve, lambda: nc.vector.tensor_scalar(out=kr[0:1, :], in0=kr[0:1, :], scalar1=0.5,
                                                                   scalar2=None, op0=ALU.mult),
                              reads=[kr], writes=[kr])
                        cx.op(dve, lambda: nc.vector.tensor_scalar(out=t[0:1, :], in0=b_[0:1, HC:2 * HC],
                                                                   scalar1=0.5 * sc, scalar2=None, op0=ALU.mult),
                              reads=[b_], writes=[t])
                        cx.op(dve, lambda: nc.vector.scalar_tensor_tensor(out=ki[0:1, :], in0=b_[0:1, 0:HC],
                                                                          scalar=0.5 * sc, in1=t[0:1, :],
                                                                          op0=ALU.mult, op1=ALU.add),
                              reads=[b_, t], writes=[ki])
                    cx.store(sp, kr, Kspec[o_, 0, ft, :, :], kr[:, :])
                    cx.store(sp, ki, Kspec[o_, 1, ft, :, :], ki[:, :])
            cx.barrier()

        if s.stop_after == "h3":
            return "stop"
        sp.wait(s.hy_ev)
        hv = s.hyG.ap().rearrange("(a h i j p) t -> a h i j p t", j=2, a=3, h=2, i=HCT, p=128)
        with ExitStack() as st:
            hcw = s.sb(st, "hcw", [128, 3, HCT, 3], F32)
            hcb = s.sb(st, "hcb", [128, 3, HCT], F32)
            cx.load(sp, hcw, hcw[:, :, :, :], hcw_in[:, :, :, :])
            cx.load(sp, hcb, hcb[:, :, :], hcb_in[:, :, :])
            ppb = [s.sb(st, "ppb%d" % i, [128, N + 4], BF16) for i in range(2)]
            uu = [s.sb(st, "uu%d" % i, [128, N], F32) for i in range(2)]
            ot = [s.sb(st, "uot%d" % i, [128, ST, 128], F32) for i in range(2)]
            otb = s.sb(st, "uotb", [128, ST, 128], BF16)
            ptp = [s.ps(st, "utp%d" % i, [128, 4, 128]) for i in range(4)]
            for i in range(2):
                cx.op(dve, lambda: nc.vector.memset(ppb[i][:, 0:2], 0.0), writes=[ppb[i]])
                cx.op(dve, lambda: nc.vector.memset(ppb[i][:, N + 2:N + 4], 0.0), writes=[ppb[i]])
            k = 0
            kt = 0
            for part in range(3):
                for i in range(HCT):
                    pb = ppb[k % 2]
                    u = uu[k % 2]
                    o = ot[k % 2]
                    k += 1
                    cx.load(sp, pb, pb[:, 2:N + 2].rearrange("p (j t) -> p j t", j=2),
                            hv[part, bass.ds(s.rank_sp, 1), i, :, :, :].rearrange("o j p t -> p (o j) t"))
                    cx.op(act, lambda: nc.scalar.activation(out=u[:, :], in_=pb[:, 2:N + 2], func=AF.Identity,
                                                            bias=hcb[:, part, i:i + 1], scale=hcw[:, part, i, 1:2]),
                          reads=[pb, hcb, hcw], writes=[u])
                    cx.op(dve, lambda: nc.vector.scalar_tensor_tensor(out=u[:, :], in0=pb[:, 1:N + 1],
                                                                      scalar=hcw[:, part, i, 0:1], in1=u[:, :],
                                                                      op0=ALU.mult, op1=ALU.add),
                          reads=[pb, hcw, u], writes=[u])
                    cx.op(dve, lambda: nc.vector.scalar_tensor_tensor(out=u[:, :], in0=pb[:, 3:N + 3],
                                                                      scalar=hcw[:, part, i, 2:3], in1=u[:, :],
                                                                      op0=ALU.mult, op1=ALU.add),
                          reads=[pb, hcw, u], writes=[u])
                    for g in range(0, ST, 4):
                        p = ptp[kt % 4]
                        cx.prewait(pe, reads=[u, s.ident], writes=[p])
                        for q in range(4):
                            sti = g + q
                            inst = nc.tensor.transpose(out=p[:, q, :], in_=u[:, sti * 128:(sti + 1) * 128],
                                                       identity=s.ident[:, :])
                        cx.mark(pe.tag(inst), [u], [p])
                        if kt % 2 == 0:
                            cx.op(act, lambda: nc.scalar.copy(out=o[:, g:g + 4, :], in_=p[:, :, :]), reads=[p],
                                  writes=[o])
                        else:
                            cx.op(dve, lambda: nc.vector.tensor_copy(out=o[:, g:g + 4, :], in_=p[:, :, :]), reads=[p],
                                  writes=[o])
                        kt += 1
                    cx.store(sp, o, u32[part].ap().rearrange("s p c -> p s c")[:, :, i * 128:(i + 1) * 128],
                             o[:, :, :])
                    if part == 0:
                        cx.op(act, lambda: nc.scalar.copy(out=otb[:, :, :], in_=o[:, :, :]), reads=[o], writes=[otb])
                        cx.store(sp, otb, vbf.ap().rearrange("s p c -> p s c")[:, :, i * 128:(i + 1) * 128],
                                 otb[:, :, :])
            cx.barrier()

        if s.stop_after == "h4":
            return "stop"
        for order in range(2):
            src_bf = vbf if order == 0 else zbf
            src32 = u32[0] if order == 0 else z32
            gate32 = u32[1] if order == 0 else u32[2]

            def epi_fwd(st, state, mi, ft, extra):
                if state is None:
                    return dict(kr=[s.sb(st, "ekr%d" % i, [128, HC], F32) for i in range(3)],
                                ki=[s.sb(st, "eki%d" % i, [128, HC], F32) for i in range(3)],
                                t=[s.sb(st, "et%d" % i, [128, HC], F32) for i in range(4)],
                                y=[s.sb(st, "ey%d" % i, [128, HC], BF16) for i in range(4)])
                sbi, pp = extra
                kr, ki = state["kr"][mi % 3], state["ki"][mi % 3]
                t1, t2, t3, t4 = state["t"]
                yr, yi = state["y"][(mi % 2) * 2], state["y"][(mi % 2) * 2 + 1]
                xr, xi = pp[0], pp[1]
                V = nc.vector
                cx.op(dve, lambda: V.tensor_tensor(out=t1[:, :], in0=xr[:, :], in1=kr[:, :], op=ALU.mult),
                      reads=[xr, kr], writes=[t1])
                cx.op(dve, lambda: V.tensor_tensor(out=t2[:, :], in0=xi[:, :], in1=ki[:, :], op=ALU.mult),
                      reads=[xi, ki], writes=[t2])
                cx.op(dve, lambda: V.tensor_tensor(out=yr[:, :], in0=t1[:, :], in1=t2[:, :], op=ALU.subtract),
                      reads=[t1, t2], writes=[yr])
                cx.op(dve, lambda: V.tensor_tensor(out=t3[:, :], in0=xr[:, :], in1=ki[:, :], op=ALU.mult),
                      reads=[xr, ki], writes=[t3])
                cx.op(dve, lambda: V.tensor_tensor(out=t4[:, :], in0=xi[:, :], in1=kr[:, :], op=ALU.mult),
                      reads=[xi, kr], writes=[t4])
                cx.op(dve, lambda: V.tensor_tensor(out=yi[:, :], in0=t3[:, :], in1=t4[:, :], op=ALU.add),
                      reads=[t3, t4], writes=[yi])
                if ft == 0:
                    cx.op(dve, lambda: V.tensor_tensor(out=yr[0:1, :], in0=xr[0:1, :], in1=kr[0:1, :], op=ALU.mult),
                          reads=[xr, kr], writes=[yr])
                    cx.op(dve, lambda: V.tensor_tensor(out=yi[0:1, :], in0=xi[0:1, :], in1=ki[0:1, :], op=ALU.mult),
                          reads=[xi, ki], writes=[yi])
                cx.store(sp, yr, Ysp[ft, :, :], yr[:, :])
                cx.store(sp, yi, Ysp[ST + ft, :, :], yi[:, :])
            def pre_fwd(state, mi, ft):
                kr, ki = state["kr"][mi % 3], state["ki"][mi % 3]
                cx.load(sp, kr, kr[:, :], Kspec[order, 0, ft, :, :])
                cx.load(sp, ki, ki[:, :], Kspec[order, 1, ft, :, :])
            s.linear("hf%d" % order, src_bf, ST, HC, 0, ["dftc", "dftsf"], list(range(ST)), HC, epi_fwd, pre=pre_fwd)

            def epi_inv(st, state, mi, tt, extra):
                if state is None:
                    d = dict(a=[s.sb(st, "ia%d" % i, [128, HC], F32) for i in range(3)],
                             g=[s.sb(st, "ig%d" % i, [128, HC], F32) for i in range(3)],
                             w=[s.sb(st, "iw%d" % i, [128, HC], F32) for i in range(2)],
                             zb=[s.sb(st, "izb%d" % i, [128, HC], BF16) for i in range(2)],
                             bias=s.sb(st, "ibias", [128, 2, HC], F32), k=[0])
                    cx.load(sp, d["bias"], d["bias"][:, :, :], bias_in[:, :, :])
                    if order == 1:
                        d["yo"] = [s.sb(st, "iyo%d" % i, [128, N], BF16) for i in range(HCT)]
                        d["tp"] = [s.ps(st, "itp%d" % i, [128, 4, 128]) for i in range(2)]
                    return d
                sbi, pp = extra
                a, g, w = state["a"][mi % 3], state["g"][mi % 3], state["w"][mi % 2]
                zb = state["zb"][mi % 2]
                bias = state["bias"]
                V = nc.vector
                cx.op(dve, lambda: V.tensor_tensor(out=w[:, :], in0=a[:, :], in1=bias[:, order, :], op=ALU.mult),
                      reads=[a, bias], writes=[w])
                cx.op(dve, lambda: V.tensor_tensor(out=w[:, :], in0=pp[0][:, :], in1=w[:, :], op=ALU.add),
                      reads=[pp[0], w], writes=[w])
                cx.op(dve, lambda: V.tensor_tensor(out=w[:, :], in0=w[:, :], in1=g[:, :], op=ALU.mult),
                      reads=[w, g], writes=[w])
                if order == 0:
                    cx.store(sp, w, z32[tt, :, :], w[:, :])
                    cx.op(act, lambda: nc.scalar.copy(out=zb[:, :], in_=w[:, :]), reads=[w], writes=[zb])
                    cx.store(sp, zb, zbf[tt, :, :], zb[:, :])
                else:
                    p = state["tp"][mi % 2]
                    cx.prewait(pe, reads=[w, s.ident], writes=[p])
                    for i in range(HCT):
                        inst = nc.tensor.transpose(out=p[:, i, :], in_=w[:, i * 128:(i + 1) * 128],
                                                   identity=s.ident[:, :])
                    cx.mark(pe.tag(inst), [w], [p])
                    for i in range(HCT):
                        yo = state["yo"][i]
                        cx.op(act, lambda: nc.scalar.copy(out=yo[:, tt * 128:(tt + 1) * 128], in_=p[:, i, :]),
                              reads=[p], writes=[yo])
                    if mi == ST - 1:
                        for i in range(HCT):
                            for hn in range(2):
                                cx.store(sp, state["yo"][i], yh[(i * 2 + hn) * 128:(i * 2 + hn + 1) * 128, :],
                                         state["yo"][i][:, hn * T:(hn + 1) * T])
            def pre_inv(state, mi, tt):
                a, g = state["a"][mi % 3], state["g"][mi % 3]
                cx.load(sp, a, a[:, :], src32[tt, :, :])
                cx.load(sp, g, g[:, :], gate32[tt, :, :])
            s.linear("hi%d" % order, Ysp, 2 * ST, HC, 0, ["dfti"], list(range(ST)), HC, epi_inv, pre=pre_inv)

        if s.stop_after == "h6":
            return "stop"
        for m in range(2 * HCT):
            ev = cx.allgather(yh[m * 128:(m + 1) * 128, :], yhG[m * 256:(m + 1) * 256, :],
                              [[0, 1], [2, 3], [4, 5], [6, 7]])
        pool.wait(ev)
        gv = yhG.ap().rearrange("(i h j p) t -> i h j p t", i=HCT, h=2, j=2, p=128)
        for j in range(2):
            for i in range(HCT):
                cx.gdma(s.ypT[NH + j * HCT + i, :, :],
                        gv[i, bass.ds(s.rank_g, 1), j, :, :].rearrange("o p t -> p (o t)"))
        pool.wait((cx.gsem, cx.gsem.v))
        s.wprep_D()
        cx.barrier()


class Mixer1:
    def mixer1(s):
        c = s.cfg
        nc, cx = s.nc, s.cx
        act, dve, pe, sp, pool = _acts(s)
        T, DT, GT, GW = c.T, c.DT, c.GT, c.GRID_W
        ROWS = T // GW
        HR = 8
        HB = HR * GW
        hT = s.dram("h1T", [DT, 128, T], F32)
        dT = s.dram("d1T", [DT, 128, T], BF16)
        hal = s.dram("hal", [DT * 128, 2 * HB], F32)
        halG = s.dram("halG", [DT * 2 * 128, 2 * HB], F32)
        rcnt_in = s.inp("rcnt", [128, 4, T])
        psc_in = s.inp("pscT", [128, DT])
        rankv_in = s.inp("rankv", [128, 4])
        s.phase_norm(s.xT, hT, T, s.modT[1], 3, F32, "m1n")
        hv = hal.ap().rearrange("(k p) t -> k p t", p=128)
        ev = cx.gdma(hv[:, :, 0:HB], hT[:, :, 0:HB])
        ev = cx.gdma(hv[:, :, HB:2 * HB], hT[:, :, T - HB:T])
        pool.wait(ev)
        for dt in range(DT):
            ev = cx.allgather(hal[dt * 128:(dt + 1) * 128, :], halG[dt * 256:(dt + 1) * 256, :],
                              [[0, 1], [2, 3], [4, 5], [6, 7]])
        gv = halG.ap().rearrange("(k j p) t -> k j p t", j=2, p=128)
        ER, EC = ROWS + 2 * HR, GW + 16
        with ExitStack() as st:
            rcnt = s.sb(st, "rcnt", [128, 4, T], F32)
            psc = s.sb(st, "psc", [128, DT], F32)
            comb = s.sb(st, "comb", [128, DT], F32)
            rankv = s.sb(st, "rankv1", [128, 4], F32)
            cx.load(sp, rcnt, rcnt[:, :, :], rcnt_in[:, :, :])
            cx.load(sp, psc, psc[:, :], psc_in[:, :])
            cx.load(sp, rankv, rankv[:, :], rankv_in[:, :])
            m5 = s.modT[1].t[:, :, :].rearrange("p j i -> p (j i)")[:, 5 * DT:6 * DT]
            cx.op(dve, lambda: nc.vector.tensor_tensor(out=comb[:, :], in0=psc[:, :], in1=m5, op=ALU.mult),
                  reads=[psc, s.modT[1]], writes=[comb])
            s.comb = comb
            hf = [s.sb(st, "phf%d" % i, [128, T], F32) for i in range(2)]
            ht = [s.sb(st, "pht%d" % i, [128, HB], F32) for i in range(2)]
            hb_ = [s.sb(st, "phb%d" % i, [128, HB], F32) for i in range(2)]
            E = [s.sb(st, "pE%d" % i, [128, ER, EC], F32) for i in range(4)]
            mo = [s.sb(st, "pmo%d" % i, [128, T], F32) for i in range(2)]
            do = [s.sb(st, "pdo%d" % i, [128, T], BF16) for i in range(2)]
            for i in range(4):
                cx.op(dve, lambda: nc.vector.memset(E[i][:, :, :], 0.0), writes=[E[i]])
            sp.wait(ev)
            for dt in range(DT):
                on_pool = (dt % 3 == 2)
                eng = pool if on_pool else dve
                V = nc.gpsimd if on_pool else nc.vector
                gi = dt // GT
                nst = gi + 1
                a = hf[dt % 2]
                t_, b_ = ht[dt % 2], hb_[dt % 2]
                cx.load(sp, a, a[:, :], hT[dt, :, :])
                cx.load(sp, t_, t_[:, :], gv[dt, 0, :, HB:2 * HB])
                cx.load(sp, b_, b_[:, :], gv[dt, 1, :, 0:HB])
                e0, e1 = (E[2], E[3]) if on_pool else (E[0], E[1])
                cx.op(act, lambda: nc.scalar.copy(out=e0[:, HR:HR + ROWS, 8:8 + GW],
                                                  in_=a[:, :].rearrange("p (r c) -> p r c", c=GW)),
                      reads=[a], writes=[e0])
                cx.op(act, lambda: nc.scalar.activation(out=e0[:, 0:HR, 8:8 + GW],
                                                        in_=t_[:, :].rearrange("p (r c) -> p r c", c=GW),
                                                        func=AF.Copy, scale=rankv[:, 2:3]),
                      reads=[t_, rankv], writes=[e0])
                cx.op(act, lambda: nc.scalar.activation(out=e0[:, HR + ROWS:ER, 8:8 + GW],
                                                        in_=b_[:, :].rearrange("p (r c) -> p r c", c=GW),
                                                        func=AF.Copy, scale=rankv[:, 3:4]),
                      reads=[b_, rankv], writes=[e0])
                cur, nxt = e0, e1
                for k in range(nst):
                    if k == 0:
                        lo, hi, sa, sb_ = 1, EC, -1, 0
                    else:
                        sh = 1 << (k - 1)
                        lo, hi, sa, sb_ = sh, EC - sh, -sh, sh
                    cx.op(eng, lambda: V.tensor_tensor(out=nxt[:, :, lo:hi], in0=cur[:, :, lo + sa:hi + sa],
                                                       in1=cur[:, :, lo + sb_:hi + sb_], op=ALU.add),
                          reads=[cur], writes=[nxt])
                    cur, nxt = nxt, cur
                for k in range(nst):
                    if k == 0:
                        lo, hi, sa, sb_ = 1, ER, -1, 0
                    else:
                        sh = 1 << (k - 1)
                        lo, hi, sa, sb_ = sh, ER - sh, -sh, sh
                    cx.op(eng, lambda: V.tensor_tensor(out=nxt[:, lo:hi, :], in0=cur[:, lo + sa:hi + sa, :],
                                                       in1=cur[:, lo + sb_:hi + sb_, :], op=ALU.add),
                          reads=[cur], writes=[nxt])
                    cur, nxt = nxt, cur
                m = mo[dt % 2]
                d = do[dt % 2]
                cx.op(eng, lambda: V.tensor_tensor(out=m[:, :].rearrange("p (r c) -> p r c", c=GW),
                                                   in0=cur[:, HR:HR + ROWS, 8:8 + GW],
                                                   in1=rcnt[:, gi, :].rearrange("p (r c) -> p r c", c=GW),
                                                   op=ALU.mult), reads=[cur, rcnt], writes=[m])
                cx.op(eng, lambda: V.tensor_tensor(out=d[:, :], in0=m[:, :], in1=a[:, :], op=ALU.subtract),
                      reads=[m, a], writes=[d])
                cx.store(sp, d, dT[dt, :, :], d[:, :])
                if (2 * nst) % 2 == 1 or True:
                    cx.op(eng, lambda: V.memset(e0[:, :, :], 0.0), writes=[e0])
            cx.barrier()
            SB = min(512, T)

            NSBP = T // SB

            def epi_pool(st2, state, mi, mt, extra):
                if state is None:
                    return dict(xs=[s.sb(st2, "ppx%d" % i, [128, SB], F32) for i in range(2 * NSBP)],
                                xo=[s.sb(st2, "ppo%d" % i, [128, SB], F32) for i in range(3)], k=[0])
                sbi, pp = extra
                k = state["k"][0]
                state["k"][0] += 1
                xs = state["xs"][(mi % 2) * NSBP + sbi]
                xo = state["xo"][k % 3]
                dtt = s.cur_gi * GT + mt
                sl = slice(sbi * SB, (sbi + 1) * SB)
                cx.op(dve, lambda: nc.vector.scalar_tensor_tensor(out=xo[:, :], in0=pp[0][:, :],
                                                                  scalar=comb[:, dtt:dtt + 1], in1=xs[:, :],
                                                                  op0=ALU.mult, op1=ALU.add),
                      reads=[pp[0], xs, comb], writes=[xo])
                cx.store(sp, xo, s.xT[dtt, :, sl], xo[:, :])

            def pre_pool(state, mi, mt):
                dtt = s.cur_gi * GT + mt
                for sbi in range(NSBP):
                    xs = state["xs"][(mi % 2) * NSBP + sbi]
                    cx.load(sp, xs, xs[:, :], s.xT[dtt, :, sbi * SB:(sbi + 1) * SB])
            for gi in range(4):
                s.cur_gi = gi
                s.linear("pl%d" % gi, dT, GT, T, 0, ["poolw"], list(range(GT)), SB, epi_pool,
                         wrow0=gi * c.G, kt0=gi * GT, pre=pre_pool)


class Program(Phases, Mixer0, Mixer1):
    def __init__(s, cfg, stop_after=None):
        super().__init__(cfg, stop_after)
        s.scr = {}

    def build(s):
        c = s.cfg
        nc, cx = s.nc, s.cx
        stop = s.stop_after
        s.x_in = s.inp("x", [c.T, c.D])
        s.ctx_in = s.inp("ctx", [c.NCTX, c.D])
        s.out = nc.dram_tensor("out", [c.T, c.D], F32, kind="ExternalOutput")
        s.consts()
        s.wprep_begin()
        if stop in ("xin", "mod", "wprep"):
            if stop == "mod":
                s.phase_mod()
            if stop == "wprep":
                for nm in ("w1", "w3"):
                    s.wcast((nm, 0, 1), "%s_0_1" % nm, c.FT * 128, c.DT * 128)
                s.wflush()
                s.cx.sp.wait(s.W[("w3", 0, 1)][1])
            s.xT = s.dram("xT", [c.DT, 128, c.T], F32)
            s.phase_xin(s.x_in, s.xT, c.T)
            return s.finish(None)
        for nm in ("w1", "w3"):
            s.wcast((nm, 0, 1), "%s_0_1" % nm, c.FT * 128, c.DT * 128)
        s.wcast(("w2", 0, 1), "w2_0_1", c.DT * 128, c.FT * 128)
        s.wflush(max_pieces=6)
        s.phase_mod()
        s.wflush()
        s.wprep_B()

        xT = s.dram("xT", [c.DT, 128, c.T], F32)
        cT = s.dram("cT", [c.DT, 128, c.NCTX], F32)
        s.xT, s.cT = xT, cT
        s.phase_xin(s.x_in, xT, c.T)
        s.phase_xin(s.ctx_in, cT, c.NCTX)

        s.ffn("a", 0, 1, xT, c.T, s.modT[0], 0)
        if stop == "ffn1":
            return s.finish(None)
        s.ffn("c", 0, 1, cT, c.NCTX, s.modC, 0)
        s.mixer0()
        if stop in ("mix0", "m0a", "m0b", "h1", "h2", "h3", "h4", "h6", "h7"):
            return s.finish(None)
        s.ffn("b", 0, 2, xT, c.T, s.modT[0], 6)
        s.ffn("d", 1, 1, xT, c.T, s.modT[1], 0)
        if stop == "ffn3":
            return s.finish(None)
        s.mixer1()
        if stop == "mix1":
            return s.finish(None)
        s.ffn("e", 1, 2, xT, c.T, s.modT[1], 6)
        gain_in = s.inp("gainT", [128, c.DT])
        gain = s.sb(s.es, "gain", [128, c.DT], F32)
        cx.load(cx.sp, gain, gain[:, :], gain_in[:, :])
        return s.finish(gain)

    def finish(s, gain):
        s.cx.sp.wait((s.wcc, s.wcc.v))
        s.cx.sp.wait((s.wsem, s.wsem.v))
        s.phase_out(s.xT, gain)
        s.es.close()
        return s.nc

    def wprep_B(s):
        c = s.cfg
        s.wcast("win", "w_in", c.PT * 128, c.DT * 128)
        s.wcast("wout", "w_out", c.DT * 128, c.DT * 128)
        s.wcast("dftc", "dft_c", c.N, c.N, BF16)
        s.wcast("dftsf", "dft_sf", c.N, c.N, BF16)
        s.wcast("dfti", "dft_i", c.N, 2 * c.N, BF16)
        s.wflush()

    def wprep_C(s):
        c = s.cfg
        for l, which in ((0, 2), (1, 1)):
            for nm in ("w1", "w3"):
                s.wcast((nm, l, which), "%s_%d_%d" % (nm, l, which), c.FT * 128, c.DT * 128)
            s.wcast(("w2", l, which), "w2_%d_%d" % (l, which), c.DT * 128, c.FT * 128)
        s.wflush()
        s.wlocal("poolw", "pool_w", 4 * c.G, c.G)

    def wprep_D(s):
        c = s.cfg
        for nm in ("w1", "w3"):
            s.wcast((nm, 1, 2), "%s_1_2" % nm, c.FT * 128, c.DT * 128)
        s.wcast(("w2", 1, 2), "w2_1_2", c.DT * 128, c.FT * 128)
        s.wflush()


def tile_w(W):
    K, M = W.shape
    KT, MT = K // 128, M // 128
    return np.ascontiguousarray(W.reshape(KT, 128, MT, 128).transpose(2, 1, 0, 3)).reshape(MT * 128, KT * 128)


def shard_rows(A, core, rows_p=None):
    n = A.shape[0] // NCORES
    if rows_p is None or rows_p == n:
        return np.ascontiguousarray(A[core * n:(core + 1) * n])
    npc = n // rows_p
    B = A.reshape(npc, NCORES, rows_p, A.shape[1])
    return np.ascontiguousarray(B[:, core]).reshape(n, A.shape[1])


def host_inputs(cfg, inp, needed, pieces):
    c = cfg
    f32 = np.float32
    maps = [dict() for _ in range(NCORES)]
    shared = {}

    def put_shard(name, A):
        if name not in needed:
            return
        for core in range(NCORES):
            maps[core][name] = shard_rows(A, core, pieces.get(name))

    def put_all(name, A):
        if name not in needed:
            return
        A = np.ascontiguousarray(A)
        for core in range(NCORES):
            maps[core][name] = A

    x = np.asarray(inp["x"], f32)
    ctx = np.asarray(inp["ctx"], f32)
    for core in range(NCORES):
        b, r = core // 2, core % 2
        maps[core]["x"] = np.ascontiguousarray(x[b, r * c.T:(r + 1) * c.T])
        maps[core]["ctx"] = np.ascontiguousarray(ctx[b])
    put_all("ident", np.eye(128, dtype=f32))
    cc = np.zeros((8, c.D), f32)
    cc[:4] = np.asarray(inp["c"], f32)
    cc[4] = np.asarray(inp["c_ctx"], f32)
    put_all("ccT", cc.reshape(8, c.DT, 128).transpose(2, 1, 0))
    for l in range(2):
        wm = np.asarray(inp["w_mod"][l], f32)
        bm = np.asarray(inp["b_mod"][l], f32)
        for core in range(NCORES):
            cols = slice(core * c.MODI * 128, (core + 1) * c.MODI * 128)
            w = wm[:, cols].reshape(c.DT, 128, c.MODI, 128).transpose(2, 1, 0, 3)
            maps[core]["wmod%d" % l] = np.ascontiguousarray(w).reshape(c.MODI, 128, c.DT * 128)
            maps[core]["bmod%d" % l] = np.ascontiguousarray(bm[cols].reshape(c.MODI, 128).T)
        ffn_w = {(1, "w1"): inp["ffn1_w1"], (1, "w3"): inp["ffn1_w3"], (1, "w2"): inp["ffn1_w2"],
                 (2, "w1"): inp["ffn2_w1"], (2, "w3"): inp["ffn2_w3"], (2, "w2"): inp["ffn2_w2"]}
        for which in (1, 2):
            for nm in ("w1", "w3", "w2"):
                key = "%s_%d_%d" % (nm, l, which)
                if key in needed:
                    put_shard(key, tile_w(np.asarray(ffn_w[(which, nm)][l], f32)))
    if "gainT" in needed:
        put_all("gainT", np.asarray(inp["final_gain"], f32).reshape(c.DT, 128).T)
    return maps, put_shard, put_all


_CACHE = {}


def run(cfg, inputs, stop_after=None):
    key = (cfg.D, cfg.DFF, cfg.N, cfg.NCTX, stop_after)
    if key not in _CACHE:
        P = Program(cfg, stop_after)
        P.build()
        print('BUILD: dsems', len(P.cx.all_dsems), 'engine tags', [(e.name, e.sem.v) for e in P.cx.engs], 'wcc', P.wcc.v, 'cc', P.cx.ccsem.v, flush=True)
        _CACHE[key] = P
    P = _CACHE[key]
    needed = set(P.inputs.keys())
    maps, put_shard, put_all = host_inputs(cfg, inputs, needed, P.wpieces)
    host_inputs_mixers(cfg, inputs, needed, maps, put_shard, put_all)
    for m in maps:
        missing = needed - set(m.keys())
        assert not missing, missing
        for k in list(m.keys()):
            if k not in needed:
                del m[k]
            else:
                shp, dt = P.inputs[k]
                assert tuple(m[k].shape) == tuple(shp), (k, m[k].shape, shp)
    res = run_bass_kernel_spmd(P.nc, maps, core_ids=list(range(NCORES)))
    out = np.zeros((cfg.B, cfg.N, cfg.D), np.float32)
    for core in range(NCORES):
        b, r = core // 2, core % 2
        out[b, r * cfg.T:(r + 1) * cfg.T] = res.results[core]["out"]
    return out


def host_inputs_mixers(cfg, inp, needed, maps, put_shard, put_all):
    c = cfg
    f32 = np.float32
    bf = ml_dtypes.bfloat16
    if "w_in" in needed:
        put_shard("w_in", tile_w(np.asarray(inp["ab_w_in"][0], f32)))
    if "w_out" in needed:
        put_shard("w_out", tile_w(np.asarray(inp["ab_w_out"][0], f32)))
    if "dft_c" in needed:
        N, N2 = c.N, 2 * c.N
        a = np.arange(N, dtype=np.int64)
        m = (a[:, None] * a[None, :]) % N2
        ang = m.astype(np.float64) * (2.0 * np.pi / N2)
        Tc = np.cos(ang)
        Sf = -np.sin(ang)
        sgn = np.where(a % 2 == 0, 1.0, -1.0)
        Sf[:, 0] = sgn
        Si = -np.sin(ang)
        Si[0, :] = sgn
        put_shard("dft_c", tile_w(Tc.astype(f32)).astype(bf))
        put_shard("dft_sf", tile_w(Sf.astype(f32)).astype(bf))
        put_shard("dft_i", tile_w(np.concatenate([Tc, Si], 0).astype(f32)).astype(bf))
        del ang, m, Tc, Sf, Si
    if "rotC" in needed:
        NH, HD, T = c.NH, c.HD, c.T
        nf = HD // 4
        inv = (f32(10000.0) ** (-np.arange(nf, dtype=f32) / f32(nf))).astype(f32)
        lg = np.asarray(inp["ret_log_decay"][0], f32).reshape(1, 2 * NH)
        put_all("lgrep", np.tile(lg, (128, 1)))
        jj = np.arange(128)[:, None]
        ii = np.arange(128)[None, :]
        rc = np.zeros((128, 4, 128), f32)
        rc[:, 0] = ii - jj
        rc[:, 1] = (ii > jj)
        rc[:, 2] = (jj > ii)
        rc[:, 3] = 2.0 * (ii == jj)
        put_all("rc128", rc)
        tl = np.arange(T) % 128
        rcT = np.zeros((128, 2, T), f32)
        rcT[:, 0] = tl + 1
        rcT[:, 1] = 128 - tl
        put_all("rcT", rcT)
        NCC = c.NCTX // 128
        p = np.arange(128)
        rcp = np.zeros((128, 2 + 2 * NCC), f32)
        rcp[:, 0] = 127 - p
        rcp[:, 1] = p
        for cc in range(NCC):
            rcp[:, 2 + cc] = c.NCTX - 1 - (cc * 128 + p)
            rcp[:, 2 + NCC + cc] = cc * 128 + p
        put_all("rcp", rcp)
        for core in range(NCORES):
            r = core % 2
            pos = r * T + np.arange(T)
            row = (pos // c.GRID_W).astype(f32)
            col = (pos % c.GRID_W).astype(f32)
            ang = np.concatenate([row[:, None] * inv, col[:, None] * inv], -1).astype(f32)
            cs, sn = np.cos(ang).astype(f32), np.sin(ang).astype(f32)
            maps[core]["rotC"] = np.ascontiguousarray(np.concatenate([cs.T, cs.T], 0))
            maps[core]["rotS"] = np.ascontiguousarray(np.concatenate([-sn.T, sn.T], 0))
            rv = np.zeros((128, 4), f32)
            rv[:, 0], rv[:, 1], rv[:, 2], rv[:, 3] = r * T, (1 - r) * T, r, 1 - r
            maps[core]["rankv"] = rv
    if "zT" in needed:
        N, HW, HC, HCT, ST = c.N, c.HW, c.HC, c.HCT, c.N // 128
        t = np.linspace(0.0, 1.0, N, dtype=f32)[:, None]
        bands = 16
        f = np.linspace(1e-4, bands - 1, bands, dtype=f32)[None, :]
        w = (f32(2.0 * math.pi) * np.arange(N, dtype=f32)[:, None] / f32(N)).astype(f32)
        z = np.concatenate([t, np.cos(f * w), -np.sin(f * w)], -1).astype(f32)
        put_all("zT", z.T)
        put_all("fw1", np.asarray(inp["hy_f_w1"][0], f32))
        put_all("fw2", np.asarray(inp["hy_f_w2"][0], f32))
        put_all("fw3", np.asarray(inp["hy_f_w3"][0], f32))
        fr = np.asarray(inp["hy_f_freq"][0], f32)
        put_all("fbf", np.stack([np.asarray(inp["hy_f_b1"][0], f32), np.asarray(inp["hy_f_b2"][0], f32),
                                 np.asarray(inp["hy_f_b3"][0], f32), fr[0], fr[1], fr[2]], 1))
        put_all("negt", -(t[:, 0].reshape(ST, 128).T))
        max_decay = math.log(1e-2) / 0.3
        min_decay = math.log(1e-2) / 1.5
        deltas = np.abs(np.linspace(min_decay, max_decay, HW, dtype=f32)).astype(f32)
        w4 = np.asarray(inp["hy_f_w4"][0], f32).reshape(64, 2, 2, HW)
        cw = np.asarray(inp["hy_conv_w"][0], f32).reshape(3, 3, HW)
        cb = np.asarray(inp["hy_conv_b"][0], f32).reshape(3, HW)
        hb = np.asarray(inp["hy_bias"][0], f32)
        for core in range(NCORES):
            r = core % 2
            sl = slice(r * HC, (r + 1) * HC)
            maps[core]["w4my"] = np.ascontiguousarray(w4[:, :, :, sl]).reshape(64, 4 * HC)
            maps[core]["deltarow"] = np.tile(deltas[sl][None, :], (128, 1))
            maps[core]["hcw"] = np.ascontiguousarray(cw[:, :, sl].reshape(3, 3, HCT, 128).transpose(3, 1, 2, 0))
            maps[core]["hcb"] = np.ascontiguousarray(cb[:, sl].reshape(3, HCT, 128).transpose(2, 0, 1))
            maps[core]["biasrow"] = np.tile(hb[:, sl][None, :, :], (128, 1, 1))
    host_inputs_mixer1(cfg, inp, needed, maps, put_shard, put_all)


def host_inputs_mixer1(cfg, inp, needed, maps, put_shard, put_all):
    c = cfg
    f32 = np.float32
    if "pool_w" in needed:
        pw = np.asarray(inp["pool_w"][0], f32)
        put_all("pool_w", np.concatenate([tile_w(pw[g]) for g in range(4)], 0))
    if "pscT" in needed:
        put_all("pscT", np.asarray(inp["pool_scale"][0], f32).reshape(c.DT, 128).T)
    if "rcnt" in needed:
        T, GW = c.T, c.GRID_W
        NR = c.N // GW
        for core in range(NCORES):
            r = core % 2
            pos = r * T + np.arange(T)
            row, col = pos // GW, pos % GW
            rc = np.zeros((4, T), f32)
            for gi, w in enumerate((2, 4, 8, 16)):
                lo, hi = -(w // 2), w - 1 - w // 2
                cr = np.minimum(row + hi, NR - 1) - np.maximum(row + lo, 0) + 1
                cc = np.minimum(col + hi, GW - 1) - np.maximum(col + lo, 0) + 1
                rc[gi] = 1.0 / (cr * cc).astype(f32)
            maps[core]["rcnt"] = np.tile(rc[None], (128, 1, 1))
            if "rankv" not in maps[core]:
                rv = np.zeros((128, 4), f32)
                rv[:, 0], rv[:, 1], rv[:, 2], rv[:, 3] = r * T, (1 - r) * T, r, 1 - r
                maps[core]["rankv"] = rv


def kernel(**inputs):
    return run(Cfg(), inputs)
```

```python
import math
from contextlib import ExitStack
import numpy as np
import ml_dtypes
import concourse.bass as bass
import concourse.mybir as mybir
from concourse.bass_utils import run_bass_kernel_spmd

F32 = mybir.dt.float32
BF16 = mybir.dt.bfloat16
I32 = mybir.dt.int32
ALU = mybir.AluOpType
AF = mybir.ActivationFunctionType
NCORES = 8


class Cfg:
    def __init__(s, D=2048, DFF=5632, N=4096, NCTX=256, NH=8, GRID_W=64):
        s.B = 4
        s.D, s.DFF, s.N, s.NCTX, s.NH, s.GRID_W = D, DFF, N, NCTX, NH, GRID_W
        s.HD = 128
        s.RW = NH * 128
        s.HW = D - s.RW
        assert s.RW == D // 2
        s.PROJ = 4 * s.RW + 3 * s.HW
        s.T = N // 2
        s.DT = D // 128
        s.FT = DFF // 128
        s.PT = s.PROJ // 128
        s.HC = s.HW // 2
        s.HCT = s.HC // 128
        s.G = D // 4
        s.GT = s.G // 128
        s.NMODT = 9 * s.DT
        s.MODI = s.NMODT // 8
        assert s.NMODT % 8 == 0
        s.ROWS = s.T // GRID_W
        s.NCH = s.T // 128
        s.NF = N
        s.EPS = 1e-6


class CSem:
    def __init__(s, nc, name):
        s.h = nc.alloc_semaphore(name)
        s.v = 0
        s.name = name


class Eng:
    def __init__(s, ctx, e, name):
        s.ctx, s.e, s.name = ctx, e, name
        s.sem = CSem(ctx.nc, "p_" + name)
        s.seen = {}

    def wait(s, ev):
        if ev is None:
            return
        sem, v = ev
        if s.seen.get(sem, 0) >= v:
            return
        s.e.wait_ge(sem.h, v)
        s.seen[sem] = v

    def tag(s, inst):
        s.sem.v += 1
        inst.then_inc(s.sem.h, 1)
        return (s.sem, s.sem.v)


class Buf:
    def __init__(s, t=None, name=""):
        s.t = t
        s.name = name
        s.w = None
        s.r = {}
        s.dsem = None

    def __getitem__(s, idx):
        return s.t[idx]


class Ctx:
    def __init__(s, nc):
        s.nc = nc
        s.pe = Eng(s, nc.tensor, "pe")
        s.act = Eng(s, nc.scalar, "act")
        s.dve = Eng(s, nc.vector, "dve")
        s.pool = Eng(s, nc.gpsimd, "pool")
        s.sp = Eng(s, nc.sync, "sp")
        s.engs = [s.pe, s.act, s.dve, s.pool, s.sp]
        s.free_dsems = []
        s.used_dsems = []
        s.all_dsems = []
        s.bar = CSem(nc, "bar")
        s.ccsem = CSem(nc, "cc")
        s.gsem = CSem(nc, "gdma")
        s.bufs = []
        s.nsem = 0

    def op(s, eng, fn, reads=(), writes=(), tag=True):
        for b in reads:
            eng.wait(b.w)
        for b in writes:
            eng.wait(b.w)
            for sem, v in list(b.r.items()):
                eng.wait((sem, v))
        inst = fn()
        if tag:
            ev = eng.tag(inst)
            s.mark(ev, reads, writes)
        return inst

    def mark(s, ev, reads, writes):
        for b in reads:
            if b.r.get(ev[0], 0) < ev[1]:
                b.r[ev[0]] = ev[1]
        for b in writes:
            b.w = ev
            b.r = {}

    def prewait(s, eng, reads=(), writes=()):
        for b in reads:
            eng.wait(b.w)
        for b in writes:
            eng.wait(b.w)
            for sem, v in list(b.r.items()):
                eng.wait((sem, v))

    def _dsem(s, b):
        if b.dsem is None:
            if s.free_dsems:
                b.dsem = s.free_dsems.pop()
            else:
                b.dsem = CSem(s.nc, "d%d" % len(s.all_dsems))
                s.all_dsems.append(b.dsem)
            s.used_dsems.append(b.dsem)
            s.bufs.append(b)
        return b.dsem

    def load(s, q, sb, out_ap, in_ap, multi=False, **kw):
        sem = s._dsem(sb)
        if not (multi and sb.w is not None and sb.w[0] is sem):
            q.wait(sb.w)
        for se, v in list(sb.r.items()):
            q.wait((se, v))
        inst = q.e.dma_start(out=out_ap, in_=in_ap, **kw)
        sem.v += 16
        inst.then_inc(sem.h, 16)
        sb.w = (sem, sem.v)
        sb.r = {}
        return inst

    def store(s, q, sb, out_ap, in_ap, **kw):
        sem = s._dsem(sb)
        q.wait(sb.w)
        inst = q.e.dma_start(out=out_ap, in_=in_ap, **kw)
        sem.v += 16
        inst.then_inc(sem.h, 16)
        sb.r[sem] = sem.v
        return inst

    def gdma(s, out_ap, in_ap, **kw):
        inst = s.nc.gpsimd.dma_start(out=out_ap, in_=in_ap, **kw)
        s.gsem.v += 16
        inst.then_inc(s.gsem.h, 16)
        return (s.gsem, s.gsem.v)

    def allgather(s, in_ap, out_ap, groups):
        inst = s.nc.gpsimd.collective_compute("AllGather", ALU.bypass, replica_groups=groups,
                                              ins=[in_ap], outs=[out_ap])
        s.ccsem.v += 1
        inst.then_inc(s.ccsem.h, 1)
        return (s.ccsem, s.ccsem.v)

    def barrier(s):
        evs = []
        for e in s.engs:
            if e.sem.v > 0:
                evs.append((e.sem, e.sem.v))
        for d in s.used_dsems:
            evs.append((d, d.v))
        if s.gsem.v:
            evs.append((s.gsem, s.gsem.v))
        if s.ccsem.v:
            evs.append((s.ccsem, s.ccsem.v))
        for ev in evs:
            s.sp.wait(ev)
        s.bar.v += 1
        s.nc.sync.sem_inc(s.bar.h, 1)
        for e in s.engs:
            if e is not s.sp:
                e.wait((s.bar, s.bar.v))
                for ev in evs:
                    e.seen[ev[0]] = max(e.seen.get(ev[0], 0), ev[1])
        for b in s.bufs:
            b.dsem = None
        s.bufs = []
        s.free_dsems.extend(s.used_dsems)
        s.used_dsems = []


class Prog:
    def __init__(s, cfg, stop_after=None):
        s.cfg = cfg
        s.stop_after = stop_after
        s.nc = bass.Bass("TRN2", target_bir_lowering=False)
        s.cx = Ctx(s.nc)
        s.inputs = {}
        s.inp_t = {}
        s.uid = 0
        s.es = ExitStack()

    def inp(s, name, shape, dt=F32):
        if name in s.inputs:
            assert s.inputs[name][0] == tuple(shape)
            return s.inp_t[name]
        t = s.nc.dram_tensor(name, list(shape), dt, kind="ExternalInput")
        s.inputs[name] = (tuple(shape), dt)
        s.inp_t[name] = t
        return t

    def dram(s, name, shape, dt):
        return s.nc.dram_tensor(name, list(shape), dt)

    def sb(s, st, name, shape, dt):
        s.uid += 1
        t = st.enter_context(s.nc.sbuf_tensor("s%d_%s" % (s.uid, name), list(shape), dt))
        return Buf(t, name)

    def ps(s, st, name, shape, dt=F32):
        s.uid += 1
        t = st.enter_context(s.nc.psum_tensor("p%d_%s" % (s.uid, name), list(shape), dt))
        return Buf(t, name)

    def castgather(s, name, rows, cols):
        cx = s.cx
        sh = s.inp(name, [rows // 8, cols])
        tmp = s.dram(name + "_b", [rows // 8, cols], BF16)
        full = s.dram(name + "_f", [rows, cols], BF16)
        n = cols
        step = 2048
        if n <= step:
            ev = cx.gdma(tmp[:, :], sh[:, :])
        else:
            assert n % step == 0 or True
            ev = cx.gdma(tmp[:, :], sh[:, :], max_dma_last_dim=step * 4)
        cx.pool.wait(ev)
        cx.allgather(tmp[:, :], full[:, :], [list(range(NCORES))])
        return full

    def gather_bf(s, name, rows, cols):
        cx = s.cx
        sh = s.inp(name, [rows // 8, cols], BF16)
        tmp = s.dram(name + "_b", [rows // 8, cols], BF16)
        full = s.dram(name + "_f", [rows, cols], BF16)
        ev = cx.gdma(tmp[:, :], sh[:, :])
        cx.pool.wait(ev)
        cx.allgather(tmp[:, :], full[:, :], [list(range(NCORES))])
        return full


def _acts(P):
    return P.cx.act, P.cx.dve, P.cx.pe, P.cx.sp, P.cx.pool


class Phases(Prog):
    def wprep_begin(s):
        s.wsem = CSem(s.nc, "wcast")
        s.wcc = CSem(s.nc, "wcc")
        s.wpending = []
        s.wpieces = {}
        s.W = {}

    def wcast(s, key, name, rows, cols, dt_in=F32):
        rows_p = 128
        while rows_p * cols > 256 * 1024 or (rows // 8) % rows_p != 0:
            rows_p //= 2
        assert rows_p >= 1
        npc = (rows // 8) // rows_p
        sh = s.inp(name, [rows // 8, cols], dt_in)
        tmp = s.dram(name + "_b", [rows // 8, cols], BF16)
        full = s.dram(name + "_f", [rows, cols], BF16)
        s.wpieces[name] = rows_p
        for k in range(npc):
            s.wpending.append((key, sh, tmp, full, k, rows_p, cols, dt_in, k == npc - 1))

    def wlocal(s, key, name, rows, cols):
        sh = s.inp(name, [rows, cols], F32)
        full = s.dram(name + "_f", [rows, cols], BF16)
        inst = s.nc.gpsimd.dma_start(out=full[:, :], in_=sh[:, :])
        s.wsem.v += 16
        inst.then_inc(s.wsem.h, 16)
        s.cx.pool.wait((s.wsem, s.wsem.v))
        s.W[key] = (full, (s.wsem, s.wsem.v), [(s.wsem, s.wsem.v)], rows)

    def wflush(s, chunk=3, max_pieces=None):
        quads = [[0, 1, 2, 3], [4, 5, 6, 7]]
        pairs = [[0, 4], [1, 5], [2, 6], [3, 7]]
        pend = sorted(s.wpending, key=lambda x: x[4])
        s.wpending = []
        if max_pieces is not None and len(pend) > max_pieces:
            s.wpending = pend[max_pieces:]
            pend = pend[:max_pieces]
        pool = s.cx.pool
        for i0 in range(0, len(pend), chunk):
            grp = pend[i0:i0 + chunk]
            for key, sh, tmp, full, k, rp, cols, dt_in, last in grp:
                kw = {}
                if dt_in == F32 and cols > 2048:
                    kw["max_dma_last_dim"] = 2048 * 4
                inst = s.nc.gpsimd.dma_start(out=tmp[k * rp:(k + 1) * rp, :], in_=sh[k * rp:(k + 1) * rp, :], **kw)
                s.wsem.v += 16
                inst.then_inc(s.wsem.h, 16)
            pool.wait((s.wsem, s.wsem.v))
            mids = []
            for key, sh, tmp, full, k, rp, cols, dt_in, last in grp:
                mid = s.dram("wq%d" % s.uid, [4 * rp, cols], BF16)
                s.uid += 1
                inst = s.nc.gpsimd.collective_compute("AllGather", ALU.bypass, replica_groups=quads,
                                                      ins=[tmp[k * rp:(k + 1) * rp, :]], outs=[mid[:, :]])
                s.wcc.v += 1
                inst.then_inc(s.wcc.h, 1)
                mids.append(mid)
            pool.wait((s.wcc, s.wcc.v))
            for (key, sh, tmp, full, k, rp, cols, dt_in, last), mid in zip(grp, mids):
                inst = s.nc.gpsimd.collective_compute("AllGather", ALU.bypass, replica_groups=pairs,
                                                      ins=[mid[:, :]], outs=[full[k * 8 * rp:(k + 1) * 8 * rp, :]])
                s.wcc.v += 1
                inst.then_inc(s.wcc.h, 1)
                if key not in s.W:
                    s.W[key] = (full, None, [], 8 * rp)
                f_, _, evs_, rpp_ = s.W[key]
                assert len(evs_) == k
                evs_.append((s.wcc, s.wcc.v))
                s.W[key] = (f_, (s.wcc, s.wcc.v), evs_, rpp_)
            pool.wait((s.wcc, s.wcc.v))

    def consts(s):
        c = s.cfg
        st = s.es
        cx = s.cx
        s.ident = s.sb(st, "ident", [128, 128], F32)
        s.identb = s.sb(st, "identb", [128, 128], BF16)
        s.onesb = s.sb(st, "onesb", [128, 128], BF16)
        idt = s.inp("ident", [128, 128])
        cx.load(cx.sp, s.ident, s.ident[:, :], idt[:, :])
        cx.op(cx.dve, lambda: s.nc.vector.tensor_copy(out=s.identb[:, :], in_=s.ident[:, :]),
              reads=[s.ident], writes=[s.identb])
        cx.op(cx.dve, lambda: s.nc.vector.memset(s.onesb[:, :], 1.0), writes=[s.onesb])
        s.epsc = s.sb(st, "epsc", [128, 1], F32)
        cx.op(cx.dve, lambda: s.nc.vector.memset(s.epsc[:, :], c.EPS), writes=[s.epsc])
        pid = s.nc.sync.partition_id()
        s.rank_sp = pid % 2
        s.b_sp = pid // 2
        pidg = s.nc.gpsimd.partition_id()
        s.rank_g = pidg % 2

    def phase_mod(s, layers=(0, 1)):
        c = s.cfg
        nc, cx = s.nc, s.cx
        act, dve, pe, sp, pool = _acts(s)
        ccT_in = s.inp("ccT", [128, c.DT, 8])
        if not hasattr(s, "modT"):
            s.modT = [s.sb(s.es, "modT%d" % l, [128, 8, c.MODI], F32) for l in range(2)]
            s.modC = s.sb(s.es, "modC", [128, 8, c.MODI], F32)
        with ExitStack() as st:
            scT = s.sb(st, "scT", [128, c.DT, 8], F32)
            cx.load(sp, scT, scT[:, :, :], ccT_in[:, :, :])
            cx.op(act, lambda: nc.scalar.activation(out=scT[:, :, :], in_=scT[:, :, :], func=AF.Silu),
                  reads=[scT], writes=[scT])
            wts = [s.sb(st, "wmt%d" % i, [128, c.DT * 128], F32) for i in range(2)]
            pss = [s.ps(st, "modps%d" % i, [128, 8]) for i in range(2)]
            k = 0
            for l in layers:
                wm = s.inp("wmod%d" % l, [c.MODI, 128, c.DT * 128])
                bm_in = s.inp("bmod%d" % l, [128, c.MODI])
                bm = s.sb(st, "bm%d" % l, [128, c.MODI], F32)
                cx.load(sp, bm, bm[:, :], bm_in[:, :])
                modS = s.sb(st, "modS%d" % l, [128, c.MODI, 8], F32)
                modR = s.sb(st, "modR%d" % l, [128, 8, c.MODI], F32)
                for i in range(c.MODI):
                    wt = wts[k % 2]
                    ps = pss[k % 2]
                    k += 1
                    cx.load(sp, wt, wt[:, :], wm[i, :, :])
                    cx.prewait(pe, reads=[wt, scT], writes=[ps])
                    for kt in range(c.DT):
                        inst = nc.tensor.matmul(ps[:, :], lhsT=wt[:, kt * 128:(kt + 1) * 128],
                                                rhs=scT[:, kt, :], start=(kt == 0), stop=(kt == c.DT - 1))
                    cx.mark(pe.tag(inst), [wt, scT], [ps])
                    cx.op(act, lambda: nc.scalar.activation(out=modS[:, i, :], in_=ps[:, :], func=AF.Identity,
                                                            bias=bm[:, i:i + 1], scale=1.0),
                          reads=[ps, bm], writes=[modS])
                cx.op(dve, lambda: nc.vector.tensor_copy(out=modR[:, :, :],
                                                         in_=modS[:, :, :].rearrange("p i r -> p r i")),
                      reads=[modS], writes=[modR])
                msh = s.dram("modsh%d" % l, [8 * 128, c.MODI], F32)
                mfull = s.dram("modfull%d" % l, [64 * 128, c.MODI], F32)
                cx.store(sp, modR, msh.ap().rearrange("(r p) i -> p r i", p=128), modR[:, :, :])
                pool.wait((modR.dsem, modR.r[modR.dsem]))
                mmid = s.dram("modmid%d" % l, [32 * 128, c.MODI], F32)
                ev = cx.allgather(msh[:, :], mmid[:, :], [[0, 1, 2, 3], [4, 5, 6, 7]])
                pool.wait(ev)
                ev = cx.allgather(mmid[:, :], mfull[:, :], [[0, 4], [1, 5], [2, 6], [3, 7]])
                pool.wait(ev)
                sp.wait(ev)
                mf3 = mfull.ap().rearrange("(j r p) i -> j r p i", r=8, p=128)
                mf4 = mfull.ap().rearrange("(j r p) i -> r p j i", r=8, p=128)
                cx.load(sp, s.modT[l], s.modT[l][:, :, :],
                        mf4[bass.ds(s.b_sp, 1), :, :, :].rearrange("o p j i -> p (o j) i"))
                if l == 0:
                    cx.load(sp, s.modC, s.modC[:, :, :], mf4[4, :, :, :])
            cx.barrier()
        for l in layers:
            s.post_mod(s.modT[l])
        if 0 in layers:
            s.post_mod(s.modC)

    def post_mod(s, mt):
        c = s.cfg
        nc, cx = s.nc, s.cx
        flat = mt.t[:, :, :].rearrange("p j i -> p (j i)")
        for m in (1, 4, 7):
            cx.op(cx.dve, lambda: nc.vector.tensor_scalar(out=flat[:, m * c.DT:(m + 1) * c.DT],
                                                          in0=flat[:, m * c.DT:(m + 1) * c.DT],
                                                          scalar1=1.0, scalar2=None, op0=ALU.add),
                  reads=[mt], writes=[mt])
        for m in (2, 8):
            cx.op(cx.dve, lambda: nc.vector.tensor_scalar(out=flat[:, m * c.DT:(m + 1) * c.DT],
                                                          in0=flat[:, m * c.DT:(m + 1) * c.DT],
                                                          scalar1=0.5, scalar2=None, op0=ALU.mult),
                  reads=[mt], writes=[mt])

    def modcol(s, mt, m, dt):
        g = m * s.cfg.DT + dt
        return mt.t[:, :, :].rearrange("p j i -> p (j i)")[:, g:g + 1]

    def phase_xin(s, src, dstT, TOK):
        c = s.cfg
        nc, cx = s.nc, s.cx
        act, dve, pe, sp, pool = _acts(s)
        GS = min(4, TOK // 128)
        with ExitStack() as st:
            xin = [s.sb(st, "xin%d" % i, [128, c.D], F32) for i in range(2)]
            xo = [s.sb(st, "xo%d" % i, [128, c.DT, GS * 128], F32) for i in range(2)]
            pst = [s.ps(st, "xps%d" % i, [128, 4, 128]) for i in range(4)]
            k = 0
            for g in range(TOK // (GS * 128)):
                o = xo[g % 2]
                for j in range(GS):
                    tt = g * GS + j
                    xi = xin[tt % 2]
                    cx.load(sp, xi, xi[:, :], src[tt * 128:(tt + 1) * 128, :])
                    for q in range(c.DT // 4):
                        p = pst[k % 4]
                        cx.prewait(pe, reads=[xi, s.ident], writes=[p])
                        for u in range(4):
                            dt = q * 4 + u
                            inst = nc.tensor.transpose(out=p[:, u, :], in_=xi[:, dt * 128:(dt + 1) * 128],
                                                       identity=s.ident[:, :])
                        cx.mark(pe.tag(inst), [xi], [p])
                        dst = o[:, q * 4:(q + 1) * 4, j * 128:(j + 1) * 128]
                        if k % 2 == 0:
                            cx.op(act, lambda: nc.scalar.copy(out=dst, in_=p[:, :, :]), reads=[p], writes=[o])
                        else:
                            cx.op(dve, lambda: nc.vector.tensor_copy(out=dst, in_=p[:, :, :]), reads=[p], writes=[o])
                        k += 1
                cx.store(sp, o, dstT.ap().rearrange("k p t -> p k t")[:, :, g * GS * 128:(g + 1) * GS * 128],
                         o[:, :, :])
            cx.barrier()

    def phase_norm(s, srcT, dstT, TOK, mt, m_shift, out_dt, name, gain=None):
        c = s.cfg
        nc, cx = s.nc, s.cx
        act, dve, pe, sp, pool = _acts(s)
        BLK = min(512, TOK)
        with ExitStack() as st:
            xs = [[s.sb(st, "%sx%d_%d" % (name, S, d), [128, BLK], F32) for d in range(c.DT)] for S in range(2)]
            sq = [s.sb(st, "%ssq%d" % (name, i), [128, BLK], BF16) for i in range(3)]
            ssq = [s.ps(st, "%sssq%d" % (name, i), [128, BLK]) for i in range(2)]
            sd = [s.sb(st, "%ssd%d" % (name, i), [128, BLK], F32) for i in range(2)]
            rstd = [s.sb(st, "%srs%d" % (name, i), [128, BLK], F32) for i in range(2)]
            tmp = [s.sb(st, "%stmp%d" % (name, i), [128, BLK], F32) for i in range(3)]
            ho = [s.sb(st, "%sho%d" % (name, i), [128, BLK], out_dt) for i in range(3)]
            for tb in range(TOK // BLK):
                S = tb % 2
                sl = slice(tb * BLK, (tb + 1) * BLK)
                for dt in range(c.DT):
                    x = xs[S][dt]
                    cx.load(sp, x, x[:, :], srcT[dt, :, sl])
                    q = sq[dt % 3]
                    cx.op(act, lambda: nc.scalar.activation(out=q[:, :], in_=x[:, :], func=AF.Square),
                          reads=[x], writes=[q])
                    cx.op(pe, lambda: nc.tensor.matmul(ssq[S][:, :], lhsT=s.onesb[:, :], rhs=q[:, :],
                                                       start=(dt == 0), stop=(dt == c.DT - 1)),
                          reads=[q, s.onesb], writes=[ssq[S]])
                cx.op(act, lambda: nc.scalar.activation(out=sd[S][:, :], in_=ssq[S][:, :], func=AF.Sqrt,
                                                        bias=s.epsc[:, 0:1], scale=1.0 / c.D),
                      reads=[ssq[S], s.epsc], writes=[sd[S]])
                cx.op(dve, lambda: nc.vector.reciprocal(out=rstd[S][:, :], in_=sd[S][:, :]),
                      reads=[sd[S]], writes=[rstd[S]])
                for dt in range(c.DT):
                    x = xs[S][dt]
                    t = tmp[dt % 3]
                    h = ho[dt % 3]
                    cx.op(dve, lambda: nc.vector.tensor_tensor(out=t[:, :], in0=x[:, :], in1=rstd[S][:, :],
                                                               op=ALU.mult),
                          reads=[x, rstd[S]], writes=[t])
                    if gain is None:
                        sc_ap = s.modcol(mt, m_shift + 1, dt)
                        sh_ap = s.modcol(mt, m_shift, dt)
                        cx.op(act, lambda: nc.scalar.activation(out=h[:, :], in_=t[:, :], func=AF.Identity,
                                                                bias=sh_ap, scale=sc_ap),
                              reads=[t, mt], writes=[h])
                    else:
                        cx.op(act, lambda: nc.scalar.activation(out=h[:, :], in_=t[:, :], func=AF.Copy,
                                                                scale=gain[:, dt:dt + 1]),
                              reads=[t, gain], writes=[h])
                    cx.store(sp, h, dstT[dt, :, sl], h[:, :])
            cx.barrier()

    def linear(s, name, mvT, KT, TOK, tok0, Wkeys, mt_list, SB, epi, wrow0=0, kt0=0, pre=None):
        c = s.cfg
        nc, cx = s.nc, s.cx
        act, dve, pe, sp, pool = _acts(s)
        nW = len(Wkeys)
        NSET = 3 if nW == 2 else 4
        NSLOT = 4 if (nW == 1 and KT * 128 * 2 * 4 <= 48 * 1024) else 3
        with ExitStack() as st:
            NSB = TOK // SB
            mvb = [s.sb(st, "%smv%d" % (name, g), [128, KT, SB], BF16) for g in range(NSB)]
            def mvload(g):
                cx.load(sp, mvb[g], mvb[g][:, :, :],
                        mvT.ap().rearrange("k p t -> p k t")[:, kt0:kt0 + KT, tok0 + g * SB:tok0 + (g + 1) * SB])
            wts = [[s.sb(st, "%sw%d_%d" % (name, wi, sl), [128, KT * 128], BF16) for sl in range(NSLOT)]
                   for wi in range(nW)]
            pss = [[s.ps(st, "%sps%d_%d" % (name, wi, se), [128, SB]) for wi in range(nW)] for se in range(NSET)]
            Wf = []
            Wev = []
            for k in Wkeys:
                full, ev, evs, rpp = s.W[k]
                Wf.append(full)
                Wev.append((evs, rpp))
            epi_state = epi(st, None, None, None, None)
            cnt = 0

            def wload(mi_):
                mt_ = mt_list[mi_]
                for wi_ in range(nW):
                    w_ = wts[wi_][mi_ % NSLOT]
                    evs_, rpp_ = Wev[wi_]
                    sp.wait(evs_[min(len(evs_) - 1, (wrow0 + mt_ * 128 + 127) // rpp_)])
                    cx.load(sp, w_, w_[:, :], Wf[wi_][wrow0 + mt_ * 128: wrow0 + (mt_ + 1) * 128, :])
            wload(0)
            mvload(0)
            if pre is not None:
                pre(epi_state, 0, mt_list[0])
            for mi in range(1, min(NSLOT - 1, len(mt_list))):
                wload(mi)
                if mi < NSB:
                    mvload(mi)
            for g in range(min(NSLOT - 1, len(mt_list)), NSB):
                mvload(g)
            for g in range(1, NSB):
                pass
            for mi, mt in enumerate(mt_list):
                slot = mi % NSLOT
                if pre is not None and mi + 1 < len(mt_list):
                    pre(epi_state, mi + 1, mt_list[mi + 1])
                if mi + NSLOT - 1 < len(mt_list):
                    wload(mi + NSLOT - 1)
                for sbi in range(TOK // SB):
                    se = cnt % NSET
                    cnt += 1
                    for wi in range(nW):
                        w = wts[wi][slot]
                        p = pss[se][wi]
                        cx.prewait(pe, reads=[w, mvb[sbi]], writes=[p])
                        for kt in range(KT):
                            inst = nc.tensor.matmul(p[:, :], lhsT=w[:, kt * 128:(kt + 1) * 128],
                                                    rhs=mvb[sbi][:, kt, :],
                                                    start=(kt == 0), stop=(kt == KT - 1))
                        cx.mark(pe.tag(inst), [w], [p])
                    epi(st, epi_state, mi, mt, (sbi, pss[se]))
            cx.barrier()

    def ffn(s, name, l, which, xT, TOK, mt, m0):
        c = s.cfg
        nc, cx = s.nc, s.cx
        act, dve, pe, sp, pool = _acts(s)
        hT = s.scr_h(TOK)
        gT = s.scr_g(TOK)
        s.phase_norm(xT, hT, TOK, mt, m0, BF16, name + "n")
        SB = min(512, TOK)

        TB = min(2048, TOK)
        for tb in range(TOK // TB):
            def epi_up(st, state, mi, ft, extra):
                if state is None:
                    return dict(sg=[s.sb(st, name + "sg%d" % i, [128, SB], F32) for i in range(3)],
                                go=[s.sb(st, name + "go%d" % i, [128, TB], BF16) for i in range(3)], k=[0])
                sbi, pp = extra
                sg = state["sg"][state["k"][0] % 3]
                state["k"][0] += 1
                go = state["go"][mi % 3]
                cx.op(act, lambda: nc.scalar.activation(out=sg[:, :], in_=pp[0][:, :], func=AF.Silu),
                      reads=[pp[0]], writes=[sg])
                cx.op(dve, lambda: nc.vector.tensor_tensor(out=go[:, sbi * SB:(sbi + 1) * SB], in0=sg[:, :],
                                                           in1=pp[1][:, :], op=ALU.mult),
                      reads=[sg, pp[1]], writes=[go])
                if sbi == TB // SB - 1:
                    cx.store(sp, go, gT[ft, :, tb * TB:(tb + 1) * TB], go[:, :])
            s.linear(name + "u", hT, c.DT, TB, tb * TB, [("w1", l, which), ("w3", l, which)],
                     list(range(c.FT)), SB, epi_up)

        TBD = min(1024, TOK)
        gcol = m0 + 2
        for tb in range(TOK // TBD):
            NSBI = TBD // SB

            def epi_dn(st, state, mi, dt, extra):
                if state is None:
                    return dict(xs=[s.sb(st, name + "dx%d" % i, [128, SB], F32) for i in range(2 * NSBI)],
                                xo=[s.sb(st, name + "do%d" % i, [128, SB], F32) for i in range(3)], k=[0])
                sbi, pp = extra
                k = state["k"][0]
                state["k"][0] += 1
                xs = state["xs"][(mi % 2) * NSBI + sbi]
                xo = state["xo"][k % 3]
                sl = slice(tb * TBD + sbi * SB, tb * TBD + (sbi + 1) * SB)
                cx.op(dve, lambda: nc.vector.scalar_tensor_tensor(out=xo[:, :], in0=pp[0][:, :],
                                                                  scalar=s.modcol(mt, gcol, dt), in1=xs[:, :],
                                                                  op0=ALU.mult, op1=ALU.add),
                      reads=[pp[0], xs, mt], writes=[xo])
                cx.store(sp, xo, xT[dt, :, sl], xo[:, :])

            def pre_dn(state, mi, dt):
                for sbi in range(NSBI):
                    xs = state["xs"][(mi % 2) * NSBI + sbi]
                    sl = slice(tb * TBD + sbi * SB, tb * TBD + (sbi + 1) * SB)
                    cx.load(sp, xs, xs[:, :], xT[dt, :, sl])
            s.linear(name + "d", gT, c.FT, TBD, tb * TBD, [("w2", l, which)], list(range(c.DT)), SB, epi_dn,
                     pre=pre_dn)

    def scr_h(s, TOK):
        key = ("h", TOK)
        if key not in s.scr:
            s.scr[key] = s.dram("hT_%d" % TOK, [s.cfg.DT, 128, TOK], BF16)
        return s.scr[key]

    def scr_g(s, TOK):
        key = ("g", TOK)
        if key not in s.scr:
            s.scr[key] = s.dram("gT_%d" % TOK, [s.cfg.FT, 128, TOK], BF16)
        return s.scr[key]

    def phase_out(s, xT, mt_gain):
        c = s.cfg
        nc, cx = s.nc, s.cx
        act, dve, pe, sp, pool = _acts(s)
        oT = s.dram("oT", [c.DT, 128, c.T], F32)
        if mt_gain is None:
            oT = xT
        else:
            s.phase_norm(xT, oT, c.T, None, 0, F32, "fn", gain=mt_gain)
        with ExitStack() as st:
            xi = [s.sb(st, "oxi%d" % i, [128, c.DT, 128], F32) for i in range(2)]
            xo = [s.sb(st, "oxo%d" % i, [128, c.D], F32) for i in range(2)]
            pst = [s.ps(st, "ops%d" % i, [128, 4, 128]) for i in range(4)]
            k = 0
            for tt in range(c.T // 128):
                a = xi[tt % 2]
                o = xo[tt % 2]
                cx.load(sp, a, a[:, :, :], oT.ap().rearrange("k p t -> p k t")[:, :, tt * 128:(tt + 1) * 128])
                for q in range(c.DT // 4):
                    p = pst[k % 4]
                    cx.prewait(pe, reads=[a, s.ident], writes=[p])
                    for u in range(4):
                        dt = q * 4 + u
                        inst = nc.tensor.transpose(out=p[:, u, :], in_=a[:, dt, :], identity=s.ident[:, :])
                    cx.mark(pe.tag(inst), [a], [p])
                    dst = o[:, q * 512:(q + 1) * 512]
                    src = p[:, :, :].rearrange("p u j -> p (u j)")
                    if k % 2 == 0:
                        cx.op(act, lambda: nc.scalar.copy(out=dst, in_=src), reads=[p], writes=[o])
                    else:
                        cx.op(dve, lambda: nc.vector.tensor_copy(out=dst, in_=src), reads=[p], writes=[o])
                    k += 1
                cx.store(sp, o, s.out[tt * 128:(tt + 1) * 128, :], o[:, :])
            cx.barrier()


class Mixer0:
    def mixer0(s):
        c = s.cfg
        nc, cx = s.nc, s.cx
        act, dve, pe, sp, pool = _acts(s)
        NH, HWT = c.NH, c.HW // 128
        hT = s.scr_h(c.T)
        hcT = s.scr_h(c.NCTX)
        s.phase_norm(s.xT, hT, c.T, s.modT[0], 3, BF16, "m0n")
        s.phase_norm(s.cT, hcT, c.NCTX, s.modC, 3, BF16, "m0c")
        pT = s.dram("pT", [4 * NH, 128, c.T], F32)
        hyT = s.dram("hyT", [3 * HWT * 128, c.T], BF16)
        hyG = s.dram("hyG", [2 * 3 * HWT * 128, c.T], BF16)
        pcT = s.dram("pcT", [2 * NH, 128, c.NCTX], F32)
        s.pT, s.pcT = pT, pcT
        SB = min(512, c.T)
        kscale = float(c.HD) ** -0.5

        def epi_in(st, state, mi, mt, extra):
            if state is None:
                return dict(o=[s.sb(st, "ipo%d" % i, [128, c.T], F32) for i in range(2)],
                            ob=[s.sb(st, "ipb%d" % i, [128, c.T], BF16) for i in range(2)], k=[0])
            sbi, pp = extra
            hy = mt >= 4 * NH
            o = (state["ob"] if hy else state["o"])[mi % 2]
            dst = o[:, sbi * SB:(sbi + 1) * SB]
            sc = kscale if (NH <= mt < 2 * NH) else 1.0
            k = state["k"][0]
            state["k"][0] += 1
            if k % 2 == 0:
                cx.op(act, lambda: nc.scalar.mul(out=dst, in_=pp[0][:, :], mul=sc), reads=[pp[0]], writes=[o])
            else:
                cx.op(dve, lambda: nc.vector.tensor_scalar(out=dst, in0=pp[0][:, :], scalar1=sc, scalar2=None,
                                                           op0=ALU.mult), reads=[pp[0]], writes=[o])
            if sbi == c.T // SB - 1:
                if hy:
                    m = mt - 4 * NH
                    cx.store(sp, o, hyT[m * 128:(m + 1) * 128, :], o[:, :])
                else:
                    cx.store(sp, o, pT[mt, :, :], o[:, :])
        s.linear("ip", hT, c.DT, c.T, 0, ["win"], list(range(c.PT)), SB, epi_in)
        for m in range(3 * HWT):
            ev = cx.allgather(hyT[m * 128:(m + 1) * 128, :], hyG[m * 256:(m + 1) * 256, :],
                              [[0, 1], [2, 3], [4, 5], [6, 7]])
        s.hy_ev = ev
        s.hyG = hyG

        SBc = min(512, c.NCTX)

        def epi_ctx(st, state, mi, mt, extra):
            if state is None:
                return dict(o=[s.sb(st, "ico%d" % i, [128, c.NCTX], F32) for i in range(2)])
            sbi, pp = extra
            o = state["o"][mi % 2]
            sc = kscale if mt < 2 * NH else 1.0
            cx.op(act, lambda: nc.scalar.mul(out=o[:, sbi * SBc:(sbi + 1) * SBc], in_=pp[0][:, :], mul=sc),
                  reads=[pp[0]], writes=[o])
            if sbi == c.NCTX // SBc - 1:
                cx.store(sp, o, pcT[mt - NH, :, :], o[:, :])
        s.linear("ic", hcT, c.DT, c.NCTX, 0, ["win"], list(range(NH, 3 * NH)), SBc, epi_ctx)

        ypT = s.dram("ypT", [c.DT, 128, c.T], BF16)
        s.ypT = ypT
        if s.stop_after == "m0a":
            return
        s.retention()
        if s.stop_after == "m0b":
            return
        if s.hyena() == "stop":
            return
        if s.stop_after == "h7":
            return

        NSBO = c.T // SB

        def epi_out(st, state, mi, dt, extra):
            if state is None:
                return dict(xs=[s.sb(st, "opx%d" % i, [128, SB], F32) for i in range(2 * NSBO)],
                            xo=[s.sb(st, "opo%d" % i, [128, SB], F32) for i in range(3)], k=[0])
            sbi, pp = extra
            k = state["k"][0]
            state["k"][0] += 1
            xs = state["xs"][(mi % 2) * NSBO + sbi]
            xo = state["xo"][k % 3]
            sl = slice(sbi * SB, (sbi + 1) * SB)
            cx.op(dve, lambda: nc.vector.scalar_tensor_tensor(out=xo[:, :], in0=pp[0][:, :],
                                                              scalar=s.modcol(s.modT[0], 5, dt), in1=xs[:, :],
                                                              op0=ALU.mult, op1=ALU.add),
                  reads=[pp[0], xs, s.modT[0]], writes=[xo])
            cx.store(sp, xo, s.xT[dt, :, sl], xo[:, :])

        def pre_out(state, mi, dt):
            for sbi in range(NSBO):
                xs = state["xs"][(mi % 2) * NSBO + sbi]
                cx.load(sp, xs, xs[:, :], s.xT[dt, :, sbi * SB:(sbi + 1) * SB])
        s.linear("op", ypT, c.DT, c.T, 0, ["wout"], list(range(c.DT)), SB, epi_out, pre=pre_out)

    def rotary(s, st, name, src_tile, C, S, out_bf):
        c = s.cfg
        nc, cx = s.nc, s.cx
        act, dve, pe, sp, pool = _acts(s)
        x, xs, t1, t2 = s.rt["x"], s.rt["xs"], s.rt["t1"], s.rt["t2"]
        cx.load(sp, x, x[:, :], src_tile)
        cx.load(sp, xs, xs[0:64, :], src_tile[64:128, :])
        cx.load(sp, xs, xs[64:128, :], src_tile[0:64, :], multi=True)
        cx.op(dve, lambda: nc.vector.tensor_tensor(out=t1[:, :], in0=x[:, :], in1=C[:, :], op=ALU.mult),
              reads=[x, C], writes=[t1])
        cx.op(dve, lambda: nc.vector.tensor_tensor(out=t2[:, :], in0=xs[:, :], in1=S[:, :], op=ALU.mult),
              reads=[xs, S], writes=[t2])
        cx.op(dve, lambda: nc.vector.tensor_tensor(out=out_bf[:, :], in0=t1[:, :], in1=t2[:, :], op=ALU.add),
              reads=[t1, t2], writes=[out_bf])

    def tposes(s, src_bf, dst_bf, nchunks, pst_list, kcount):
        nc, cx = s.nc, s.cx
        act, dve, pe, sp, pool = _acts(s)
        per = 8
        for g in range(0, nchunks, per):
            n = min(per, nchunks - g)
            p = pst_list[kcount[0] % len(pst_list)]
            cx.prewait(pe, reads=[src_bf, s.identb], writes=[p])
            for u in range(n):
                cc = g + u
                inst = nc.tensor.transpose(out=p[:, u, :], in_=src_bf[:, cc * 128:(cc + 1) * 128],
                                           identity=s.identb[:, :])
            cx.mark(pe.tag(inst), [src_bf], [p])
            if kcount[0] % 2 == 0:
                cx.op(act, lambda: nc.scalar.copy(out=dst_bf[:, g:g + n, :], in_=p[:, 0:n, :]),
                      reads=[p], writes=[dst_bf])
            else:
                cx.op(dve, lambda: nc.vector.tensor_copy(out=dst_bf[:, g:g + n, :], in_=p[:, 0:n, :]),
                      reads=[p], writes=[dst_bf])
            kcount[0] += 1

    def retention(s):
        c = s.cfg
        nc, cx = s.nc, s.cx
        act, dve, pe, sp, pool = _acts(s)
        NH, NCH, T = c.NH, c.NCH, c.T
        NCC = c.NCTX // 128
        pT, pcT, ypT = s.pT, s.pcT, s.ypT
        rotC_in = s.inp("rotC", [128, T])
        rotS_in = s.inp("rotS", [128, T])
        lg_in = s.inp("lgrep", [128, 2 * NH])
        rc128_in = s.inp("rc128", [128, 4, 128])
        rcT_in = s.inp("rcT", [128, 2, T])
        rcp_in = s.inp("rcp", [128, 2 + 2 * NCC])
        rankv_in = s.inp("rankv", [128, 4])
        rscr = s.dram("rscr_k", [NH, 128, T], BF16)
        rscr_v = s.dram("rscr_v", [NH, 128, T], BF16)
        rscr_u = s.dram("rscr_u", [NH, 2, 128, T], F32)
        exs = s.dram("exs", [2 * 128, NH * 128], F32)
        exg = s.dram("exg", [2 * 2 * 128, NH * 128], F32)
        with ExitStack() as st:
            C = s.sb(st, "rotC", [128, T], F32)
            S = s.sb(st, "rotS", [128, T], F32)
            lg = s.sb(st, "lg", [128, 2 * NH], F32)
            nlg = s.sb(st, "nlg", [128, 2 * NH], F32)
            rc128 = s.sb(st, "rc128", [128, 4, 128], F32)
            rcT = s.sb(st, "rcT", [128, 2, T], F32)
            rcp = s.sb(st, "rcp", [128, 2 + 2 * NCC], F32)
            rankv = s.sb(st, "rankv", [128, 4], F32)
            for b_, i_ in ((C, rotC_in), (S, rotS_in), (lg, lg_in), (rcp, rcp_in), (rankv, rankv_in)):
                cx.load(sp, b_, b_[:, :], i_[:, :])
            cx.load(sp, rc128, rc128[:, :, :], rc128_in[:, :, :])
            cx.load(sp, rcT, rcT[:, :, :], rcT_in[:, :, :])
            cx.op(dve, lambda: nc.vector.tensor_scalar(out=nlg[:, :], in0=lg[:, :], scalar1=-1.0, scalar2=None,
                                                       op0=ALU.mult), reads=[lg], writes=[nlg])
            s.rt = dict(x=s.sb(st, "rx", [128, T], F32), xs=s.sb(st, "rxs", [128, T], F32),
                        t1=s.sb(st, "rt1", [128, T], F32), t2=s.sb(st, "rt2", [128, T], F32))
            kr = s.sb(st, "kr", [128, T], BF16)
            vb = s.sb(st, "vb", [128, T], BF16)
            vf = s.rt["x"]
            ktm = s.sb(st, "ktm", [128, NCH, 128], BF16)
            vtm = s.sb(st, "vtm", [128, NCH, 128], BF16)
            vdf = s.sb(st, "vdf", [128, NCH, 128], BF16)
            vdb = s.sb(st, "vdb", [128, NCH, 128], BF16)
            U = [s.sb(st, "U%d" % i, [128, NCH, 128], F32) for i in range(2)]
            dec = s.sb(st, "dec", [128, 8], F32)
            ex = s.sb(st, "ex", [128, NH, 2, 128], F32)
            ctxS = s.sb(st, "ctxS", [128, NH, 2, 128], F32)
            Sst = [s.sb(st, "Sst%d" % i, [128, 128], F32) for i in range(2)]
            pst = [s.ps(st, "rtp%d" % i, [128, 8, 128], BF16) for i in range(2)]
            pu = [s.ps(st, "rup%d" % i, [128, 4, 128]) for i in range(3)]
            kc = [0]
            kcb = s.sb(st, "kcb", [128, c.NCTX], BF16)
            vcb = s.sb(st, "vcb", [128, c.NCTX], BF16)
            kcf = s.sb(st, "kcf", [128, c.NCTX], F32)
            vcf = s.sb(st, "vcf", [128, c.NCTX], F32)
            kctm = s.sb(st, "kctm", [128, NCC, 128], BF16)
            vctm = s.sb(st, "vctm", [128, NCC, 128], BF16)
            vcd = s.sb(st, "vcd", [128, 2, NCC, 128], BF16)
            cw = s.sb(st, "cw", [128, 2 * NCC], F32)

            def expcol(dst_ap, in_ap, scale_ap, reads, wbuf):
                cx.op(act, lambda: nc.scalar.activation(out=dst_ap, in_=in_ap, func=AF.Exp, scale=scale_ap),
                      reads=reads, writes=[wbuf])

            for h in range(NH):
                lf = lg[:, h:h + 1]
                lb = lg[:, NH + h:NH + h + 1]
                expcol(dec[:, 0:1], rcp[:, 0:1], lf, [rcp, lg], dec)
                expcol(dec[:, 1:2], rcp[:, 1:2], lb, [rcp, lg], dec)
                cx.op(act, lambda: nc.scalar.activation(out=dec[:, 2:3], in_=lf, func=AF.Exp, scale=128.0),
                      reads=[lg], writes=[dec])
                cx.op(act, lambda: nc.scalar.activation(out=dec[:, 3:4], in_=lb, func=AF.Exp, scale=128.0),
                      reads=[lg], writes=[dec])
                s.rotary(st, "k", pT[NH + h, :, :], C, S, kr)
                cx.store(sp, kr, rscr[h, :, :], kr[:, :])
                cx.load(sp, vf, vf[:, :], pT[2 * NH + h, :, :])
                cx.op(act, lambda: nc.scalar.copy(out=vb[:, :], in_=vf[:, :]), reads=[vf], writes=[vb])
                s.tposes(kr, ktm, NCH, pst, kc)
                s.tposes(vb, vtm, NCH, pst, kc)
                cx.store(sp, vtm, rscr_v[h, :, :], vtm[:, :, :].rearrange("p c e -> p (c e)"))
                cx.op(dve, lambda: nc.vector.tensor_scalar(out=vdf[:, :, :], in0=vtm[:, :, :], scalar1=dec[:, 0:1],
                                                           scalar2=None, op0=ALU.mult),
                      reads=[vtm, dec], writes=[vdf])
                cx.op(dve, lambda: nc.vector.tensor_scalar(out=vdb[:, :, :], in0=vtm[:, :, :], scalar1=dec[:, 1:2],
                                                           scalar2=None, op0=ALU.mult),
                      reads=[vtm, dec], writes=[vdb])
                ku = 0
                for di, vd in enumerate((vdf, vdb)):
                    for g in range(0, NCH, 4):
                        n = min(4, NCH - g)
                        p = pu[ku % 3]
                        ku += 1
                        cx.prewait(pe, reads=[ktm, vd], writes=[p])
                        for u in range(n):
                            inst = nc.tensor.matmul(p[:, u, :], lhsT=ktm[:, g + u, :], rhs=vd[:, g + u, :],
                                                    start=True, stop=True)
                        cx.mark(pe.tag(inst), [ktm, vd], [p])
                        cx.op(act, lambda: nc.scalar.copy(out=U[di][:, g:g + n, :], in_=p[:, 0:n, :]),
                              reads=[p], writes=[U[di]])
                    cx.store(sp, U[di], rscr_u[h, di, :, :], U[di][:, :, :].rearrange("p c e -> p (c e)"))
                for di in range(2):
                    order = range(NCH) if di == 0 else range(NCH - 1, -1, -1)
                    first = True
                    for cc in order:
                        if first:
                            cx.op(dve, lambda: nc.vector.tensor_copy(out=ex[:, h, di, :], in_=U[di][:, cc, :]),
                                  reads=[U[di]], writes=[ex])
                            first = False
                        else:
                            cx.op(dve, lambda: nc.vector.scalar_tensor_tensor(
                                out=ex[:, h, di, :], in0=ex[:, h, di, :], scalar=dec[:, 2 + di:3 + di],
                                in1=U[di][:, cc, :], op0=ALU.mult, op1=ALU.add),
                                reads=[ex, U[di], dec], writes=[ex])
                for cc in range(NCC):
                    expcol(cw[:, cc:cc + 1], rcp[:, 2 + cc:3 + cc], lf, [rcp, lg], cw)
                    expcol(cw[:, NCC + cc:NCC + cc + 1], rcp[:, 2 + NCC + cc:3 + NCC + cc], lb, [rcp, lg], cw)
                cx.load(sp, kcf, kcf[:, :], pcT[h, :, :])
                cx.load(sp, vcf, vcf[:, :], pcT[NH + h, :, :])
                cx.op(act, lambda: nc.scalar.copy(out=kcb[:, :], in_=kcf[:, :]), reads=[kcf], writes=[kcb])
                cx.op(act, lambda: nc.scalar.copy(out=vcb[:, :], in_=vcf[:, :]), reads=[vcf], writes=[vcb])
                s.tposes(kcb, kctm, NCC, pst, kc)
                s.tposes(vcb, vctm, NCC, pst, kc)
                for di in range(2):
                    for cc in range(NCC):
                        cx.op(dve, lambda: nc.vector.tensor_scalar(
                            out=vcd[:, di, cc, :], in0=vctm[:, cc, :], scalar1=cw[:, di * NCC + cc:di * NCC + cc + 1],
                            scalar2=None, op0=ALU.mult), reads=[vctm, cw], writes=[vcd])
                p = pu[ku % 3]
                ku += 1
                cx.prewait(pe, reads=[kctm, vcd], writes=[p])
                for di in range(2):
                    for cc in range(NCC):
                        inst = nc.tensor.matmul(p[:, di, :], lhsT=kctm[:, cc, :], rhs=vcd[:, di, cc, :],
                                                start=(cc == 0), stop=(cc == NCC - 1))
                cx.mark(pe.tag(inst), [kctm, vcd], [p])
                cx.op(act, lambda: nc.scalar.copy(out=ctxS[:, h, :, :], in_=p[:, 0:2, :]), reads=[p], writes=[ctxS])

            for di in range(2):
                cx.store(sp, ex, exs.ap().rearrange("(d p) (h e) -> d p h e", d=2, h=NH)[di], ex[:, :, di, :])
            pool.wait((ex.dsem, ex.r[ex.dsem]))
            for di in range(2):
                ev = cx.allgather(exs[di * 128:(di + 1) * 128, :], exg[di * 256:(di + 1) * 256, :],
                                  [[0, 1], [2, 3], [4, 5], [6, 7]])
            s.wprep_C()
            exr = s.sb(st, "exr", [128, 2, NH, 128], F32)
            sp.wait(ev)
            egv = exg.ap().rearrange("(d j p) (h e) -> d j p h e", d=2, j=2, h=NH)
            cx.load(sp, exr, exr[:, 0, :, :], egv[0, 0])
            cx.load(sp, exr, exr[:, 1, :, :], egv[1, 1], multi=True)

            qr = s.sb(st, "qr", [128, T], BF16)
            qf = s.sb(st, "qf", [128, T], BF16)
            qb = s.sb(st, "qb", [128, T], BF16)
            Gf = s.sb(st, "Gf", [128, T], BF16)
            Gb = s.sb(st, "Gb", [128, T], BF16)
            DT_ = s.sb(st, "DTm", [128, 128], F32)
            Et = [s.sb(st, "Et%d" % i, [128, 128], F32) for i in range(2)]
            SD = s.sb(st, "SD", [128, NCH, 128], BF16)
            Sbf = [s.sb(st, "Sbf%d" % i, [128, NCH, 128], BF16) for i in range(2)]
            o = s.rt["t2"]
            osq = s.sb(st, "rosq", [128, T], BF16)
            gf = s.rt["x"]
            sg = s.sb(st, "rsg", [128, T], BF16)
            rs = s.rt["t1"]
            rout = s.sb(st, "rout", [128, T], BF16)
            alpha = s.sb(st, "alpha", [128, 2], F32)
            pss = [s.ps(st, "rsp%d" % i, [128, 4, 128]) for i in range(2)]
            NB = max(1, T // 512)
            BL = T // NB
            for h in range(NH):
                lf = lg[:, h:h + 1]
                lb = lg[:, NH + h:NH + h + 1]
                cx.op(act, lambda: nc.scalar.activation(out=dec[:, 2:3], in_=lf, func=AF.Exp, scale=128.0),
                      reads=[lg], writes=[dec])
                cx.op(act, lambda: nc.scalar.activation(out=dec[:, 3:4], in_=lb, func=AF.Exp, scale=128.0),
                      reads=[lg], writes=[dec])
                expcol(alpha[:, 0:1], lf, rankv[:, 0:1], [lg, rankv], alpha)
                expcol(alpha[:, 1:2], lb, rankv[:, 1:2], [lg, rankv], alpha)
                expcol(Et[0][:, :], rc128[:, 0, :], lf, [rc128, lg], Et[0])
                expcol(Et[1][:, :], rc128[:, 0, :], nlg[:, NH + h:NH + h + 1], [rc128, nlg], Et[1])
                cx.op(dve, lambda: nc.vector.tensor_tensor(out=Et[0][:, :], in0=Et[0][:, :], in1=rc128[:, 1, :],
                                                           op=ALU.mult), reads=[Et[0], rc128], writes=[Et[0]])
                cx.op(dve, lambda: nc.vector.tensor_tensor(out=Et[1][:, :], in0=Et[1][:, :], in1=rc128[:, 2, :],
                                                           op=ALU.mult), reads=[Et[1], rc128], writes=[Et[1]])
                cx.op(dve, lambda: nc.vector.tensor_tensor(out=DT_[:, :], in0=Et[0][:, :], in1=Et[1][:, :],
                                                           op=ALU.add), reads=[Et[0], Et[1]], writes=[DT_])
                cx.op(dve, lambda: nc.vector.tensor_tensor(out=DT_[:, :], in0=DT_[:, :], in1=rc128[:, 3, :],
                                                           op=ALU.add), reads=[DT_, rc128], writes=[DT_])
                expcol(Gf[:, :], rcT[:, 0, :], lf, [rcT, lg], Gf)
                expcol(Gb[:, :], rcT[:, 1, :], lb, [rcT, lg], Gb)
                s.rotary(st, "q", pT[h, :, :], C, S, qr)
                cx.op(dve, lambda: nc.vector.tensor_tensor(out=qf[:, :], in0=qr[:, :], in1=Gf[:, :], op=ALU.mult),
                      reads=[qr, Gf], writes=[qf])
                cx.op(dve, lambda: nc.vector.tensor_tensor(out=qb[:, :], in0=qr[:, :], in1=Gb[:, :], op=ALU.mult),
                      reads=[qr, Gb], writes=[qb])
                cx.load(sp, kr, kr[:, :], rscr[h, :, :])
                cx.load(sp, vtm, vtm[:, :, :].rearrange("p c e -> p (c e)"), rscr_v[h, :, :])
                for di in range(2):
                    cx.load(sp, U[di], U[di][:, :, :].rearrange("p c e -> p (c e)"), rscr_u[h, di, :, :])
                for di in range(2):
                    Sx = Sst[di]
                    cx.op(dve, lambda: nc.vector.tensor_scalar(out=Sx[:, :], in0=ctxS[:, h, di, :],
                                                               scalar1=alpha[:, di:di + 1], scalar2=None,
                                                               op0=ALU.mult), reads=[ctxS, alpha], writes=[Sx])
                    src = exr[:, di, h, :]
                    cx.op(dve, lambda: nc.vector.scalar_tensor_tensor(out=Sx[:, :], in0=src,
                                                                      scalar=rankv[:, 2 + di:3 + di], in1=Sx[:, :],
                                                                      op0=ALU.mult, op1=ALU.add),
                          reads=[exr, rankv, Sx], writes=[Sx])
                    order = range(NCH) if di == 0 else range(NCH - 1, -1, -1)
                    for cc in order:
                        cx.op(act, lambda: nc.scalar.copy(out=Sbf[di][:, cc, :], in_=Sx[:, :]),
                              reads=[Sx], writes=[Sbf[di]])
                        cx.op(dve, lambda: nc.vector.scalar_tensor_tensor(
                            out=Sx[:, :], in0=Sx[:, :], scalar=dec[:, 2 + di:3 + di], in1=U[di][:, cc, :],
                            op0=ALU.mult, op1=ALU.add), reads=[Sx, U[di], dec], writes=[Sx])
                k2 = 0
                for g in range(0, NCH, 4):
                    n = min(4, NCH - g)
                    p = pss[k2 % 2]
                    k2 += 1
                    cx.prewait(pe, reads=[kr, qr], writes=[p])
                    for u in range(n):
                        cc = g + u
                        inst = nc.tensor.matmul(p[:, u, :], lhsT=kr[:, cc * 128:(cc + 1) * 128],
                                                rhs=qr[:, cc * 128:(cc + 1) * 128], start=True, stop=True)
                    cx.mark(pe.tag(inst), [kr, qr], [p])
                    cx.op(dve, lambda: nc.vector.tensor_tensor(
                        out=SD[:, g:g + n, :], in0=p[:, 0:n, :],
                        in1=DT_[:, :].unsqueeze(1).broadcast_to([128, n, 128]), op=ALU.mult),
                        reads=[p, DT_], writes=[SD])
                for g in range(0, NCH, 4):
                    n = min(4, NCH - g)
                    p = pss[k2 % 2]
                    k2 += 1
                    cx.prewait(pe, reads=[vtm, SD, Sbf[0], Sbf[1], qf, qb], writes=[p])
                    for u in range(n):
                        cc = g + u
                        sl = slice(cc * 128, (cc + 1) * 128)
                        nc.tensor.matmul(p[:, u, :], lhsT=vtm[:, cc, :], rhs=SD[:, cc, :], start=True, stop=False)
                        nc.tensor.matmul(p[:, u, :], lhsT=Sbf[0][:, cc, :], rhs=qf[:, sl], start=False, stop=False)
                        inst = nc.tensor.matmul(p[:, u, :], lhsT=Sbf[1][:, cc, :], rhs=qb[:, sl], start=False,
                                                stop=True)
                    cx.mark(pe.tag(inst), [vtm, SD, Sbf[0], Sbf[1], qf, qb], [p])
                    dsl = slice(g * 128, (g + n) * 128)
                    cx.op(act, lambda: nc.scalar.copy(out=o[:, dsl], in_=p[:, 0:n, :].rearrange("p u i -> p (u i)")),
                          reads=[p], writes=[o])
                    cx.op(act, lambda: nc.scalar.activation(out=osq[:, dsl],
                                                            in_=p[:, 0:n, :].rearrange("p u i -> p (u i)"),
                                                            func=AF.Square), reads=[p], writes=[osq])
                cx.load(sp, gf, gf[:, :], pT[3 * NH + h, :, :])
                cx.op(act, lambda: nc.scalar.activation(out=sg[:, :], in_=gf[:, :], func=AF.Silu),
                      reads=[gf], writes=[sg])
                for nb in range(NB):
                    bsl = slice(nb * BL, (nb + 1) * BL)
                    p = pss[k2 % 2]
                    k2 += 1
                    pv = p[:, :, :].rearrange("p u i -> p (u i)")[:, 0:BL]
                    cx.op(pe, lambda: nc.tensor.matmul(pv, lhsT=s.onesb[:, :], rhs=osq[:, bsl], start=True, stop=True),
                          reads=[osq, s.onesb], writes=[p])
                    cx.op(act, lambda: nc.scalar.activation(out=rs[:, bsl], in_=pv, func=AF.Sqrt,
                                                            bias=s.epsc[:, 0:1], scale=1.0 / c.HD),
                          reads=[p, s.epsc], writes=[rs])
                cx.op(dve, lambda: nc.vector.reciprocal(out=rs[:, :], in_=rs[:, :]), reads=[rs], writes=[rs])
                cx.op(dve, lambda: nc.vector.tensor_tensor(out=o[:, :], in0=o[:, :], in1=rs[:, :], op=ALU.mult),
                      reads=[o, rs], writes=[o])
                cx.op(dve, lambda: nc.vector.tensor_tensor(out=rout[:, :], in0=o[:, :], in1=sg[:, :], op=ALU.mult),
                      reads=[o, sg], writes=[rout])
                cx.store(sp, rout, ypT[h, :, :], rout[:, :])
            cx.barrier()

    def hyena(s):
        c = s.cfg
        nc, cx = s.nc, s.cx
        act, dve, pe, sp, pool = _acts(s)
        N, T, HC, HCT, NH = c.N, c.T, c.HC, c.HCT, c.NH
        ST = N // 128
        HWT = c.HW // 128
        C4 = 4 * HC
        zT_in = s.inp("zT", [33, N])
        fw1_in = s.inp("fw1", [33, 64])
        fw2_in = s.inp("fw2", [64, 64])
        fw3_in = s.inp("fw3", [64, 64])
        fbf_in = s.inp("fbf", [64, 6])
        w4_in = s.inp("w4my", [64, C4])
        delta_in = s.inp("deltarow", [128, HC])
        negt_in = s.inp("negt", [128, ST])
        hcw_in = s.inp("hcw", [128, 3, HCT, 3])
        hcb_in = s.inp("hcb", [128, 3, HCT])
        bias_in = s.inp("biasrow", [128, 2, HC])
        hfilt = s.dram("hfilt", [ST, 128, C4], BF16)
        Hspec = s.dram("Hspec", [2, ST, 128, C4], F32)
        Kspec = s.dram("Kspec", [2, 2, ST, 128, HC], F32)
        u32 = [s.dram("u32_%d" % i, [ST, 128, HC], F32) for i in range(3)]
        vbf = s.dram("vbf", [ST, 128, HC], BF16)
        z32 = s.dram("z32", [ST, 128, HC], F32)
        zbf = s.dram("zbf", [ST, 128, HC], BF16)
        Ysp = s.dram("Ysp", [2 * ST, 128, HC], BF16)
        yh = s.dram("yh", [HCT * 2 * 128, T], BF16)
        yhG = s.dram("yhG", [HCT * 2 * 2 * 128, T], BF16)
        TWO_PI = 2.0 * math.pi
        MAGIC = 12582912.0
        PI_LO = 3.1415925

        with ExitStack() as st:
            zT = s.sb(st, "zT", [33, N], F32)
            fw1 = s.sb(st, "fw1", [33, 64], F32)
            fw2 = s.sb(st, "fw2", [64, 64], F32)
            fw3 = s.sb(st, "fw3", [64, 64], F32)
            fbf = s.sb(st, "fbf", [64, 6], F32)
            fb = s.sb(st, "fbm", [64, 3], F32)
            w4 = s.sb(st, "w4", [64, C4], F32)
            delta = s.sb(st, "delta", [128, HC], F32)
            negt = s.sb(st, "negt", [128, ST], F32)
            for b_, i_ in ((zT, zT_in), (fw1, fw1_in), (fw2, fw2_in), (fw3, fw3_in), (fbf, fbf_in), (w4, w4_in),
                           (delta, delta_in), (negt, negt_in)):
                cx.load(sp, b_, b_[:, :], i_[:, :])
            cx.op(dve, lambda: nc.vector.tensor_tensor(out=fb[:, :], in0=fbf[:, 0:3], in1=fbf[:, 3:6], op=ALU.mult),
                  reads=[fbf], writes=[fb])
            a3 = s.sb(st, "a3", [64, N], F32)
            BL = min(512, N)
            cur = [s.sb(st, "fa%d" % i, [64, BL], F32) for i in range(2)]
            v_ = s.sb(st, "fv", [64, BL], F32)
            t_ = s.sb(st, "ft", [64, BL], F32)
            n_ = s.sb(st, "fn", [64, BL], F32)
            psm = [s.ps(st, "fps%d" % i, [64, BL]) for i in range(2)]
            km = 0
            for blk in range(N // BL):
                bsl = slice(blk * BL, (blk + 1) * BL)
                for layer in range(3):
                    p = psm[km % 2]
                    km += 1
                    if layer == 0:
                        cx.op(pe, lambda: nc.tensor.matmul(p[:, :], lhsT=fw1[:, :], rhs=zT[:, bsl], start=True,
                                                           stop=True), reads=[fw1, zT], writes=[p])
                    else:
                        w = fw2 if layer == 1 else fw3
                        src = cur[(layer - 1) % 2]
                        cx.op(pe, lambda: nc.tensor.matmul(p[:, :], lhsT=w[:, :], rhs=src[:, :], start=True,
                                                           stop=True), reads=[w, src], writes=[p])
                    cx.op(act, lambda: nc.scalar.activation(out=v_[:, :], in_=p[:, :], func=AF.Identity,
                                                            bias=fb[:, layer:layer + 1],
                                                            scale=fbf[:, 3 + layer:4 + layer]),
                          reads=[p, fb, fbf], writes=[v_])
                    cx.op(dve, lambda: nc.vector.tensor_scalar(out=t_[:, :], in0=v_[:, :], scalar1=1.0 / TWO_PI,
                                                               scalar2=MAGIC, op0=ALU.mult, op1=ALU.add),
                          reads=[v_], writes=[t_])
                    cx.op(dve, lambda: nc.vector.tensor_scalar(out=n_[:, :], in0=t_[:, :], scalar1=-MAGIC,
                                                               scalar2=None, op0=ALU.add), reads=[t_], writes=[n_])
                    cx.op(dve, lambda: nc.vector.scalar_tensor_tensor(out=t_[:, :], in0=n_[:, :], scalar=-TWO_PI,
                                                                      in1=v_[:, :], op0=ALU.mult, op1=ALU.add),
                          reads=[n_, v_], writes=[t_])
                    cx.op(dve, lambda: nc.vector.tensor_scalar(out=t_[:, :], in0=t_[:, :], scalar1=-PI_LO,
                                                               scalar2=PI_LO, op0=ALU.max, op1=ALU.min),
                          reads=[t_], writes=[t_])
                    dst = a3[:, bsl] if layer == 2 else cur[layer % 2][:, :]
                    dbuf = a3 if layer == 2 else cur[layer % 2]
                    cx.op(act, lambda: nc.scalar.activation(out=dst, in_=t_[:, :], func=AF.Sin),
                          reads=[t_], writes=[dbuf])
            nbk = C4 // 512 if C4 >= 512 else 1
            BW = min(512, C4)
            psf = [[s.ps(st, "hps%d_%d" % (i, j), [128, BW]) for j in range(nbk)] for i in range(1)]
            win = [s.sb(st, "win%d" % i, [128, HC], F32) for i in range(2)]
            fo = [s.sb(st, "fo%d" % i, [128, C4], BF16) for i in range(2)]
            for pt in range(ST):
                pp = psf[0]
                for j in range(nbk):
                    cx.op(pe, lambda: nc.tensor.matmul(pp[j][:, :], lhsT=a3[:, pt * 128:(pt + 1) * 128],
                                                       rhs=w4[:, j * BW:(j + 1) * BW], start=True, stop=True),
                          reads=[a3, w4], writes=[pp[j]])
                wn = win[pt % 2]
                cx.op(act, lambda: nc.scalar.activation(out=wn[:, :], in_=delta[:, :], func=AF.Exp,
                                                        scale=negt[:, pt:pt + 1]),
                      reads=[delta, negt], writes=[wn])
                f = fo[pt % 2]
                for g in range(4):
                    j = (g * HC) // BW
                    off = (g * HC) % BW
                    cx.op(dve, lambda: nc.vector.tensor_tensor(out=f[:, g * HC:(g + 1) * HC],
                                                               in0=pp[j][:, off:off + HC], in1=wn[:, :],
                                                               op=ALU.mult), reads=[pp[j], wn], writes=[f])
                if pt == 0:
                    for g in (1, 3):
                        cx.op(dve, lambda: nc.vector.memset(f[0:1, g * HC:(g + 1) * HC], 0.0), writes=[f])
                cx.store(sp, f, hfilt[pt, :, :], f[:, :])
            cx.barrier()

        if s.stop_after == "h1":
            return "stop"
        TK = min(1024, C4)
        SBF = min(512, TK)
        for run in range(C4 // TK):
            def epi_spec(st, state, mi, ft, extra):
                if state is None:
                    return dict(o=[s.sb(st, "hso%d" % i, [128, SBF], F32) for i in range(4)], k=[0])
                sbi, pp = extra
                for ri in range(2):
                    k = state["k"][0]
                    state["k"][0] += 1
                    o = state["o"][k % 4]
                    if ri == 0:
                        cx.op(act, lambda: nc.scalar.copy(out=o[:, :], in_=pp[ri][:, :]), reads=[pp[ri]], writes=[o])
                    else:
                        cx.op(dve, lambda: nc.vector.tensor_copy(out=o[:, :], in_=pp[ri][:, :]), reads=[pp[ri]],
                              writes=[o])
                    c0 = run * TK + sbi * SBF
                    cx.store(sp, o, Hspec[ri, ft, :, c0:c0 + SBF], o[:, :])
            s.linear("hs%d" % run, hfilt, ST, TK, run * TK, ["dftc", "dftsf"], list(range(ST)), SBF, epi_spec)

        if s.stop_after == "h2":
            return "stop"
        with ExitStack() as st:
            hr = [s.sb(st, "hr%d" % i, [128, 2 * HC], F32) for i in range(2)]
            hi = [s.sb(st, "hi%d" % i, [128, 2 * HC], F32) for i in range(2)]
            tq = [s.sb(st, "tq%d" % i, [128, HC], F32) for i in range(2)]
            kr_ = [s.sb(st, "kkr%d" % i, [128, HC], F32) for i in range(2)]
            ki_ = [s.sb(st, "kki%d" % i, [128, HC], F32) for i in range(2)]
            sc = 1.0 / N
            k = 0
            for o_ in range(2):
                for ft in range(ST):
                    a, b_ = hr[k % 2], hi[k % 2]
                    t = tq[k % 2]
                    kr, ki = kr_[k % 2], ki_[k % 2]
                    k += 1
                    cs = slice(o_ * 2 * HC, (o_ + 1) * 2 * HC)
                    cx.load(sp, a, a[:, :], Hspec[0, ft, :, cs])
                    cx.load(sp, b_, b_[:, :], Hspec[1, ft, :, cs])
                    cx.op(dve, lambda: nc.vector.tensor_scalar(out=t[:, :], in0=a[:, HC:2 * HC], scalar1=sc,
                                                               scalar2=None, op0=ALU.mult), reads=[a], writes=[t])
                    cx.op(dve, lambda: nc.vector.scalar_tensor_tensor(out=kr[:, :], in0=a[:, 0:HC], scalar=sc,
                                                                      in1=t[:, :], op0=ALU.mult, op1=ALU.add),
                          reads=[a, t], writes=[kr])
                    cx.op(dve, lambda: nc.vector.tensor_scalar(out=t[:, :], in0=b_[:, HC:2 * HC], scalar1=-sc,
                                                               scalar2=None, op0=ALU.mult), reads=[b_], writes=[t])
                    cx.op(dve, lambda: nc.vector.scalar_tensor_tensor(out=ki[:, :], in0=b_[:, 0:HC], scalar=sc,
                                                                      in1=t[:, :], op0=ALU.mult, op1=ALU.add),
                          reads=[b_, t], writes=[ki])
                    if ft == 0:
                        cx.op(dve, lambda: nc.vector.tensor_scalar(out=kr[0:1, :], in0=kr[0:1, :], scalar1=0.5,
                                                                   scalar2=None, op0=ALU.mult),
                              reads=[kr], writes=[kr])
                        cx.op(dve, lambda: nc.vector.tensor_scalar(out=t[0:1, :], in0=b_[0:1, HC:2 * HC],
                                                                   scalar1=0.5 * sc, scalar2=None, op0=ALU.mult),
                              reads=[b_], writes=[t])
                        cx.op(dve, lambda: nc.vector.scalar_tensor_tensor(out=ki[0:1, :], in0=b_[0:1, 0:HC],
                                                                          scalar=0.5 * sc, in1=t[0:1, :],
                                                                          op0=ALU.mult, op1=ALU.add),
                              reads=[b_, t], writes=[ki])
                    cx.store(sp, kr, Kspec[o_, 0, ft, :, :], kr[:, :])
                    cx.store(sp, ki, Kspec[o_, 1, ft, :, :], ki[:, :])
            cx.barrier()

        if s.stop_after == "h3":
            return "stop"
        sp.wait(s.hy_ev)
        hv = s.hyG.ap().rearrange("(a h i j p) t -> a h i j p t", j=2, a=3, h=2, i=HCT, p=128)
        with ExitStack() as st:
            hcw = s.sb(st, "hcw", [128, 3, HCT, 3], F32)
            hcb = s.sb(st, "hcb", [128, 3, HCT], F32)
            cx.load(sp, hcw, hcw[:, :, :, :], hcw_in[:, :, :, :])
            cx.load(sp, hcb, hcb[:, :, :], hcb_in[:, :, :])
            ppb = [s.sb(st, "ppb%d" % i, [128, N + 4], BF16) for i in range(2)]
            uu = [s.sb(st, "uu%d" % i, [128, N], F32) for i in range(2)]
            ot = [s.sb(st, "uot%d" % i, [128, ST, 128], F32) for i in range(2)]
            otb = s.sb(st, "uotb", [128, ST, 128], BF16)
            ptp = [s.ps(st, "utp%d" % i, [128, 4, 128]) for i in range(4)]
            for i in range(2):
                cx.op(dve, lambda: nc.vector.memset(ppb[i][:, 0:2], 0.0), writes=[ppb[i]])
                cx.op(dve, lambda: nc.vector.memset(ppb[i][:, N + 2:N + 4], 0.0), writes=[ppb[i]])
            k = 0
            kt = 0
            for part in range(3):
                for i in range(HCT):
                    pb = ppb[k % 2]
                    u = uu[k % 2]
                    o = ot[k % 2]
                    k += 1
                    cx.load(sp, pb, pb[:, 2:N + 2].rearrange("p (j t) -> p j t", j=2),
                            hv[part, bass.ds(s.rank_sp, 1), i, :, :, :].rearrange("o j p t -> p (o j) t"))
                    cx.op(act, lambda: nc.scalar.activation(out=u[:, :], in_=pb[:, 2:N + 2], func=AF.Identity,
                                                            bias=hcb[:, part, i:i + 1], scale=hcw[:, part, i, 1:2]),
                          reads=[pb, hcb, hcw], writes=[u])
                    cx.op(dve, lambda: nc.vector.scalar_tensor_tensor(out=u[:, :], in0=pb[:, 1:N + 1],
                                                                      scalar=hcw[:, part, i, 0:1], in1=u[:, :],
                                                                      op0=ALU.mult, op1=ALU.add),
                          reads=[pb, hcw, u], writes=[u])
                    cx.op(dve, lambda: nc.vector.scalar_tensor_tensor(out=u[:, :], in0=pb[:, 3:N + 3],
                                                                      scalar=hcw[:, part, i, 2:3], in1=u[:, :],
                                                                      op0=ALU.mult, op1=ALU.add),
                          reads=[pb, hcw, u], writes=[u])
                    for g in range(0, ST, 4):
                        p = ptp[kt % 4]
                        cx.prewait(pe, reads=[u, s.ident], writes=[p])
                        for q in range(4):
                            sti = g + q
                            inst = nc.tensor.transpose(out=p[:, q, :], in_=u[:, sti * 128:(sti + 1) * 128],
                                                       identity=s.ident[:, :])
                        cx.mark(pe.tag(inst), [u], [p])
                        if kt % 2 == 0:
                            cx.op(act, lambda: nc.scalar.copy(out=o[:, g:g + 4, :], in_=p[:, :, :]), reads=[p],
                                  writes=[o])
                        else:
                            cx.op(dve, lambda: nc.vector.tensor_copy(out=o[:, g:g + 4, :], in_=p[:, :, :]), reads=[p],
                                  writes=[o])
                        kt += 1
                    cx.store(sp, o, u32[part].ap().rearrange("s p c -> p s c")[:, :, i * 128:(i + 1) * 128],
                             o[:, :, :])
                    if part == 0:
                        cx.op(act, lambda: nc.scalar.copy(out=otb[:, :, :], in_=o[:, :, :]), reads=[o], writes=[otb])
                        cx.store(sp, otb, vbf.ap().rearrange("s p c -> p s c")[:, :, i * 128:(i + 1) * 128],
                                 otb[:, :, :])
            cx.barrier()

        if s.stop_after == "h4":
            return "stop"
        for order in range(2):
            src_bf = vbf if order == 0 else zbf
            src32 = u32[0] if order == 0 else z32
            gate32 = u32[1] if order == 0 else u32[2]

            def epi_fwd(st, state, mi, ft, extra):
                if state is None:
                    return dict(kr=[s.sb(st, "ekr%d" % i, [128, HC], F32) for i in range(3)],
                                ki=[s.sb(st, "eki%d" % i, [128, HC], F32) for i in range(3)],
                                t=[s.sb(st, "et%d" % i, [128, HC], F32) for i in range(4)],
                                y=[s.sb(st, "ey%d" % i, [128, HC], BF16) for i in range(4)])
                sbi, pp = extra
                kr, ki = state["kr"][mi % 3], state["ki"][mi % 3]
                t1, t2, t3, t4 = state["t"]
                yr, yi = state["y"][(mi % 2) * 2], state["y"][(mi % 2) * 2 + 1]
                xr, xi = pp[0], pp[1]
                V = nc.vector
                cx.op(dve, lambda: V.tensor_tensor(out=t1[:, :], in0=xr[:, :], in1=kr[:, :], op=ALU.mult),
                      reads=[xr, kr], writes=[t1])
                cx.op(dve, lambda: V.tensor_tensor(out=t2[:, :], in0=xi[:, :], in1=ki[:, :], op=ALU.mult),
                      reads=[xi, ki], writes=[t2])
                cx.op(dve, lambda: V.tensor_tensor(out=yr[:, :], in0=t1[:, :], in1=t2[:, :], op=ALU.subtract),
                      reads=[t1, t2], writes=[yr])
                cx.op(dve, lambda: V.tensor_tensor(out=t3[:, :], in0=xr[:, :], in1=ki[:, :], op=ALU.mult),
                      reads=[xr, ki], writes=[t3])
                cx.op(dve, lambda: V.tensor_tensor(out=t4[:, :], in0=xi[:, :], in1=kr[:, :], op=ALU.mult),
                      reads=[xi, kr], writes=[t4])
                cx.op(dve, lambda: V.tensor_tensor(out=yi[:, :], in0=t3[:, :], in1=t4[:, :], op=ALU.add),
                      reads=[t3, t4], writes=[yi])
                if ft == 0:
                    cx.op(dve, lambda: V.tensor_tensor(out=yr[0:1, :], in0=xr[0:1, :], in1=kr[0:1, :], op=ALU.mult),
                          reads=[xr, kr], writes=[yr])
                    cx.op(dve, lambda: V.tensor_tensor(out=yi[0:1, :], in0=xi[0:1, :], in1=ki[0:1, :], op=ALU.mult),
                          reads=[xi, ki], writes=[yi])
                cx.store(sp, yr, Ysp[ft, :, :], yr[:, :])
                cx.store(sp, yi, Ysp[ST + ft, :, :], yi[:, :])
            def pre_fwd(state, mi, ft):
                kr, ki = state["kr"][mi % 3], state["ki"][mi % 3]
                cx.load(sp, kr, kr[:, :], Kspec[order, 0, ft, :, :])
                cx.load(sp, ki, ki[:, :], Kspec[order, 1, ft, :, :])
            s.linear("hf%d" % order, src_bf, ST, HC, 0, ["dftc", "dftsf"], list(range(ST)), HC, epi_fwd, pre=pre_fwd)

            def epi_inv(st, state, mi, tt, extra):
                if state is None:
                    d = dict(a=[s.sb(st, "ia%d" % i, [128, HC], F32) for i in range(3)],
                             g=[s.sb(st, "ig%d" % i, [128, HC], F32) for i in range(3)],
                             w=[s.sb(st, "iw%d" % i, [128, HC], F32) for i in range(2)],
                             zb=[s.sb(st, "izb%d" % i, [128, HC], BF16) for i in range(2)],
                             bias=s.sb(st, "ibias", [128, 2, HC], F32), k=[0])
                    cx.load(sp, d["bias"], d["bias"][:, :, :], bias_in[:, :, :])
                    if order == 1:
                        d["yo"] = [s.sb(st, "iyo%d" % i, [128, N], BF16) for i in range(HCT)]
                        d["tp"] = [s.ps(st, "itp%d" % i, [128, 4, 128]) for i in range(2)]
                    return d
                sbi, pp = extra
                a, g, w = state["a"][mi % 3], state["g"][mi % 3], state["w"][mi % 2]
                zb = state["zb"][mi % 2]
                bias = state["bias"]
                V = nc.vector
                cx.op(dve, lambda: V.tensor_tensor(out=w[:, :], in0=a[:, :], in1=bias[:, order, :], op=ALU.mult),
                      reads=[a, bias], writes=[w])
                cx.op(dve, lambda: V.tensor_tensor(out=w[:, :], in0=pp[0][:, :], in1=w[:, :], op=ALU.add),
                      reads=[pp[0], w], writes=[w])
                cx.op(dve, lambda: V.tensor_tensor(out=w[:, :], in0=w[:, :], in1=g[:, :], op=ALU.mult),
                      reads=[w, g], writes=[w])
                if order == 0:
                    cx.store(sp, w, z32[tt, :, :], w[:, :])
                    cx.op(act, lambda: nc.scalar.copy(out=zb[:, :], in_=w[:, :]), reads=[w], writes=[zb])
                    cx.store(sp, zb, zbf[tt, :, :], zb[:, :])
                else:
                    p = state["tp"][mi % 2]
                    cx.prewait(pe, reads=[w, s.ident], writes=[p])
                    for i in range(HCT):
                        inst = nc.tensor.transpose(out=p[:, i, :], in_=w[:, i * 128:(i + 1) * 128],
                                                   identity=s.ident[:, :])
                    cx.mark(pe.tag(inst), [w], [p])
                    for i in range(HCT):
                        yo = state["yo"][i]
                        cx.op(act, lambda: nc.scalar.copy(out=yo[:, tt * 128:(tt + 1) * 128], in_=p[:, i, :]),
                              reads=[p], writes=[yo])
                    if mi == ST - 1:
                        for i in range(HCT):
                            for hn in range(2):
                                cx.store(sp, state["yo"][i], yh[(i * 2 + hn) * 128:(i * 2 + hn + 1) * 128, :],
                                         state["yo"][i][:, hn * T:(hn + 1) * T])
            def pre_inv(state, mi, tt):
                a, g = state["a"][mi % 3], state["g"][mi % 3]
                cx.load(sp, a, a[:, :], src32[tt, :, :])
                cx.load(sp, g, g[:, :], gate32[tt, :, :])
            s.linear("hi%d" % order, Ysp, 2 * ST, HC, 0, ["dfti"], list(range(ST)), HC, epi_inv, pre=pre_inv)

        if s.stop_after == "h6":
            return "stop"
        for m in range(2 * HCT):
            ev = cx.allgather(yh[m * 128:(m + 1) * 128, :], yhG[m * 256:(m + 1) * 256, :],
                              [[0, 1], [2, 3], [4, 5], [6, 7]])
        pool.wait(ev)
        gv = yhG.ap().rearrange("(i h j p) t -> i h j p t", i=HCT, h=2, j=2, p=128)
        for j in range(2):
            for i in range(HCT):
                cx.gdma(s.ypT[NH + j * HCT + i, :, :],
                        gv[i, bass.ds(s.rank_g, 1), j, :, :].rearrange("o p t -> p (o t)"))
        pool.wait((cx.gsem, cx.gsem.v))
        s.wprep_D()
        cx.barrier()


class Mixer1:
    def mixer1(s):
        c = s.cfg
        nc, cx = s.nc, s.cx
        act, dve, pe, sp, pool = _acts(s)
        T, DT, GT, GW = c.T, c.DT, c.GT, c.GRID_W
        ROWS = T // GW
        HR = 8
        HB = HR * GW
        hT = s.dram("h1T", [DT, 128, T], F32)
        dT = s.dram("d1T", [DT, 128, T], BF16)
        hal = s.dram("hal", [DT * 128, 2 * HB], F32)
        halG = s.dram("halG", [DT * 2 * 128, 2 * HB], F32)
        rcnt_in = s.inp("rcnt", [128, 4, T])
        psc_in = s.inp("pscT", [128, DT])
        rankv_in = s.inp("rankv", [128, 4])
        s.phase_norm(s.xT, hT, T, s.modT[1], 3, F32, "m1n")
        hv = hal.ap().rearrange("(k p) t -> k p t", p=128)
        ev = cx.gdma(hv[:, :, 0:HB], hT[:, :, 0:HB])
        ev = cx.gdma(hv[:, :, HB:2 * HB], hT[:, :, T - HB:T])
        pool.wait(ev)
        for dt in range(DT):
            ev = cx.allgather(hal[dt * 128:(dt + 1) * 128, :], halG[dt * 256:(dt + 1) * 256, :],
                              [[0, 1], [2, 3], [4, 5], [6, 7]])
        gv = halG.ap().rearrange("(k j p) t -> k j p t", j=2, p=128)
        ER, EC = ROWS + 2 * HR, GW + 16
        with ExitStack() as st:
            rcnt = s.sb(st, "rcnt", [128, 4, T], F32)
            psc = s.sb(st, "psc", [128, DT], F32)
            comb = s.sb(st, "comb", [128, DT], F32)
            rankv = s.sb(st, "rankv1", [128, 4], F32)
            cx.load(sp, rcnt, rcnt[:, :, :], rcnt_in[:, :, :])
            cx.load(sp, psc, psc[:, :], psc_in[:, :])
            cx.load(sp, rankv, rankv[:, :], rankv_in[:, :])
            m5 = s.modT[1].t[:, :, :].rearrange("p j i -> p (j i)")[:, 5 * DT:6 * DT]
            cx.op(dve, lambda: nc.vector.tensor_tensor(out=comb[:, :], in0=psc[:, :], in1=m5, op=ALU.mult),
                  reads=[psc, s.modT[1]], writes=[comb])
            s.comb = comb
            hf = [s.sb(st, "phf%d" % i, [128, T], F32) for i in range(2)]
            ht = [s.sb(st, "pht%d" % i, [128, HB], F32) for i in range(2)]
            hb_ = [s.sb(st, "phb%d" % i, [128, HB], F32) for i in range(2)]
            E = [s.sb(st, "pE%d" % i, [128, ER, EC], F32) for i in range(4)]
            mo = [s.sb(st, "pmo%d" % i, [128, T], F32) for i in range(2)]
            do = [s.sb(st, "pdo%d" % i, [128, T], BF16) for i in range(2)]
            for i in range(4):
                cx.op(dve, lambda: nc.vector.memset(E[i][:, :, :], 0.0), writes=[E[i]])
            sp.wait(ev)
            for dt in range(DT):
                on_pool = (dt % 3 == 2)
                eng = pool if on_pool else dve
                V = nc.gpsimd if on_pool else nc.vector
                gi = dt // GT
                nst = gi + 1
                a = hf[dt % 2]
                t_, b_ = ht[dt % 2], hb_[dt % 2]
                cx.load(sp, a, a[:, :], hT[dt, :, :])
                cx.load(sp, t_, t_[:, :], gv[dt, 0, :, HB:2 * HB])
                cx.load(sp, b_, b_[:, :], gv[dt, 1, :, 0:HB])
                e0, e1 = (E[2], E[3]) if on_pool else (E[0], E[1])
                cx.op(act, lambda: nc.scalar.copy(out=e0[:, HR:HR + ROWS, 8:8 + GW],
                                                  in_=a[:, :].rearrange("p (r c) -> p r c", c=GW)),
                      reads=[a], writes=[e0])
                cx.op(act, lambda: nc.scalar.activation(out=e0[:, 0:HR, 8:8 + GW],
                                                        in_=t_[:, :].rearrange("p (r c) -> p r c", c=GW),
                                                        func=AF.Copy, scale=rankv[:, 2:3]),
                      reads=[t_, rankv], writes=[e0])
                cx.op(act, lambda: nc.scalar.activation(out=e0[:, HR + ROWS:ER, 8:8 + GW],
                                                        in_=b_[:, :].rearrange("p (r c) -> p r c", c=GW),
                                                        func=AF.Copy, scale=rankv[:, 3:4]),
                      reads=[b_, rankv], writes=[e0])
                cur, nxt = e0, e1
                for k in range(nst):
                    if k == 0:
                        lo, hi, sa, sb_ = 1, EC, -1, 0
                    else:
                        sh = 1 << (k - 1)
                        lo, hi, sa, sb_ = sh, EC - sh, -sh, sh
                    cx.op(eng, lambda: V.tensor_tensor(out=nxt[:, :, lo:hi], in0=cur[:, :, lo + sa:hi + sa],
                                                       in1=cur[:, :, lo + sb_:hi + sb_], op=ALU.add),
                          reads=[cur], writes=[nxt])
                    cur, nxt = nxt, cur
                for k in range(nst):
                    if k == 0:
                        lo, hi, sa, sb_ = 1, ER, -1, 0
                    else:
                        sh = 1 << (k - 1)
                        lo, hi, sa, sb_ = sh, ER - sh, -sh, sh
                    cx.op(eng, lambda: V.tensor_tensor(out=nxt[:, lo:hi, :], in0=cur[:, lo + sa:hi + sa, :],
                                                       in1=cur[:, lo + sb_:hi + sb_, :], op=ALU.add),
                          reads=[cur], writes=[nxt])
                    cur, nxt = nxt, cur
                m = mo[dt % 2]
                d = do[dt % 2]
                cx.op(eng, lambda: V.tensor_tensor(out=m[:, :].rearrange("p (r c) -> p r c", c=GW),
                                                   in0=cur[:, HR:HR + ROWS, 8:8 + GW],
                                                   in1=rcnt[:, gi, :].rearrange("p (r c) -> p r c", c=GW),
                                                   op=ALU.mult), reads=[cur, rcnt], writes=[m])
                cx.op(eng, lambda: V.tensor_tensor(out=d[:, :], in0=m[:, :], in1=a[:, :], op=ALU.subtract),
                      reads=[m, a], writes=[d])
                cx.store(sp, d, dT[dt, :, :], d[:, :])
                if (2 * nst) % 2 == 1 or True:
                    cx.op(eng, lambda: V.memset(e0[:, :, :], 0.0), writes=[e0])
            cx.barrier()
            SB = min(512, T)

            NSBP = T // SB

            def epi_pool(st2, state, mi, mt, extra):
                if state is None:
                    return dict(xs=[s.sb(st2, "ppx%d" % i, [128, SB], F32) for i in range(2 * NSBP)],
                                xo=[s.sb(st2, "ppo%d" % i, [128, SB], F32) for i in range(3)], k=[0])
                sbi, pp = extra
                k = state["k"][0]
                state["k"][0] += 1
                xs = state["xs"][(mi % 2) * NSBP + sbi]
                xo = state["xo"][k % 3]
                dtt = s.cur_gi * GT + mt
                sl = slice(sbi * SB, (sbi + 1) * SB)
                cx.op(dve, lambda: nc.vector.scalar_tensor_tensor(out=xo[:, :], in0=pp[0][:, :],
                                                                  scalar=comb[:, dtt:dtt + 1], in1=xs[:, :],
                                                                  op0=ALU.mult, op1=ALU.add),
                      reads=[pp[0], xs, comb], writes=[xo])
                cx.store(sp, xo, s.xT[dtt, :, sl], xo[:, :])

            def pre_pool(state, mi, mt):
                dtt = s.cur_gi * GT + mt
                for sbi in range(NSBP):
                    xs = state["xs"][(mi % 2) * NSBP + sbi]
                    cx.load(sp, xs, xs[:, :], s.xT[dtt, :, sbi * SB:(sbi + 1) * SB])
            for gi in range(4):
                s.cur_gi = gi
                s.linear("pl%d" % gi, dT, GT, T, 0, ["poolw"], list(range(GT)), SB, epi_pool,
                         wrow0=gi * c.G, kt0=gi * GT, pre=pre_pool)


class Program(Phases, Mixer0, Mixer1):
    def __init__(s, cfg, stop_after=None):
        super().__init__(cfg, stop_after)
        s.scr = {}

    def build(s):
        c = s.cfg
        nc, cx = s.nc, s.cx
        stop = s.stop_after
        s.x_in = s.inp("x", [c.T, c.D])
        s.ctx_in = s.inp("ctx", [c.NCTX, c.D])
        s.out = nc.dram_tensor("out", [c.T, c.D], F32, kind="ExternalOutput")
        s.consts()
        s.wprep_begin()
        if stop in ("xin", "mod", "wprep"):
            if stop == "mod":
                s.phase_mod()
            if stop == "wprep":
                for nm in ("w1", "w3"):
                    s.wcast((nm, 0, 1), "%s_0_1" % nm, c.FT * 128, c.DT * 128)
                s.wflush()
                s.cx.sp.wait(s.W[("w3", 0, 1)][1])
            s.xT = s.dram("xT", [c.DT, 128, c.T], F32)
            s.phase_xin(s.x_in, s.xT, c.T)
            return s.finish(None)
        for nm in ("w1", "w3"):
            s.wcast((nm, 0, 1), "%s_0_1" % nm, c.FT * 128, c.DT * 128)
        s.wcast(("w2", 0, 1), "w2_0_1", c.DT * 128, c.FT * 128)
        s.wflush(max_pieces=6)
        xT = s.dram("xT", [c.DT, 128, c.T], F32)
        cT = s.dram("cT", [c.DT, 128, c.NCTX], F32)
        s.xT, s.cT = xT, cT
        s.phase_xin(s.x_in, xT, c.T)
        s.phase_mod(layers=(0,))
        s.wflush()
        s.wprep_B()
        s.phase_xin(s.ctx_in, cT, c.NCTX)

        s.ffn("a", 0, 1, xT, c.T, s.modT[0], 0)
        if stop == "ffn1":
            return s.finish(None)
        s.ffn("c", 0, 1, cT, c.NCTX, s.modC, 0)
        s.phase_mod(layers=(1,))
        s.mixer0()
        if stop in ("mix0", "m0a", "m0b", "h1", "h2", "h3", "h4", "h6", "h7"):
            return s.finish(None)
        s.ffn("b", 0, 2, xT, c.T, s.modT[0], 6)
        s.ffn("d", 1, 1, xT, c.T, s.modT[1], 0)
        if stop == "ffn3":
            return s.finish(None)
        s.mixer1()
        if stop == "mix1":
            return s.finish(None)
        s.ffn("e", 1, 2, xT, c.T, s.modT[1], 6)
        gain_in = s.inp("gainT", [128, c.DT])
        gain = s.sb(s.es, "gain", [128, c.DT], F32)
        cx.load(cx.sp, gain, gain[:, :], gain_in[:, :])
        return s.finish(gain)

    def finish(s, gain):
        s.cx.sp.wait((s.wcc, s.wcc.v))
        s.cx.sp.wait((s.wsem, s.wsem.v))
        s.phase_out(s.xT, gain)
        s.es.close()
        return s.nc

    def wprep_B(s):
        c = s.cfg
        s.wcast("win", "w_in", c.PT * 128, c.DT * 128)
        s.wcast("wout", "w_out", c.DT * 128, c.DT * 128)
        s.wcast("dftc", "dft_c", c.N, c.N, BF16)
        s.wcast("dftsf", "dft_sf", c.N, c.N, BF16)
        s.wcast("dfti", "dft_i", c.N, 2 * c.N, BF16)
        s.wflush()

    def wprep_C(s):
        c = s.cfg
        for l, which in ((0, 2), (1, 1)):
            for nm in ("w1", "w3"):
                s.wcast((nm, l, which), "%s_%d_%d" % (nm, l, which), c.FT * 128, c.DT * 128)
            s.wcast(("w2", l, which), "w2_%d_%d" % (l, which), c.DT * 128, c.FT * 128)
        s.wflush()
        s.wlocal("poolw", "pool_w", 4 * c.G, c.G)

    def wprep_D(s):
        c = s.cfg
        for nm in ("w1", "w3"):
            s.wcast((nm, 1, 2), "%s_1_2" % nm, c.FT * 128, c.DT * 128)
        s.wcast(("w2", 1, 2), "w2_1_2", c.DT * 128, c.FT * 128)
        s.wflush()


def tile_w(W):
    K, M = W.shape
    KT, MT = K // 128, M // 128
    return np.ascontiguousarray(W.reshape(KT, 128, MT, 128).transpose(2, 1, 0, 3)).reshape(MT * 128, KT * 128)


def shard_rows(A, core, rows_p=None):
    n = A.shape[0] // NCORES
    if rows_p is None or rows_p == n:
        return np.ascontiguousarray(A[core * n:(core + 1) * n])
    npc = n // rows_p
    B = A.reshape(npc, NCORES, rows_p, A.shape[1])
    return np.ascontiguousarray(B[:, core]).reshape(n, A.shape[1])


def host_inputs(cfg, inp, needed, pieces):
    c = cfg
    f32 = np.float32
    maps = [dict() for _ in range(NCORES)]
    shared = {}

    def put_shard(name, A):
        if name not in needed:
            return
        for core in range(NCORES):
            maps[core][name] = shard_rows(A, core, pieces.get(name))

    def put_all(name, A):
        if name not in needed:
            return
        A = np.ascontiguousarray(A)
        for core in range(NCORES):
            maps[core][name] = A

    x = np.asarray(inp["x"], f32)
    ctx = np.asarray(inp["ctx"], f32)
    for core in range(NCORES):
        b, r = core // 2, core % 2
        maps[core]["x"] = np.ascontiguousarray(x[b, r * c.T:(r + 1) * c.T])
        maps[core]["ctx"] = np.ascontiguousarray(ctx[b])
    put_all("ident", np.eye(128, dtype=f32))
    cc = np.zeros((8, c.D), f32)
    cc[:4] = np.asarray(inp["c"], f32)
    cc[4] = np.asarray(inp["c_ctx"], f32)
    put_all("ccT", cc.reshape(8, c.DT, 128).transpose(2, 1, 0))
    for l in range(2):
        wm = np.asarray(inp["w_mod"][l], f32)
        bm = np.asarray(inp["b_mod"][l], f32)
        for core in range(NCORES):
            cols = slice(core * c.MODI * 128, (core + 1) * c.MODI * 128)
            w = wm[:, cols].reshape(c.DT, 128, c.MODI, 128).transpose(2, 1, 0, 3)
            maps[core]["wmod%d" % l] = np.ascontiguousarray(w).reshape(c.MODI, 128, c.DT * 128)
            maps[core]["bmod%d" % l] = np.ascontiguousarray(bm[cols].reshape(c.MODI, 128).T)
        ffn_w = {(1, "w1"): inp["ffn1_w1"], (1, "w3"): inp["ffn1_w3"], (1, "w2"): inp["ffn1_w2"],
                 (2, "w1"): inp["ffn2_w1"], (2, "w3"): inp["ffn2_w3"], (2, "w2"): inp["ffn2_w2"]}
        for which in (1, 2):
            for nm in ("w1", "w3", "w2"):
                key = "%s_%d_%d" % (nm, l, which)
                if key in needed:
                    put_shard(key, tile_w(np.asarray(ffn_w[(which, nm)][l], f32)))
    if "gainT" in needed:
        put_all("gainT", np.asarray(inp["final_gain"], f32).reshape(c.DT, 128).T)
    return maps, put_shard, put_all


_CACHE = {}


def run(cfg, inputs, stop_after=None):
    key = (cfg.D, cfg.DFF, cfg.N, cfg.NCTX, stop_after)
    if key not in _CACHE:
        P = Program(cfg, stop_after)
        P.build()
        print('BUILD: dsems', len(P.cx.all_dsems), 'engine tags', [(e.name, e.sem.v) for e in P.cx.engs], 'wcc', P.wcc.v, 'cc', P.cx.ccsem.v, flush=True)
        _CACHE[key] = P
    P = _CACHE[key]
    needed = set(P.inputs.keys())
    maps, put_shard, put_all = host_inputs(cfg, inputs, needed, P.wpieces)
    host_inputs_mixers(cfg, inputs, needed, maps, put_shard, put_all)
    for m in maps:
        missing = needed - set(m.keys())
        assert not missing, missing
        for k in list(m.keys()):
            if k not in needed:
                del m[k]
            else:
                shp, dt = P.inputs[k]
                assert tuple(m[k].shape) == tuple(shp), (k, m[k].shape, shp)
    res = run_bass_kernel_spmd(P.nc, maps, core_ids=list(range(NCORES)))
    out = np.zeros((cfg.B, cfg.N, cfg.D), np.float32)
    for core in range(NCORES):
        b, r = core // 2, core % 2
        out[b, r * cfg.T:(r + 1) * cfg.T] = res.results[core]["out"]
    return out


def host_inputs_mixers(cfg, inp, needed, maps, put_shard, put_all):
    c = cfg
    f32 = np.float32
    bf = ml_dtypes.bfloat16
    if "w_in" in needed:
        put_shard("w_in", tile_w(np.asarray(inp["ab_w_in"][0], f32)))
    if "w_out" in needed:
        put_shard("w_out", tile_w(np.asarray(inp["ab_w_out"][0], f32)))
    if "dft_c" in needed:
        N, N2 = c.N, 2 * c.N
        a = np.arange(N, dtype=np.int64)
        m = (a[:, None] * a[None, :]) % N2
        ang = m.astype(np.float64) * (2.0 * np.pi / N2)
        Tc = np.cos(ang)
        Sf = -np.sin(ang)
        sgn = np.where(a % 2 == 0, 1.0, -1.0)
        Sf[:, 0] = sgn
        Si = -np.sin(ang)
        Si[0, :] = sgn
        put_shard("dft_c", tile_w(Tc.astype(f32)).astype(bf))
        put_shard("dft_sf", tile_w(Sf.astype(f32)).astype(bf))
        put_shard("dft_i", tile_w(np.concatenate([Tc, Si], 0).astype(f32)).astype(bf))
        del ang, m, Tc, Sf, Si
    if "rotC" in needed:
        NH, HD, T = c.NH, c.HD, c.T
        nf = HD // 4
        inv = (f32(10000.0) ** (-np.arange(nf, dtype=f32) / f32(nf))).astype(f32)
        lg = np.asarray(inp["ret_log_decay"][0], f32).reshape(1, 2 * NH)
        put_all("lgrep", np.tile(lg, (128, 1)))
        jj = np.arange(128)[:, None]
        ii = np.arange(128)[None, :]
        rc = np.zeros((128, 4, 128), f32)
        rc[:, 0] = ii - jj
        rc[:, 1] = (ii > jj)
        rc[:, 2] = (jj > ii)
        rc[:, 3] = 2.0 * (ii == jj)
        put_all("rc128", rc)
        tl = np.arange(T) % 128
        rcT = np.zeros((128, 2, T), f32)
        rcT[:, 0] = tl + 1
        rcT[:, 1] = 128 - tl
        put_all("rcT", rcT)
        NCC = c.NCTX // 128
        p = np.arange(128)
        rcp = np.zeros((128, 2 + 2 * NCC), f32)
        rcp[:, 0] = 127 - p
        rcp[:, 1] = p
        for cc in range(NCC):
            rcp[:, 2 + cc] = c.NCTX - 1 - (cc * 128 + p)
            rcp[:, 2 + NCC + cc] = cc * 128 + p
        put_all("rcp", rcp)
        for core in range(NCORES):
            r = core % 2
            pos = r * T + np.arange(T)
            row = (pos // c.GRID_W).astype(f32)
            col = (pos % c.GRID_W).astype(f32)
            ang = np.concatenate([row[:, None] * inv, col[:, None] * inv], -1).astype(f32)
            cs, sn = np.cos(ang).astype(f32), np.sin(ang).astype(f32)
            maps[core]["rotC"] = np.ascontiguousarray(np.concatenate([cs.T, cs.T], 0))
            maps[core]["rotS"] = np.ascontiguousarray(np.concatenate([-sn.T, sn.T], 0))
            rv = np.zeros((128, 4), f32)
            rv[:, 0], rv[:, 1], rv[:, 2], rv[:, 3] = r * T, (1 - r) * T, r, 1 - r
            maps[core]["rankv"] = rv
    if "zT" in needed:
        N, HW, HC, HCT, ST = c.N, c.HW, c.HC, c.HCT, c.N // 128
        t = np.linspace(0.0, 1.0, N, dtype=f32)[:, None]
        bands = 16
        f = np.linspace(1e-4, bands - 1, bands, dtype=f32)[None, :]
        w = (f32(2.0 * math.pi) * np.arange(N, dtype=f32)[:, None] / f32(N)).astype(f32)
        z = np.concatenate([t, np.cos(f * w), -np.sin(f * w)], -1).astype(f32)
        put_all("zT", z.T)
        put_all("fw1", np.asarray(inp["hy_f_w1"][0], f32))
        put_all("fw2", np.asarray(inp["hy_f_w2"][0], f32))
        put_all("fw3", np.asarray(inp["hy_f_w3"][0], f32))
        fr = np.asarray(inp["hy_f_freq"][0], f32)
        put_all("fbf", np.stack([np.asarray(inp["hy_f_b1"][0], f32), np.asarray(inp["hy_f_b2"][0], f32),
                                 np.asarray(inp["hy_f_b3"][0], f32), fr[0], fr[1], fr[2]], 1))
        put_all("negt", -(t[:, 0].reshape(ST, 128).T))
        max_decay = math.log(1e-2) / 0.3
        min_decay = math.log(1e-2) / 1.5
        deltas = np.abs(np.linspace(min_decay, max_decay, HW, dtype=f32)).astype(f32)
        w4 = np.asarray(inp["hy_f_w4"][0], f32).reshape(64, 2, 2, HW)
        cw = np.asarray(inp["hy_conv_w"][0], f32).reshape(3, 3, HW)
        cb = np.asarray(inp["hy_conv_b"][0], f32).reshape(3, HW)
        hb = np.asarray(inp["hy_bias"][0], f32)
        for core in range(NCORES):
            r = core % 2
            sl = slice(r * HC, (r + 1) * HC)
            maps[core]["w4my"] = np.ascontiguousarray(w4[:, :, :, sl]).reshape(64, 4 * HC)
            maps[core]["deltarow"] = np.tile(deltas[sl][None, :], (128, 1))
            maps[core]["hcw"] = np.ascontiguousarray(cw[:, :, sl].reshape(3, 3, HCT, 128).transpose(3, 1, 2, 0))
            maps[core]["hcb"] = np.ascontiguousarray(cb[:, sl].reshape(3, HCT, 128).transpose(2, 0, 1))
            maps[core]["biasrow"] = np.tile(hb[:, sl][None, :, :], (128, 1, 1))
    host_inputs_mixer1(cfg, inp, needed, maps, put_shard, put_all)


def host_inputs_mixer1(cfg, inp, needed, maps, put_shard, put_all):
    c = cfg
    f32 = np.float32
    if "pool_w" in needed:
        pw = np.asarray(inp["pool_w"][0], f32)
        put_all("pool_w", np.concatenate([tile_w(pw[g]) for g in range(4)], 0))
    if "pscT" in needed:
        put_all("pscT", np.asarray(inp["pool_scale"][0], f32).reshape(c.DT, 128).T)
    if "rcnt" in needed:
        T, GW = c.T, c.GRID_W
        NR = c.N // GW
        for core in range(NCORES):
            r = core % 2
            pos = r * T + np.arange(T)
            row, col = pos // GW, pos % GW
            rc = np.zeros((4, T), f32)
            for gi, w in enumerate((2, 4, 8, 16)):
                lo, hi = -(w // 2), w - 1 - w // 2
                cr = np.minimum(row + hi, NR - 1) - np.maximum(row + lo, 0) + 1
                cc = np.minimum(col + hi, GW - 1) - np.maximum(col + lo, 0) + 1
                rc[gi] = 1.0 / (cr * cc).astype(f32)
            maps[core]["rcnt"] = np.tile(rc[None], (128, 1, 1))
            if "rankv" not in maps[core]:
                rv = np.zeros((128, 4), f32)
                rv[:, 0], rv[:, 1], rv[:, 2], rv[:, 3] = r * T, (1 - r) * T, r, 1 - r
                maps[core]["rankv"] = rv


def kernel(**inputs):
    return run(Cfg(), inputs)
```

```python
import math
from contextlib import ExitStack
import numpy as np
import ml_dtypes
import concourse.bass as bass
import concourse.mybir as mybir
from concourse.bass_utils import run_bass_kernel_spmd

F32 = mybir.dt.float32
BF16 = mybir.dt.bfloat16
I32 = mybir.dt.int32
ALU = mybir.AluOpType
AF = mybir.ActivationFunctionType
NCORES = 8


class Cfg:
    def __init__(s, D=2048, DFF=5632, N=4096, NCTX=256, NH=8, GRID_W=64):
        s.B = 4
        s.D, s.DFF, s.N, s.NCTX, s.NH, s.GRID_W = D, DFF, N, NCTX, NH, GRID_W
        s.HD = 128
        s.RW = NH * 128
        s.HW = D - s.RW
        assert s.RW == D // 2
        s.PROJ = 4 * s.RW + 3 * s.HW
        s.T = N // 2
        s.DT = D // 128
        s.FT = DFF // 128
        s.PT = s.PROJ // 128
        s.HC = s.HW // 2
        s.HCT = s.HC // 128
        s.G = D // 4
        s.GT = s.G // 128
        s.NMODT = 9 * s.DT
        s.MODI = s.NMODT // 8
        assert s.NMODT % 8 == 0
        s.ROWS = s.T // GRID_W
        s.NCH = s.T // 128
        s.NF = N
        s.EPS = 1e-6


class CSem:
    def __init__(s, nc, name):
        s.h = nc.alloc_semaphore(name)
        s.v = 0
        s.name = name


class Eng:
    def __init__(s, ctx, e, name):
        s.ctx, s.e, s.name = ctx, e, name
        s.sem = CSem(ctx.nc, "p_" + name)
        s.seen = {}

    def wait(s, ev):
        if ev is None:
            return
        sem, v = ev
        if s.seen.get(sem, 0) >= v:
            return
        s.e.wait_ge(sem.h, v)
        s.seen[sem] = v

    def tag(s, inst):
        s.sem.v += 1
        inst.then_inc(s.sem.h, 1)
        return (s.sem, s.sem.v)


class Buf:
    def __init__(s, t=None, name=""):
        s.t = t
        s.name = name
        s.w = None
        s.r = {}
        s.dsem = None

    def __getitem__(s, idx):
        return s.t[idx]


class Ctx:
    def __init__(s, nc):
        s.nc = nc
        s.pe = Eng(s, nc.tensor, "pe")
        s.act = Eng(s, nc.scalar, "act")
        s.dve = Eng(s, nc.vector, "dve")
        s.pool = Eng(s, nc.gpsimd, "pool")
        s.sp = Eng(s, nc.sync, "sp")
        s.engs = [s.pe, s.act, s.dve, s.pool, s.sp]
        s.free_dsems = []
        s.used_dsems = []
        s.all_dsems = []
        s.bar = CSem(nc, "bar")
        s.ccsem = CSem(nc, "cc")
        s.gsem = CSem(nc, "gdma")
        s.bufs = []
        s.nsem = 0

    def op(s, eng, fn, reads=(), writes=(), tag=True):
        for b in reads:
            eng.wait(b.w)
        for b in writes:
            eng.wait(b.w)
            for sem, v in list(b.r.items()):
                eng.wait((sem, v))
        inst = fn()
        if tag:
            ev = eng.tag(inst)
            s.mark(ev, reads, writes)
        return inst

    def mark(s, ev, reads, writes):
        for b in reads:
            if b.r.get(ev[0], 0) < ev[1]:
                b.r[ev[0]] = ev[1]
        for b in writes:
            b.w = ev
            b.r = {}

    def prewait(s, eng, reads=(), writes=()):
        for b in reads:
            eng.wait(b.w)
        for b in writes:
            eng.wait(b.w)
            for sem, v in list(b.r.items()):
                eng.wait((sem, v))

    def _dsem(s, b):
        if b.dsem is None:
            if s.free_dsems:
                b.dsem = s.free_dsems.pop()
            else:
                b.dsem = CSem(s.nc, "d%d" % len(s.all_dsems))
                s.all_dsems.append(b.dsem)
            s.used_dsems.append(b.dsem)
            s.bufs.append(b)
        return b.dsem

    def load(s, q, sb, out_ap, in_ap, multi=False, **kw):
        sem = s._dsem(sb)
        if not (multi and sb.w is not None and sb.w[0] is sem):
            q.wait(sb.w)
        for se, v in list(sb.r.items()):
            q.wait((se, v))
        inst = q.e.dma_start(out=out_ap, in_=in_ap, **kw)
        sem.v += 16
        inst.then_inc(sem.h, 16)
        sb.w = (sem, sem.v)
        sb.r = {}
        return inst

    def store(s, q, sb, out_ap, in_ap, **kw):
        sem = s._dsem(sb)
        q.wait(sb.w)
        inst = q.e.dma_start(out=out_ap, in_=in_ap, **kw)
        sem.v += 16
        inst.then_inc(sem.h, 16)
        sb.r[sem] = sem.v
        return inst

    def gdma(s, out_ap, in_ap, **kw):
        inst = s.nc.gpsimd.dma_start(out=out_ap, in_=in_ap, **kw)
        s.gsem.v += 16
        inst.then_inc(s.gsem.h, 16)
        return (s.gsem, s.gsem.v)

    def allgather(s, in_ap, out_ap, groups):
        inst = s.nc.gpsimd.collective_compute("AllGather", ALU.bypass, replica_groups=groups,
                                              ins=[in_ap], outs=[out_ap])
        s.ccsem.v += 1
        inst.then_inc(s.ccsem.h, 1)
        return (s.ccsem, s.ccsem.v)

    def barrier(s):
        evs = []
        for e in s.engs:
            if e.sem.v > 0:
                evs.append((e.sem, e.sem.v))
        for d in s.used_dsems:
            evs.append((d, d.v))
        if s.gsem.v:
            evs.append((s.gsem, s.gsem.v))
        if s.ccsem.v:
            evs.append((s.ccsem, s.ccsem.v))
        for ev in evs:
            s.sp.wait(ev)
        s.bar.v += 1
        s.nc.sync.sem_inc(s.bar.h, 1)
        for e in s.engs:
            if e is not s.sp:
                e.wait((s.bar, s.bar.v))
                for ev in evs:
                    e.seen[ev[0]] = max(e.seen.get(ev[0], 0), ev[1])
        for b in s.bufs:
            b.dsem = None
        s.bufs = []
        s.free_dsems.extend(s.used_dsems)
        s.used_dsems = []


class Prog:
    def __init__(s, cfg, stop_after=None):
        s.cfg = cfg
        s.stop_after = stop_after
        s.nc = bass.Bass("TRN2", target_bir_lowering=False)
        s.cx = Ctx(s.nc)
        s.inputs = {}
        s.inp_t = {}
        s.uid = 0
        s.es = ExitStack()

    def inp(s, name, shape, dt=F32):
        if name in s.inputs:
            assert s.inputs[name][0] == tuple(shape)
            return s.inp_t[name]
        t = s.nc.dram_tensor(name, list(shape), dt, kind="ExternalInput")
        s.inputs[name] = (tuple(shape), dt)
        s.inp_t[name] = t
        return t

    def dram(s, name, shape, dt):
        return s.nc.dram_tensor(name, list(shape), dt)

    def sb(s, st, name, shape, dt):
        s.uid += 1
        t = st.enter_context(s.nc.sbuf_tensor("s%d_%s" % (s.uid, name), list(shape), dt))
        return Buf(t, name)

    def ps(s, st, name, shape, dt=F32):
        s.uid += 1
        t = st.enter_context(s.nc.psum_tensor("p%d_%s" % (s.uid, name), list(shape), dt))
        return Buf(t, name)

    def castgather(s, name, rows, cols):
        cx = s.cx
        sh = s.inp(name, [rows // 8, cols])
        tmp = s.dram(name + "_b", [rows // 8, cols], BF16)
        full = s.dram(name + "_f", [rows, cols], BF16)
        n = cols
        step = 2048
        if n <= step:
            ev = cx.gdma(tmp[:, :], sh[:, :])
        else:
            assert n % step == 0 or True
            ev = cx.gdma(tmp[:, :], sh[:, :], max_dma_last_dim=step * 4)
        cx.pool.wait(ev)
        cx.allgather(tmp[:, :], full[:, :], [list(range(NCORES))])
        return full

    def gather_bf(s, name, rows, cols):
        cx = s.cx
        sh = s.inp(name, [rows // 8, cols], BF16)
        tmp = s.dram(name + "_b", [rows // 8, cols], BF16)
        full = s.dram(name + "_f", [rows, cols], BF16)
        ev = cx.gdma(tmp[:, :], sh[:, :])
        cx.pool.wait(ev)
        cx.allgather(tmp[:, :], full[:, :], [list(range(NCORES))])
        return full


def _acts(P):
    return P.cx.act, P.cx.dve, P.cx.pe, P.cx.sp, P.cx.pool


class Phases(Prog):
    def wprep_begin(s):
        s.wsem = CSem(s.nc, "wcast")
        s.wcc = CSem(s.nc, "wcc")
        s.wpending = []
        s.wpieces = {}
        s.W = {}

    def wcast(s, key, name, rows, cols, dt_in=F32):
        rows_p = 128
        while rows_p * cols > 256 * 1024 or (rows // 8) % rows_p != 0:
            rows_p //= 2
        assert rows_p >= 1
        npc = (rows // 8) // rows_p
        sh = s.inp(name, [rows // 8, cols], dt_in)
        tmp = s.dram(name + "_b", [rows // 8, cols], BF16)
        full = s.dram(name + "_f", [rows, cols], BF16)
        s.wpieces[name] = rows_p
        for k in range(npc):
            s.wpending.append((key, sh, tmp, full, k, rows_p, cols, dt_in, k == npc - 1))

    def wlocal(s, key, name, rows, cols):
        sh = s.inp(name, [rows, cols], F32)
        full = s.dram(name + "_f", [rows, cols], BF16)
        inst = s.nc.gpsimd.dma_start(out=full[:, :], in_=sh[:, :])
        s.wsem.v += 16
        inst.then_inc(s.wsem.h, 16)
        s.cx.pool.wait((s.wsem, s.wsem.v))
        s.W[key] = (full, (s.wsem, s.wsem.v), [(s.wsem, s.wsem.v)], rows)

    def wflush(s, chunk=3):
        quads = [[0, 1, 2, 3], [4, 5, 6, 7]]
        pairs = [[0, 4], [1, 5], [2, 6], [3, 7]]
        pend = sorted(s.wpending, key=lambda x: x[4])
        s.wpending = []
        pool = s.cx.pool
        for i0 in range(0, len(pend), chunk):
            grp = pend[i0:i0 + chunk]
            for key, sh, tmp, full, k, rp, cols, dt_in, last in grp:
                kw = {}
                if dt_in == F32 and cols > 2048:
                    kw["max_dma_last_dim"] = 2048 * 4
                inst = s.nc.gpsimd.dma_start(out=tmp[k * rp:(k + 1) * rp, :], in_=sh[k * rp:(k + 1) * rp, :], **kw)
                s.wsem.v += 16
                inst.then_inc(s.wsem.h, 16)
            pool.wait((s.wsem, s.wsem.v))
            mids = []
            for key, sh, tmp, full, k, rp, cols, dt_in, last in grp:
                mid = s.dram("wq%d" % s.uid, [4 * rp, cols], BF16)
                s.uid += 1
                inst = s.nc.gpsimd.collective_compute("AllGather", ALU.bypass, replica_groups=quads,
                                                      ins=[tmp[k * rp:(k + 1) * rp, :]], outs=[mid[:, :]])
                s.wcc.v += 1
                inst.then_inc(s.wcc.h, 1)
                mids.append(mid)
            pool.wait((s.wcc, s.wcc.v))
            for (key, sh, tmp, full, k, rp, cols, dt_in, last), mid in zip(grp, mids):
                inst = s.nc.gpsimd.collective_compute("AllGather", ALU.bypass, replica_groups=pairs,
                                                      ins=[mid[:, :]], outs=[full[k * 8 * rp:(k + 1) * 8 * rp, :]])
                s.wcc.v += 1
                inst.then_inc(s.wcc.h, 1)
                if key not in s.W:
                    s.W[key] = (full, None, [], 8 * rp)
                f_, _, evs_, rpp_ = s.W[key]
                assert len(evs_) == k
                evs_.append((s.wcc, s.wcc.v))
                s.W[key] = (f_, (s.wcc, s.wcc.v), evs_, rpp_)
            pool.wait((s.wcc, s.wcc.v))

    def consts(s):
        c = s.cfg
        st = s.es
        cx = s.cx
        s.ident = s.sb(st, "ident", [128, 128], F32)
        s.identb = s.sb(st, "identb", [128, 128], BF16)
        s.onesb = s.sb(st, "onesb", [128, 128], BF16)
        idt = s.inp("ident", [128, 128])
        cx.load(cx.sp, s.ident, s.ident[:, :], idt[:, :])
        cx.op(cx.dve, lambda: s.nc.vector.tensor_copy(out=s.identb[:, :], in_=s.ident[:, :]),
              reads=[s.ident], writes=[s.identb])
        cx.op(cx.dve, lambda: s.nc.vector.memset(s.onesb[:, :], 1.0), writes=[s.onesb])
        s.epsc = s.sb(st, "epsc", [128, 1], F32)
        cx.op(cx.dve, lambda: s.nc.vector.memset(s.epsc[:, :], c.EPS), writes=[s.epsc])
        pid = s.nc.sync.partition_id()
        s.rank_sp = pid % 2
        s.b_sp = pid // 2
        pidg = s.nc.gpsimd.partition_id()
        s.rank_g = pidg % 2

    def phase_mod(s):
        c = s.cfg
        nc, cx = s.nc, s.cx
        act, dve, pe, sp, pool = _acts(s)
        ccT_in = s.inp("ccT", [128, c.DT, 8])
        s.modT = [s.sb(s.es, "modT%d" % l, [128, 8, c.MODI], F32) for l in range(2)]
        s.modC = s.sb(s.es, "modC", [128, 8, c.MODI], F32)
        with ExitStack() as st:
            scT = s.sb(st, "scT", [128, c.DT, 8], F32)
            cx.load(sp, scT, scT[:, :, :], ccT_in[:, :, :])
            cx.op(act, lambda: nc.scalar.activation(out=scT[:, :, :], in_=scT[:, :, :], func=AF.Silu),
                  reads=[scT], writes=[scT])
            wts = [s.sb(st, "wmt%d" % i, [128, c.DT * 128], F32) for i in range(2)]
            pss = [s.ps(st, "modps%d" % i, [128, 8]) for i in range(2)]
            k = 0
            for l in range(2):
                wm = s.inp("wmod%d" % l, [c.MODI, 128, c.DT * 128])
                bm_in = s.inp("bmod%d" % l, [128, c.MODI])
                bm = s.sb(st, "bm%d" % l, [128, c.MODI], F32)
                cx.load(sp, bm, bm[:, :], bm_in[:, :])
                modS = s.sb(st, "modS%d" % l, [128, c.MODI, 8], F32)
                modR = s.sb(st, "modR%d" % l, [128, 8, c.MODI], F32)
                for i in range(c.MODI):
                    wt = wts[k % 2]
                    ps = pss[k % 2]
                    k += 1
                    cx.load(sp, wt, wt[:, :], wm[i, :, :])
                    cx.prewait(pe, reads=[wt, scT], writes=[ps])
                    for kt in range(c.DT):
                        inst = nc.tensor.matmul(ps[:, :], lhsT=wt[:, kt * 128:(kt + 1) * 128],
                                                rhs=scT[:, kt, :], start=(kt == 0), stop=(kt == c.DT - 1))
                    cx.mark(pe.tag(inst), [wt, scT], [ps])
                    cx.op(act, lambda: nc.scalar.activation(out=modS[:, i, :], in_=ps[:, :], func=AF.Identity,
                                                            bias=bm[:, i:i + 1], scale=1.0),
                          reads=[ps, bm], writes=[modS])
                cx.op(dve, lambda: nc.vector.tensor_copy(out=modR[:, :, :],
                                                         in_=modS[:, :, :].rearrange("p i r -> p r i")),
                      reads=[modS], writes=[modR])
                msh = s.dram("modsh%d" % l, [8 * 128, c.MODI], F32)
                mfull = s.dram("modfull%d" % l, [64 * 128, c.MODI], F32)
                cx.store(sp, modR, msh.ap().rearrange("(r p) i -> p r i", p=128), modR[:, :, :])
                pool.wait((modR.dsem, modR.r[modR.dsem]))
                mmid = s.dram("modmid%d" % l, [32 * 128, c.MODI], F32)
                ev = cx.allgather(msh[:, :], mmid[:, :], [[0, 1, 2, 3], [4, 5, 6, 7]])
                pool.wait(ev)
                ev = cx.allgather(mmid[:, :], mfull[:, :], [[0, 4], [1, 5], [2, 6], [3, 7]])
                pool.wait(ev)
                sp.wait(ev)
                mf3 = mfull.ap().rearrange("(j r p) i -> j r p i", r=8, p=128)
                mf4 = mfull.ap().rearrange("(j r p) i -> r p j i", r=8, p=128)
                cx.load(sp, s.modT[l], s.modT[l][:, :, :],
                        mf4[bass.ds(s.b_sp, 1), :, :, :].rearrange("o p j i -> p (o j) i"))
                if l == 0:
                    cx.load(sp, s.modC, s.modC[:, :, :], mf4[4, :, :, :])
            cx.barrier()
        for l in range(2):
            s.post_mod(s.modT[l])
        s.post_mod(s.modC)

    def post_mod(s, mt):
        c = s.cfg
        nc, cx = s.nc, s.cx
        flat = mt.t[:, :, :].rearrange("p j i -> p (j i)")
        for m in (1, 4, 7):
            cx.op(cx.dve, lambda: nc.vector.tensor_scalar(out=flat[:, m * c.DT:(m + 1) * c.DT],
                                                          in0=flat[:, m * c.DT:(m + 1) * c.DT],
                                                          scalar1=1.0, scalar2=None, op0=ALU.add),
                  reads=[mt], writes=[mt])
        for m in (2, 8):
            cx.op(cx.dve, lambda: nc.vector.tensor_scalar(out=flat[:, m * c.DT:(m + 1) * c.DT],
                                                          in0=flat[:, m * c.DT:(m + 1) * c.DT],
                                                          scalar1=0.5, scalar2=None, op0=ALU.mult),
                  reads=[mt], writes=[mt])

    def modcol(s, mt, m, dt):
        g = m * s.cfg.DT + dt
        return mt.t[:, :, :].rearrange("p j i -> p (j i)")[:, g:g + 1]

    def phase_xin(s, src, dstT, TOK):
        c = s.cfg
        nc, cx = s.nc, s.cx
        act, dve, pe, sp, pool = _acts(s)
        GS = min(4, TOK // 128)
        with ExitStack() as st:
            xin = [s.sb(st, "xin%d" % i, [128, c.D], F32) for i in range(2)]
            xo = [s.sb(st, "xo%d" % i, [128, c.DT, GS * 128], F32) for i in range(2)]
            pst = [s.ps(st, "xps%d" % i, [128, 4, 128]) for i in range(4)]
            k = 0
            for g in range(TOK // (GS * 128)):
                o = xo[g % 2]
                for j in range(GS):
                    tt = g * GS + j
                    xi = xin[tt % 2]
                    cx.load(sp, xi, xi[:, :], src[tt * 128:(tt + 1) * 128, :])
                    for q in range(c.DT // 4):
                        p = pst[k % 4]
                        cx.prewait(pe, reads=[xi, s.ident], writes=[p])
                        for u in range(4):
                            dt = q * 4 + u
                            inst = nc.tensor.transpose(out=p[:, u, :], in_=xi[:, dt * 128:(dt + 1) * 128],
                                                       identity=s.ident[:, :])
                        cx.mark(pe.tag(inst), [xi], [p])
                        dst = o[:, q * 4:(q + 1) * 4, j * 128:(j + 1) * 128]
                        if k % 2 == 0:
                            cx.op(act, lambda: nc.scalar.copy(out=dst, in_=p[:, :, :]), reads=[p], writes=[o])
                        else:
                            cx.op(dve, lambda: nc.vector.tensor_copy(out=dst, in_=p[:, :, :]), reads=[p], writes=[o])
                        k += 1
                cx.store(sp, o, dstT.ap().rearrange("k p t -> p k t")[:, :, g * GS * 128:(g + 1) * GS * 128],
                         o[:, :, :])
            cx.barrier()

    def phase_norm(s, srcT, dstT, TOK, mt, m_shift, out_dt, name, gain=None):
        c = s.cfg
        nc, cx = s.nc, s.cx
        act, dve, pe, sp, pool = _acts(s)
        BLK = min(512, TOK)
        with ExitStack() as st:
            xs = [[s.sb(st, "%sx%d_%d" % (name, S, d), [128, BLK], F32) for d in range(c.DT)] for S in range(2)]
            sq = [s.sb(st, "%ssq%d" % (name, i), [128, BLK], BF16) for i in range(3)]
            ssq = [s.ps(st, "%sssq%d" % (name, i), [128, BLK]) for i in range(2)]
            sd = [s.sb(st, "%ssd%d" % (name, i), [128, BLK], F32) for i in range(2)]
            rstd = [s.sb(st, "%srs%d" % (name, i), [128, BLK], F32) for i in range(2)]
            tmp = [s.sb(st, "%stmp%d" % (name, i), [128, BLK], F32) for i in range(3)]
            ho = [s.sb(st, "%sho%d" % (name, i), [128, BLK], out_dt) for i in range(3)]
            for tb in range(TOK // BLK):
                S = tb % 2
                sl = slice(tb * BLK, (tb + 1) * BLK)
                for dt in range(c.DT):
                    x = xs[S][dt]
                    cx.load(sp, x, x[:, :], srcT[dt, :, sl])
                    q = sq[dt % 3]
                    cx.op(act, lambda: nc.scalar.activation(out=q[:, :], in_=x[:, :], func=AF.Square),
                          reads=[x], writes=[q])
                    cx.op(pe, lambda: nc.tensor.matmul(ssq[S][:, :], lhsT=s.onesb[:, :], rhs=q[:, :],
                                                       start=(dt == 0), stop=(dt == c.DT - 1)),
                          reads=[q, s.onesb], writes=[ssq[S]])
                cx.op(act, lambda: nc.scalar.activation(out=sd[S][:, :], in_=ssq[S][:, :], func=AF.Sqrt,
                                                        bias=s.epsc[:, 0:1], scale=1.0 / c.D),
                      reads=[ssq[S], s.epsc], writes=[sd[S]])
                cx.op(dve, lambda: nc.vector.reciprocal(out=rstd[S][:, :], in_=sd[S][:, :]),
                      reads=[sd[S]], writes=[rstd[S]])
                for dt in range(c.DT):
                    x = xs[S][dt]
                    t = tmp[dt % 3]
                    h = ho[dt % 3]
                    cx.op(dve, lambda: nc.vector.tensor_tensor(out=t[:, :], in0=x[:, :], in1=rstd[S][:, :],
                                                               op=ALU.mult),
                          reads=[x, rstd[S]], writes=[t])
                    if gain is None:
                        sc_ap = s.modcol(mt, m_shift + 1, dt)
                        sh_ap = s.modcol(mt, m_shift, dt)
                        cx.op(act, lambda: nc.scalar.activation(out=h[:, :], in_=t[:, :], func=AF.Identity,
                                                                bias=sh_ap, scale=sc_ap),
                              reads=[t, mt], writes=[h])
                    else:
                        cx.op(act, lambda: nc.scalar.activation(out=h[:, :], in_=t[:, :], func=AF.Copy,
                                                                scale=gain[:, dt:dt + 1]),
                              reads=[t, gain], writes=[h])
                    cx.store(sp, h, dstT[dt, :, sl], h[:, :])
            cx.barrier()

    def linear(s, name, mvT, KT, TOK, tok0, Wkeys, mt_list, SB, epi, wrow0=0, kt0=0, pre=None, extra=None):
        c = s.cfg
        nc, cx = s.nc, s.cx
        act, dve, pe, sp, pool = _acts(s)
        nW = len(Wkeys)
        NSET = 3 if nW == 2 else 4
        NSLOT = 4 if (nW == 1 and KT * 128 * 2 * 4 <= 48 * 1024) else 3
        with ExitStack() as st:
            NSB = TOK // SB
            sizes = [SB] * NSB
            mvb = [s.sb(st, "%smv%d" % (name, g), [128, KT, SB], BF16) for g in range(NSB)]
            if extra is not None:
                mvT2, TOK2 = extra
                mvb.append(s.sb(st, "%smvx" % name, [128, KT, TOK2], BF16))
                sizes.append(TOK2)
            NREG = NSB
            NSB = len(sizes)

            def mvload(g):
                if g < NREG:
                    cx.load(sp, mvb[g], mvb[g][:, :, :],
                            mvT.ap().rearrange("k p t -> p k t")[:, kt0:kt0 + KT, tok0 + g * SB:tok0 + (g + 1) * SB])
                else:
                    cx.load(sp, mvb[g], mvb[g][:, :, :], mvT2.ap().rearrange("k p t -> p k t")[:, 0:KT, :])
            wts = [[s.sb(st, "%sw%d_%d" % (name, wi, sl), [128, KT * 128], BF16) for sl in range(NSLOT)]
                   for wi in range(nW)]
            pss = [[s.ps(st, "%sps%d_%d" % (name, wi, se), [128, SB]) for wi in range(nW)] for se in range(NSET)]
            Wf = []
            Wev = []
            for k in Wkeys:
                full, ev, evs, rpp = s.W[k]
                Wf.append(full)
                Wev.append((evs, rpp))
            epi_state = epi(st, None, None, None, None)
            cnt = 0

            def wload(mi_):
                mt_ = mt_list[mi_]
                for wi_ in range(nW):
                    w_ = wts[wi_][mi_ % NSLOT]
                    evs_, rpp_ = Wev[wi_]
                    sp.wait(evs_[min(len(evs_) - 1, (wrow0 + mt_ * 128 + 127) // rpp_)])
                    cx.load(sp, w_, w_[:, :], Wf[wi_][wrow0 + mt_ * 128: wrow0 + (mt_ + 1) * 128, :])
            wload(0)
            mvload(0)
            if pre is not None:
                pre(epi_state, 0, mt_list[0])
            for mi in range(1, min(NSLOT - 1, len(mt_list))):
                wload(mi)
                if mi < NSB:
                    mvload(mi)
            for g in range(min(NSLOT - 1, len(mt_list)), NSB):
                mvload(g)
            for g in range(1, NSB):
                pass
            for mi, mt in enumerate(mt_list):
                slot = mi % NSLOT
                if pre is not None and mi + 1 < len(mt_list):
                    pre(epi_state, mi + 1, mt_list[mi + 1])
                if mi + NSLOT - 1 < len(mt_list):
                    wload(mi + NSLOT - 1)
                for sbi in range(NSB):
                    se = cnt % NSET
                    cnt += 1
                    wd = sizes[sbi]
                    for wi in range(nW):
                        w = wts[wi][slot]
                        p = pss[se][wi]
                        cx.prewait(pe, reads=[w, mvb[sbi]], writes=[p])
                        for kt in range(KT):
                            inst = nc.tensor.matmul(p[:, 0:wd], lhsT=w[:, kt * 128:(kt + 1) * 128],
                                                    rhs=mvb[sbi][:, kt, :],
                                                    start=(kt == 0), stop=(kt == KT - 1))
                        cx.mark(pe.tag(inst), [w], [p])
                    epi(st, epi_state, mi, mt, (sbi, pss[se]))
            cx.barrier()

    def ffn(s, name, l, which, xT, TOK, mt, m0, ctx=None):
        c = s.cfg
        nc, cx = s.nc, s.cx
        act, dve, pe, sp, pool = _acts(s)
        hT = s.scr_h(TOK)
        gT = s.scr_g(TOK)
        s.phase_norm(xT, hT, TOK, mt, m0, BF16, name + "n")
        if ctx is not None:
            cT, NC, mtc = ctx
            hcT = s.scr_h(NC)
            gcT = s.scr_g(NC)
            s.phase_norm(cT, hcT, NC, mtc, m0, BF16, name + "nc")
        SB = min(512, TOK)

        TB = min(2048, TOK)
        for tb in range(TOK // TB):
            NREG = TB // SB
            with_x = ctx is not None and tb == TOK // TB - 1

            def epi_up(st, state, mi, ft, extra):
                if state is None:
                    d = dict(sg=[s.sb(st, name + "sg%d" % i, [128, SB], F32) for i in range(3)],
                             go=[s.sb(st, name + "go%d" % i, [128, TB], BF16) for i in range(3)], k=[0])
                    if with_x:
                        d["gc"] = [s.sb(st, name + "gc%d" % i, [128, NC], BF16) for i in range(2)]
                    return d
                sbi, pp = extra
                sg = state["sg"][state["k"][0] % 3]
                state["k"][0] += 1
                if sbi < NREG:
                    go = state["go"][mi % 3]
                    cx.op(act, lambda: nc.scalar.activation(out=sg[:, :], in_=pp[0][:, :], func=AF.Silu),
                          reads=[pp[0]], writes=[sg])
                    cx.op(dve, lambda: nc.vector.tensor_tensor(out=go[:, sbi * SB:(sbi + 1) * SB], in0=sg[:, :],
                                                               in1=pp[1][:, :], op=ALU.mult),
                          reads=[sg, pp[1]], writes=[go])
                    if sbi == NREG - 1:
                        cx.store(sp, go, gT[ft, :, tb * TB:(tb + 1) * TB], go[:, :])
                else:
                    gc = state["gc"][mi % 2]
                    cx.op(act, lambda: nc.scalar.activation(out=sg[:, 0:NC], in_=pp[0][:, 0:NC], func=AF.Silu),
                          reads=[pp[0]], writes=[sg])
                    cx.op(dve, lambda: nc.vector.tensor_tensor(out=gc[:, :], in0=sg[:, 0:NC], in1=pp[1][:, 0:NC],
                                                               op=ALU.mult), reads=[sg, pp[1]], writes=[gc])
                    cx.store(sp, gc, gcT[ft, :, :], gc[:, :])
            s.linear(name + "u", hT, c.DT, TB, tb * TB, [("w1", l, which), ("w3", l, which)],
                     list(range(c.FT)), SB, epi_up, extra=((hcT, NC) if with_x else None))

        TBD = min(1024, TOK)
        gcol = m0 + 2
        for tb in range(TOK // TBD):
            NSBI = TBD // SB
            with_x = ctx is not None and tb == TOK // TBD - 1

            def epi_dn(st, state, mi, dt, extra):
                if state is None:
                    d = dict(xs=[s.sb(st, name + "dx%d" % i, [128, SB], F32) for i in range(2 * NSBI)],
                             xo=[s.sb(st, name + "do%d" % i, [128, SB], F32) for i in range(3)], k=[0])
                    if with_x:
                        d["cs"] = [s.sb(st, name + "dc%d" % i, [128, NC], F32) for i in range(2)]
                        d["co"] = [s.sb(st, name + "dco%d" % i, [128, NC], F32) for i in range(2)]
                    return d
                sbi, pp = extra
                k = state["k"][0]
                state["k"][0] += 1
                if sbi < NSBI:
                    xs = state["xs"][(mi % 2) * NSBI + sbi]
                    xo = state["xo"][k % 3]
                    sl = slice(tb * TBD + sbi * SB, tb * TBD + (sbi + 1) * SB)
                    cx.op(dve, lambda: nc.vector.scalar_tensor_tensor(out=xo[:, :], in0=pp[0][:, :],
                                                                      scalar=s.modcol(mt, gcol, dt), in1=xs[:, :],
                                                                      op0=ALU.mult, op1=ALU.add),
                          reads=[pp[0], xs, mt], writes=[xo])
                    cx.store(sp, xo, xT[dt, :, sl], xo[:, :])
                else:
                    cs = state["cs"][mi % 2]
                    co = state["co"][mi % 2]
                    cx.op(dve, lambda: nc.vector.scalar_tensor_tensor(out=co[:, :], in0=pp[0][:, 0:NC],
                                                                      scalar=s.modcol(mtc, gcol, dt), in1=cs[:, :],
                                                                      op0=ALU.mult, op1=ALU.add),
                          reads=[pp[0], cs, mtc], writes=[co])
                    cx.store(sp, co, cT[dt, :, :], co[:, :])

            def pre_dn(state, mi, dt):
                for sbi in range(NSBI):
                    xs = state["xs"][(mi % 2) * NSBI + sbi]
                    sl = slice(tb * TBD + sbi * SB, tb * TBD + (sbi + 1) * SB)
                    cx.load(sp, xs, xs[:, :], xT[dt, :, sl])
                if with_x:
                    cs = state["cs"][mi % 2]
                    cx.load(sp, cs, cs[:, :], cT[dt, :, :])
            s.linear(name + "d", gT, c.FT, TBD, tb * TBD, [("w2", l, which)], list(range(c.DT)), SB, epi_dn,
                     pre=pre_dn, extra=((gcT, NC) if with_x else None))

    def scr_h(s, TOK):
        key = ("h", TOK)
        if key not in s.scr:
            s.scr[key] = s.dram("hT_%d" % TOK, [s.cfg.DT, 128, TOK], BF16)
        return s.scr[key]

    def scr_g(s, TOK):
        key = ("g", TOK)
        if key not in s.scr:
            s.scr[key] = s.dram("gT_%d" % TOK, [s.cfg.FT, 128, TOK], BF16)
        return s.scr[key]

    def phase_out(s, xT, mt_gain):
        c = s.cfg
        nc, cx = s.nc, s.cx
        act, dve, pe, sp, pool = _acts(s)
        oT = s.dram("oT", [c.DT, 128, c.T], F32)
        if mt_gain is None:
            oT = xT
        else:
            s.phase_norm(xT, oT, c.T, None, 0, F32, "fn", gain=mt_gain)
        with ExitStack() as st:
            xi = [s.sb(st, "oxi%d" % i, [128, c.DT, 128], F32) for i in range(2)]
            xo = [s.sb(st, "oxo%d" % i, [128, c.D], F32) for i in range(2)]
            pst = [s.ps(st, "ops%d" % i, [128, 4, 128]) for i in range(4)]
            k = 0
            for tt in range(c.T // 128):
                a = xi[tt % 2]
                o = xo[tt % 2]
                cx.load(sp, a, a[:, :, :], oT.ap().rearrange("k p t -> p k t")[:, :, tt * 128:(tt + 1) * 128])
                for q in range(c.DT // 4):
                    p = pst[k % 4]
                    cx.prewait(pe, reads=[a, s.ident], writes=[p])
                    for u in range(4):
                        dt = q * 4 + u
                        inst = nc.tensor.transpose(out=p[:, u, :], in_=a[:, dt, :], identity=s.ident[:, :])
                    cx.mark(pe.tag(inst), [a], [p])
                    dst = o[:, q * 512:(q + 1) * 512]
                    src = p[:, :, :].rearrange("p u j -> p (u j)")
                    if k % 2 == 0:
                        cx.op(act, lambda: nc.scalar.copy(out=dst, in_=src), reads=[p], writes=[o])
                    else:
                        cx.op(dve, lambda: nc.vector.tensor_copy(out=dst, in_=src), reads=[p], writes=[o])
                    k += 1
                cx.store(sp, o, s.out[tt * 128:(tt + 1) * 128, :], o[:, :])
            cx.barrier()


class Mixer0:
    def mixer0(s):
        c = s.cfg
        nc, cx = s.nc, s.cx
        act, dve, pe, sp, pool = _acts(s)
        NH, HWT = c.NH, c.HW // 128
        hT = s.scr_h(c.T)
        hcT = s.scr_h(c.NCTX)
        s.phase_norm(s.xT, hT, c.T, s.modT[0], 3, BF16, "m0n")
        s.phase_norm(s.cT, hcT, c.NCTX, s.modC, 3, BF16, "m0c")
        pT = s.dram("pT", [4 * NH, 128, c.T], F32)
        hyT = s.dram("hyT", [3 * HWT * 128, c.T], BF16)
        hyG = s.dram("hyG", [2 * 3 * HWT * 128, c.T], BF16)
        pcT = s.dram("pcT", [2 * NH, 128, c.NCTX], F32)
        s.pT, s.pcT = pT, pcT
        SB = min(512, c.T)
        kscale = float(c.HD) ** -0.5

        def epi_in(st, state, mi, mt, extra):
            if state is None:
                return dict(o=[s.sb(st, "ipo%d" % i, [128, c.T], F32) for i in range(2)],
                            ob=[s.sb(st, "ipb%d" % i, [128, c.T], BF16) for i in range(2)], k=[0])
            sbi, pp = extra
            hy = mt >= 4 * NH
            o = (state["ob"] if hy else state["o"])[mi % 2]
            dst = o[:, sbi * SB:(sbi + 1) * SB]
            sc = kscale if (NH <= mt < 2 * NH) else 1.0
            k = state["k"][0]
            state["k"][0] += 1
            if k % 2 == 0:
                cx.op(act, lambda: nc.scalar.mul(out=dst, in_=pp[0][:, :], mul=sc), reads=[pp[0]], writes=[o])
            else:
                cx.op(dve, lambda: nc.vector.tensor_scalar(out=dst, in0=pp[0][:, :], scalar1=sc, scalar2=None,
                                                           op0=ALU.mult), reads=[pp[0]], writes=[o])
            if sbi == c.T // SB - 1:
                if hy:
                    m = mt - 4 * NH
                    cx.store(sp, o, hyT[m * 128:(m + 1) * 128, :], o[:, :])
                else:
                    cx.store(sp, o, pT[mt, :, :], o[:, :])
        s.linear("ip", hT, c.DT, c.T, 0, ["win"], list(range(c.PT)), SB, epi_in)
        for m in range(3 * HWT):
            ev = cx.allgather(hyT[m * 128:(m + 1) * 128, :], hyG[m * 256:(m + 1) * 256, :],
                              [[0, 1], [2, 3], [4, 5], [6, 7]])
        s.hy_ev = ev
        s.hyG = hyG

        SBc = min(512, c.NCTX)

        def epi_ctx(st, state, mi, mt, extra):
            if state is None:
                return dict(o=[s.sb(st, "ico%d" % i, [128, c.NCTX], F32) for i in range(2)])
            sbi, pp = extra
            o = state["o"][mi % 2]
            sc = kscale if mt < 2 * NH else 1.0
            cx.op(act, lambda: nc.scalar.mul(out=o[:, sbi * SBc:(sbi + 1) * SBc], in_=pp[0][:, :], mul=sc),
                  reads=[pp[0]], writes=[o])
            if sbi == c.NCTX // SBc - 1:
                cx.store(sp, o, pcT[mt - NH, :, :], o[:, :])
        s.linear("ic", hcT, c.DT, c.NCTX, 0, ["win"], list(range(NH, 3 * NH)), SBc, epi_ctx)

        ypT = s.dram("ypT", [c.DT, 128, c.T], BF16)
        s.ypT = ypT
        if s.stop_after == "m0a":
            return
        s.retention()
        if s.stop_after == "m0b":
            return
        if s.hyena() == "stop":
            return
        if s.stop_after == "h7":
            return

        NSBO = c.T // SB

        def epi_out(st, state, mi, dt, extra):
            if state is None:
                return dict(xs=[s.sb(st, "opx%d" % i, [128, SB], F32) for i in range(2 * NSBO)],
                            xo=[s.sb(st, "opo%d" % i, [128, SB], F32) for i in range(3)], k=[0])
            sbi, pp = extra
            k = state["k"][0]
            state["k"][0] += 1
            xs = state["xs"][(mi % 2) * NSBO + sbi]
            xo = state["xo"][k % 3]
            sl = slice(sbi * SB, (sbi + 1) * SB)
            cx.op(dve, lambda: nc.vector.scalar_tensor_tensor(out=xo[:, :], in0=pp[0][:, :],
                                                              scalar=s.modcol(s.modT[0], 5, dt), in1=xs[:, :],
                                                              op0=ALU.mult, op1=ALU.add),
                  reads=[pp[0], xs, s.modT[0]], writes=[xo])
            cx.store(sp, xo, s.xT[dt, :, sl], xo[:, :])

        def pre_out(state, mi, dt):
            for sbi in range(NSBO):
                xs = state["xs"][(mi % 2) * NSBO + sbi]
                cx.load(sp, xs, xs[:, :], s.xT[dt, :, sbi * SB:(sbi + 1) * SB])
        s.linear("op", ypT, c.DT, c.T, 0, ["wout"], list(range(c.DT)), SB, epi_out, pre=pre_out)

    def rotary(s, st, name, src_tile, C, S, out_bf):
        c = s.cfg
        nc, cx = s.nc, s.cx
        act, dve, pe, sp, pool = _acts(s)
        x, xs, t1, t2 = s.rt["x"], s.rt["xs"], s.rt["t1"], s.rt["t2"]
        cx.load(sp, x, x[:, :], src_tile)
        cx.load(sp, xs, xs[0:64, :], src_tile[64:128, :])
        cx.load(sp, xs, xs[64:128, :], src_tile[0:64, :], multi=True)
        cx.op(dve, lambda: nc.vector.tensor_tensor(out=t1[:, :], in0=x[:, :], in1=C[:, :], op=ALU.mult),
              reads=[x, C], writes=[t1])
        cx.op(dve, lambda: nc.vector.tensor_tensor(out=t2[:, :], in0=xs[:, :], in1=S[:, :], op=ALU.mult),
              reads=[xs, S], writes=[t2])
        cx.op(dve, lambda: nc.vector.tensor_tensor(out=out_bf[:, :], in0=t1[:, :], in1=t2[:, :], op=ALU.add),
              reads=[t1, t2], writes=[out_bf])

    def tposes(s, src_bf, dst_bf, nchunks, pst_list, kcount):
        nc, cx = s.nc, s.cx
        act, dve, pe, sp, pool = _acts(s)
        per = 8
        for g in range(0, nchunks, per):
            n = min(per, nchunks - g)
            p = pst_list[kcount[0] % len(pst_list)]
            cx.prewait(pe, reads=[src_bf, s.identb], writes=[p])
            for u in range(n):
                cc = g + u
                inst = nc.tensor.transpose(out=p[:, u, :], in_=src_bf[:, cc * 128:(cc + 1) * 128],
                                           identity=s.identb[:, :])
            cx.mark(pe.tag(inst), [src_bf], [p])
            if kcount[0] % 2 == 0:
                cx.op(act, lambda: nc.scalar.copy(out=dst_bf[:, g:g + n, :], in_=p[:, 0:n, :]),
                      reads=[p], writes=[dst_bf])
            else:
                cx.op(dve, lambda: nc.vector.tensor_copy(out=dst_bf[:, g:g + n, :], in_=p[:, 0:n, :]),
                      reads=[p], writes=[dst_bf])
            kcount[0] += 1

    def retention(s):
        c = s.cfg
        nc, cx = s.nc, s.cx
        act, dve, pe, sp, pool = _acts(s)
        NH, NCH, T = c.NH, c.NCH, c.T
        NCC = c.NCTX // 128
        pT, pcT, ypT = s.pT, s.pcT, s.ypT
        rotC_in = s.inp("rotC", [128, T])
        rotS_in = s.inp("rotS", [128, T])
        lg_in = s.inp("lgrep", [128, 2 * NH])
        rc128_in = s.inp("rc128", [128, 4, 128])
        rcT_in = s.inp("rcT", [128, 2, T])
        rcp_in = s.inp("rcp", [128, 2 + 2 * NCC])
        rankv_in = s.inp("rankv", [128, 4])
        rscr = s.dram("rscr_k", [NH, 128, T], BF16)
        rscr_v = s.dram("rscr_v", [NH, 128, T], BF16)
        rscr_u = s.dram("rscr_u", [NH, 2, 128, T], F32)
        exs = s.dram("exs", [2 * 128, NH * 128], F32)
        exg = s.dram("exg", [2 * 2 * 128, NH * 128], F32)
        with ExitStack() as st:
            C = s.sb(st, "rotC", [128, T], F32)
            S = s.sb(st, "rotS", [128, T], F32)
            lg = s.sb(st, "lg", [128, 2 * NH], F32)
            nlg = s.sb(st, "nlg", [128, 2 * NH], F32)
            rc128 = s.sb(st, "rc128", [128, 4, 128], F32)
            rcT = s.sb(st, "rcT", [128, 2, T], F32)
            rcp = s.sb(st, "rcp", [128, 2 + 2 * NCC], F32)
            rankv = s.sb(st, "rankv", [128, 4], F32)
            for b_, i_ in ((C, rotC_in), (S, rotS_in), (lg, lg_in), (rcp, rcp_in), (rankv, rankv_in)):
                cx.load(sp, b_, b_[:, :], i_[:, :])
            cx.load(sp, rc128, rc128[:, :, :], rc128_in[:, :, :])
            cx.load(sp, rcT, rcT[:, :, :], rcT_in[:, :, :])
            cx.op(dve, lambda: nc.vector.tensor_scalar(out=nlg[:, :], in0=lg[:, :], scalar1=-1.0, scalar2=None,
                                                       op0=ALU.mult), reads=[lg], writes=[nlg])
            s.rt = dict(x=s.sb(st, "rx", [128, T], F32), xs=s.sb(st, "rxs", [128, T], F32),
                        t1=s.sb(st, "rt1", [128, T], F32), t2=s.sb(st, "rt2", [128, T], F32))
            kr = s.sb(st, "kr", [128, T], BF16)
            vb = s.sb(st, "vb", [128, T], BF16)
            vf = s.rt["x"]
            ktm = s.sb(st, "ktm", [128, NCH, 128], BF16)
            vtm = s.sb(st, "vtm", [128, NCH, 128], BF16)
            vdf = s.sb(st, "vdf", [128, NCH, 128], BF16)
            vdb = s.sb(st, "vdb", [128, NCH, 128], BF16)
            U = [s.sb(st, "U%d" % i, [128, NCH, 128], F32) for i in range(2)]
            dec = s.sb(st, "dec", [128, 8], F32)
            ex = s.sb(st, "ex", [128, NH, 2, 128], F32)
            ctxS = s.sb(st, "ctxS", [128, NH, 2, 128], F32)
            Sst = [s.sb(st, "Sst%d" % i, [128, 128], F32) for i in range(2)]
            pst = [s.ps(st, "rtp%d" % i, [128, 8, 128], BF16) for i in range(2)]
            pu = [s.ps(st, "rup%d" % i, [128, 4, 128]) for i in range(3)]
            kc = [0]
            kcb = s.sb(st, "kcb", [128, c.NCTX], BF16)
            vcb = s.sb(st, "vcb", [128, c.NCTX], BF16)
            kcf = s.sb(st, "kcf", [128, c.NCTX], F32)
            vcf = s.sb(st, "vcf", [128, c.NCTX], F32)
            kctm = s.sb(st, "kctm", [128, NCC, 128], BF16)
            vctm = s.sb(st, "vctm", [128, NCC, 128], BF16)
            vcd = s.sb(st, "vcd", [128, 2, NCC, 128], BF16)
            cw = s.sb(st, "cw", [128, 2 * NCC], F32)

            def expcol(dst_ap, in_ap, scale_ap, reads, wbuf):
                cx.op(act, lambda: nc.scalar.activation(out=dst_ap, in_=in_ap, func=AF.Exp, scale=scale_ap),
                      reads=reads, writes=[wbuf])

            for h in range(NH):
                lf = lg[:, h:h + 1]
                lb = lg[:, NH + h:NH + h + 1]
                expcol(dec[:, 0:1], rcp[:, 0:1], lf, [rcp, lg], dec)
                expcol(dec[:, 1:2], rcp[:, 1:2], lb, [rcp, lg], dec)
                cx.op(act, lambda: nc.scalar.activation(out=dec[:, 2:3], in_=lf, func=AF.Exp, scale=128.0),
                      reads=[lg], writes=[dec])
                cx.op(act, lambda: nc.scalar.activation(out=dec[:, 3:4], in_=lb, func=AF.Exp, scale=128.0),
                      reads=[lg], writes=[dec])
                s.rotary(st, "k", pT[NH + h, :, :], C, S, kr)
                cx.store(sp, kr, rscr[h, :, :], kr[:, :])
                cx.load(sp, vf, vf[:, :], pT[2 * NH + h, :, :])
                cx.op(act, lambda: nc.scalar.copy(out=vb[:, :], in_=vf[:, :]), reads=[vf], writes=[vb])
                s.tposes(kr, ktm, NCH, pst, kc)
                s.tposes(vb, vtm, NCH, pst, kc)
                cx.store(sp, vtm, rscr_v[h, :, :], vtm[:, :, :].rearrange("p c e -> p (c e)"))
                cx.op(dve, lambda: nc.vector.tensor_scalar(out=vdf[:, :, :], in0=vtm[:, :, :], scalar1=dec[:, 0:1],
                                                           scalar2=None, op0=ALU.mult),
                      reads=[vtm, dec], writes=[vdf])
                cx.op(dve, lambda: nc.vector.tensor_scalar(out=vdb[:, :, :], in0=vtm[:, :, :], scalar1=dec[:, 1:2],
                                                           scalar2=None, op0=ALU.mult),
                      reads=[vtm, dec], writes=[vdb])
                ku = 0
                for di, vd in enumerate((vdf, vdb)):
                    for g in range(0, NCH, 4):
                        n = min(4, NCH - g)
                        p = pu[ku % 3]
                        ku += 1
                        cx.prewait(pe, reads=[ktm, vd], writes=[p])
                        for u in range(n):
                            inst = nc.tensor.matmul(p[:, u, :], lhsT=ktm[:, g + u, :], rhs=vd[:, g + u, :],
                                                    start=True, stop=True)
                        cx.mark(pe.tag(inst), [ktm, vd], [p])
                        cx.op(act, lambda: nc.scalar.copy(out=U[di][:, g:g + n, :], in_=p[:, 0:n, :]),
                              reads=[p], writes=[U[di]])
                    cx.store(sp, U[di], rscr_u[h, di, :, :], U[di][:, :, :].rearrange("p c e -> p (c e)"))
                for di in range(2):
                    order = range(NCH) if di == 0 else range(NCH - 1, -1, -1)
                    first = True
                    for cc in order:
                        if first:
                            cx.op(dve, lambda: nc.vector.tensor_copy(out=ex[:, h, di, :], in_=U[di][:, cc, :]),
                                  reads=[U[di]], writes=[ex])
                            first = False
                        else:
                            cx.op(dve, lambda: nc.vector.scalar_tensor_tensor(
                                out=ex[:, h, di, :], in0=ex[:, h, di, :], scalar=dec[:, 2 + di:3 + di],
                                in1=U[di][:, cc, :], op0=ALU.mult, op1=ALU.add),
                                reads=[ex, U[di], dec], writes=[ex])
                for cc in range(NCC):
                    expcol(cw[:, cc:cc + 1], rcp[:, 2 + cc:3 + cc], lf, [rcp, lg], cw)
                    expcol(cw[:, NCC + cc:NCC + cc + 1], rcp[:, 2 + NCC + cc:3 + NCC + cc], lb, [rcp, lg], cw)
                cx.load(sp, kcf, kcf[:, :], pcT[h, :, :])
                cx.load(sp, vcf, vcf[:, :], pcT[NH + h, :, :])
                cx.op(act, lambda: nc.scalar.copy(out=kcb[:, :], in_=kcf[:, :]), reads=[kcf], writes=[kcb])
                cx.op(act, lambda: nc.scalar.copy(out=vcb[:, :], in_=vcf[:, :]), reads=[vcf], writes=[vcb])
                s.tposes(kcb, kctm, NCC, pst, kc)
                s.tposes(vcb, vctm, NCC, pst, kc)
                for di in range(2):
                    for cc in range(NCC):
                        cx.op(dve, lambda: nc.vector.tensor_scalar(
                            out=vcd[:, di, cc, :], in0=vctm[:, cc, :], scalar1=cw[:, di * NCC + cc:di * NCC + cc + 1],
                            scalar2=None, op0=ALU.mult), reads=[vctm, cw], writes=[vcd])
                p = pu[ku % 3]
                ku += 1
                cx.prewait(pe, reads=[kctm, vcd], writes=[p])
                for di in range(2):
                    for cc in range(NCC):
                        inst = nc.tensor.matmul(p[:, di, :], lhsT=kctm[:, cc, :], rhs=vcd[:, di, cc, :],
                                                start=(cc == 0), stop=(cc == NCC - 1))
                cx.mark(pe.tag(inst), [kctm, vcd], [p])
                cx.op(act, lambda: nc.scalar.copy(out=ctxS[:, h, :, :], in_=p[:, 0:2, :]), reads=[p], writes=[ctxS])

            for di in range(2):
                cx.store(sp, ex, exs.ap().rearrange("(d p) (h e) -> d p h e", d=2, h=NH)[di], ex[:, :, di, :])
            pool.wait((ex.dsem, ex.r[ex.dsem]))
            for di in range(2):
                ev = cx.allgather(exs[di * 128:(di + 1) * 128, :], exg[di * 256:(di + 1) * 256, :],
                                  [[0, 1], [2, 3], [4, 5], [6, 7]])
            s.wprep_C()
            exr = s.sb(st, "exr", [128, 2, NH, 128], F32)
            sp.wait(ev)
            egv = exg.ap().rearrange("(d j p) (h e) -> d j p h e", d=2, j=2, h=NH)
            cx.load(sp, exr, exr[:, 0, :, :], egv[0, 0])
            cx.load(sp, exr, exr[:, 1, :, :], egv[1, 1], multi=True)

            qr = s.sb(st, "qr", [128, T], BF16)
            qf = s.sb(st, "qf", [128, T], BF16)
            qb = s.sb(st, "qb", [128, T], BF16)
            Gf = s.sb(st, "Gf", [128, T], BF16)
            Gb = s.sb(st, "Gb", [128, T], BF16)
            DT_ = s.sb(st, "DTm", [128, 128], F32)
            Et = [s.sb(st, "Et%d" % i, [128, 128], F32) for i in range(2)]
            SD = s.sb(st, "SD", [128, NCH, 128], BF16)
            Sbf = [s.sb(st, "Sbf%d" % i, [128, NCH, 128], BF16) for i in range(2)]
            o = s.rt["t2"]
            osq = s.sb(st, "rosq", [128, T], BF16)
            gf = s.rt["x"]
            sg = s.sb(st, "rsg", [128, T], BF16)
            rs = s.rt["t1"]
            rout = s.sb(st, "rout", [128, T], BF16)
            alpha = s.sb(st, "alpha", [128, 2], F32)
            pss = [s.ps(st, "rsp%d" % i, [128, 4, 128]) for i in range(2)]
            NB = max(1, T // 512)
            BL = T // NB
            for h in range(NH):
                lf = lg[:, h:h + 1]
                lb = lg[:, NH + h:NH + h + 1]
                cx.op(act, lambda: nc.scalar.activation(out=dec[:, 2:3], in_=lf, func=AF.Exp, scale=128.0),
                      reads=[lg], writes=[dec])
                cx.op(act, lambda: nc.scalar.activation(out=dec[:, 3:4], in_=lb, func=AF.Exp, scale=128.0),
                      reads=[lg], writes=[dec])
                expcol(alpha[:, 0:1], lf, rankv[:, 0:1], [lg, rankv], alpha)
                expcol(alpha[:, 1:2], lb, rankv[:, 1:2], [lg, rankv], alpha)
                expcol(Et[0][:, :], rc128[:, 0, :], lf, [rc128, lg], Et[0])
                expcol(Et[1][:, :], rc128[:, 0, :], nlg[:, NH + h:NH + h + 1], [rc128, nlg], Et[1])
                cx.op(dve, lambda: nc.vector.tensor_tensor(out=Et[0][:, :], in0=Et[0][:, :], in1=rc128[:, 1, :],
                                                           op=ALU.mult), reads=[Et[0], rc128], writes=[Et[0]])
                cx.op(dve, lambda: nc.vector.tensor_tensor(out=Et[1][:, :], in0=Et[1][:, :], in1=rc128[:, 2, :],
                                                           op=ALU.mult), reads=[Et[1], rc128], writes=[Et[1]])
                cx.op(dve, lambda: nc.vector.tensor_tensor(out=DT_[:, :], in0=Et[0][:, :], in1=Et[1][:, :],
                                                           op=ALU.add), reads=[Et[0], Et[1]], writes=[DT_])
                cx.op(dve, lambda: nc.vector.tensor_tensor(out=DT_[:, :], in0=DT_[:, :], in1=rc128[:, 3, :],
                                                           op=ALU.add), reads=[DT_, rc128], writes=[DT_])
                expcol(Gf[:, :], rcT[:, 0, :], lf, [rcT, lg], Gf)
                expcol(Gb[:, :], rcT[:, 1, :], lb, [rcT, lg], Gb)
                s.rotary(st, "q", pT[h, :, :], C, S, qr)
                cx.op(dve, lambda: nc.vector.tensor_tensor(out=qf[:, :], in0=qr[:, :], in1=Gf[:, :], op=ALU.mult),
                      reads=[qr, Gf], writes=[qf])
                cx.op(dve, lambda: nc.vector.tensor_tensor(out=qb[:, :], in0=qr[:, :], in1=Gb[:, :], op=ALU.mult),
                      reads=[qr, Gb], writes=[qb])
                cx.load(sp, kr, kr[:, :], rscr[h, :, :])
                cx.load(sp, vtm, vtm[:, :, :].rearrange("p c e -> p (c e)"), rscr_v[h, :, :])
                for di in range(2):
                    cx.load(sp, U[di], U[di][:, :, :].rearrange("p c e -> p (c e)"), rscr_u[h, di, :, :])
                for di in range(2):
                    Sx = Sst[di]
                    cx.op(dve, lambda: nc.vector.tensor_scalar(out=Sx[:, :], in0=ctxS[:, h, di, :],
                                                               scalar1=alpha[:, di:di + 1], scalar2=None,
                                                               op0=ALU.mult), reads=[ctxS, alpha], writes=[Sx])
                    src = exr[:, di, h, :]
                    cx.op(dve, lambda: nc.vector.scalar_tensor_tensor(out=Sx[:, :], in0=src,
                                                                      scalar=rankv[:, 2 + di:3 + di], in1=Sx[:, :],
                                                                      op0=ALU.mult, op1=ALU.add),
                          reads=[exr, rankv, Sx], writes=[Sx])
                    order = range(NCH) if di == 0 else range(NCH - 1, -1, -1)
                    for cc in order:
                        cx.op(act, lambda: nc.scalar.copy(out=Sbf[di][:, cc, :], in_=Sx[:, :]),
                              reads=[Sx], writes=[Sbf[di]])
                        cx.op(dve, lambda: nc.vector.scalar_tensor_tensor(
                            out=Sx[:, :], in0=Sx[:, :], scalar=dec[:, 2 + di:3 + di], in1=U[di][:, cc, :],
                            op0=ALU.mult, op1=ALU.add), reads=[Sx, U[di], dec], writes=[Sx])
                k2 = 0
                for g in range(0, NCH, 4):
                    n = min(4, NCH - g)
                    p = pss[k2 % 2]
                    k2 += 1
                    cx.prewait(pe, reads=[kr, qr], writes=[p])
                    for u in range(n):
                        cc = g + u
                        inst = nc.tensor.matmul(p[:, u, :], lhsT=kr[:, cc * 128:(cc + 1) * 128],
                                                rhs=qr[:, cc * 128:(cc + 1) * 128], start=True, stop=True)
                    cx.mark(pe.tag(inst), [kr, qr], [p])
                    cx.op(dve, lambda: nc.vector.tensor_tensor(
                        out=SD[:, g:g + n, :], in0=p[:, 0:n, :],
                        in1=DT_[:, :].unsqueeze(1).broadcast_to([128, n, 128]), op=ALU.mult),
                        reads=[p, DT_], writes=[SD])
                for g in range(0, NCH, 4):
                    n = min(4, NCH - g)
                    p = pss[k2 % 2]
                    k2 += 1
                    cx.prewait(pe, reads=[vtm, SD, Sbf[0], Sbf[1], qf, qb], writes=[p])
                    for u in range(n):
                        cc = g + u
                        sl = slice(cc * 128, (cc + 1) * 128)
                        nc.tensor.matmul(p[:, u, :], lhsT=vtm[:, cc, :], rhs=SD[:, cc, :], start=True, stop=False)
                        nc.tensor.matmul(p[:, u, :], lhsT=Sbf[0][:, cc, :], rhs=qf[:, sl], start=False, stop=False)
                        inst = nc.tensor.matmul(p[:, u, :], lhsT=Sbf[1][:, cc, :], rhs=qb[:, sl], start=False,
                                                stop=True)
                    cx.mark(pe.tag(inst), [vtm, SD, Sbf[0], Sbf[1], qf, qb], [p])
                    dsl = slice(g * 128, (g + n) * 128)
                    cx.op(act, lambda: nc.scalar.copy(out=o[:, dsl], in_=p[:, 0:n, :].rearrange("p u i -> p (u i)")),
                          reads=[p], writes=[o])
                    cx.op(act, lambda: nc.scalar.activation(out=osq[:, dsl],
                                                            in_=p[:, 0:n, :].rearrange("p u i -> p (u i)"),
                                                            func=AF.Square), reads=[p], writes=[osq])
                cx.load(sp, gf, gf[:, :], pT[3 * NH + h, :, :])
                cx.op(act, lambda: nc.scalar.activation(out=sg[:, :], in_=gf[:, :], func=AF.Silu),
                      reads=[gf], writes=[sg])
                for nb in range(NB):
                    bsl = slice(nb * BL, (nb + 1) * BL)
                    p = pss[k2 % 2]
                    k2 += 1
                    pv = p[:, :, :].rearrange("p u i -> p (u i)")[:, 0:BL]
                    cx.op(pe, lambda: nc.tensor.matmul(pv, lhsT=s.onesb[:, :], rhs=osq[:, bsl], start=True, stop=True),
                          reads=[osq, s.onesb], writes=[p])
                    cx.op(act, lambda: nc.scalar.activation(out=rs[:, bsl], in_=pv, func=AF.Sqrt,
                                                            bias=s.epsc[:, 0:1], scale=1.0 / c.HD),
                          reads=[p, s.epsc], writes=[rs])
                cx.op(dve, lambda: nc.vector.reciprocal(out=rs[:, :], in_=rs[:, :]), reads=[rs], writes=[rs])
                cx.op(dve, lambda: nc.vector.tensor_tensor(out=o[:, :], in0=o[:, :], in1=rs[:, :], op=ALU.mult),
                      reads=[o, rs], writes=[o])
                cx.op(dve, lambda: nc.vector.tensor_tensor(out=rout[:, :], in0=o[:, :], in1=sg[:, :], op=ALU.mult),
                      reads=[o, sg], writes=[rout])
                cx.store(sp, rout, ypT[h, :, :], rout[:, :])
            cx.barrier()

    def hyena(s):
        c = s.cfg
        nc, cx = s.nc, s.cx
        act, dve, pe, sp, pool = _acts(s)
        N, T, HC, HCT, NH = c.N, c.T, c.HC, c.HCT, c.NH
        ST = N // 128
        HWT = c.HW // 128
        C4 = 4 * HC
        zT_in = s.inp("zT", [33, N])
        fw1_in = s.inp("fw1", [33, 64])
        fw2_in = s.inp("fw2", [64, 64])
        fw3_in = s.inp("fw3", [64, 64])
        fbf_in = s.inp("fbf", [64, 6])
        w4_in = s.inp("w4my", [64, C4])
        delta_in = s.inp("deltarow", [128, HC])
        negt_in = s.inp("negt", [128, ST])
        hcw_in = s.inp("hcw", [128, 3, HCT, 3])
        hcb_in = s.inp("hcb", [128, 3, HCT])
        bias_in = s.inp("biasrow", [128, 2, HC])
        hfilt = s.dram("hfilt", [ST, 128, C4], BF16)
        Hspec = s.dram("Hspec", [2, ST, 128, C4], F32)
        Kspec = s.dram("Kspec", [2, 2, ST, 128, HC], F32)
        u32 = [s.dram("u32_%d" % i, [ST, 128, HC], F32) for i in range(3)]
        vbf = s.dram("vbf", [ST, 128, HC], BF16)
        z32 = s.dram("z32", [ST, 128, HC], F32)
        zbf = s.dram("zbf", [ST, 128, HC], BF16)
        Ysp = s.dram("Ysp", [2 * ST, 128, HC], BF16)
        yh = s.dram("yh", [HCT * 2 * 128, T], BF16)
        yhG = s.dram("yhG", [HCT * 2 * 2 * 128, T], BF16)
        TWO_PI = 2.0 * math.pi
        MAGIC = 12582912.0
        PI_LO = 3.1415925

        with ExitStack() as st:
            zT = s.sb(st, "zT", [33, N], F32)
            fw1 = s.sb(st, "fw1", [33, 64], F32)
            fw2 = s.sb(st, "fw2", [64, 64], F32)
            fw3 = s.sb(st, "fw3", [64, 64], F32)
            fbf = s.sb(st, "fbf", [64, 6], F32)
            fb = s.sb(st, "fbm", [64, 3], F32)
            w4 = s.sb(st, "w4", [64, C4], F32)
            delta = s.sb(st, "delta", [128, HC], F32)
            negt = s.sb(st, "negt", [128, ST], F32)
            for b_, i_ in ((zT, zT_in), (fw1, fw1_in), (fw2, fw2_in), (fw3, fw3_in), (fbf, fbf_in), (w4, w4_in),
                           (delta, delta_in), (negt, negt_in)):
                cx.load(sp, b_, b_[:, :], i_[:, :])
            cx.op(dve, lambda: nc.vector.tensor_tensor(out=fb[:, :], in0=fbf[:, 0:3], in1=fbf[:, 3:6], op=ALU.mult),
                  reads=[fbf], writes=[fb])
            a3 = s.sb(st, "a3", [64, N], F32)
            BL = min(512, N)
            cur = [s.sb(st, "fa%d" % i, [64, BL], F32) for i in range(2)]
            v_ = s.sb(st, "fv", [64, BL], F32)
            t_ = s.sb(st, "ft", [64, BL], F32)
            n_ = s.sb(st, "fn", [64, BL], F32)
            psm = [s.ps(st, "fps%d" % i, [64, BL]) for i in range(2)]
            km = 0
            for blk in range(N // BL):
                bsl = slice(blk * BL, (blk + 1) * BL)
                for layer in range(3):
                    p = psm[km % 2]
                    km += 1
                    if layer == 0:
                        cx.op(pe, lambda: nc.tensor.matmul(p[:, :], lhsT=fw1[:, :], rhs=zT[:, bsl], start=True,
                                                           stop=True), reads=[fw1, zT], writes=[p])
                    else:
                        w = fw2 if layer == 1 else fw3
                        src = cur[(layer - 1) % 2]
                        cx.op(pe, lambda: nc.tensor.matmul(p[:, :], lhsT=w[:, :], rhs=src[:, :], start=True,
                                                           stop=True), reads=[w, src], writes=[p])
                    cx.op(act, lambda: nc.scalar.activation(out=v_[:, :], in_=p[:, :], func=AF.Identity,
                                                            bias=fb[:, layer:layer + 1],
                                                            scale=fbf[:, 3 + layer:4 + layer]),
                          reads=[p, fb, fbf], writes=[v_])
                    cx.op(dve, lambda: nc.vector.tensor_scalar(out=t_[:, :], in0=v_[:, :], scalar1=1.0 / TWO_PI,
                                                               scalar2=MAGIC, op0=ALU.mult, op1=ALU.add),
                          reads=[v_], writes=[t_])
                    cx.op(dve, lambda: nc.vector.tensor_scalar(out=n_[:, :], in0=t_[:, :], scalar1=-MAGIC,
                                                               scalar2=None, op0=ALU.add), reads=[t_], writes=[n_])
                    cx.op(dve, lambda: nc.vector.scalar_tensor_tensor(out=t_[:, :], in0=n_[:, :], scalar=-TWO_PI,
                                                                      in1=v_[:, :], op0=ALU.mult, op1=ALU.add),
                          reads=[n_, v_], writes=[t_])
                    cx.op(dve, lambda: nc.vector.tensor_scalar(out=t_[:, :], in0=t_[:, :], scalar1=-PI_LO,
                                                               scalar2=PI_LO, op0=ALU.max, op1=ALU.min),
                          reads=[t_], writes=[t_])
                    dst = a3[:, bsl] if layer == 2 else cur[layer % 2][:, :]
                    dbuf = a3 if layer == 2 else cur[layer % 2]
                    cx.op(act, lambda: nc.scalar.activation(out=dst, in_=t_[:, :], func=AF.Sin),
                          reads=[t_], writes=[dbuf])
            nbk = C4 // 512 if C4 >= 512 else 1
            BW = min(512, C4)
            psf = [[s.ps(st, "hps%d_%d" % (i, j), [128, BW]) for j in range(nbk)] for i in range(1)]
            win = [s.sb(st, "win%d" % i, [128, HC], F32) for i in range(2)]
            fo = [s.sb(st, "fo%d" % i, [128, C4], BF16) for i in range(2)]
            for pt in range(ST):
                pp = psf[0]
                for j in range(nbk):
                    cx.op(pe, lambda: nc.tensor.matmul(pp[j][:, :], lhsT=a3[:, pt * 128:(pt + 1) * 128],
                                                       rhs=w4[:, j * BW:(j + 1) * BW], start=True, stop=True),
                          reads=[a3, w4], writes=[pp[j]])
                wn = win[pt % 2]
                cx.op(act, lambda: nc.scalar.activation(out=wn[:, :], in_=delta[:, :], func=AF.Exp,
                                                        scale=negt[:, pt:pt + 1]),
                      reads=[delta, negt], writes=[wn])
                f = fo[pt % 2]
                for g in range(4):
                    j = (g * HC) // BW
                    off = (g * HC) % BW
                    cx.op(dve, lambda: nc.vector.tensor_tensor(out=f[:, g * HC:(g + 1) * HC],
                                                               in0=pp[j][:, off:off + HC], in1=wn[:, :],
                                                               op=ALU.mult), reads=[pp[j], wn], writes=[f])
                if pt == 0:
                    for g in (1, 3):
                        cx.op(dve, lambda: nc.vector.memset(f[0:1, g * HC:(g + 1) * HC], 0.0), writes=[f])
                cx.store(sp, f, hfilt[pt, :, :], f[:, :])
            cx.barrier()

        if s.stop_after == "h1":
            return "stop"
        TK = min(1024, C4)
        SBF = min(512, TK)
        for run in range(C4 // TK):
            def epi_spec(st, state, mi, ft, extra):
                if state is None:
                    return dict(o=[s.sb(st, "hso%d" % i, [128, SBF], F32) for i in range(4)], k=[0])
                sbi, pp = extra
                for ri in range(2):
                    k = state["k"][0]
                    state["k"][0] += 1
                    o = state["o"][k % 4]
                    if ri == 0:
                        cx.op(act, lambda: nc.scalar.copy(out=o[:, :], in_=pp[ri][:, :]), reads=[pp[ri]], writes=[o])
                    else:
                        cx.op(dve, lambda: nc.vector.tensor_copy(out=o[:, :], in_=pp[ri][:, :]), reads=[pp[ri]],
                              writes=[o])
                    c0 = run * TK + sbi * SBF
                    cx.store(sp, o, Hspec[ri, ft, :, c0:c0 + SBF], o[:, :])
            s.linear("hs%d" % run, hfilt, ST, TK, run * TK, ["dftc", "dftsf"], list(range(ST)), SBF, epi_spec)

        if s.stop_after == "h2":
            return "stop"
        with ExitStack() as st:
            hr = [s.sb(st, "hr%d" % i, [128, 2 * HC], F32) for i in range(2)]
            hi = [s.sb(st, "hi%d" % i, [128, 2 * HC], F32) for i in range(2)]
            tq = [s.sb(st, "tq%d" % i, [128, HC], F32) for i in range(2)]
            kr_ = [s.sb(st, "kkr%d" % i, [128, HC], F32) for i in range(2)]
            ki_ = [s.sb(st, "kki%d" % i, [128, HC], F32) for i in range(2)]
            sc = 1.0 / N
            k = 0
            for o_ in range(2):
                for ft in range(ST):
                    a, b_ = hr[k % 2], hi[k % 2]
                    t = tq[k % 2]
                    kr, ki = kr_[k % 2], ki_[k % 2]
                    k += 1
                    cs = slice(o_ * 2 * HC, (o_ + 1) * 2 * HC)
                    cx.load(sp, a, a[:, :], Hspec[0, ft, :, cs])
                    cx.load(sp, b_, b_[:, :], Hspec[1, ft, :, cs])
                    cx.op(dve, lambda: nc.vector.tensor_scalar(out=t[:, :], in0=a[:, HC:2 * HC], scalar1=sc,
                                                               scalar2=None, op0=ALU.mult), reads=[a], writes=[t])
                    cx.op(dve, lambda: nc.vector.scalar_tensor_tensor(out=kr[:, :], in0=a[:, 0:HC], scalar=sc,
                                                                      in1=t[:, :], op0=ALU.mult, op1=ALU.add),
                          reads=[a, t], writes=[kr])
                    cx.op(dve, lambda: nc.vector.tensor_scalar(out=t[:, :], in0=b_[:, HC:2 * HC], scalar1=-sc,
                                                               scalar2=None, op0=ALU.mult), reads=[b_], writes=[t])
                    cx.op(dve, lambda: nc.vector.scalar_tensor_tensor(out=ki[:, :], in0=b_[:, 0:HC], scalar=sc,
                                                                      in1=t[:, :], op0=ALU.mult, op1=ALU.add),
                          reads=[b_, t], writes=[ki])
                    if ft == 0:
                        cx.op(dve, lambda: nc.vector.tensor_scalar(out=kr[0:1, :], in0=kr[0:1, :], scalar1=0.5,
                                                                   scalar2=None, op0=ALU.mult),
                              reads=[kr], writes=[kr])
                        cx.op(dve, lambda: nc.vector.tensor_scalar(out=t[0:1, :], in0=b_[0:1, HC:2 * HC],
                                                                   scalar1=0.5 * sc, scalar2=None, op0=ALU.mult),
                              reads=[b_], writes=[t])
                        cx.op(dve, lambda: nc.vector.scalar_tensor_tensor(out=ki[0:1, :], in0=b_[0:1, 0:HC],
                                                                          scalar=0.5 * sc, in1=t[0:1, :],
                                                                          op0=ALU.mult, op1=ALU.add),
                              reads=[b_, t], writes=[ki])
                    cx.store(sp, kr, Kspec[o_, 0, ft, :, :], kr[:, :])
                    cx.store(sp, ki, Kspec[o_, 1, ft, :, :], ki[:, :])
            cx.barrier()

        if s.stop_after == "h3":
            return "stop"
        sp.wait(s.hy_ev)
        hv = s.hyG.ap().rearrange("(a h i j p) t -> a h i j p t", j=2, a=3, h=2, i=HCT, p=128)
        with ExitStack() as st:
            hcw = s.sb(st, "hcw", [128, 3, HCT, 3], F32)
            hcb = s.sb(st, "hcb", [128, 3, HCT], F32)
            cx.load(sp, hcw, hcw[:, :, :, :], hcw_in[:, :, :, :])
            cx.load(sp, hcb, hcb[:, :, :], hcb_in[:, :, :])
            ppb = [s.sb(st, "ppb%d" % i, [128, N + 4], BF16) for i in range(2)]
            uu = [s.sb(st, "uu%d" % i, [128, N], F32) for i in range(2)]
            ot = [s.sb(st, "uot%d" % i, [128, ST, 128], F32) for i in range(2)]
            otb = s.sb(st, "uotb", [128, ST, 128], BF16)
            ptp = [s.ps(st, "utp%d" % i, [128, 4, 128]) for i in range(4)]
            for i in range(2):
                cx.op(dve, lambda: nc.vector.memset(ppb[i][:, 0:2], 0.0), writes=[ppb[i]])
                cx.op(dve, lambda: nc.vector.memset(ppb[i][:, N + 2:N + 4], 0.0), writes=[ppb[i]])
            k = 0
            kt = 0
            for part in range(3):
                for i in range(HCT):
                    pb = ppb[k % 2]
                    u = uu[k % 2]
                    o = ot[k % 2]
                    k += 1
                    cx.load(sp, pb, pb[:, 2:N + 2].rearrange("p (j t) -> p j t", j=2),
                            hv[part, bass.ds(s.rank_sp, 1), i, :, :, :].rearrange("o j p t -> p (o j) t"))
                    cx.op(act, lambda: nc.scalar.activation(out=u[:, :], in_=pb[:, 2:N + 2], func=AF.Identity,
                                                            bias=hcb[:, part, i:i + 1], scale=hcw[:, part, i, 1:2]),
                          reads=[pb, hcb, hcw], writes=[u])
                    cx.op(dve, lambda: nc.vector.scalar_tensor_tensor(out=u[:, :], in0=pb[:, 1:N + 1],
                                                                      scalar=hcw[:, part, i, 0:1], in1=u[:, :],
                                                                      op0=ALU.mult, op1=ALU.add),
                          reads=[pb, hcw, u], writes=[u])
                    cx.op(dve, lambda: nc.vector.scalar_tensor_tensor(out=u[:, :], in0=pb[:, 3:N + 3],
                                                                      scalar=hcw[:, part, i, 2:3], in1=u[:, :],
                                                                      op0=ALU.mult, op1=ALU.add),
                          reads=[pb, hcw, u], writes=[u])
                    for g in range(0, ST, 4):
                        p = ptp[kt % 4]
                        cx.prewait(pe, reads=[u, s.ident], writes=[p])
                        for q in range(4):
                            sti = g + q
                            inst = nc.tensor.transpose(out=p[:, q, :], in_=u[:, sti * 128:(sti + 1) * 128],
                                                       identity=s.ident[:, :])
                        cx.mark(pe.tag(inst), [u], [p])
                        if kt % 2 == 0:
                            cx.op(act, lambda: nc.scalar.copy(out=o[:, g:g + 4, :], in_=p[:, :, :]), reads=[p],
                                  writes=[o])
                        else:
                            cx.op(dve, lambda: nc.vector.tensor_copy(out=o[:, g:g + 4, :], in_=p[:, :, :]), reads=[p],
                                  writes=[o])
                        kt += 1
                    cx.store(sp, o, u32[part].ap().rearrange("s p c -> p s c")[:, :, i * 128:(i + 1) * 128],
                             o[:, :, :])
                    if part == 0:
                        cx.op(act, lambda: nc.scalar.copy(out=otb[:, :, :], in_=o[:, :, :]), reads=[o], writes=[otb])
                        cx.store(sp, otb, vbf.ap().rearrange("s p c -> p s c")[:, :, i * 128:(i + 1) * 128],
                                 otb[:, :, :])
            cx.barrier()

        if s.stop_after == "h4":
            return "stop"
        for order in range(2):
            src_bf = vbf if order == 0 else zbf
            src32 = u32[0] if order == 0 else z32
            gate32 = u32[1] if order == 0 else u32[2]

            def epi_fwd(st, state, mi, ft, extra):
                if state is None:
                    return dict(kr=[s.sb(st, "ekr%d" % i, [128, HC], F32) for i in range(3)],
                                ki=[s.sb(st, "eki%d" % i, [128, HC], F32) for i in range(3)],
                                t=[s.sb(st, "et%d" % i, [128, HC], F32) for i in range(4)],
                                y=[s.sb(st, "ey%d" % i, [128, HC], BF16) for i in range(4)])
                sbi, pp = extra
                kr, ki = state["kr"][mi % 3], state["ki"][mi % 3]
                t1, t2, t3, t4 = state["t"]
                yr, yi = state["y"][(mi % 2) * 2], state["y"][(mi % 2) * 2 + 1]
                xr, xi = pp[0], pp[1]
                V = nc.vector
                cx.op(dve, lambda: V.tensor_tensor(out=t1[:, :], in0=xr[:, :], in1=kr[:, :], op=ALU.mult),
                      reads=[xr, kr], writes=[t1])
                cx.op(dve, lambda: V.tensor_tensor(out=t2[:, :], in0=xi[:, :], in1=ki[:, :], op=ALU.mult),
                      reads=[xi, ki], writes=[t2])
                cx.op(dve, lambda: V.tensor_tensor(out=yr[:, :], in0=t1[:, :], in1=t2[:, :], op=ALU.subtract),
                      reads=[t1, t2], writes=[yr])
                cx.op(dve, lambda: V.tensor_tensor(out=t3[:, :], in0=xr[:, :], in1=ki[:, :], op=ALU.mult),
                      reads=[xr, ki], writes=[t3])
                cx.op(dve, lambda: V.tensor_tensor(out=t4[:, :], in0=xi[:, :], in1=kr[:, :], op=ALU.mult),
                      reads=[xi, kr], writes=[t4])
                cx.op(dve, lambda: V.tensor_tensor(out=yi[:, :], in0=t3[:, :], in1=t4[:, :], op=ALU.add),
                      reads=[t3, t4], writes=[yi])
                if ft == 0:
                    cx.op(dve, lambda: V.tensor_tensor(out=yr[0:1, :], in0=xr[0:1, :], in1=kr[0:1, :], op=ALU.mult),
                          reads=[xr, kr], writes=[yr])
                    cx.op(dve, lambda: V.tensor_tensor(out=yi[0:1, :], in0=xi[0:1, :], in1=ki[0:1, :], op=ALU.mult),
                          reads=[xi, ki], writes=[yi])
                cx.store(sp, yr, Ysp[ft, :, :], yr[:, :])
                cx.store(sp, yi, Ysp[ST + ft, :, :], yi[:, :])
            def pre_fwd(state, mi, ft):
                kr, ki = state["kr"][mi % 3], state["ki"][mi % 3]
                cx.load(sp, kr, kr[:, :], Kspec[order, 0, ft, :, :])
                cx.load(sp, ki, ki[:, :], Kspec[order, 1, ft, :, :])
            s.linear("hf%d" % order, src_bf, ST, HC, 0, ["dftc", "dftsf"], list(range(ST)), HC, epi_fwd, pre=pre_fwd)

            def epi_inv(st, state, mi, tt, extra):
                if state is None:
                    d = dict(a=[s.sb(st, "ia%d" % i, [128, HC], F32) for i in range(3)],
                             g=[s.sb(st, "ig%d" % i, [128, HC], F32) for i in range(3)],
                             w=[s.sb(st, "iw%d" % i, [128, HC], F32) for i in range(2)],
                             zb=[s.sb(st, "izb%d" % i, [128, HC], BF16) for i in range(2)],
                             bias=s.sb(st, "ibias", [128, 2, HC], F32), k=[0])
                    cx.load(sp, d["bias"], d["bias"][:, :, :], bias_in[:, :, :])
                    if order == 1:
                        d["yo"] = [s.sb(st, "iyo%d" % i, [128, N], BF16) for i in range(HCT)]
                        d["tp"] = [s.ps(st, "itp%d" % i, [128, 4, 128]) for i in range(2)]
                    return d
                sbi, pp = extra
                a, g, w = state["a"][mi % 3], state["g"][mi % 3], state["w"][mi % 2]
                zb = state["zb"][mi % 2]
                bias = state["bias"]
                V = nc.vector
                cx.op(dve, lambda: V.tensor_tensor(out=w[:, :], in0=a[:, :], in1=bias[:, order, :], op=ALU.mult),
                      reads=[a, bias], writes=[w])
                cx.op(dve, lambda: V.tensor_tensor(out=w[:, :], in0=pp[0][:, :], in1=w[:, :], op=ALU.add),
                      reads=[pp[0], w], writes=[w])
                cx.op(dve, lambda: V.tensor_tensor(out=w[:, :], in0=w[:, :], in1=g[:, :], op=ALU.mult),
                      reads=[w, g], writes=[w])
                if order == 0:
                    cx.store(sp, w, z32[tt, :, :], w[:, :])
                    cx.op(act, lambda: nc.scalar.copy(out=zb[:, :], in_=w[:, :]), reads=[w], writes=[zb])
                    cx.store(sp, zb, zbf[tt, :, :], zb[:, :])
                else:
                    p = state["tp"][mi % 2]
                    cx.prewait(pe, reads=[w, s.ident], writes=[p])
                    for i in range(HCT):
                        inst = nc.tensor.transpose(out=p[:, i, :], in_=w[:, i * 128:(i + 1) * 128],
                                                   identity=s.ident[:, :])
                    cx.mark(pe.tag(inst), [w], [p])
                    for i in range(HCT):
                        yo = state["yo"][i]
                        cx.op(act, lambda: nc.scalar.copy(out=yo[:, tt * 128:(tt + 1) * 128], in_=p[:, i, :]),
                              reads=[p], writes=[yo])
                    if mi == ST - 1:
                        for i in range(HCT):
                            for hn in range(2):
                                cx.store(sp, state["yo"][i], yh[(i * 2 + hn) * 128:(i * 2 + hn + 1) * 128, :],
                                         state["yo"][i][:, hn * T:(hn + 1) * T])
            def pre_inv(state, mi, tt):
                a, g = state["a"][mi % 3], state["g"][mi % 3]
                cx.load(sp, a, a[:, :], src32[tt, :, :])
                cx.load(sp, g, g[:, :], gate32[tt, :, :])
            s.linear("hi%d" % order, Ysp, 2 * ST, HC, 0, ["dfti"], list(range(ST)), HC, epi_inv, pre=pre_inv)

        if s.stop_after == "h6":
            return "stop"
        for m in range(2 * HCT):
            ev = cx.allgather(yh[m * 128:(m + 1) * 128, :], yhG[m * 256:(m + 1) * 256, :],
                              [[0, 1], [2, 3], [4, 5], [6, 7]])
        pool.wait(ev)
        gv = yhG.ap().rearrange("(i h j p) t -> i h j p t", i=HCT, h=2, j=2, p=128)
        for j in range(2):
            for i in range(HCT):
                cx.gdma(s.ypT[NH + j * HCT + i, :, :],
                        gv[i, bass.ds(s.rank_g, 1), j, :, :].rearrange("o p t -> p (o t)"))
        pool.wait((cx.gsem, cx.gsem.v))
        s.wprep_D()
        cx.barrier()


class Mixer1:
    def mixer1(s):
        c = s.cfg
        nc, cx = s.nc, s.cx
        act, dve, pe, sp, pool = _acts(s)
        T, DT, GT, GW = c.T, c.DT, c.GT, c.GRID_W
        ROWS = T // GW
        HR = 8
        HB = HR * GW
        hT = s.dram("h1T", [DT, 128, T], F32)
        dT = s.dram("d1T", [DT, 128, T], BF16)
        hal = s.dram("hal", [DT * 128, 2 * HB], F32)
        halG = s.dram("halG", [DT * 2 * 128, 2 * HB], F32)
        rcnt_in = s.inp("rcnt", [128, 4, T])
        psc_in = s.inp("pscT", [128, DT])
        rankv_in = s.inp("rankv", [128, 4])
        s.phase_norm(s.xT, hT, T, s.modT[1], 3, F32, "m1n")
        hv = hal.ap().rearrange("(k p) t -> k p t", p=128)
        ev = cx.gdma(hv[:, :, 0:HB], hT[:, :, 0:HB])
        ev = cx.gdma(hv[:, :, HB:2 * HB], hT[:, :, T - HB:T])
        pool.wait(ev)
        for dt in range(DT):
            ev = cx.allgather(hal[dt * 128:(dt + 1) * 128, :], halG[dt * 256:(dt + 1) * 256, :],
                              [[0, 1], [2, 3], [4, 5], [6, 7]])
        gv = halG.ap().rearrange("(k j p) t -> k j p t", j=2, p=128)
        ER, EC = ROWS + 2 * HR, GW + 16
        with ExitStack() as st:
            rcnt = s.sb(st, "rcnt", [128, 4, T], F32)
            psc = s.sb(st, "psc", [128, DT], F32)
            comb = s.sb(st, "comb", [128, DT], F32)
            rankv = s.sb(st, "rankv1", [128, 4], F32)
            cx.load(sp, rcnt, rcnt[:, :, :], rcnt_in[:, :, :])
            cx.load(sp, psc, psc[:, :], psc_in[:, :])
            cx.load(sp, rankv, rankv[:, :], rankv_in[:, :])
            m5 = s.modT[1].t[:, :, :].rearrange("p j i -> p (j i)")[:, 5 * DT:6 * DT]
            cx.op(dve, lambda: nc.vector.tensor_tensor(out=comb[:, :], in0=psc[:, :], in1=m5, op=ALU.mult),
                  reads=[psc, s.modT[1]], writes=[comb])
            s.comb = comb
            hf = [s.sb(st, "phf%d" % i, [128, T], F32) for i in range(2)]
            ht = [s.sb(st, "pht%d" % i, [128, HB], F32) for i in range(2)]
            hb_ = [s.sb(st, "phb%d" % i, [128, HB], F32) for i in range(2)]
            E = [s.sb(st, "pE%d" % i, [128, ER, EC], F32) for i in range(4)]
            mo = [s.sb(st, "pmo%d" % i, [128, T], F32) for i in range(2)]
            do = [s.sb(st, "pdo%d" % i, [128, T], BF16) for i in range(2)]
            for i in range(4):
                cx.op(dve, lambda: nc.vector.memset(E[i][:, :, :], 0.0), writes=[E[i]])
            sp.wait(ev)
            for dt in range(DT):
                on_pool = (dt % 3 == 2)
                eng = pool if on_pool else dve
                V = nc.gpsimd if on_pool else nc.vector
                gi = dt // GT
                nst = gi + 1
                a = hf[dt % 2]
                t_, b_ = ht[dt % 2], hb_[dt % 2]
                cx.load(sp, a, a[:, :], hT[dt, :, :])
                cx.load(sp, t_, t_[:, :], gv[dt, 0, :, HB:2 * HB])
                cx.load(sp, b_, b_[:, :], gv[dt, 1, :, 0:HB])
                e0, e1 = (E[2], E[3]) if on_pool else (E[0], E[1])
                cx.op(act, lambda: nc.scalar.copy(out=e0[:, HR:HR + ROWS, 8:8 + GW],
                                                  in_=a[:, :].rearrange("p (r c) -> p r c", c=GW)),
                      reads=[a], writes=[e0])
                cx.op(act, lambda: nc.scalar.activation(out=e0[:, 0:HR, 8:8 + GW],
                                                        in_=t_[:, :].rearrange("p (r c) -> p r c", c=GW),
                                                        func=AF.Copy, scale=rankv[:, 2:3]),
                      reads=[t_, rankv], writes=[e0])
                cx.op(act, lambda: nc.scalar.activation(out=e0[:, HR + ROWS:ER, 8:8 + GW],
                                                        in_=b_[:, :].rearrange("p (r c) -> p r c", c=GW),
                                                        func=AF.Copy, scale=rankv[:, 3:4]),
                      reads=[b_, rankv], writes=[e0])
                cur, nxt = e0, e1
                for k in range(nst):
                    if k == 0:
                        lo, hi, sa, sb_ = 1, EC, -1, 0
                    else:
                        sh = 1 << (k - 1)
                        lo, hi, sa, sb_ = sh, EC - sh, -sh, sh
                    cx.op(eng, lambda: V.tensor_tensor(out=nxt[:, :, lo:hi], in0=cur[:, :, lo + sa:hi + sa],
                                                       in1=cur[:, :, lo + sb_:hi + sb_], op=ALU.add),
                          reads=[cur], writes=[nxt])
                    cur, nxt = nxt, cur
                for k in range(nst):
                    if k == 0:
                        lo, hi, sa, sb_ = 1, ER, -1, 0
                    else:
                        sh = 1 << (k - 1)
                        lo, hi, sa, sb_ = sh, ER - sh, -sh, sh
                    cx.op(eng, lambda: V.tensor_tensor(out=nxt[:, lo:hi, :], in0=cur[:, lo + sa:hi + sa, :],
                                                       in1=cur[:, lo + sb_:hi + sb_, :], op=ALU.add),
                          reads=[cur], writes=[nxt])
                    cur, nxt = nxt, cur
                m = mo[dt % 2]
                d = do[dt % 2]
                cx.op(eng, lambda: V.tensor_tensor(out=m[:, :].rearrange("p (r c) -> p r c", c=GW),
                                                   in0=cur[:, HR:HR + ROWS, 8:8 + GW],
                                                   in1=rcnt[:, gi, :].rearrange("p (r c) -> p r c", c=GW),
                                                   op=ALU.mult), reads=[cur, rcnt], writes=[m])
                cx.op(eng, lambda: V.tensor_tensor(out=d[:, :], in0=m[:, :], in1=a[:, :], op=ALU.subtract),
                      reads=[m, a], writes=[d])
                cx.store(sp, d, dT[dt, :, :], d[:, :])
                if (2 * nst) % 2 == 1 or True:
                    cx.op(eng, lambda: V.memset(e0[:, :, :], 0.0), writes=[e0])
            cx.barrier()
            SB = min(512, T)

            NSBP = T // SB

            def epi_pool(st2, state, mi, mt, extra):
                if state is None:
                    return dict(xs=[s.sb(st2, "ppx%d" % i, [128, SB], F32) for i in range(2 * NSBP)],
                                xo=[s.sb(st2, "ppo%d" % i, [128, SB], F32) for i in range(3)], k=[0])
                sbi, pp = extra
                k = state["k"][0]
                state["k"][0] += 1
                xs = state["xs"][(mi % 2) * NSBP + sbi]
                xo = state["xo"][k % 3]
                dtt = s.cur_gi * GT + mt
                sl = slice(sbi * SB, (sbi + 1) * SB)
                cx.op(dve, lambda: nc.vector.scalar_tensor_tensor(out=xo[:, :], in0=pp[0][:, :],
                                                                  scalar=comb[:, dtt:dtt + 1], in1=xs[:, :],
                                                                  op0=ALU.mult, op1=ALU.add),
                      reads=[pp[0], xs, comb], writes=[xo])
                cx.store(sp, xo, s.xT[dtt, :, sl], xo[:, :])

            def pre_pool(state, mi, mt):
                dtt = s.cur_gi * GT + mt
                for sbi in range(NSBP):
                    xs = state["xs"][(mi % 2) * NSBP + sbi]
                    cx.load(sp, xs, xs[:, :], s.xT[dtt, :, sbi * SB:(sbi + 1) * SB])
            for gi in range(4):
                s.cur_gi = gi
                s.linear("pl%d" % gi, dT, GT, T, 0, ["poolw"], list(range(GT)), SB, epi_pool,
                         wrow0=gi * c.G, kt0=gi * GT, pre=pre_pool)


class Program(Phases, Mixer0, Mixer1):
    def __init__(s, cfg, stop_after=None):
        super().__init__(cfg, stop_after)
        s.scr = {}

    def build(s):
        c = s.cfg
        nc, cx = s.nc, s.cx
        stop = s.stop_after
        s.x_in = s.inp("x", [c.T, c.D])
        s.ctx_in = s.inp("ctx", [c.NCTX, c.D])
        s.out = nc.dram_tensor("out", [c.T, c.D], F32, kind="ExternalOutput")
        s.consts()
        s.wprep_begin()
        if stop in ("xin", "mod", "wprep"):
            if stop == "mod":
                s.phase_mod()
            if stop == "wprep":
                for nm in ("w1", "w3"):
                    s.wcast((nm, 0, 1), "%s_0_1" % nm, c.FT * 128, c.DT * 128)
                s.wflush()
                s.cx.sp.wait(s.W[("w3", 0, 1)][1])
            s.xT = s.dram("xT", [c.DT, 128, c.T], F32)
            s.phase_xin(s.x_in, s.xT, c.T)
            return s.finish(None)
        for nm in ("w1", "w3"):
            s.wcast((nm, 0, 1), "%s_0_1" % nm, c.FT * 128, c.DT * 128)
        s.wcast(("w2", 0, 1), "w2_0_1", c.DT * 128, c.FT * 128)
        s.phase_mod()
        s.wflush()
        s.wprep_B()

        xT = s.dram("xT", [c.DT, 128, c.T], F32)
        cT = s.dram("cT", [c.DT, 128, c.NCTX], F32)
        s.xT, s.cT = xT, cT
        s.phase_xin(s.x_in, xT, c.T)
        s.phase_xin(s.ctx_in, cT, c.NCTX)

        if stop == "ffn1":
            s.ffn("a", 0, 1, xT, c.T, s.modT[0], 0)
            return s.finish(None)
        s.ffn("a", 0, 1, xT, c.T, s.modT[0], 0, ctx=(cT, c.NCTX, s.modC))
        s.mixer0()
        if stop in ("mix0", "m0a", "m0b", "h1", "h2", "h3", "h4", "h6", "h7"):
            return s.finish(None)
        s.ffn("b", 0, 2, xT, c.T, s.modT[0], 6)
        s.ffn("d", 1, 1, xT, c.T, s.modT[1], 0)
        if stop == "ffn3":
            return s.finish(None)
        s.mixer1()
        if stop == "mix1":
            return s.finish(None)
        s.ffn("e", 1, 2, xT, c.T, s.modT[1], 6)
        gain_in = s.inp("gainT", [128, c.DT])
        gain = s.sb(s.es, "gain", [128, c.DT], F32)
        cx.load(cx.sp, gain, gain[:, :], gain_in[:, :])
        return s.finish(gain)

    def finish(s, gain):
        s.cx.sp.wait((s.wcc, s.wcc.v))
        s.cx.sp.wait((s.wsem, s.wsem.v))
        s.phase_out(s.xT, gain)
        s.es.close()
        return s.nc

    def wprep_B(s):
        c = s.cfg
        s.wcast("win", "w_in", c.PT * 128, c.DT * 128)
        s.wcast("wout", "w_out", c.DT * 128, c.DT * 128)
        s.wcast("dftc", "dft_c", c.N, c.N, BF16)
        s.wcast("dftsf", "dft_sf", c.N, c.N, BF16)
        s.wcast("dfti", "dft_i", c.N, 2 * c.N, BF16)
        s.wflush()

    def wprep_C(s):
        c = s.cfg
        for l, which in ((0, 2), (1, 1)):
            for nm in ("w1", "w3"):
                s.wcast((nm, l, which), "%s_%d_%d" % (nm, l, which), c.FT * 128, c.DT * 128)
            s.wcast(("w2", l, which), "w2_%d_%d" % (l, which), c.DT * 128, c.FT * 128)
        s.wflush()
        s.wlocal("poolw", "pool_w", 4 * c.G, c.G)

    def wprep_D(s):
        c = s.cfg
        for nm in ("w1", "w3"):
            s.wcast((nm, 1, 2), "%s_1_2" % nm, c.FT * 128, c.DT * 128)
        s.wcast(("w2", 1, 2), "w2_1_2", c.DT * 128, c.FT * 128)
        s.wflush()


def tile_w(W):
    K, M = W.shape
    KT, MT = K // 128, M // 128
    return np.ascontiguousarray(W.reshape(KT, 128, MT, 128).transpose(2, 1, 0, 3)).reshape(MT * 128, KT * 128)


def shard_rows(A, core, rows_p=None):
    n = A.shape[0] // NCORES
    if rows_p is None or rows_p == n:
        return np.ascontiguousarray(A[core * n:(core + 1) * n])
    npc = n // rows_p
    B = A.reshape(npc, NCORES, rows_p, A.shape[1])
    return np.ascontiguousarray(B[:, core]).reshape(n, A.shape[1])


def host_inputs(cfg, inp, needed, pieces):
    c = cfg
    f32 = np.float32
    maps = [dict() for _ in range(NCORES)]
    shared = {}

    def put_shard(name, A):
        if name not in needed:
            return
        for core in range(NCORES):
            maps[core][name] = shard_rows(A, core, pieces.get(name))

    def put_all(name, A):
        if name not in needed:
            return
        A = np.ascontiguousarray(A)
        for core in range(NCORES):
            maps[core][name] = A

    x = np.asarray(inp["x"], f32)
    ctx = np.asarray(inp["ctx"], f32)
    for core in range(NCORES):
        b, r = core // 2, core % 2
        maps[core]["x"] = np.ascontiguousarray(x[b, r * c.T:(r + 1) * c.T])
        maps[core]["ctx"] = np.ascontiguousarray(ctx[b])
    put_all("ident", np.eye(128, dtype=f32))
    cc = np.zeros((8, c.D), f32)
    cc[:4] = np.asarray(inp["c"], f32)
    cc[4] = np.asarray(inp["c_ctx"], f32)
    put_all("ccT", cc.reshape(8, c.DT, 128).transpose(2, 1, 0))
    for l in range(2):
        wm = np.asarray(inp["w_mod"][l], f32)
        bm = np.asarray(inp["b_mod"][l], f32)
        for core in range(NCORES):
            cols = slice(core * c.MODI * 128, (core + 1) * c.MODI * 128)
            w = wm[:, cols].reshape(c.DT, 128, c.MODI, 128).transpose(2, 1, 0, 3)
            maps[core]["wmod%d" % l] = np.ascontiguousarray(w).reshape(c.MODI, 128, c.DT * 128)
            maps[core]["bmod%d" % l] = np.ascontiguousarray(bm[cols].reshape(c.MODI, 128).T)
        ffn_w = {(1, "w1"): inp["ffn1_w1"], (1, "w3"): inp["ffn1_w3"], (1, "w2"): inp["ffn1_w2"],
                 (2, "w1"): inp["ffn2_w1"], (2, "w3"): inp["ffn2_w3"], (2, "w2"): inp["ffn2_w2"]}
        for which in (1, 2):
            for nm in ("w1", "w3", "w2"):
                key = "%s_%d_%d" % (nm, l, which)
                if key in needed:
                    put_shard(key, tile_w(np.asarray(ffn_w[(which, nm)][l], f32)))
    if "gainT" in needed:
        put_all("gainT", np.asarray(inp["final_gain"], f32).reshape(c.DT, 128).T)
    return maps, put_shard, put_all


_CACHE = {}


def run(cfg, inputs, stop_after=None):
    key = (cfg.D, cfg.DFF, cfg.N, cfg.NCTX, stop_after)
    if key not in _CACHE:
        P = Program(cfg, stop_after)
        P.build()
        print('BUILD: dsems', len(P.cx.all_dsems), 'engine tags', [(e.name, e.sem.v) for e in P.cx.engs], 'wcc', P.wcc.v, 'cc', P.cx.ccsem.v, flush=True)
        _CACHE[key] = P
    P = _CACHE[key]
    needed = set(P.inputs.keys())
    maps, put_shard, put_all = host_inputs(cfg, inputs, needed, P.wpieces)
    host_inputs_mixers(cfg, inputs, needed, maps, put_shard, put_all)
    for m in maps:
        missing = needed - set(m.keys())
        assert not missing, missing
        for k in list(m.keys()):
            if k not in needed:
                del m[k]
            else:
                shp, dt = P.inputs[k]
                assert tuple(m[k].shape) == tuple(shp), (k, m[k].shape, shp)
    res = run_bass_kernel_spmd(P.nc, maps, core_ids=list(range(NCORES)))
    out = np.zeros((cfg.B, cfg.N, cfg.D), np.float32)
    for core in range(NCORES):
        b, r = core // 2, core % 2
        out[b, r * cfg.T:(r + 1) * cfg.T] = res.results[core]["out"]
    return out


def host_inputs_mixers(cfg, inp, needed, maps, put_shard, put_all):
    c = cfg
    f32 = np.float32
    bf = ml_dtypes.bfloat16
    if "w_in" in needed:
        put_shard("w_in", tile_w(np.asarray(inp["ab_w_in"][0], f32)))
    if "w_out" in needed:
        put_shard("w_out", tile_w(np.asarray(inp["ab_w_out"][0], f32)))
    if "dft_c" in needed:
        N, N2 = c.N, 2 * c.N
        a = np.arange(N, dtype=np.int64)
        m = (a[:, None] * a[None, :]) % N2
        ang = m.astype(np.float64) * (2.0 * np.pi / N2)
        Tc = np.cos(ang)
        Sf = -np.sin(ang)
        sgn = np.where(a % 2 == 0, 1.0, -1.0)
        Sf[:, 0] = sgn
        Si = -np.sin(ang)
        Si[0, :] = sgn
        put_shard("dft_c", tile_w(Tc.astype(f32)).astype(bf))
        put_shard("dft_sf", tile_w(Sf.astype(f32)).astype(bf))
        put_shard("dft_i", tile_w(np.concatenate([Tc, Si], 0).astype(f32)).astype(bf))
        del ang, m, Tc, Sf, Si
    if "rotC" in needed:
        NH, HD, T = c.NH, c.HD, c.T
        nf = HD // 4
        inv = (f32(10000.0) ** (-np.arange(nf, dtype=f32) / f32(nf))).astype(f32)
        lg = np.asarray(inp["ret_log_decay"][0], f32).reshape(1, 2 * NH)
        put_all("lgrep", np.tile(lg, (128, 1)))
        jj = np.arange(128)[:, None]
        ii = np.arange(128)[None, :]
        rc = np.zeros((128, 4, 128), f32)
        rc[:, 0] = ii - jj
        rc[:, 1] = (ii > jj)
        rc[:, 2] = (jj > ii)
        rc[:, 3] = 2.0 * (ii == jj)
        put_all("rc128", rc)
        tl = np.arange(T) % 128
        rcT = np.zeros((128, 2, T), f32)
        rcT[:, 0] = tl + 1
        rcT[:, 1] = 128 - tl
        put_all("rcT", rcT)
        NCC = c.NCTX // 128
        p = np.arange(128)
        rcp = np.zeros((128, 2 + 2 * NCC), f32)
        rcp[:, 0] = 127 - p
        rcp[:, 1] = p
        for cc in range(NCC):
            rcp[:, 2 + cc] = c.NCTX - 1 - (cc * 128 + p)
            rcp[:, 2 + NCC + cc] = cc * 128 + p
        put_all("rcp", rcp)
        for core in range(NCORES):
            r = core % 2
            pos = r * T + np.arange(T)
            row = (pos // c.GRID_W).astype(f32)
            col = (pos % c.GRID_W).astype(f32)
            ang = np.concatenate([row[:, None] * inv, col[:, None] * inv], -1).astype(f32)
            cs, sn = np.cos(ang).astype(f32), np.sin(ang).astype(f32)
            maps[core]["rotC"] = np.ascontiguousarray(np.concatenate([cs.T, cs.T], 0))
            maps[core]["rotS"] = np.ascontiguousarray(np.concatenate([-sn.T, sn.T], 0))
            rv = np.zeros((128, 4), f32)
            rv[:, 0], rv[:, 1], rv[:, 2], rv[:, 3] = r * T, (1 - r) * T, r, 1 - r
            maps[core]["rankv"] = rv
    if "zT" in needed:
        N, HW, HC, HCT, ST = c.N, c.HW, c.HC, c.HCT, c.N // 128
        t = np.linspace(0.0, 1.0, N, dtype=f32)[:, None]
        bands = 16
        f = np.linspace(1e-4, bands - 1, bands, dtype=f32)[None, :]
        w = (f32(2.0 * math.pi) * np.arange(N, dtype=f32)[:, None] / f32(N)).astype(f32)
        z = np.concatenate([t, np.cos(f * w), -np.sin(f * w)], -1).astype(f32)
        put_all("zT", z.T)
        put_all("fw1", np.asarray(inp["hy_f_w1"][0], f32))
        put_all("fw2", np.asarray(inp["hy_f_w2"][0], f32))
        put_all("fw3", np.asarray(inp["hy_f_w3"][0], f32))
        fr = np.asarray(inp["hy_f_freq"][0], f32)
        put_all("fbf", np.stack([np.asarray(inp["hy_f_b1"][0], f32), np.asarray(inp["hy_f_b2"][0], f32),
                                 np.asarray(inp["hy_f_b3"][0], f32), fr[0], fr[1], fr[2]], 1))
        put_all("negt", -(t[:, 0].reshape(ST, 128).T))
        max_decay = math.log(1e-2) / 0.3
        min_decay = math.log(1e-2) / 1.5
        deltas = np.abs(np.linspace(min_decay, max_decay, HW, dtype=f32)).astype(f32)
        w4 = np.asarray(inp["hy_f_w4"][0], f32).reshape(64, 2, 2, HW)
        cw = np.asarray(inp["hy_conv_w"][0], f32).reshape(3, 3, HW)
        cb = np.asarray(inp["hy_conv_b"][0], f32).reshape(3, HW)
        hb = np.asarray(inp["hy_bias"][0], f32)
        for core in range(NCORES):
            r = core % 2
            sl = slice(r * HC, (r + 1) * HC)
            maps[core]["w4my"] = np.ascontiguousarray(w4[:, :, :, sl]).reshape(64, 4 * HC)
            maps[core]["deltarow"] = np.tile(deltas[sl][None, :], (128, 1))
            maps[core]["hcw"] = np.ascontiguousarray(cw[:, :, sl].reshape(3, 3, HCT, 128).transpose(3, 1, 2, 0))
            maps[core]["hcb"] = np.ascontiguousarray(cb[:, sl].reshape(3, HCT, 128).transpose(2, 0, 1))
            maps[core]["biasrow"] = np.tile(hb[:, sl][None, :, :], (128, 1, 1))
    host_inputs_mixer1(cfg, inp, needed, maps, put_shard, put_all)


def host_inputs_mixer1(cfg, inp, needed, maps, put_shard, put_all):
    c = cfg
    f32 = np.float32
    if "pool_w" in needed:
        pw = np.asarray(inp["pool_w"][0], f32)
        put_all("pool_w", np.concatenate([tile_w(pw[g]) for g in range(4)], 0))
    if "pscT" in needed:
        put_all("pscT", np.asarray(inp["pool_scale"][0], f32).reshape(c.DT, 128).T)
    if "rcnt" in needed:
        T, GW = c.T, c.GRID_W
        NR = c.N // GW
        for core in range(NCORES):
            r = core % 2
            pos = r * T + np.arange(T)
            row, col = pos // GW, pos % GW
            rc = np.zeros((4, T), f32)
            for gi, w in enumerate((2, 4, 8, 16)):
                lo, hi = -(w // 2), w - 1 - w // 2
                cr = np.minimum(row + hi, NR - 1) - np.maximum(row + lo, 0) + 1
                cc = np.minimum(col + hi, GW - 1) - np.maximum(col + lo, 0) + 1
                rc[gi] = 1.0 / (cr * cc).astype(f32)
            maps[core]["rcnt"] = np.tile(rc[None], (128, 1, 1))
            if "rankv" not in maps[core]:
                rv = np.zeros((128, 4), f32)
                rv[:, 0], rv[:, 1], rv[:, 2], rv[:, 3] = r * T, (1 - r) * T, r, 1 - r
                maps[core]["rankv"] = rv


def kernel(**inputs):
    return run(Cfg(), inputs)
```

```python
import math
from contextlib import ExitStack
import numpy as np
import ml_dtypes
import concourse.bass as bass
import concourse.mybir as mybir
from concourse.bass_utils import run_bass_kernel_spmd

F32 = mybir.dt.float32
BF16 = mybir.dt.bfloat16
I32 = mybir.dt.int32
ALU = mybir.AluOpType
AF = mybir.ActivationFunctionType
NCORES = 8


class Cfg:
    def __init__(s, D=2048, DFF=5632, N=4096, NCTX=256, NH=8, GRID_W=64):
        s.B = 4
        s.D, s.DFF, s.N, s.NCTX, s.NH, s.GRID_W = D, DFF, N, NCTX, NH, GRID_W
        s.HD = 128
        s.RW = NH * 128
        s.HW = D - s.RW
        assert s.RW == D // 2
        s.PROJ = 4 * s.RW + 3 * s.HW
        s.T = N // 2
        s.DT = D // 128
        s.FT = DFF // 128
        s.PT = s.PROJ // 128
        s.HC = s.HW // 2
        s.HCT = s.HC // 128
        s.G = D // 4
        s.GT = s.G // 128
        s.NMODT = 9 * s.DT
        s.MODI = s.NMODT // 8
        assert s.NMODT % 8 == 0
        s.ROWS = s.T // GRID_W
        s.NCH = s.T // 128
        s.NF = N
        s.EPS = 1e-6


class CSem:
    def __init__(s, nc, name):
        s.h = nc.alloc_semaphore(name)
        s.v = 0
        s.name = name


class Eng:
    def __init__(s, ctx, e, name):
        s.ctx, s.e, s.name = ctx, e, name
        s.sem = CSem(ctx.nc, "p_" + name)
        s.seen = {}

    def wait(s, ev):
        if ev is None:
            return
        sem, v = ev
        if s.seen.get(sem, 0) >= v:
            return
        s.e.wait_ge(sem.h, v)
        s.seen[sem] = v

    def tag(s, inst):
        s.sem.v += 1
        inst.then_inc(s.sem.h, 1)
        return (s.sem, s.sem.v)


class Buf:
    def __init__(s, t=None, name=""):
        s.t = t
        s.name = name
        s.w = None
        s.r = {}
        s.dsem = None

    def __getitem__(s, idx):
        return s.t[idx]


class Ctx:
    def __init__(s, nc):
        s.nc = nc
        s.pe = Eng(s, nc.tensor, "pe")
        s.act = Eng(s, nc.scalar, "act")
        s.dve = Eng(s, nc.vector, "dve")
        s.pool = Eng(s, nc.gpsimd, "pool")
        s.sp = Eng(s, nc.sync, "sp")
        s.engs = [s.pe, s.act, s.dve, s.pool, s.sp]
        s.free_dsems = []
        s.used_dsems = []
        s.all_dsems = []
        s.bar = CSem(nc, "bar")
        s.ccsem = CSem(nc, "cc")
        s.gsem = CSem(nc, "gdma")
        s.bufs = []
        s.nsem = 0

    def op(s, eng, fn, reads=(), writes=(), tag=True):
        for b in reads:
            eng.wait(b.w)
        for b in writes:
            eng.wait(b.w)
            for sem, v in list(b.r.items()):
                eng.wait((sem, v))
        inst = fn()
        if tag:
            ev = eng.tag(inst)
            s.mark(ev, reads, writes)
        return inst

    def mark(s, ev, reads, writes):
        for b in reads:
            if b.r.get(ev[0], 0) < ev[1]:
                b.r[ev[0]] = ev[1]
        for b in writes:
            b.w = ev
            b.r = {}

    def prewait(s, eng, reads=(), writes=()):
        for b in reads:
            eng.wait(b.w)
        for b in writes:
            eng.wait(b.w)
            for sem, v in list(b.r.items()):
                eng.wait((sem, v))

    def _dsem(s, b):
        if b.dsem is None:
            if s.free_dsems:
                b.dsem = s.free_dsems.pop()
            else:
                b.dsem = CSem(s.nc, "d%d" % len(s.all_dsems))
                s.all_dsems.append(b.dsem)
            s.used_dsems.append(b.dsem)
            s.bufs.append(b)
        return b.dsem

    def load(s, q, sb, out_ap, in_ap, multi=False, **kw):
        sem = s._dsem(sb)
        if not (multi and sb.w is not None and sb.w[0] is sem):
            q.wait(sb.w)
        for se, v in list(sb.r.items()):
            q.wait((se, v))
        inst = q.e.dma_start(out=out_ap, in_=in_ap, **kw)
        sem.v += 16
        inst.then_inc(sem.h, 16)
        sb.w = (sem, sem.v)
        sb.r = {}
        return inst

    def store(s, q, sb, out_ap, in_ap, **kw):
        sem = s._dsem(sb)
        q.wait(sb.w)
        inst = q.e.dma_start(out=out_ap, in_=in_ap, **kw)
        sem.v += 16
        inst.then_inc(sem.h, 16)
        sb.r[sem] = sem.v
        return inst

    def gdma(s, out_ap, in_ap, **kw):
        inst = s.nc.gpsimd.dma_start(out=out_ap, in_=in_ap, **kw)
        s.gsem.v += 16
        inst.then_inc(s.gsem.h, 16)
        return (s.gsem, s.gsem.v)

    def allgather(s, in_ap, out_ap, groups):
        inst = s.nc.gpsimd.collective_compute("AllGather", ALU.bypass, replica_groups=groups,
                                              ins=[in_ap], outs=[out_ap])
        s.ccsem.v += 1
        inst.then_inc(s.ccsem.h, 1)
        return (s.ccsem, s.ccsem.v)

    def barrier(s):
        evs = []
        for e in s.engs:
            if e.sem.v > 0:
                evs.append((e.sem, e.sem.v))
        for d in s.used_dsems:
            evs.append((d, d.v))
        if s.gsem.v:
            evs.append((s.gsem, s.gsem.v))
        if s.ccsem.v:
            evs.append((s.ccsem, s.ccsem.v))
        for ev in evs:
            s.sp.wait(ev)
        s.bar.v += 1
        s.nc.sync.sem_inc(s.bar.h, 1)
        for e in s.engs:
            if e is not s.sp:
                e.wait((s.bar, s.bar.v))
                for ev in evs:
                    e.seen[ev[0]] = max(e.seen.get(ev[0], 0), ev[1])
        for b in s.bufs:
            b.dsem = None
        s.bufs = []
        s.free_dsems.extend(s.used_dsems)
        s.used_dsems = []


class Prog:
    def __init__(s, cfg, stop_after=None):
        s.cfg = cfg
        s.stop_after = stop_after
        s.nc = bass.Bass("TRN2", target_bir_lowering=False)
        s.cx = Ctx(s.nc)
        s.inputs = {}
        s.inp_t = {}
        s.uid = 0
        s.es = ExitStack()

    def inp(s, name, shape, dt=F32):
        if name in s.inputs:
            assert s.inputs[name][0] == tuple(shape)
            return s.inp_t[name]
        t = s.nc.dram_tensor(name, list(shape), dt, kind="ExternalInput")
        s.inputs[name] = (tuple(shape), dt)
        s.inp_t[name] = t
        return t

    def dram(s, name, shape, dt):
        return s.nc.dram_tensor(name, list(shape), dt)

    def sb(s, st, name, shape, dt):
        s.uid += 1
        t = st.enter_context(s.nc.sbuf_tensor("s%d_%s" % (s.uid, name), list(shape), dt))
        return Buf(t, name)

    def ps(s, st, name, shape, dt=F32):
        s.uid += 1
        t = st.enter_context(s.nc.psum_tensor("p%d_%s" % (s.uid, name), list(shape), dt))
        return Buf(t, name)

    def castgather(s, name, rows, cols):
        cx = s.cx
        sh = s.inp(name, [rows // 8, cols])
        tmp = s.dram(name + "_b", [rows // 8, cols], BF16)
        full = s.dram(name + "_f", [rows, cols], BF16)
        n = cols
        step = 2048
        if n <= step:
            ev = cx.gdma(tmp[:, :], sh[:, :])
        else:
            assert n % step == 0 or True
            ev = cx.gdma(tmp[:, :], sh[:, :], max_dma_last_dim=step * 4)
        cx.pool.wait(ev)
        cx.allgather(tmp[:, :], full[:, :], [list(range(NCORES))])
        return full

    def gather_bf(s, name, rows, cols):
        cx = s.cx
        sh = s.inp(name, [rows // 8, cols], BF16)
        tmp = s.dram(name + "_b", [rows // 8, cols], BF16)
        full = s.dram(name + "_f", [rows, cols], BF16)
        ev = cx.gdma(tmp[:, :], sh[:, :])
        cx.pool.wait(ev)
        cx.allgather(tmp[:, :], full[:, :], [list(range(NCORES))])
        return full


def _acts(P):
    return P.cx.act, P.cx.dve, P.cx.pe, P.cx.sp, P.cx.pool


class Phases(Prog):
    def wprep_begin(s):
        s.wsem = CSem(s.nc, "wcast")
        s.wcc = CSem(s.nc, "wcc")
        s.wpending = []
        s.wpieces = {}
        s.W = {}

    def wcast(s, key, name, rows, cols, dt_in=F32):
        rows_p = 128
        while rows_p * cols > 256 * 1024 or (rows // 8) % rows_p != 0:
            rows_p //= 2
        assert rows_p >= 1
        npc = (rows // 8) // rows_p
        sh = s.inp(name, [rows // 8, cols], dt_in)
        tmp = s.dram(name + "_b", [rows // 8, cols], BF16)
        full = s.dram(name + "_f", [rows, cols], BF16)
        s.wpieces[name] = rows_p
        for k in range(npc):
            s.wpending.append((key, sh, tmp, full, k, rows_p, cols, dt_in, k == npc - 1))

    def wlocal(s, key, name, rows, cols):
        sh = s.inp(name, [rows, cols], F32)
        full = s.dram(name + "_f", [rows, cols], BF16)
        inst = s.nc.gpsimd.dma_start(out=full[:, :], in_=sh[:, :])
        s.wsem.v += 16
        inst.then_inc(s.wsem.h, 16)
        s.cx.pool.wait((s.wsem, s.wsem.v))
        s.W[key] = (full, (s.wsem, s.wsem.v), [(s.wsem, s.wsem.v)], rows)

    def wflush(s, chunk=3):
        quads = [[0, 1, 2, 3], [4, 5, 6, 7]]
        pairs = [[0, 4], [1, 5], [2, 6], [3, 7]]
        pend = sorted(s.wpending, key=lambda x: x[4])
        s.wpending = []
        pool = s.cx.pool
        for i0 in range(0, len(pend), chunk):
            grp = pend[i0:i0 + chunk]
            for key, sh, tmp, full, k, rp, cols, dt_in, last in grp:
                kw = {}
                if dt_in == F32 and cols > 2048:
                    kw["max_dma_last_dim"] = 2048 * 4
                inst = s.nc.gpsimd.dma_start(out=tmp[k * rp:(k + 1) * rp, :], in_=sh[k * rp:(k + 1) * rp, :], **kw)
                s.wsem.v += 16
                inst.then_inc(s.wsem.h, 16)
            pool.wait((s.wsem, s.wsem.v))
            mids = []
            for key, sh, tmp, full, k, rp, cols, dt_in, last in grp:
                mid = s.dram("wq%d" % s.uid, [4 * rp, cols], BF16)
                s.uid += 1
                inst = s.nc.gpsimd.collective_compute("AllGather", ALU.bypass, replica_groups=quads,
                                                      ins=[tmp[k * rp:(k + 1) * rp, :]], outs=[mid[:, :]])
                s.wcc.v += 1
                inst.then_inc(s.wcc.h, 1)
                mids.append(mid)
            pool.wait((s.wcc, s.wcc.v))
            for (key, sh, tmp, full, k, rp, cols, dt_in, last), mid in zip(grp, mids):
                inst = s.nc.gpsimd.collective_compute("AllGather", ALU.bypass, replica_groups=pairs,
                                                      ins=[mid[:, :]], outs=[full[k * 8 * rp:(k + 1) * 8 * rp, :]])
                s.wcc.v += 1
                inst.then_inc(s.wcc.h, 1)
                if key not in s.W:
                    s.W[key] = (full, None, [], 8 * rp)
                f_, _, evs_, rpp_ = s.W[key]
                assert len(evs_) == k
                evs_.append((s.wcc, s.wcc.v))
                s.W[key] = (f_, (s.wcc, s.wcc.v), evs_, rpp_)
            pool.wait((s.wcc, s.wcc.v))

    def consts(s):
        c = s.cfg
        st = s.es
        cx = s.cx
        s.ident = s.sb(st, "ident", [128, 128], F32)
        s.identb = s.sb(st, "identb", [128, 128], BF16)
        s.onesb = s.sb(st, "onesb", [128, 128], BF16)
        idt = s.inp("ident", [128, 128])
        cx.load(cx.sp, s.ident, s.ident[:, :], idt[:, :])
        cx.op(cx.dve, lambda: s.nc.vector.tensor_copy(out=s.identb[:, :], in_=s.ident[:, :]),
              reads=[s.ident], writes=[s.identb])
        cx.op(cx.dve, lambda: s.nc.vector.memset(s.onesb[:, :], 1.0), writes=[s.onesb])
        s.epsc = s.sb(st, "epsc", [128, 1], F32)
        cx.op(cx.dve, lambda: s.nc.vector.memset(s.epsc[:, :], c.EPS), writes=[s.epsc])
        pid = s.nc.sync.partition_id()
        s.rank_sp = pid % 2
        s.b_sp = pid // 2
        pidg = s.nc.gpsimd.partition_id()
        s.rank_g = pidg % 2

    def phase_mod(s):
        c = s.cfg
        nc, cx = s.nc, s.cx
        act, dve, pe, sp, pool = _acts(s)
        ccT_in = s.inp("ccT", [128, c.DT, 8])
        s.modT = [s.sb(s.es, "modT%d" % l, [128, 8, c.MODI], F32) for l in range(2)]
        s.modC = s.sb(s.es, "modC", [128, 8, c.MODI], F32)
        with ExitStack() as st:
            scT = s.sb(st, "scT", [128, c.DT, 8], F32)
            cx.load(sp, scT, scT[:, :, :], ccT_in[:, :, :])
            cx.op(act, lambda: nc.scalar.activation(out=scT[:, :, :], in_=scT[:, :, :], func=AF.Silu),
                  reads=[scT], writes=[scT])
            wts = [s.sb(st, "wmt%d" % i, [128, c.DT * 128], F32) for i in range(2)]
            pss = [s.ps(st, "modps%d" % i, [128, 8]) for i in range(2)]
            k = 0
            for l in range(2):
                wm = s.inp("wmod%d" % l, [c.MODI, 128, c.DT * 128])
                bm_in = s.inp("bmod%d" % l, [128, c.MODI])
                bm = s.sb(st, "bm%d" % l, [128, c.MODI], F32)
                cx.load(sp, bm, bm[:, :], bm_in[:, :])
                modS = s.sb(st, "modS%d" % l, [128, c.MODI, 8], F32)
                modR = s.sb(st, "modR%d" % l, [128, 8, c.MODI], F32)
                for i in range(c.MODI):
                    wt = wts[k % 2]
                    ps = pss[k % 2]
                    k += 1
                    cx.load(sp, wt, wt[:, :], wm[i, :, :])
                    cx.prewait(pe, reads=[wt, scT], writes=[ps])
                    for kt in range(c.DT):
                        inst = nc.tensor.matmul(ps[:, :], lhsT=wt[:, kt * 128:(kt + 1) * 128],
                                                rhs=scT[:, kt, :], start=(kt == 0), stop=(kt == c.DT - 1))
                    cx.mark(pe.tag(inst), [wt, scT], [ps])
                    cx.op(act, lambda: nc.scalar.activation(out=modS[:, i, :], in_=ps[:, :], func=AF.Identity,
                                                            bias=bm[:, i:i + 1], scale=1.0),
                          reads=[ps, bm], writes=[modS])
                cx.op(dve, lambda: nc.vector.tensor_copy(out=modR[:, :, :],
                                                         in_=modS[:, :, :].rearrange("p i r -> p r i")),
                      reads=[modS], writes=[modR])
                msh = s.dram("modsh%d" % l, [8 * 128, c.MODI], F32)
                mfull = s.dram("modfull%d" % l, [64 * 128, c.MODI], F32)
                cx.store(sp, modR, msh.ap().rearrange("(r p) i -> p r i", p=128), modR[:, :, :])
                pool.wait((modR.dsem, modR.r[modR.dsem]))
                mmid = s.dram("modmid%d" % l, [32 * 128, c.MODI], F32)
                ev = cx.allgather(msh[:, :], mmid[:, :], [[0, 1, 2, 3], [4, 5, 6, 7]])
                pool.wait(ev)
                ev = cx.allgather(mmid[:, :], mfull[:, :], [[0, 4], [1, 5], [2, 6], [3, 7]])
                pool.wait(ev)
                sp.wait(ev)
                mf3 = mfull.ap().rearrange("(j r p) i -> j r p i", r=8, p=128)
                mf4 = mfull.ap().rearrange("(j r p) i -> r p j i", r=8, p=128)
                cx.load(sp, s.modT[l], s.modT[l][:, :, :],
                        mf4[bass.ds(s.b_sp, 1), :, :, :].rearrange("o p j i -> p (o j) i"))
                if l == 0:
                    cx.load(sp, s.modC, s.modC[:, :, :], mf4[4, :, :, :])
            cx.barrier()
        for l in range(2):
            s.post_mod(s.modT[l])
        s.post_mod(s.modC)

    def post_mod(s, mt):
        c = s.cfg
        nc, cx = s.nc, s.cx
        flat = mt.t[:, :, :].rearrange("p j i -> p (j i)")
        for m in (1, 4, 7):
            cx.op(cx.dve, lambda: nc.vector.tensor_scalar(out=flat[:, m * c.DT:(m + 1) * c.DT],
                                                          in0=flat[:, m * c.DT:(m + 1) * c.DT],
                                                          scalar1=1.0, scalar2=None, op0=ALU.add),
                  reads=[mt], writes=[mt])
        for m in (2, 8):
            cx.op(cx.dve, lambda: nc.vector.tensor_scalar(out=flat[:, m * c.DT:(m + 1) * c.DT],
                                                          in0=flat[:, m * c.DT:(m + 1) * c.DT],
                                                          scalar1=0.5, scalar2=None, op0=ALU.mult),
                  reads=[mt], writes=[mt])

    def modcol(s, mt, m, dt):
        g = m * s.cfg.DT + dt
        return mt.t[:, :, :].rearrange("p j i -> p (j i)")[:, g:g + 1]

    def phase_xin(s, src, dstT, TOK):
        c = s.cfg
        nc, cx = s.nc, s.cx
        act, dve, pe, sp, pool = _acts(s)
        GS = min(4, TOK // 128)
        with ExitStack() as st:
            xin = [s.sb(st, "xin%d" % i, [128, c.D], F32) for i in range(2)]
            xo = [s.sb(st, "xo%d" % i, [128, c.DT, GS * 128], F32) for i in range(2)]
            pst = [s.ps(st, "xps%d" % i, [128, 4, 128]) for i in range(4)]
            k = 0
            for g in range(TOK // (GS * 128)):
                o = xo[g % 2]
                for j in range(GS):
                    tt = g * GS + j
                    xi = xin[tt % 2]
                    cx.load(sp, xi, xi[:, :], src[tt * 128:(tt + 1) * 128, :])
                    for q in range(c.DT // 4):
                        p = pst[k % 4]
                        cx.prewait(pe, reads=[xi, s.ident], writes=[p])
                        for u in range(4):
                            dt = q * 4 + u
                            inst = nc.tensor.transpose(out=p[:, u, :], in_=xi[:, dt * 128:(dt + 1) * 128],
                                                       identity=s.ident[:, :])
                        cx.mark(pe.tag(inst), [xi], [p])
                        dst = o[:, q * 4:(q + 1) * 4, j * 128:(j + 1) * 128]
                        if k % 2 == 0:
                            cx.op(act, lambda: nc.scalar.copy(out=dst, in_=p[:, :, :]), reads=[p], writes=[o])
                        else:
                            cx.op(dve, lambda: nc.vector.tensor_copy(out=dst, in_=p[:, :, :]), reads=[p], writes=[o])
                        k += 1
                cx.store(sp, o, dstT.ap().rearrange("k p t -> p k t")[:, :, g * GS * 128:(g + 1) * GS * 128],
                         o[:, :, :])
            cx.barrier()

    def phase_norm(s, srcT, dstT, TOK, mt, m_shift, out_dt, name, gain=None):
        c = s.cfg
        nc, cx = s.nc, s.cx
        act, dve, pe, sp, pool = _acts(s)
        BLK = min(512, TOK)
        with ExitStack() as st:
            xs = [[s.sb(st, "%sx%d_%d" % (name, S, d), [128, BLK], F32) for d in range(c.DT)] for S in range(2)]
            sq = [s.sb(st, "%ssq%d" % (name, i), [128, BLK], BF16) for i in range(3)]
            ssq = [s.ps(st, "%sssq%d" % (name, i), [128, BLK]) for i in range(2)]
            sd = [s.sb(st, "%ssd%d" % (name, i), [128, BLK], F32) for i in range(2)]
            rstd = [s.sb(st, "%srs%d" % (name, i), [128, BLK], F32) for i in range(2)]
            tmp = [s.sb(st, "%stmp%d" % (name, i), [128, BLK], F32) for i in range(3)]
            ho = [s.sb(st, "%sho%d" % (name, i), [128, BLK], out_dt) for i in range(3)]
            for tb in range(TOK // BLK):
                S = tb % 2
                sl = slice(tb * BLK, (tb + 1) * BLK)
                for dt in range(c.DT):
                    x = xs[S][dt]
                    cx.load(sp, x, x[:, :], srcT[dt, :, sl])
                    q = sq[dt % 3]
                    cx.op(act, lambda: nc.scalar.activation(out=q[:, :], in_=x[:, :], func=AF.Square),
                          reads=[x], writes=[q])
                    cx.op(pe, lambda: nc.tensor.matmul(ssq[S][:, :], lhsT=s.onesb[:, :], rhs=q[:, :],
                                                       start=(dt == 0), stop=(dt == c.DT - 1)),
                          reads=[q, s.onesb], writes=[ssq[S]])
                cx.op(act, lambda: nc.scalar.activation(out=sd[S][:, :], in_=ssq[S][:, :], func=AF.Sqrt,
                                                        bias=s.epsc[:, 0:1], scale=1.0 / c.D),
                      reads=[ssq[S], s.epsc], writes=[sd[S]])
                cx.op(dve, lambda: nc.vector.reciprocal(out=rstd[S][:, :], in_=sd[S][:, :]),
                      reads=[sd[S]], writes=[rstd[S]])
                for dt in range(c.DT):
                    x = xs[S][dt]
                    t = tmp[dt % 3]
                    h = ho[dt % 3]
                    cx.op(dve, lambda: nc.vector.tensor_tensor(out=t[:, :], in0=x[:, :], in1=rstd[S][:, :],
                                                               op=ALU.mult),
                          reads=[x, rstd[S]], writes=[t])
                    if gain is None:
                        sc_ap = s.modcol(mt, m_shift + 1, dt)
                        sh_ap = s.modcol(mt, m_shift, dt)
                        cx.op(act, lambda: nc.scalar.activation(out=h[:, :], in_=t[:, :], func=AF.Identity,
                                                                bias=sh_ap, scale=sc_ap),
                              reads=[t, mt], writes=[h])
                    else:
                        cx.op(act, lambda: nc.scalar.activation(out=h[:, :], in_=t[:, :], func=AF.Copy,
                                                                scale=gain[:, dt:dt + 1]),
                              reads=[t, gain], writes=[h])
                    cx.store(sp, h, dstT[dt, :, sl], h[:, :])
            cx.barrier()

    def linear(s, name, mvT, KT, TOK, tok0, Wkeys, mt_list, SB, epi, wrow0=0, kt0=0, pre=None, extra=None):
        c = s.cfg
        nc, cx = s.nc, s.cx
        act, dve, pe, sp, pool = _acts(s)
        nW = len(Wkeys)
        NSET = 3 if nW == 2 else 4
        NSLOT = 4 if (nW == 1 and KT * 128 * 2 * 4 <= 48 * 1024) else 3
        with ExitStack() as st:
            NSB = TOK // SB
            sizes = [SB] * NSB
            KG = 4 if KT % 4 == 0 else (2 if KT % 2 == 0 else 1)
            kper = KT // KG
            mvb = [[s.sb(st, "%smv%d_%d" % (name, g, q), [128, kper, SB], BF16) for q in range(KG)]
                   for g in range(NSB)]
            if extra is not None:
                mvT2, TOK2 = extra
                mvb.append([s.sb(st, "%smvx%d" % (name, q), [128, kper, TOK2], BF16) for q in range(KG)])
                sizes.append(TOK2)
            NREG = NSB
            NSB = len(sizes)

            def mvload(g):
                for q in range(KG):
                    t_ = mvb[g][q]
                    if g < NREG:
                        cx.load(sp, t_, t_[:, :, :],
                                mvT.ap().rearrange("k p t -> p k t")[:, kt0 + q * kper:kt0 + (q + 1) * kper,
                                                                     tok0 + g * SB:tok0 + (g + 1) * SB])
                    else:
                        cx.load(sp, t_, t_[:, :, :],
                                mvT2.ap().rearrange("k p t -> p k t")[:, q * kper:(q + 1) * kper, :])
            wts = [[s.sb(st, "%sw%d_%d" % (name, wi, sl), [128, KT * 128], BF16) for sl in range(NSLOT)]
                   for wi in range(nW)]
            pss = [[s.ps(st, "%sps%d_%d" % (name, wi, se), [128, SB]) for wi in range(nW)] for se in range(NSET)]
            Wf = []
            Wev = []
            for k in Wkeys:
                full, ev, evs, rpp = s.W[k]
                Wf.append(full)
                Wev.append((evs, rpp))
            epi_state = epi(st, None, None, None, None)
            cnt = 0

            def wload(mi_):
                mt_ = mt_list[mi_]
                for wi_ in range(nW):
                    w_ = wts[wi_][mi_ % NSLOT]
                    evs_, rpp_ = Wev[wi_]
                    sp.wait(evs_[min(len(evs_) - 1, (wrow0 + mt_ * 128 + 127) // rpp_)])
                    cx.load(sp, w_, w_[:, :], Wf[wi_][wrow0 + mt_ * 128: wrow0 + (mt_ + 1) * 128, :])
            wload(0)
            mvload(0)
            if pre is not None:
                pre(epi_state, 0, mt_list[0])
            for mi in range(1, min(NSLOT - 1, len(mt_list))):
                wload(mi)
                if mi < NSB:
                    mvload(mi)
            for g in range(min(NSLOT - 1, len(mt_list)), NSB):
                mvload(g)
            for g in range(1, NSB):
                pass
            for mi, mt in enumerate(mt_list):
                slot = mi % NSLOT
                if pre is not None and mi + 1 < len(mt_list):
                    pre(epi_state, mi + 1, mt_list[mi + 1])
                if mi + NSLOT - 1 < len(mt_list):
                    wload(mi + NSLOT - 1)
                for sbi in range(NSB):
                    se = cnt % NSET
                    cnt += 1
                    wd = sizes[sbi]
                    for wi in range(nW):
                        w = wts[wi][slot]
                        p = pss[se][wi]
                        cx.prewait(pe, reads=[w], writes=[p])
                        for kt in range(KT):
                            if kt % kper == 0:
                                pe.wait(mvb[sbi][kt // kper].w)
                            inst = nc.tensor.matmul(p[:, 0:wd], lhsT=w[:, kt * 128:(kt + 1) * 128],
                                                    rhs=mvb[sbi][kt // kper][:, kt % kper, :],
                                                    start=(kt == 0), stop=(kt == KT - 1))
                        cx.mark(pe.tag(inst), [w], [p])
                    epi(st, epi_state, mi, mt, (sbi, pss[se]))
            cx.barrier()

    def ffn(s, name, l, which, xT, TOK, mt, m0, ctx=None):
        c = s.cfg
        nc, cx = s.nc, s.cx
        act, dve, pe, sp, pool = _acts(s)
        hT = s.scr_h(TOK)
        gT = s.scr_g(TOK)
        s.phase_norm(xT, hT, TOK, mt, m0, BF16, name + "n")
        if ctx is not None:
            cT, NC, mtc = ctx
            hcT = s.scr_h(NC)
            gcT = s.scr_g(NC)
            s.phase_norm(cT, hcT, NC, mtc, m0, BF16, name + "nc")
        SB = min(512, TOK)

        TB = min(2048, TOK)
        for tb in range(TOK // TB):
            NREG = TB // SB
            with_x = ctx is not None and tb == TOK // TB - 1

            def epi_up(st, state, mi, ft, extra):
                if state is None:
                    d = dict(sg=[s.sb(st, name + "sg%d" % i, [128, SB], F32) for i in range(3)],
                             go=[s.sb(st, name + "go%d" % i, [128, TB], BF16) for i in range(3)], k=[0])
                    if with_x:
                        d["gc"] = [s.sb(st, name + "gc%d" % i, [128, NC], BF16) for i in range(2)]
                    return d
                sbi, pp = extra
                sg = state["sg"][state["k"][0] % 3]
                state["k"][0] += 1
                if sbi < NREG:
                    go = state["go"][mi % 3]
                    cx.op(act, lambda: nc.scalar.activation(out=sg[:, :], in_=pp[0][:, :], func=AF.Silu),
                          reads=[pp[0]], writes=[sg])
                    cx.op(dve, lambda: nc.vector.tensor_tensor(out=go[:, sbi * SB:(sbi + 1) * SB], in0=sg[:, :],
                                                               in1=pp[1][:, :], op=ALU.mult),
                          reads=[sg, pp[1]], writes=[go])
                    if sbi == NREG - 1:
                        cx.store(sp, go, gT[ft, :, tb * TB:(tb + 1) * TB], go[:, :])
                else:
                    gc = state["gc"][mi % 2]
                    cx.op(act, lambda: nc.scalar.activation(out=sg[:, 0:NC], in_=pp[0][:, 0:NC], func=AF.Silu),
                          reads=[pp[0]], writes=[sg])
                    cx.op(dve, lambda: nc.vector.tensor_tensor(out=gc[:, :], in0=sg[:, 0:NC], in1=pp[1][:, 0:NC],
                                                               op=ALU.mult), reads=[sg, pp[1]], writes=[gc])
                    cx.store(sp, gc, gcT[ft, :, :], gc[:, :])
            s.linear(name + "u", hT, c.DT, TB, tb * TB, [("w1", l, which), ("w3", l, which)],
                     list(range(c.FT)), SB, epi_up, extra=((hcT, NC) if with_x else None))

        TBD = min(1024, TOK)
        gcol = m0 + 2
        for tb in range(TOK // TBD):
            NSBI = TBD // SB
            with_x = ctx is not None and tb == TOK // TBD - 1

            def epi_dn(st, state, mi, dt, extra):
                if state is None:
                    d = dict(xs=[s.sb(st, name + "dx%d" % i, [128, SB], F32) for i in range(2 * NSBI)],
                             xo=[s.sb(st, name + "do%d" % i, [128, SB], F32) for i in range(3)], k=[0])
                    if with_x:
                        d["cs"] = [s.sb(st, name + "dc%d" % i, [128, NC], F32) for i in range(2)]
                        d["co"] = [s.sb(st, name + "dco%d" % i, [128, NC], F32) for i in range(2)]
                    return d
                sbi, pp = extra
                k = state["k"][0]
                state["k"][0] += 1
                if sbi < NSBI:
                    xs = state["xs"][(mi % 2) * NSBI + sbi]
                    xo = state["xo"][k % 3]
                    sl = slice(tb * TBD + sbi * SB, tb * TBD + (sbi + 1) * SB)
                    cx.op(dve, lambda: nc.vector.scalar_tensor_tensor(out=xo[:, :], in0=pp[0][:, :],
                                                                      scalar=s.modcol(mt, gcol, dt), in1=xs[:, :],
                                                                      op0=ALU.mult, op1=ALU.add),
                          reads=[pp[0], xs, mt], writes=[xo])
                    cx.store(sp, xo, xT[dt, :, sl], xo[:, :])
                else:
                    cs = state["cs"][mi % 2]
                    co = state["co"][mi % 2]
                    cx.op(dve, lambda: nc.vector.scalar_tensor_tensor(out=co[:, :], in0=pp[0][:, 0:NC],
                                                                      scalar=s.modcol(mtc, gcol, dt), in1=cs[:, :],
                                                                      op0=ALU.mult, op1=ALU.add),
                          reads=[pp[0], cs, mtc], writes=[co])
                    cx.store(sp, co, cT[dt, :, :], co[:, :])

            def pre_dn(state, mi, dt):
                for sbi in range(NSBI):
                    xs = state["xs"][(mi % 2) * NSBI + sbi]
                    sl = slice(tb * TBD + sbi * SB, tb * TBD + (sbi + 1) * SB)
                    cx.load(sp, xs, xs[:, :], xT[dt, :, sl])
                if with_x:
                    cs = state["cs"][mi % 2]
                    cx.load(sp, cs, cs[:, :], cT[dt, :, :])
            s.linear(name + "d", gT, c.FT, TBD, tb * TBD, [("w2", l, which)], list(range(c.DT)), SB, epi_dn,
                     pre=pre_dn, extra=((gcT, NC) if with_x else None))

    def scr_h(s, TOK):
        key = ("h", TOK)
        if key not in s.scr:
            s.scr[key] = s.dram("hT_%d" % TOK, [s.cfg.DT, 128, TOK], BF16)
        return s.scr[key]

    def scr_g(s, TOK):
        key = ("g", TOK)
        if key not in s.scr:
            s.scr[key] = s.dram("gT_%d" % TOK, [s.cfg.FT, 128, TOK], BF16)
        return s.scr[key]

    def phase_out(s, xT, mt_gain):
        c = s.cfg
        nc, cx = s.nc, s.cx
        act, dve, pe, sp, pool = _acts(s)
        oT = s.dram("oT", [c.DT, 128, c.T], F32)
        if mt_gain is None:
            oT = xT
        else:
            s.phase_norm(xT, oT, c.T, None, 0, F32, "fn", gain=mt_gain)
        with ExitStack() as st:
            xi = [s.sb(st, "oxi%d" % i, [128, c.DT, 128], F32) for i in range(2)]
            xo = [s.sb(st, "oxo%d" % i, [128, c.D], F32) for i in range(2)]
            pst = [s.ps(st, "ops%d" % i, [128, 4, 128]) for i in range(4)]
            k = 0
            for tt in range(c.T // 128):
                a = xi[tt % 2]
                o = xo[tt % 2]
                cx.load(sp, a, a[:, :, :], oT.ap().rearrange("k p t -> p k t")[:, :, tt * 128:(tt + 1) * 128])
                for q in range(c.DT // 4):
                    p = pst[k % 4]
                    cx.prewait(pe, reads=[a, s.ident], writes=[p])
                    for u in range(4):
                        dt = q * 4 + u
                        inst = nc.tensor.transpose(out=p[:, u, :], in_=a[:, dt, :], identity=s.ident[:, :])
                    cx.mark(pe.tag(inst), [a], [p])
                    dst = o[:, q * 512:(q + 1) * 512]
                    src = p[:, :, :].rearrange("p u j -> p (u j)")
                    if k % 2 == 0:
                        cx.op(act, lambda: nc.scalar.copy(out=dst, in_=src), reads=[p], writes=[o])
                    else:
                        cx.op(dve, lambda: nc.vector.tensor_copy(out=dst, in_=src), reads=[p], writes=[o])
                    k += 1
                cx.store(sp, o, s.out[tt * 128:(tt + 1) * 128, :], o[:, :])
            cx.barrier()


class Mixer0:
    def mixer0(s):
        c = s.cfg
        nc, cx = s.nc, s.cx
        act, dve, pe, sp, pool = _acts(s)
        NH, HWT = c.NH, c.HW // 128
        hT = s.scr_h(c.T)
        hcT = s.scr_h(c.NCTX)
        s.phase_norm(s.xT, hT, c.T, s.modT[0], 3, BF16, "m0n")
        s.phase_norm(s.cT, hcT, c.NCTX, s.modC, 3, BF16, "m0c")
        pT = s.dram("pT", [4 * NH, 128, c.T], F32)
        hyT = s.dram("hyT", [3 * HWT * 128, c.T], BF16)
        hyG = s.dram("hyG", [2 * 3 * HWT * 128, c.T], BF16)
        pcT = s.dram("pcT", [2 * NH, 128, c.NCTX], F32)
        s.pT, s.pcT = pT, pcT
        SB = min(512, c.T)
        kscale = float(c.HD) ** -0.5

        def epi_in(st, state, mi, mt, extra):
            if state is None:
                return dict(o=[s.sb(st, "ipo%d" % i, [128, c.T], F32) for i in range(2)],
                            ob=[s.sb(st, "ipb%d" % i, [128, c.T], BF16) for i in range(2)], k=[0])
            sbi, pp = extra
            hy = mt >= 4 * NH
            o = (state["ob"] if hy else state["o"])[mi % 2]
            dst = o[:, sbi * SB:(sbi + 1) * SB]
            sc = kscale if (NH <= mt < 2 * NH) else 1.0
            k = state["k"][0]
            state["k"][0] += 1
            if k % 2 == 0:
                cx.op(act, lambda: nc.scalar.mul(out=dst, in_=pp[0][:, :], mul=sc), reads=[pp[0]], writes=[o])
            else:
                cx.op(dve, lambda: nc.vector.tensor_scalar(out=dst, in0=pp[0][:, :], scalar1=sc, scalar2=None,
                                                           op0=ALU.mult), reads=[pp[0]], writes=[o])
            if sbi == c.T // SB - 1:
                if hy:
                    m = mt - 4 * NH
                    cx.store(sp, o, hyT[m * 128:(m + 1) * 128, :], o[:, :])
                else:
                    cx.store(sp, o, pT[mt, :, :], o[:, :])
        s.linear("ip", hT, c.DT, c.T, 0, ["win"], list(range(c.PT)), SB, epi_in)
        for m in range(3 * HWT):
            ev = cx.allgather(hyT[m * 128:(m + 1) * 128, :], hyG[m * 256:(m + 1) * 256, :],
                              [[0, 1], [2, 3], [4, 5], [6, 7]])
        s.hy_ev = ev
        s.hyG = hyG

        SBc = min(512, c.NCTX)

        def epi_ctx(st, state, mi, mt, extra):
            if state is None:
                return dict(o=[s.sb(st, "ico%d" % i, [128, c.NCTX], F32) for i in range(2)])
            sbi, pp = extra
            o = state["o"][mi % 2]
            sc = kscale if mt < 2 * NH else 1.0
            cx.op(act, lambda: nc.scalar.mul(out=o[:, sbi * SBc:(sbi + 1) * SBc], in_=pp[0][:, :], mul=sc),
                  reads=[pp[0]], writes=[o])
            if sbi == c.NCTX // SBc - 1:
                cx.store(sp, o, pcT[mt - NH, :, :], o[:, :])
        s.linear("ic", hcT, c.DT, c.NCTX, 0, ["win"], list(range(NH, 3 * NH)), SBc, epi_ctx)

        ypT = s.dram("ypT", [c.DT, 128, c.T], BF16)
        s.ypT = ypT
        if s.stop_after == "m0a":
            return
        s.retention()
        if s.stop_after == "m0b":
            return
        if s.hyena() == "stop":
            return
        if s.stop_after == "h7":
            return

        NSBO = c.T // SB

        def epi_out(st, state, mi, dt, extra):
            if state is None:
                return dict(xs=[s.sb(st, "opx%d" % i, [128, SB], F32) for i in range(2 * NSBO)],
                            xo=[s.sb(st, "opo%d" % i, [128, SB], F32) for i in range(3)], k=[0])
            sbi, pp = extra
            k = state["k"][0]
            state["k"][0] += 1
            xs = state["xs"][(mi % 2) * NSBO + sbi]
            xo = state["xo"][k % 3]
            sl = slice(sbi * SB, (sbi + 1) * SB)
            cx.op(dve, lambda: nc.vector.scalar_tensor_tensor(out=xo[:, :], in0=pp[0][:, :],
                                                              scalar=s.modcol(s.modT[0], 5, dt), in1=xs[:, :],
                                                              op0=ALU.mult, op1=ALU.add),
                  reads=[pp[0], xs, s.modT[0]], writes=[xo])
            cx.store(sp, xo, s.xT[dt, :, sl], xo[:, :])

        def pre_out(state, mi, dt):
            for sbi in range(NSBO):
                xs = state["xs"][(mi % 2) * NSBO + sbi]
                cx.load(sp, xs, xs[:, :], s.xT[dt, :, sbi * SB:(sbi + 1) * SB])
        s.linear("op", ypT, c.DT, c.T, 0, ["wout"], list(range(c.DT)), SB, epi_out, pre=pre_out)

    def rotary(s, st, name, src_tile, C, S, out_bf):
        c = s.cfg
        nc, cx = s.nc, s.cx
        act, dve, pe, sp, pool = _acts(s)
        x, xs, t1, t2 = s.rt["x"], s.rt["xs"], s.rt["t1"], s.rt["t2"]
        cx.load(sp, x, x[:, :], src_tile)
        cx.load(sp, xs, xs[0:64, :], src_tile[64:128, :])
        cx.load(sp, xs, xs[64:128, :], src_tile[0:64, :], multi=True)
        cx.op(dve, lambda: nc.vector.tensor_tensor(out=t1[:, :], in0=x[:, :], in1=C[:, :], op=ALU.mult),
              reads=[x, C], writes=[t1])
        cx.op(dve, lambda: nc.vector.tensor_tensor(out=t2[:, :], in0=xs[:, :], in1=S[:, :], op=ALU.mult),
              reads=[xs, S], writes=[t2])
        cx.op(dve, lambda: nc.vector.tensor_tensor(out=out_bf[:, :], in0=t1[:, :], in1=t2[:, :], op=ALU.add),
              reads=[t1, t2], writes=[out_bf])

    def tposes(s, src_bf, dst_bf, nchunks, pst_list, kcount):
        nc, cx = s.nc, s.cx
        act, dve, pe, sp, pool = _acts(s)
        per = 8
        for g in range(0, nchunks, per):
            n = min(per, nchunks - g)
            p = pst_list[kcount[0] % len(pst_list)]
            cx.prewait(pe, reads=[src_bf, s.identb], writes=[p])
            for u in range(n):
                cc = g + u
                inst = nc.tensor.transpose(out=p[:, u, :], in_=src_bf[:, cc * 128:(cc + 1) * 128],
                                           identity=s.identb[:, :])
            cx.mark(pe.tag(inst), [src_bf], [p])
            if kcount[0] % 2 == 0:
                cx.op(act, lambda: nc.scalar.copy(out=dst_bf[:, g:g + n, :], in_=p[:, 0:n, :]),
                      reads=[p], writes=[dst_bf])
            else:
                cx.op(dve, lambda: nc.vector.tensor_copy(out=dst_bf[:, g:g + n, :], in_=p[:, 0:n, :]),
                      reads=[p], writes=[dst_bf])
            kcount[0] += 1

    def retention(s):
        c = s.cfg
        nc, cx = s.nc, s.cx
        act, dve, pe, sp, pool = _acts(s)
        NH, NCH, T = c.NH, c.NCH, c.T
        NCC = c.NCTX // 128
        pT, pcT, ypT = s.pT, s.pcT, s.ypT
        rotC_in = s.inp("rotC", [128, T])
        rotS_in = s.inp("rotS", [128, T])
        lg_in = s.inp("lgrep", [128, 2 * NH])
        rc128_in = s.inp("rc128", [128, 4, 128])
        rcT_in = s.inp("rcT", [128, 2, T])
        rcp_in = s.inp("rcp", [128, 2 + 2 * NCC])
        rankv_in = s.inp("rankv", [128, 4])
        rscr = s.dram("rscr_k", [NH, 128, T], BF16)
        rscr_v = s.dram("rscr_v", [NH, 128, T], BF16)
        rscr_u = s.dram("rscr_u", [NH, 2, 128, T], F32)
        exs = s.dram("exs", [2 * 128, NH * 128], F32)
        exg = s.dram("exg", [2 * 2 * 128, NH * 128], F32)
        with ExitStack() as st:
            C = s.sb(st, "rotC", [128, T], F32)
            S = s.sb(st, "rotS", [128, T], F32)
            lg = s.sb(st, "lg", [128, 2 * NH], F32)
            nlg = s.sb(st, "nlg", [128, 2 * NH], F32)
            rc128 = s.sb(st, "rc128", [128, 4, 128], F32)
            rcT = s.sb(st, "rcT", [128, 2, T], F32)
            rcp = s.sb(st, "rcp", [128, 2 + 2 * NCC], F32)
            rankv = s.sb(st, "rankv", [128, 4], F32)
            for b_, i_ in ((C, rotC_in), (S, rotS_in), (lg, lg_in), (rcp, rcp_in), (rankv, rankv_in)):
                cx.load(sp, b_, b_[:, :], i_[:, :])
            cx.load(sp, rc128, rc128[:, :, :], rc128_in[:, :, :])
            cx.load(sp, rcT, rcT[:, :, :], rcT_in[:, :, :])
            cx.op(dve, lambda: nc.vector.tensor_scalar(out=nlg[:, :], in0=lg[:, :], scalar1=-1.0, scalar2=None,
                                                       op0=ALU.mult), reads=[lg], writes=[nlg])
            s.rt = dict(x=s.sb(st, "rx", [128, T], F32), xs=s.sb(st, "rxs", [128, T], F32),
                        t1=s.sb(st, "rt1", [128, T], F32), t2=s.sb(st, "rt2", [128, T], F32))
            kr = s.sb(st, "kr", [128, T], BF16)
            vb = s.sb(st, "vb", [128, T], BF16)
            vf = s.rt["x"]
            ktm = s.sb(st, "ktm", [128, NCH, 128], BF16)
            vtm = s.sb(st, "vtm", [128, NCH, 128], BF16)
            vdf = s.sb(st, "vdf", [128, NCH, 128], BF16)
            vdb = s.sb(st, "vdb", [128, NCH, 128], BF16)
            U = [s.sb(st, "U%d" % i, [128, NCH, 128], F32) for i in range(2)]
            dec = s.sb(st, "dec", [128, 8], F32)
            ex = s.sb(st, "ex", [128, NH, 2, 128], F32)
            ctxS = s.sb(st, "ctxS", [128, NH, 2, 128], F32)
            Sst = [s.sb(st, "Sst%d" % i, [128, 128], F32) for i in range(2)]
            pst = [s.ps(st, "rtp%d" % i, [128, 8, 128], BF16) for i in range(2)]
            pu = [s.ps(st, "rup%d" % i, [128, 4, 128]) for i in range(3)]
            kc = [0]
            kcb = s.sb(st, "kcb", [128, c.NCTX], BF16)
            vcb = s.sb(st, "vcb", [128, c.NCTX], BF16)
            kcf = s.sb(st, "kcf", [128, c.NCTX], F32)
            vcf = s.sb(st, "vcf", [128, c.NCTX], F32)
            kctm = s.sb(st, "kctm", [128, NCC, 128], BF16)
            vctm = s.sb(st, "vctm", [128, NCC, 128], BF16)
            vcd = s.sb(st, "vcd", [128, 2, NCC, 128], BF16)
            cw = s.sb(st, "cw", [128, 2 * NCC], F32)

            def expcol(dst_ap, in_ap, scale_ap, reads, wbuf):
                cx.op(act, lambda: nc.scalar.activation(out=dst_ap, in_=in_ap, func=AF.Exp, scale=scale_ap),
                      reads=reads, writes=[wbuf])

            for h in range(NH):
                lf = lg[:, h:h + 1]
                lb = lg[:, NH + h:NH + h + 1]
                expcol(dec[:, 0:1], rcp[:, 0:1], lf, [rcp, lg], dec)
                expcol(dec[:, 1:2], rcp[:, 1:2], lb, [rcp, lg], dec)
                cx.op(act, lambda: nc.scalar.activation(out=dec[:, 2:3], in_=lf, func=AF.Exp, scale=128.0),
                      reads=[lg], writes=[dec])
                cx.op(act, lambda: nc.scalar.activation(out=dec[:, 3:4], in_=lb, func=AF.Exp, scale=128.0),
                      reads=[lg], writes=[dec])
                s.rotary(st, "k", pT[NH + h, :, :], C, S, kr)
                cx.store(sp, kr, rscr[h, :, :], kr[:, :])
                cx.load(sp, vf, vf[:, :], pT[2 * NH + h, :, :])
                cx.op(act, lambda: nc.scalar.copy(out=vb[:, :], in_=vf[:, :]), reads=[vf], writes=[vb])
                s.tposes(kr, ktm, NCH, pst, kc)
                s.tposes(vb, vtm, NCH, pst, kc)
                cx.store(sp, vtm, rscr_v[h, :, :], vtm[:, :, :].rearrange("p c e -> p (c e)"))
                cx.op(dve, lambda: nc.vector.tensor_scalar(out=vdf[:, :, :], in0=vtm[:, :, :], scalar1=dec[:, 0:1],
                                                           scalar2=None, op0=ALU.mult),
                      reads=[vtm, dec], writes=[vdf])
                cx.op(dve, lambda: nc.vector.tensor_scalar(out=vdb[:, :, :], in0=vtm[:, :, :], scalar1=dec[:, 1:2],
                                                           scalar2=None, op0=ALU.mult),
                      reads=[vtm, dec], writes=[vdb])
                ku = 0
                for di, vd in enumerate((vdf, vdb)):
                    for g in range(0, NCH, 4):
                        n = min(4, NCH - g)
                        p = pu[ku % 3]
                        ku += 1
                        cx.prewait(pe, reads=[ktm, vd], writes=[p])
                        for u in range(n):
                            inst = nc.tensor.matmul(p[:, u, :], lhsT=ktm[:, g + u, :], rhs=vd[:, g + u, :],
                                                    start=True, stop=True)
                        cx.mark(pe.tag(inst), [ktm, vd], [p])
                        cx.op(act, lambda: nc.scalar.copy(out=U[di][:, g:g + n, :], in_=p[:, 0:n, :]),
                              reads=[p], writes=[U[di]])
                    cx.store(sp, U[di], rscr_u[h, di, :, :], U[di][:, :, :].rearrange("p c e -> p (c e)"))
                for di in range(2):
                    order = range(NCH) if di == 0 else range(NCH - 1, -1, -1)
                    first = True
                    for cc in order:
                        if first:
                            cx.op(dve, lambda: nc.vector.tensor_copy(out=ex[:, h, di, :], in_=U[di][:, cc, :]),
                                  reads=[U[di]], writes=[ex])
                            first = False
                        else:
                            cx.op(dve, lambda: nc.vector.scalar_tensor_tensor(
                                out=ex[:, h, di, :], in0=ex[:, h, di, :], scalar=dec[:, 2 + di:3 + di],
                                in1=U[di][:, cc, :], op0=ALU.mult, op1=ALU.add),
                                reads=[ex, U[di], dec], writes=[ex])
                for cc in range(NCC):
                    expcol(cw[:, cc:cc + 1], rcp[:, 2 + cc:3 + cc], lf, [rcp, lg], cw)
                    expcol(cw[:, NCC + cc:NCC + cc + 1], rcp[:, 2 + NCC + cc:3 + NCC + cc], lb, [rcp, lg], cw)
                cx.load(sp, kcf, kcf[:, :], pcT[h, :, :])
                cx.load(sp, vcf, vcf[:, :], pcT[NH + h, :, :])
                cx.op(act, lambda: nc.scalar.copy(out=kcb[:, :], in_=kcf[:, :]), reads=[kcf], writes=[kcb])
                cx.op(act, lambda: nc.scalar.copy(out=vcb[:, :], in_=vcf[:, :]), reads=[vcf], writes=[vcb])
                s.tposes(kcb, kctm, NCC, pst, kc)
                s.tposes(vcb, vctm, NCC, pst, kc)
                for di in range(2):
                    for cc in range(NCC):
                        cx.op(dve, lambda: nc.vector.tensor_scalar(
                            out=vcd[:, di, cc, :], in0=vctm[:, cc, :], scalar1=cw[:, di * NCC + cc:di * NCC + cc + 1],
                            scalar2=None, op0=ALU.mult), reads=[vctm, cw], writes=[vcd])
                p = pu[ku % 3]
                ku += 1
                cx.prewait(pe, reads=[kctm, vcd], writes=[p])
                for di in range(2):
                    for cc in range(NCC):
                        inst = nc.tensor.matmul(p[:, di, :], lhsT=kctm[:, cc, :], rhs=vcd[:, di, cc, :],
                                                start=(cc == 0), stop=(cc == NCC - 1))
                cx.mark(pe.tag(inst), [kctm, vcd], [p])
                cx.op(act, lambda: nc.scalar.copy(out=ctxS[:, h, :, :], in_=p[:, 0:2, :]), reads=[p], writes=[ctxS])

            for di in range(2):
                cx.store(sp, ex, exs.ap().rearrange("(d p) (h e) -> d p h e", d=2, h=NH)[di], ex[:, :, di, :])
            pool.wait((ex.dsem, ex.r[ex.dsem]))
            for di in range(2):
                ev = cx.allgather(exs[di * 128:(di + 1) * 128, :], exg[di * 256:(di + 1) * 256, :],
                                  [[0, 1], [2, 3], [4, 5], [6, 7]])
            s.wprep_C()
            exr = s.sb(st, "exr", [128, 2, NH, 128], F32)
            sp.wait(ev)
            egv = exg.ap().rearrange("(d j p) (h e) -> d j p h e", d=2, j=2, h=NH)
            cx.load(sp, exr, exr[:, 0, :, :], egv[0, 0])
            cx.load(sp, exr, exr[:, 1, :, :], egv[1, 1], multi=True)

            qr = s.sb(st, "qr", [128, T], BF16)
            qf = s.sb(st, "qf", [128, T], BF16)
            qb = s.sb(st, "qb", [128, T], BF16)
            Gf = s.sb(st, "Gf", [128, T], BF16)
            Gb = s.sb(st, "Gb", [128, T], BF16)
            DT_ = s.sb(st, "DTm", [128, 128], F32)
            Et = [s.sb(st, "Et%d" % i, [128, 128], F32) for i in range(2)]
            SD = s.sb(st, "SD", [128, NCH, 128], BF16)
            Sbf = [s.sb(st, "Sbf%d" % i, [128, NCH, 128], BF16) for i in range(2)]
            o = s.rt["t2"]
            osq = s.sb(st, "rosq", [128, T], BF16)
            gf = s.rt["x"]
            sg = s.sb(st, "rsg", [128, T], BF16)
            rs = s.rt["t1"]
            rout = s.sb(st, "rout", [128, T], BF16)
            alpha = s.sb(st, "alpha", [128, 2], F32)
            pss = [s.ps(st, "rsp%d" % i, [128, 4, 128]) for i in range(2)]
            NB = max(1, T // 512)
            BL = T // NB
            for h in range(NH):
                lf = lg[:, h:h + 1]
                lb = lg[:, NH + h:NH + h + 1]
                cx.op(act, lambda: nc.scalar.activation(out=dec[:, 2:3], in_=lf, func=AF.Exp, scale=128.0),
                      reads=[lg], writes=[dec])
                cx.op(act, lambda: nc.scalar.activation(out=dec[:, 3:4], in_=lb, func=AF.Exp, scale=128.0),
                      reads=[lg], writes=[dec])
                expcol(alpha[:, 0:1], lf, rankv[:, 0:1], [lg, rankv], alpha)
                expcol(alpha[:, 1:2], lb, rankv[:, 1:2], [lg, rankv], alpha)
                expcol(Et[0][:, :], rc128[:, 0, :], lf, [rc128, lg], Et[0])
                expcol(Et[1][:, :], rc128[:, 0, :], nlg[:, NH + h:NH + h + 1], [rc128, nlg], Et[1])
                cx.op(dve, lambda: nc.vector.tensor_tensor(out=Et[0][:, :], in0=Et[0][:, :], in1=rc128[:, 1, :],
                                                           op=ALU.mult), reads=[Et[0], rc128], writes=[Et[0]])
                cx.op(dve, lambda: nc.vector.tensor_tensor(out=Et[1][:, :], in0=Et[1][:, :], in1=rc128[:, 2, :],
                                                           op=ALU.mult), reads=[Et[1], rc128], writes=[Et[1]])
                cx.op(dve, lambda: nc.vector.tensor_tensor(out=DT_[:, :], in0=Et[0][:, :], in1=Et[1][:, :],
                                                           op=ALU.add), reads=[Et[0], Et[1]], writes=[DT_])
                cx.op(dve, lambda: nc.vector.tensor_tensor(out=DT_[:, :], in0=DT_[:, :], in1=rc128[:, 3, :],
                                                           op=ALU.add), reads=[DT_, rc128], writes=[DT_])
                expcol(Gf[:, :], rcT[:, 0, :], lf, [rcT, lg], Gf)
                expcol(Gb[:, :], rcT[:, 1, :], lb, [rcT, lg], Gb)
                s.rotary(st, "q", pT[h, :, :], C, S, qr)
                cx.op(dve, lambda: nc.vector.tensor_tensor(out=qf[:, :], in0=qr[:, :], in1=Gf[:, :], op=ALU.mult),
                      reads=[qr, Gf], writes=[qf])
                cx.op(dve, lambda: nc.vector.tensor_tensor(out=qb[:, :], in0=qr[:, :], in1=Gb[:, :], op=ALU.mult),
                      reads=[qr, Gb], writes=[qb])
                cx.load(sp, kr, kr[:, :], rscr[h, :, :])
                cx.load(sp, vtm, vtm[:, :, :].rearrange("p c e -> p (c e)"), rscr_v[h, :, :])
                for di in range(2):
                    cx.load(sp, U[di], U[di][:, :, :].rearrange("p c e -> p (c e)"), rscr_u[h, di, :, :])
                for di in range(2):
                    Sx = Sst[di]
                    cx.op(dve, lambda: nc.vector.tensor_scalar(out=Sx[:, :], in0=ctxS[:, h, di, :],
                                                               scalar1=alpha[:, di:di + 1], scalar2=None,
                                                               op0=ALU.mult), reads=[ctxS, alpha], writes=[Sx])
                    src = exr[:, di, h, :]
                    cx.op(dve, lambda: nc.vector.scalar_tensor_tensor(out=Sx[:, :], in0=src,
                                                                      scalar=rankv[:, 2 + di:3 + di], in1=Sx[:, :],
                                                                      op0=ALU.mult, op1=ALU.add),
                          reads=[exr, rankv, Sx], writes=[Sx])
                    order = range(NCH) if di == 0 else range(NCH - 1, -1, -1)
                    for cc in order:
                        cx.op(act, lambda: nc.scalar.copy(out=Sbf[di][:, cc, :], in_=Sx[:, :]),
                              reads=[Sx], writes=[Sbf[di]])
                        cx.op(dve, lambda: nc.vector.scalar_tensor_tensor(
                            out=Sx[:, :], in0=Sx[:, :], scalar=dec[:, 2 + di:3 + di], in1=U[di][:, cc, :],
                            op0=ALU.mult, op1=ALU.add), reads=[Sx, U[di], dec], writes=[Sx])
                k2 = 0
                for g in range(0, NCH, 4):
                    n = min(4, NCH - g)
                    p = pss[k2 % 2]
                    k2 += 1
                    cx.prewait(pe, reads=[kr, qr], writes=[p])
                    for u in range(n):
                        cc = g + u
                        inst = nc.tensor.matmul(p[:, u, :], lhsT=kr[:, cc * 128:(cc + 1) * 128],
                                                rhs=qr[:, cc * 128:(cc + 1) * 128], start=True, stop=True)
                    cx.mark(pe.tag(inst), [kr, qr], [p])
                    cx.op(dve, lambda: nc.vector.tensor_tensor(
                        out=SD[:, g:g + n, :], in0=p[:, 0:n, :],
                        in1=DT_[:, :].unsqueeze(1).broadcast_to([128, n, 128]), op=ALU.mult),
                        reads=[p, DT_], writes=[SD])
                for g in range(0, NCH, 4):
                    n = min(4, NCH - g)
                    p = pss[k2 % 2]
                    k2 += 1
                    cx.prewait(pe, reads=[vtm, SD, Sbf[0], Sbf[1], qf, qb], writes=[p])
                    for u in range(n):
                        cc = g + u
                        sl = slice(cc * 128, (cc + 1) * 128)
                        nc.tensor.matmul(p[:, u, :], lhsT=vtm[:, cc, :], rhs=SD[:, cc, :], start=True, stop=False)
                        nc.tensor.matmul(p[:, u, :], lhsT=Sbf[0][:, cc, :], rhs=qf[:, sl], start=False, stop=False)
                        inst = nc.tensor.matmul(p[:, u, :], lhsT=Sbf[1][:, cc, :], rhs=qb[:, sl], start=False,
                                                stop=True)
                    cx.mark(pe.tag(inst), [vtm, SD, Sbf[0], Sbf[1], qf, qb], [p])
                    dsl = slice(g * 128, (g + n) * 128)
                    cx.op(act, lambda: nc.scalar.copy(out=o[:, dsl], in_=p[:, 0:n, :].rearrange("p u i -> p (u i)")),
                          reads=[p], writes=[o])
                    cx.op(act, lambda: nc.scalar.activation(out=osq[:, dsl],
                                                            in_=p[:, 0:n, :].rearrange("p u i -> p (u i)"),
                                                            func=AF.Square), reads=[p], writes=[osq])
                cx.load(sp, gf, gf[:, :], pT[3 * NH + h, :, :])
                cx.op(act, lambda: nc.scalar.activation(out=sg[:, :], in_=gf[:, :], func=AF.Silu),
                      reads=[gf], writes=[sg])
                for nb in range(NB):
                    bsl = slice(nb * BL, (nb + 1) * BL)
                    p = pss[k2 % 2]
                    k2 += 1
                    pv = p[:, :, :].rearrange("p u i -> p (u i)")[:, 0:BL]
                    cx.op(pe, lambda: nc.tensor.matmul(pv, lhsT=s.onesb[:, :], rhs=osq[:, bsl], start=True, stop=True),
                          reads=[osq, s.onesb], writes=[p])
                    cx.op(act, lambda: nc.scalar.activation(out=rs[:, bsl], in_=pv, func=AF.Sqrt,
                                                            bias=s.epsc[:, 0:1], scale=1.0 / c.HD),
                          reads=[p, s.epsc], writes=[rs])
                cx.op(dve, lambda: nc.vector.reciprocal(out=rs[:, :], in_=rs[:, :]), reads=[rs], writes=[rs])
                cx.op(dve, lambda: nc.vector.tensor_tensor(out=o[:, :], in0=o[:, :], in1=rs[:, :], op=ALU.mult),
                      reads=[o, rs], writes=[o])
                cx.op(dve, lambda: nc.vector.tensor_tensor(out=rout[:, :], in0=o[:, :], in1=sg[:, :], op=ALU.mult),
                      reads=[o, sg], writes=[rout])
                cx.store(sp, rout, ypT[h, :, :], rout[:, :])
            cx.barrier()

    def hyena(s):
        c = s.cfg
        nc, cx = s.nc, s.cx
        act, dve, pe, sp, pool = _acts(s)
        N, T, HC, HCT, NH = c.N, c.T, c.HC, c.HCT, c.NH
        ST = N // 128
        HWT = c.HW // 128
        C4 = 4 * HC
        zT_in = s.inp("zT", [33, N])
        fw1_in = s.inp("fw1", [33, 64])
        fw2_in = s.inp("fw2", [64, 64])
        fw3_in = s.inp("fw3", [64, 64])
        fbf_in = s.inp("fbf", [64, 6])
        w4_in = s.inp("w4my", [64, C4])
        delta_in = s.inp("deltarow", [128, HC])
        negt_in = s.inp("negt", [128, ST])
        hcw_in = s.inp("hcw", [128, 3, HCT, 3])
        hcb_in = s.inp("hcb", [128, 3, HCT])
        bias_in = s.inp("biasrow", [128, 2, HC])
        hfilt = s.dram("hfilt", [ST, 128, C4], BF16)
        Hspec = s.dram("Hspec", [2, ST, 128, C4], F32)
        Kspec = s.dram("Kspec", [2, 2, ST, 128, HC], F32)
        u32 = [s.dram("u32_%d" % i, [ST, 128, HC], F32) for i in range(3)]
        vbf = s.dram("vbf", [ST, 128, HC], BF16)
        z32 = s.dram("z32", [ST, 128, HC], F32)
        zbf = s.dram("zbf", [ST, 128, HC], BF16)
        Ysp = s.dram("Ysp", [2 * ST, 128, HC], BF16)
        yh = s.dram("yh", [HCT * 2 * 128, T], BF16)
        yhG = s.dram("yhG", [HCT * 2 * 2 * 128, T], BF16)
        TWO_PI = 2.0 * math.pi
        MAGIC = 12582912.0
        PI_LO = 3.1415925

        with ExitStack() as st:
            zT = s.sb(st, "zT", [33, N], F32)
            fw1 = s.sb(st, "fw1", [33, 64], F32)
            fw2 = s.sb(st, "fw2", [64, 64], F32)
            fw3 = s.sb(st, "fw3", [64, 64], F32)
            fbf = s.sb(st, "fbf", [64, 6], F32)
            fb = s.sb(st, "fbm", [64, 3], F32)
            w4 = s.sb(st, "w4", [64, C4], F32)
            delta = s.sb(st, "delta", [128, HC], F32)
            negt = s.sb(st, "negt", [128, ST], F32)
            for b_, i_ in ((zT, zT_in), (fw1, fw1_in), (fw2, fw2_in), (fw3, fw3_in), (fbf, fbf_in), (w4, w4_in),
                           (delta, delta_in), (negt, negt_in)):
                cx.load(sp, b_, b_[:, :], i_[:, :])
            cx.op(dve, lambda: nc.vector.tensor_tensor(out=fb[:, :], in0=fbf[:, 0:3], in1=fbf[:, 3:6], op=ALU.mult),
                  reads=[fbf], writes=[fb])
            a3 = s.sb(st, "a3", [64, N], F32)
            BL = min(512, N)
            cur = [s.sb(st, "fa%d" % i, [64, BL], F32) for i in range(2)]
            v_ = s.sb(st, "fv", [64, BL], F32)
            t_ = s.sb(st, "ft", [64, BL], F32)
            n_ = s.sb(st, "fn", [64, BL], F32)
            psm = [s.ps(st, "fps%d" % i, [64, BL]) for i in range(2)]
            km = 0
            for blk in range(N // BL):
                bsl = slice(blk * BL, (blk + 1) * BL)
                for layer in range(3):
                    p = psm[km % 2]
                    km += 1
                    if layer == 0:
                        cx.op(pe, lambda: nc.tensor.matmul(p[:, :], lhsT=fw1[:, :], rhs=zT[:, bsl], start=True,
                                                           stop=True), reads=[fw1, zT], writes=[p])
                    else:
                        w = fw2 if layer == 1 else fw3
                        src = cur[(layer - 1) % 2]
                        cx.op(pe, lambda: nc.tensor.matmul(p[:, :], lhsT=w[:, :], rhs=src[:, :], start=True,
                                                           stop=True), reads=[w, src], writes=[p])
                    cx.op(act, lambda: nc.scalar.activation(out=v_[:, :], in_=p[:, :], func=AF.Identity,
                                                            bias=fb[:, layer:layer + 1],
                                                            scale=fbf[:, 3 + layer:4 + layer]),
                          reads=[p, fb, fbf], writes=[v_])
                    cx.op(dve, lambda: nc.vector.tensor_scalar(out=t_[:, :], in0=v_[:, :], scalar1=1.0 / TWO_PI,
                                                               scalar2=MAGIC, op0=ALU.mult, op1=ALU.add),
                          reads=[v_], writes=[t_])
                    cx.op(dve, lambda: nc.vector.tensor_scalar(out=n_[:, :], in0=t_[:, :], scalar1=-MAGIC,
                                                               scalar2=None, op0=ALU.add), reads=[t_], writes=[n_])
                    cx.op(dve, lambda: nc.vector.scalar_tensor_tensor(out=t_[:, :], in0=n_[:, :], scalar=-TWO_PI,
                                                                      in1=v_[:, :], op0=ALU.mult, op1=ALU.add),
                          reads=[n_, v_], writes=[t_])
                    cx.op(dve, lambda: nc.vector.tensor_scalar(out=t_[:, :], in0=t_[:, :], scalar1=-PI_LO,
                                                               scalar2=PI_LO, op0=ALU.max, op1=ALU.min),
                          reads=[t_], writes=[t_])
                    dst = a3[:, bsl] if layer == 2 else cur[layer % 2][:, :]
                    dbuf = a3 if layer == 2 else cur[layer % 2]
                    cx.op(act, lambda: nc.scalar.activation(out=dst, in_=t_[:, :], func=AF.Sin),
                          reads=[t_], writes=[dbuf])
            nbk = C4 // 512 if C4 >= 512 else 1
            BW = min(512, C4)
            psf = [[s.ps(st, "hps%d_%d" % (i, j), [128, BW]) for j in range(nbk)] for i in range(1)]
            win = [s.sb(st, "win%d" % i, [128, HC], F32) for i in range(2)]
            fo = [s.sb(st, "fo%d" % i, [128, C4], BF16) for i in range(2)]
            for pt in range(ST):
                pp = psf[0]
                for j in range(nbk):
                    cx.op(pe, lambda: nc.tensor.matmul(pp[j][:, :], lhsT=a3[:, pt * 128:(pt + 1) * 128],
                                                       rhs=w4[:, j * BW:(j + 1) * BW], start=True, stop=True),
                          reads=[a3, w4], writes=[pp[j]])
                wn = win[pt % 2]
                cx.op(act, lambda: nc.scalar.activation(out=wn[:, :], in_=delta[:, :], func=AF.Exp,
                                                        scale=negt[:, pt:pt + 1]),
                      reads=[delta, negt], writes=[wn])
                f = fo[pt % 2]
                for g in range(4):
                    j = (g * HC) // BW
                    off = (g * HC) % BW
                    cx.op(dve, lambda: nc.vector.tensor_tensor(out=f[:, g * HC:(g + 1) * HC],
                                                               in0=pp[j][:, off:off + HC], in1=wn[:, :],
                                                               op=ALU.mult), reads=[pp[j], wn], writes=[f])
                if pt == 0:
                    for g in (1, 3):
                        cx.op(dve, lambda: nc.vector.memset(f[0:1, g * HC:(g + 1) * HC], 0.0), writes=[f])
                cx.store(sp, f, hfilt[pt, :, :], f[:, :])
            cx.barrier()

        if s.stop_after == "h1":
            return "stop"
        TK = min(1024, C4)
        SBF = min(512, TK)
        for run in range(C4 // TK):
            def epi_spec(st, state, mi, ft, extra):
                if state is None:
                    return dict(o=[s.sb(st, "hso%d" % i, [128, SBF], F32) for i in range(4)], k=[0])
                sbi, pp = extra
                for ri in range(2):
                    k = state["k"][0]
                    state["k"][0] += 1
                    o = state["o"][k % 4]
                    if ri == 0:
                        cx.op(act, lambda: nc.scalar.copy(out=o[:, :], in_=pp[ri][:, :]), reads=[pp[ri]], writes=[o])
                    else:
                        cx.op(dve, lambda: nc.vector.tensor_copy(out=o[:, :], in_=pp[ri][:, :]), reads=[pp[ri]],
                              writes=[o])
                    c0 = run * TK + sbi * SBF
                    cx.store(sp, o, Hspec[ri, ft, :, c0:c0 + SBF], o[:, :])
            s.linear("hs%d" % run, hfilt, ST, TK, run * TK, ["dftc", "dftsf"], list(range(ST)), SBF, epi_spec)

        if s.stop_after == "h2":
            return "stop"
        with ExitStack() as st:
            hr = [s.sb(st, "hr%d" % i, [128, 2 * HC], F32) for i in range(2)]
            hi = [s.sb(st, "hi%d" % i, [128, 2 * HC], F32) for i in range(2)]
            tq = [s.sb(st, "tq%d" % i, [128, HC], F32) for i in range(2)]
            kr_ = [s.sb(st, "kkr%d" % i, [128, HC], F32) for i in range(2)]
            ki_ = [s.sb(st, "kki%d" % i, [128, HC], F32) for i in range(2)]
            sc = 1.0 / N
            k = 0
            for o_ in range(2):
                for ft in range(ST):
                    a, b_ = hr[k % 2], hi[k % 2]
                    t = tq[k % 2]
                    kr, ki = kr_[k % 2], ki_[k % 2]
                    k += 1
                    cs = slice(o_ * 2 * HC, (o_ + 1) * 2 * HC)
                    cx.load(sp, a, a[:, :], Hspec[0, ft, :, cs])
                    cx.load(sp, b_, b_[:, :], Hspec[1, ft, :, cs])
                    cx.op(dve, lambda: nc.vector.tensor_scalar(out=t[:, :], in0=a[:, HC:2 * HC], scalar1=sc,
                                                               scalar2=None, op0=ALU.mult), reads=[a], writes=[t])
                    cx.op(dve, lambda: nc.vector.scalar_tensor_tensor(out=kr[:, :], in0=a[:, 0:HC], scalar=sc,
                                                                      in1=t[:, :], op0=ALU.mult, op1=ALU.add),
                          reads=[a, t], writes=[kr])
                    cx.op(dve, lambda: nc.vector.tensor_scalar(out=t[:, :], in0=b_[:, HC:2 * HC], scalar1=-sc,
                                                               scalar2=None, op0=ALU.mult), reads=[b_], writes=[t])
                    cx.op(dve, lambda: nc.vector.scalar_tensor_tensor(out=ki[:, :], in0=b_[:, 0:HC], scalar=sc,
                                                                      in1=t[:, :], op0=ALU.mult, op1=ALU.add),
                          reads=[b_, t], writes=[ki])
                    if ft == 0:
                        cx.op(dve, lambda: nc.vector.tensor_scalar(out=kr[0:1, :], in0=kr[0:1, :], scalar1=0.5,
                                                                   scalar2=None, op0=ALU.mult),
                              reads=[kr], writes=[kr])
                        cx.op(dve, lambda: nc.vector.tensor_scalar(out=t[0:1, :], in0=b_[0:1, HC:2 * HC],
                                                                   scalar1=0.5 * sc, scalar2=None, op0=ALU.mult),
                              reads=[b_], writes=[t])
                        cx.op(dve, lambda: nc.vector.scalar_tensor_tensor(out=ki[0:1, :], in0=b_[0:1, 0:HC],
                                                                          scalar=0.5 * sc, in1=t[0:1, :],
                                                                          op0=ALU.mult, op1=ALU.add),
                              reads=[b_, t], writes=[ki])
                    cx.store(sp, kr, Kspec[o_, 0, ft, :, :], kr[:, :])
                    cx.store(sp, ki, Kspec[o_, 1, ft, :, :], ki[:, :])
            cx.barrier()

        if s.stop_after == "h3":
            return "stop"
        sp.wait(s.hy_ev)
        hv = s.hyG.ap().rearrange("(a h i j p) t -> a h i j p t", j=2, a=3, h=2, i=HCT, p=128)
        with ExitStack() as st:
            hcw = s.sb(st, "hcw", [128, 3, HCT, 3], F32)
            hcb = s.sb(st, "hcb", [128, 3, HCT], F32)
            cx.load(sp, hcw, hcw[:, :, :, :], hcw_in[:, :, :, :])
            cx.load(sp, hcb, hcb[:, :, :], hcb_in[:, :, :])
            ppb = [s.sb(st, "ppb%d" % i, [128, N + 4], BF16) for i in range(2)]
            uu = [s.sb(st, "uu%d" % i, [128, N], F32) for i in range(2)]
            ot = [s.sb(st, "uot%d" % i, [128, ST, 128], F32) for i in range(2)]
            otb = s.sb(st, "uotb", [128, ST, 128], BF16)
            ptp = [s.ps(st, "utp%d" % i, [128, 4, 128]) for i in range(4)]
            for i in range(2):
                cx.op(dve, lambda: nc.vector.memset(ppb[i][:, 0:2], 0.0), writes=[ppb[i]])
                cx.op(dve, lambda: nc.vector.memset(ppb[i][:, N + 2:N + 4], 0.0), writes=[ppb[i]])
            k = 0
            kt = 0
            for part in range(3):
                for i in range(HCT):
                    pb = ppb[k % 2]
                    u = uu[k % 2]
                    o = ot[k % 2]
                    k += 1
                    cx.load(sp, pb, pb[:, 2:N + 2].rearrange("p (j t) -> p j t", j=2),
                            hv[part, bass.ds(s.rank_sp, 1), i, :, :, :].rearrange("o j p t -> p (o j) t"))
                    cx.op(act, lambda: nc.scalar.activation(out=u[:, :], in_=pb[:, 2:N + 2], func=AF.Identity,
                                                            bias=hcb[:, part, i:i + 1], scale=hcw[:, part, i, 1:2]),
                          reads=[pb, hcb, hcw], writes=[u])
                    cx.op(dve, lambda: nc.vector.scalar_tensor_tensor(out=u[:, :], in0=pb[:, 1:N + 1],
                                                                      scalar=hcw[:, part, i, 0:1], in1=u[:, :],
                                                                      op0=ALU.mult, op1=ALU.add),
                          reads=[pb, hcw, u], writes=[u])
                    cx.op(dve, lambda: nc.vector.scalar_tensor_tensor(out=u[:, :], in0=pb[:, 3:N + 3],
                                                                      scalar=hcw[:, part, i, 2:3], in1=u[:, :],
                                                                      op0=ALU.mult, op1=ALU.add),
                          reads=[pb, hcw, u], writes=[u])
                    for g in range(0, ST, 4):
                        p = ptp[kt % 4]
                        cx.prewait(pe, reads=[u, s.ident], writes=[p])
                        for q in range(4):
                            sti = g + q
                            inst = nc.tensor.transpose(out=p[:, q, :], in_=u[:, sti * 128:(sti + 1) * 128],
                                                       identity=s.ident[:, :])
                        cx.mark(pe.tag(inst), [u], [p])
                        if kt % 2 == 0:
                            cx.op(act, lambda: nc.scalar.copy(out=o[:, g:g + 4, :], in_=p[:, :, :]), reads=[p],
                                  writes=[o])
                        else:
                            cx.op(dve, lambda: nc.vector.tensor_copy(out=o[:, g:g + 4, :], in_=p[:, :, :]), reads=[p],
                                  writes=[o])
                        kt += 1
                    cx.store(sp, o, u32[part].ap().rearrange("s p c -> p s c")[:, :, i * 128:(i + 1) * 128],
                             o[:, :, :])
                    if part == 0:
                        cx.op(act, lambda: nc.scalar.copy(out=otb[:, :, :], in_=o[:, :, :]), reads=[o], writes=[otb])
                        cx.store(sp, otb, vbf.ap().rearrange("s p c -> p s c")[:, :, i * 128:(i + 1) * 128],
                                 otb[:, :, :])
            cx.barrier()

        if s.stop_after == "h4":
            return "stop"
        for order in range(2):
            src_bf = vbf if order == 0 else zbf
            src32 = u32[0] if order == 0 else z32
            gate32 = u32[1] if order == 0 else u32[2]

            def epi_fwd(st, state, mi, ft, extra):
                if state is None:
                    return dict(kr=[s.sb(st, "ekr%d" % i, [128, HC], F32) for i in range(3)],
                                ki=[s.sb(st, "eki%d" % i, [128, HC], F32) for i in range(3)],
                                t=[s.sb(st, "et%d" % i, [128, HC], F32) for i in range(4)],
                                y=[s.sb(st, "ey%d" % i, [128, HC], BF16) for i in range(4)])
                sbi, pp = extra
                kr, ki = state["kr"][mi % 3], state["ki"][mi % 3]
                t1, t2, t3, t4 = state["t"]
                yr, yi = state["y"][(mi % 2) * 2], state["y"][(mi % 2) * 2 + 1]
                xr, xi = pp[0], pp[1]
                V = nc.vector
                cx.op(dve, lambda: V.tensor_tensor(out=t1[:, :], in0=xr[:, :], in1=kr[:, :], op=ALU.mult),
                      reads=[xr, kr], writes=[t1])
                cx.op(dve, lambda: V.tensor_tensor(out=t2[:, :], in0=xi[:, :], in1=ki[:, :], op=ALU.mult),
                      reads=[xi, ki], writes=[t2])
                cx.op(dve, lambda: V.tensor_tensor(out=yr[:, :], in0=t1[:, :], in1=t2[:, :], op=ALU.subtract),
                      reads=[t1, t2], writes=[yr])
                cx.op(dve, lambda: V.tensor_tensor(out=t3[:, :], in0=xr[:, :], in1=ki[:, :], op=ALU.mult),
                      reads=[xr, ki], writes=[t3])
                cx.op(dve, lambda: V.tensor_tensor(out=t4[:, :], in0=xi[:, :], in1=kr[:, :], op=ALU.mult),
                      reads=[xi, kr], writes=[t4])
                cx.op(dve, lambda: V.tensor_tensor(out=yi[:, :], in0=t3[:, :], in1=t4[:, :], op=ALU.add),
                      reads=[t3, t4], writes=[yi])
                if ft == 0:
                    cx.op(dve, lambda: V.tensor_tensor(out=yr[0:1, :], in0=xr[0:1, :], in1=kr[0:1, :], op=ALU.mult),
                          reads=[xr, kr], writes=[yr])
                    cx.op(dve, lambda: V.tensor_tensor(out=yi[0:1, :], in0=xi[0:1, :], in1=ki[0:1, :], op=ALU.mult),
                          reads=[xi, ki], writes=[yi])
                cx.store(sp, yr, Ysp[ft, :, :], yr[:, :])
                cx.store(sp, yi, Ysp[ST + ft, :, :], yi[:, :])
            def pre_fwd(state, mi, ft):
                kr, ki = state["kr"][mi % 3], state["ki"][mi % 3]
                cx.load(sp, kr, kr[:, :], Kspec[order, 0, ft, :, :])
                cx.load(sp, ki, ki[:, :], Kspec[order, 1, ft, :, :])
            s.linear("hf%d" % order, src_bf, ST, HC, 0, ["dftc", "dftsf"], list(range(ST)), HC, epi_fwd, pre=pre_fwd)

            def epi_inv(st, state, mi, tt, extra):
                if state is None:
                    d = dict(a=[s.sb(st, "ia%d" % i, [128, HC], F32) for i in range(3)],
                             g=[s.sb(st, "ig%d" % i, [128, HC], F32) for i in range(3)],
                             w=[s.sb(st, "iw%d" % i, [128, HC], F32) for i in range(2)],
                             zb=[s.sb(st, "izb%d" % i, [128, HC], BF16) for i in range(2)],
                             bias=s.sb(st, "ibias", [128, 2, HC], F32), k=[0])
                    cx.load(sp, d["bias"], d["bias"][:, :, :], bias_in[:, :, :])
                    if order == 1:
                        d["yo"] = [s.sb(st, "iyo%d" % i, [128, N], BF16) for i in range(HCT)]
                        d["tp"] = [s.ps(st, "itp%d" % i, [128, 4, 128]) for i in range(2)]
                    return d
                sbi, pp = extra
                a, g, w = state["a"][mi % 3], state["g"][mi % 3], state["w"][mi % 2]
                zb = state["zb"][mi % 2]
                bias = state["bias"]
                V = nc.vector
                cx.op(dve, lambda: V.tensor_tensor(out=w[:, :], in0=a[:, :], in1=bias[:, order, :], op=ALU.mult),
                      reads=[a, bias], writes=[w])
                cx.op(dve, lambda: V.tensor_tensor(out=w[:, :], in0=pp[0][:, :], in1=w[:, :], op=ALU.add),
                      reads=[pp[0], w], writes=[w])
                cx.op(dve, lambda: V.tensor_tensor(out=w[:, :], in0=w[:, :], in1=g[:, :], op=ALU.mult),
                      reads=[w, g], writes=[w])
                if order == 0:
                    cx.store(sp, w, z32[tt, :, :], w[:, :])
                    cx.op(act, lambda: nc.scalar.copy(out=zb[:, :], in_=w[:, :]), reads=[w], writes=[zb])
                    cx.store(sp, zb, zbf[tt, :, :], zb[:, :])
                else:
                    p = state["tp"][mi % 2]
                    cx.prewait(pe, reads=[w, s.ident], writes=[p])
                    for i in range(HCT):
                        inst = nc.tensor.transpose(out=p[:, i, :], in_=w[:, i * 128:(i + 1) * 128],
                                                   identity=s.ident[:, :])
                    cx.mark(pe.tag(inst), [w], [p])
                    for i in range(HCT):
                        yo = state["yo"][i]
                        cx.op(act, lambda: nc.scalar.copy(out=yo[:, tt * 128:(tt + 1) * 128], in_=p[:, i, :]),
                              reads=[p], writes=[yo])
                    if mi == ST - 1:
                        for i in range(HCT):
                            for hn in range(2):
                                cx.store(sp, state["yo"][i], yh[(i * 2 + hn) * 128:(i * 2 + hn + 1) * 128, :],
                                         state["yo"][i][:, hn * T:(hn + 1) * T])
            def pre_inv(state, mi, tt):
                a, g = state["a"][mi % 3], state["g"][mi % 3]
                cx.load(sp, a, a[:, :], src32[tt, :, :])
                cx.load(sp, g, g[:, :], gate32[tt, :, :])
            s.linear("hi%d" % order, Ysp, 2 * ST, HC, 0, ["dfti"], list(range(ST)), HC, epi_inv, pre=pre_inv)

        if s.stop_after == "h6":
            return "stop"
        for m in range(2 * HCT):
            ev = cx.allgather(yh[m * 128:(m + 1) * 128, :], yhG[m * 256:(m + 1) * 256, :],
                              [[0, 1], [2, 3], [4, 5], [6, 7]])
        pool.wait(ev)
        gv = yhG.ap().rearrange("(i h j p) t -> i h j p t", i=HCT, h=2, j=2, p=128)
        for j in range(2):
            for i in range(HCT):
                cx.gdma(s.ypT[NH + j * HCT + i, :, :],
                        gv[i, bass.ds(s.rank_g, 1), j, :, :].rearrange("o p t -> p (o t)"))
        pool.wait((cx.gsem, cx.gsem.v))
        s.wprep_D()
        cx.barrier()


class Mixer1:
    def mixer1(s):
        c = s.cfg
        nc, cx = s.nc, s.cx
        act, dve, pe, sp, pool = _acts(s)
        T, DT, GT, GW = c.T, c.DT, c.GT, c.GRID_W
        ROWS = T // GW
        HR = 8
        HB = HR * GW
        hT = s.dram("h1T", [DT, 128, T], F32)
        dT = s.dram("d1T", [DT, 128, T], BF16)
        hal = s.dram("hal", [DT * 128, 2 * HB], F32)
        halG = s.dram("halG", [DT * 2 * 128, 2 * HB], F32)
        rcnt_in = s.inp("rcnt", [128, 4, T])
        psc_in = s.inp("pscT", [128, DT])
        rankv_in = s.inp("rankv", [128, 4])
        s.phase_norm(s.xT, hT, T, s.modT[1], 3, F32, "m1n")
        hv = hal.ap().rearrange("(k p) t -> k p t", p=128)
        ev = cx.gdma(hv[:, :, 0:HB], hT[:, :, 0:HB])
        ev = cx.gdma(hv[:, :, HB:2 * HB], hT[:, :, T - HB:T])
        pool.wait(ev)
        for dt in range(DT):
            ev = cx.allgather(hal[dt * 128:(dt + 1) * 128, :], halG[dt * 256:(dt + 1) * 256, :],
                              [[0, 1], [2, 3], [4, 5], [6, 7]])
        gv = halG.ap().rearrange("(k j p) t -> k j p t", j=2, p=128)
        ER, EC = ROWS + 2 * HR, GW + 16
        with ExitStack() as st:
            rcnt = s.sb(st, "rcnt", [128, 4, T], F32)
            psc = s.sb(st, "psc", [128, DT], F32)
            comb = s.sb(st, "comb", [128, DT], F32)
            rankv = s.sb(st, "rankv1", [128, 4], F32)
            cx.load(sp, rcnt, rcnt[:, :, :], rcnt_in[:, :, :])
            cx.load(sp, psc, psc[:, :], psc_in[:, :])
            cx.load(sp, rankv, rankv[:, :], rankv_in[:, :])
            m5 = s.modT[1].t[:, :, :].rearrange("p j i -> p (j i)")[:, 5 * DT:6 * DT]
            cx.op(dve, lambda: nc.vector.tensor_tensor(out=comb[:, :], in0=psc[:, :], in1=m5, op=ALU.mult),
                  reads=[psc, s.modT[1]], writes=[comb])
            s.comb = comb
            hf = [s.sb(st, "phf%d" % i, [128, T], F32) for i in range(2)]
            ht = [s.sb(st, "pht%d" % i, [128, HB], F32) for i in range(2)]
            hb_ = [s.sb(st, "phb%d" % i, [128, HB], F32) for i in range(2)]
            E = [s.sb(st, "pE%d" % i, [128, ER, EC], F32) for i in range(4)]
            mo = [s.sb(st, "pmo%d" % i, [128, T], F32) for i in range(2)]
            do = [s.sb(st, "pdo%d" % i, [128, T], BF16) for i in range(2)]
            for i in range(4):
                cx.op(dve, lambda: nc.vector.memset(E[i][:, :, :], 0.0), writes=[E[i]])
            sp.wait(ev)
            for dt in range(DT):
                on_pool = (dt % 3 == 2)
                eng = pool if on_pool else dve
                V = nc.gpsimd if on_pool else nc.vector
                gi = dt // GT
                nst = gi + 1
                a = hf[dt % 2]
                t_, b_ = ht[dt % 2], hb_[dt % 2]
                cx.load(sp, a, a[:, :], hT[dt, :, :])
                cx.load(sp, t_, t_[:, :], gv[dt, 0, :, HB:2 * HB])
                cx.load(sp, b_, b_[:, :], gv[dt, 1, :, 0:HB])
                e0, e1 = (E[2], E[3]) if on_pool else (E[0], E[1])
                cx.op(act, lambda: nc.scalar.copy(out=e0[:, HR:HR + ROWS, 8:8 + GW],
                                                  in_=a[:, :].rearrange("p (r c) -> p r c", c=GW)),
                      reads=[a], writes=[e0])
                cx.op(act, lambda: nc.scalar.activation(out=e0[:, 0:HR, 8:8 + GW],
                                                        in_=t_[:, :].rearrange("p (r c) -> p r c", c=GW),
                                                        func=AF.Copy, scale=rankv[:, 2:3]),
                      reads=[t_, rankv], writes=[e0])
                cx.op(act, lambda: nc.scalar.activation(out=e0[:, HR + ROWS:ER, 8:8 + GW],
                                                        in_=b_[:, :].rearrange("p (r c) -> p r c", c=GW),
                                                        func=AF.Copy, scale=rankv[:, 3:4]),
                      reads=[b_, rankv], writes=[e0])
                cur, nxt = e0, e1
                for k in range(nst):
                    if k == 0:
                        lo, hi, sa, sb_ = 1, EC, -1, 0
                    else:
                        sh = 1 << (k - 1)
                        lo, hi, sa, sb_ = sh, EC - sh, -sh, sh
                    cx.op(eng, lambda: V.tensor_tensor(out=nxt[:, :, lo:hi], in0=cur[:, :, lo + sa:hi + sa],
                                                       in1=cur[:, :, lo + sb_:hi + sb_], op=ALU.add),
                          reads=[cur], writes=[nxt])
                    cur, nxt = nxt, cur
                for k in range(nst):
                    if k == 0:
                        lo, hi, sa, sb_ = 1, ER, -1, 0
                    else:
                        sh = 1 << (k - 1)
                        lo, hi, sa, sb_ = sh, ER - sh, -sh, sh
                    cx.op(eng, lambda: V.tensor_tensor(out=nxt[:, lo:hi, :], in0=cur[:, lo + sa:hi + sa, :],
                                                       in1=cur[:, lo + sb_:hi + sb_, :], op=ALU.add),
                          reads=[cur], writes=[nxt])
                    cur, nxt = nxt, cur
                m = mo[dt % 2]
                d = do[dt % 2]
                cx.op(eng, lambda: V.tensor_tensor(out=m[:, :].rearrange("p (r c) -> p r c", c=GW),
                                                   in0=cur[:, HR:HR + ROWS, 8:8 + GW],
                                                   in1=rcnt[:, gi, :].rearrange("p (r c) -> p r c", c=GW),
                                                   op=ALU.mult), reads=[cur, rcnt], writes=[m])
                cx.op(eng, lambda: V.tensor_tensor(out=d[:, :], in0=m[:, :], in1=a[:, :], op=ALU.subtract),
                      reads=[m, a], writes=[d])
                cx.store(sp, d, dT[dt, :, :], d[:, :])
                if (2 * nst) % 2 == 1 or True:
                    cx.op(eng, lambda: V.memset(e0[:, :, :], 0.0), writes=[e0])
            cx.barrier()
            SB = min(512, T)

            NSBP = T // SB

            def epi_pool(st2, state, mi, mt, extra):
                if state is None:
                    return dict(xs=[s.sb(st2, "ppx%d" % i, [128, SB], F32) for i in range(2 * NSBP)],
                                xo=[s.sb(st2, "ppo%d" % i, [128, SB], F32) for i in range(3)], k=[0])
                sbi, pp = extra
                k = state["k"][0]
                state["k"][0] += 1
                xs = state["xs"][(mi % 2) * NSBP + sbi]
                xo = state["xo"][k % 3]
                dtt = s.cur_gi * GT + mt
                sl = slice(sbi * SB, (sbi + 1) * SB)
                cx.op(dve, lambda: nc.vector.scalar_tensor_tensor(out=xo[:, :], in0=pp[0][:, :],
                                                                  scalar=comb[:, dtt:dtt + 1], in1=xs[:, :],
                                                                  op0=ALU.mult, op1=ALU.add),
                      reads=[pp[0], xs, comb], writes=[xo])
                cx.store(sp, xo, s.xT[dtt, :, sl], xo[:, :])

            def pre_pool(state, mi, mt):
                dtt = s.cur_gi * GT + mt
                for sbi in range(NSBP):
                    xs = state["xs"][(mi % 2) * NSBP + sbi]
                    cx.load(sp, xs, xs[:, :], s.xT[dtt, :, sbi * SB:(sbi + 1) * SB])
            for gi in range(4):
                s.cur_gi = gi
                s.linear("pl%d" % gi, dT, GT, T, 0, ["poolw"], list(range(GT)), SB, epi_pool,
                         wrow0=gi * c.G, kt0=gi * GT, pre=pre_pool)


class Program(Phases, Mixer0, Mixer1):
    def __init__(s, cfg, stop_after=None):
        super().__init__(cfg, stop_after)
        s.scr = {}

    def build(s):
        c = s.cfg
        nc, cx = s.nc, s.cx
        stop = s.stop_after
        s.x_in = s.inp("x", [c.T, c.D])
        s.ctx_in = s.inp("ctx", [c.NCTX, c.D])
        s.out = nc.dram_tensor("out", [c.T, c.D], F32, kind="ExternalOutput")
        s.consts()
        s.wprep_begin()
        if stop in ("xin", "mod", "wprep"):
            if stop == "mod":
                s.phase_mod()
            if stop == "wprep":
                for nm in ("w1", "w3"):
                    s.wcast((nm, 0, 1), "%s_0_1" % nm, c.FT * 128, c.DT * 128)
                s.wflush()
                s.cx.sp.wait(s.W[("w3", 0, 1)][1])
            s.xT = s.dram("xT", [c.DT, 128, c.T], F32)
            s.phase_xin(s.x_in, s.xT, c.T)
            return s.finish(None)
        for nm in ("w1", "w3"):
            s.wcast((nm, 0, 1), "%s_0_1" % nm, c.FT * 128, c.DT * 128)
        s.wcast(("w2", 0, 1), "w2_0_1", c.DT * 128, c.FT * 128)
        s.phase_mod()
        s.wflush()
        s.wprep_B()

        xT = s.dram("xT", [c.DT, 128, c.T], F32)
        cT = s.dram("cT", [c.DT, 128, c.NCTX], F32)
        s.xT, s.cT = xT, cT
        s.phase_xin(s.x_in, xT, c.T)
        s.phase_xin(s.ctx_in, cT, c.NCTX)

        if stop == "ffn1":
            s.ffn("a", 0, 1, xT, c.T, s.modT[0], 0)
            return s.finish(None)
        s.ffn("a", 0, 1, xT, c.T, s.modT[0], 0, ctx=(cT, c.NCTX, s.modC))
        s.mixer0()
        if stop in ("mix0", "m0a", "m0b", "h1", "h2", "h3", "h4", "h6", "h7"):
            return s.finish(None)
        s.ffn("b", 0, 2, xT, c.T, s.modT[0], 6)
        s.ffn("d", 1, 1, xT, c.T, s.modT[1], 0)
        if stop == "ffn3":
            return s.finish(None)
        s.mixer1()
        if stop == "mix1":
            return s.finish(None)
        s.ffn("e", 1, 2, xT, c.T, s.modT[1], 6)
        gain_in = s.inp("gainT", [128, c.DT])
        gain = s.sb(s.es, "gain", [128, c.DT], F32)
        cx.load(cx.sp, gain, gain[:, :], gain_in[:, :])
        return s.finish(gain)

    def finish(s, gain):
        s.cx.sp.wait((s.wcc, s.wcc.v))
        s.cx.sp.wait((s.wsem, s.wsem.v))
        s.phase_out(s.xT, gain)
        s.es.close()
        return s.nc

    def wprep_B(s):
        c = s.cfg
        s.wcast("win", "w_in", c.PT * 128, c.DT * 128)
        s.wcast("wout", "w_out", c.DT * 128, c.DT * 128)
        s.wcast("dftc", "dft_c", c.N, c.N, BF16)
        s.wcast("dftsf", "dft_sf", c.N, c.N, BF16)
        s.wcast("dfti", "dft_i", c.N, 2 * c.N, BF16)
        s.wflush()

    def wprep_C(s):
        c = s.cfg
        for l, which in ((0, 2), (1, 1)):
            for nm in ("w1", "w3"):
                s.wcast((nm, l, which), "%s_%d_%d" % (nm, l, which), c.FT * 128, c.DT * 128)
            s.wcast(("w2", l, which), "w2_%d_%d" % (l, which), c.DT * 128, c.FT * 128)
        s.wflush()
        s.wlocal("poolw", "pool_w", 4 * c.G, c.G)

    def wprep_D(s):
        c = s.cfg
        for nm in ("w1", "w3"):
            s.wcast((nm, 1, 2), "%s_1_2" % nm, c.FT * 128, c.DT * 128)
        s.wcast(("w2", 1, 2), "w2_1_2", c.DT * 128, c.FT * 128)
        s.wflush()


def tile_w(W):
    K, M = W.shape
    KT, MT = K // 128, M // 128
    return np.ascontiguousarray(W.reshape(KT, 128, MT, 128).transpose(2, 1, 0, 3)).reshape(MT * 128, KT * 128)


def shard_rows(A, core, rows_p=None):
    n = A.shape[0] // NCORES
    if rows_p is None or rows_p == n:
        return np.ascontiguousarray(A[core * n:(core + 1) * n])
    npc = n // rows_p
    B = A.reshape(npc, NCORES, rows_p, A.shape[1])
    return np.ascontiguousarray(B[:, core]).reshape(n, A.shape[1])


def host_inputs(cfg, inp, needed, pieces):
    c = cfg
    f32 = np.float32
    maps = [dict() for _ in range(NCORES)]
    shared = {}

    def put_shard(name, A):
        if name not in needed:
            return
        for core in range(NCORES):
            maps[core][name] = shard_rows(A, core, pieces.get(name))

    def put_all(name, A):
        if name not in needed:
            return
        A = np.ascontiguousarray(A)
        for core in range(NCORES):
            maps[core][name] = A

    x = np.asarray(inp["x"], f32)
    ctx = np.asarray(inp["ctx"], f32)
    for core in range(NCORES):
        b, r = core // 2, core % 2
        maps[core]["x"] = np.ascontiguousarray(x[b, r * c.T:(r + 1) * c.T])
        maps[core]["ctx"] = np.ascontiguousarray(ctx[b])
    put_all("ident", np.eye(128, dtype=f32))
    cc = np.zeros((8, c.D), f32)
    cc[:4] = np.asarray(inp["c"], f32)
    cc[4] = np.asarray(inp["c_ctx"], f32)
    put_all("ccT", cc.reshape(8, c.DT, 128).transpose(2, 1, 0))
    for l in range(2):
        wm = np.asarray(inp["w_mod"][l], f32)
        bm = np.asarray(inp["b_mod"][l], f32)
        for core in range(NCORES):
            cols = slice(core * c.MODI * 128, (core + 1) * c.MODI * 128)
            w = wm[:, cols].reshape(c.DT, 128, c.MODI, 128).transpose(2, 1, 0, 3)
            maps[core]["wmod%d" % l] = np.ascontiguousarray(w).reshape(c.MODI, 128, c.DT * 128)
            maps[core]["bmod%d" % l] = np.ascontiguousarray(bm[cols].reshape(c.MODI, 128).T)
        ffn_w = {(1, "w1"): inp["ffn1_w1"], (1, "w3"): inp["ffn1_w3"], (1, "w2"): inp["ffn1_w2"],
                 (2, "w1"): inp["ffn2_w1"], (2, "w3"): inp["ffn2_w3"], (2, "w2"): inp["ffn2_w2"]}
        for which in (1, 2):
            for nm in ("w1", "w3", "w2"):
                key = "%s_%d_%d" % (nm, l, which)
                if key in needed:
                    put_shard(key, tile_w(np.asarray(ffn_w[(which, nm)][l], f32)))
    if "gainT" in needed:
        put_all("gainT", np.asarray(inp["final_gain"], f32).reshape(c.DT, 128).T)
    return maps, put_shard, put_all


_CACHE = {}


def run(cfg, inputs, stop_after=None):
    key = (cfg.D, cfg.DFF, cfg.N, cfg.NCTX, stop_after)
    if key not in _CACHE:
        P = Program(cfg, stop_after)
        P.build()
        print('BUILD: dsems', len(P.cx.all_dsems), 'engine tags', [(e.name, e.sem.v) for e in P.cx.engs], 'wcc', P.wcc.v, 'cc', P.cx.ccsem.v, flush=True)
        _CACHE[key] = P
    P = _CACHE[key]
    needed = set(P.inputs.keys())
    maps, put_shard, put_all = host_inputs(cfg, inputs, needed, P.wpieces)
    host_inputs_mixers(cfg, inputs, needed, maps, put_shard, put_all)
    for m in maps:
        missing = needed - set(m.keys())
        assert not missing, missing
        for k in list(m.keys()):
            if k not in needed:
                del m[k]
            else:
                shp, dt = P.inputs[k]
                assert tuple(m[k].shape) == tuple(shp), (k, m[k].shape, shp)
    res = run_bass_kernel_spmd(P.nc, maps, core_ids=list(range(NCORES)))
    out = np.zeros((cfg.B, cfg.N, cfg.D), np.float32)
    for core in range(NCORES):
        b, r = core // 2, core % 2
        out[b, r * cfg.T:(r + 1) * cfg.T] = res.results[core]["out"]
    return out


def host_inputs_mixers(cfg, inp, needed, maps, put_shard, put_all):
    c = cfg
    f32 = np.float32
    bf = ml_dtypes.bfloat16
    if "w_in" in needed:
        put_shard("w_in", tile_w(np.asarray(inp["ab_w_in"][0], f32)))
    if "w_out" in needed:
        put_shard("w_out", tile_w(np.asarray(inp["ab_w_out"][0], f32)))
    if "dft_c" in needed:
        N, N2 = c.N, 2 * c.N
        a = np.arange(N, dtype=np.int64)
        m = (a[:, None] * a[None, :]) % N2
        ang = m.astype(np.float64) * (2.0 * np.pi / N2)
        Tc = np.cos(ang)
        Sf = -np.sin(ang)
        sgn = np.where(a % 2 == 0, 1.0, -1.0)
        Sf[:, 0] = sgn
        Si = -np.sin(ang)
        Si[0, :] = sgn
        put_shard("dft_c", tile_w(Tc.astype(f32)).astype(bf))
        put_shard("dft_sf", tile_w(Sf.astype(f32)).astype(bf))
        put_shard("dft_i", tile_w(np.concatenate([Tc, Si], 0).astype(f32)).astype(bf))
        del ang, m, Tc, Sf, Si
    if "rotC" in needed:
        NH, HD, T = c.NH, c.HD, c.T
        nf = HD // 4
        inv = (f32(10000.0) ** (-np.arange(nf, dtype=f32) / f32(nf))).astype(f32)
        lg = np.asarray(inp["ret_log_decay"][0], f32).reshape(1, 2 * NH)
        put_all("lgrep", np.tile(lg, (128, 1)))
        jj = np.arange(128)[:, None]
        ii = np.arange(128)[None, :]
        rc = np.zeros((128, 4, 128), f32)
        rc[:, 0] = ii - jj
        rc[:, 1] = (ii > jj)
        rc[:, 2] = (jj > ii)
        rc[:, 3] = 2.0 * (ii == jj)
        put_all("rc128", rc)
        tl = np.arange(T) % 128
        rcT = np.zeros((128, 2, T), f32)
        rcT[:, 0] = tl + 1
        rcT[:, 1] = 128 - tl
        put_all("rcT", rcT)
        NCC = c.NCTX // 128
        p = np.arange(128)
        rcp = np.zeros((128, 2 + 2 * NCC), f32)
        rcp[:, 0] = 127 - p
        rcp[:, 1] = p
        for cc in range(NCC):
            rcp[:, 2 + cc] = c.NCTX - 1 - (cc * 128 + p)
            rcp[:, 2 + NCC + cc] = cc * 128 + p
        put_all("rcp", rcp)
        for core in range(NCORES):
            r = core % 2
            pos = r * T + np.arange(T)
            row = (pos // c.GRID_W).astype(f32)
            col = (pos % c.GRID_W).astype(f32)
            ang = np.concatenate([row[:, None] * inv, col[:, None] * inv], -1).astype(f32)
            cs, sn = np.cos(ang).astype(f32), np.sin(ang).astype(f32)
            maps[core]["rotC"] = np.ascontiguousarray(np.concatenate([cs.T, cs.T], 0))
            maps[core]["rotS"] = np.ascontiguousarray(np.concatenate([-sn.T, sn.T], 0))
            rv = np.zeros((128, 4), f32)
            rv[:, 0], rv[:, 1], rv[:, 2], rv[:, 3] = r * T, (1 - r) * T, r, 1 - r
            maps[core]["rankv"] = rv
    if "zT" in needed:
        N, HW, HC, HCT, ST = c.N, c.HW, c.HC, c.HCT, c.N // 128
        t = np.linspace(0.0, 1.0, N, dtype=f32)[:, None]
        bands = 16
        f = np.linspace(1e-4, bands - 1, bands, dtype=f32)[None, :]
        w = (f32(2.0 * math.pi) * np.arange(N, dtype=f32)[:, None] / f32(N)).astype(f32)
        z = np.concatenate([t, np.cos(f * w), -np.sin(f * w)], -1).astype(f32)
        put_all("zT", z.T)
        put_all("fw1", np.asarray(inp["hy_f_w1"][0], f32))
        put_all("fw2", np.asarray(inp["hy_f_w2"][0], f32))
        put_all("fw3", np.asarray(inp["hy_f_w3"][0], f32))
        fr = np.asarray(inp["hy_f_freq"][0], f32)
        put_all("fbf", np.stack([np.asarray(inp["hy_f_b1"][0], f32), np.asarray(inp["hy_f_b2"][0], f32),
                                 np.asarray(inp["hy_f_b3"][0], f32), fr[0], fr[1], fr[2]], 1))
        put_all("negt", -(t[:, 0].reshape(ST, 128).T))
        max_decay = math.log(1e-2) / 0.3
        min_decay = math.log(1e-2) / 1.5
        deltas = np.abs(np.linspace(min_decay, max_decay, HW, dtype=f32)).astype(f32)
        w4 = np.asarray(inp["hy_f_w4"][0], f32).reshape(64, 2, 2, HW)
        cw = np.asarray(inp["hy_conv_w"][0], f32).reshape(3, 3, HW)
        cb = np.asarray(inp["hy_conv_b"][0], f32).reshape(3, HW)
        hb = np.asarray(inp["hy_bias"][0], f32)
        for core in range(NCORES):
            r = core % 2
            sl = slice(r * HC, (r + 1) * HC)
            maps[core]["w4my"] = np.ascontiguousarray(w4[:, :, :, sl]).reshape(64, 4 * HC)
            maps[core]["deltarow"] = np.tile(deltas[sl][None, :], (128, 1))
            maps[core]["hcw"] = np.ascontiguousarray(cw[:, :, sl].reshape(3, 3, HCT, 128).transpose(3, 1, 2, 0))
            maps[core]["hcb"] = np.ascontiguousarray(cb[:, sl].reshape(3, HCT, 128).transpose(2, 0, 1))
            maps[core]["biasrow"] = np.tile(hb[:, sl][None, :, :], (128, 1, 1))
    host_inputs_mixer1(cfg, inp, needed, maps, put_shard, put_all)


def host_inputs_mixer1(cfg, inp, needed, maps, put_shard, put_all):
    c = cfg
    f32 = np.float32
    if "pool_w" in needed:
        pw = np.asarray(inp["pool_w"][0], f32)
        put_all("pool_w", np.concatenate([tile_w(pw[g]) for g in range(4)], 0))
    if "pscT" in needed:
        put_all("pscT", np.asarray(inp["pool_scale"][0], f32).reshape(c.DT, 128).T)
    if "rcnt" in needed:
        T, GW = c.T, c.GRID_W
        NR = c.N // GW
        for core in range(NCORES):
            r = core % 2
            pos = r * T + np.arange(T)
            row, col = pos // GW, pos % GW
            rc = np.zeros((4, T), f32)
            for gi, w in enumerate((2, 4, 8, 16)):
                lo, hi = -(w // 2), w - 1 - w // 2
                cr = np.minimum(row + hi, NR - 1) - np.maximum(row + lo, 0) + 1
                cc = np.minimum(col + hi, GW - 1) - np.maximum(col + lo, 0) + 1
                rc[gi] = 1.0 / (cr * cc).astype(f32)
            maps[core]["rcnt"] = np.tile(rc[None], (128, 1, 1))
            if "rankv" not in maps[core]:
                rv = np.zeros((128, 4), f32)
                rv[:, 0], rv[:, 1], rv[:, 2], rv[:, 3] = r * T, (1 - r) * T, r, 1 - r
                maps[core]["rankv"] = rv


def kernel(**inputs):
    return run(Cfg(), inputs)
```
